# Optimizing a Trainium2 kernel written in Bass

```python
import jax
import jax.numpy as jnp
from jax import lax
import numpy as np

D_MODEL = 2048
BATCH = 4
SEQ = 4096
DEPTH = 1

MIX_WIDTH = 2 * D_MODEL
FOURIER_WIDTH = MIX_WIDTH // 4
FOURIER_GROUPS = 8
FOURIER_GROUP_DIM = FOURIER_WIDTH // FOURIER_GROUPS
SSM_WIDTH = MIX_WIDTH - FOURIER_WIDTH
SSM_HEAD_DIM = 64
SSM_HEADS = SSM_WIDTH // SSM_HEAD_DIM
SSM_GROUPS = 8
HEADS_PER_GROUP = SSM_HEADS // SSM_GROUPS
SSM_STATE = 128
SSM_CHUNK = 128
SSM_CONV = 5
XBC_WIDTH = SSM_WIDTH + 2 * SSM_GROUPS * SSM_STATE
IN_PROJ_WIDTH = FOURIER_WIDTH + SSM_WIDTH + XBC_WIDTH + SSM_HEADS
FFN_DIM = ((8 * D_MODEL // 3 + 255) // 256) * 256
FFN_CONV = 3
NORM_EPS = 1e-5
DT_MIN = 1e-3
DT_MAX = 1e-1

kernel_name = "fnet_bissd_hymba_convffn_encoder"


def rms_norm(x, gain):
    xf = x.astype(jnp.float32)
    var = jnp.mean(xf * xf, axis=-1, keepdims=True)
    return (xf * lax.rsqrt(var + NORM_EPS) * gain.astype(jnp.float32)).astype(x.dtype)


def depthwise_conv(x, w, b):
    width = w.shape[0]
    pad = width // 2
    y = lax.conv_general_dilated(
        x, w[:, None, :].astype(x.dtype), window_strides=(1,), padding=((pad, pad),),
        dimension_numbers=("NWC", "WIO", "NWC"), feature_group_count=x.shape[-1])
    return y + b.astype(y.dtype)


def fourier_mixer(u, w_mix):
    b, s, _ = u.shape
    ug = u.reshape(b, s, FOURIER_GROUPS, FOURIER_GROUP_DIM).astype(jnp.float32)
    f = jnp.fft.fftn(ug, axes=(1, 3), norm="ortho").real
    out = jnp.einsum("bsgc,gcd->bsgd", f, w_mix.astype(jnp.float32))
    return out.reshape(b, s, FOURIER_WIDTH).astype(u.dtype)


def ssd_scan(x, dt, a, bmat, cmat):
    b, s, h, p = x.shape
    nc = s // SSM_CHUNK
    g, r, n = SSM_GROUPS, HEADS_PER_GROUP, SSM_STATE
    xd = (x * dt[..., None]).reshape(b, nc, SSM_CHUNK, g, r, p)
    adt = (dt * a).reshape(b, nc, SSM_CHUNK, g, r)
    bc = bmat.reshape(b, nc, SSM_CHUNK, g, n)
    cc = cmat.reshape(b, nc, SSM_CHUNK, g, n)
    a_cum = jnp.cumsum(adt, axis=2)
    a_cum_t = jnp.moveaxis(a_cum, 2, -1)
    seg = a_cum_t[..., :, None] - a_cum_t[..., None, :]
    lower = jnp.tril(jnp.ones((SSM_CHUNK, SSM_CHUNK), dtype=bool))
    decay = jnp.exp(jnp.where(lower, seg, -jnp.inf))
    cb = jnp.einsum("bclgn,bcsgn->bcgls", cc, bc)
    y_diag = jnp.einsum("bcgls,bcgrls,bcsgrp->bclgrp", cb, decay, xd)
    decay_to_end = jnp.exp(a_cum[:, :, -1:] - a_cum)
    states = jnp.einsum("bclgn,bclgr,bclgrp->bcgrpn", bc, decay_to_end, xd)
    chunk_decay = jnp.exp(a_cum[:, :, -1])

    def step(carry, inp):
        st, dec = inp
        return carry * dec[..., None, None] + st, carry

    init = jnp.zeros((b, g, r, p, n), jnp.float32)
    _, prev = lax.scan(step, init, (jnp.moveaxis(states, 1, 0), jnp.moveaxis(chunk_decay, 1, 0)))
    prev = jnp.moveaxis(prev, 0, 1)
    y_off = jnp.einsum("bclgn,bcgrpn,bclgr->bclgrp", cc, prev, jnp.exp(a_cum))
    return (y_diag + y_off).reshape(b, s, h, p)


def bidirectional_ssd_mixer(z, xbc, dt_raw, conv_w, conv_b, dt_bias_fwd, a_log_fwd,
                            dt_bias_bwd, a_log_bwd, d_skip, norm_w):
    b, s, _ = z.shape
    gn = SSM_GROUPS * SSM_STATE
    xbc = jax.nn.silu(depthwise_conv(xbc, conv_w, conv_b)).astype(jnp.float32)
    xs = xbc[..., :SSM_WIDTH].reshape(b, s, SSM_HEADS, SSM_HEAD_DIM)
    bm = xbc[..., SSM_WIDTH:SSM_WIDTH + gn].reshape(b, s, SSM_GROUPS, SSM_STATE)
    cm = xbc[..., SSM_WIDTH + gn:].reshape(b, s, SSM_GROUPS, SSM_STATE)
    dt_raw = dt_raw.astype(jnp.float32)
    dt_f = jax.nn.softplus(dt_raw + dt_bias_fwd.astype(jnp.float32))
    dt_b = jax.nn.softplus(dt_raw + dt_bias_bwd.astype(jnp.float32))
    a_f = -jnp.exp(a_log_fwd.astype(jnp.float32))
    a_b = -jnp.exp(a_log_bwd.astype(jnp.float32))
    flip = lambda t: jnp.flip(t, axis=1)
    y_f = ssd_scan(xs, dt_f, a_f, bm, cm)
    y_b = flip(ssd_scan(flip(xs), flip(dt_b), a_b, flip(bm), flip(cm)))
    y = y_f + y_b + d_skip.astype(jnp.float32)[:, None] * xs
    y = y.reshape(b, s, SSM_WIDTH) * jax.nn.silu(z.astype(jnp.float32))
    yg = y.reshape(b, s, SSM_GROUPS, SSM_WIDTH // SSM_GROUPS)
    yg = yg * lax.rsqrt(jnp.mean(yg * yg, axis=-1, keepdims=True) + NORM_EPS)
    y = yg.reshape(b, s, SSM_WIDTH) * norm_w.astype(jnp.float32)
    return y.astype(z.dtype)


def setup_inputs(seed: int = 0) -> dict:
    key = jax.random.key(seed)
    ks = jax.random.split(key, 20)
    L = DEPTH
    f32 = jnp.float32
    x = jax.random.normal(ks[0], (BATCH, SEQ, D_MODEL), f32)
    norm_mix_w = 1.0 + 0.02 * jax.random.normal(ks[1], (L, D_MODEL), f32)
    w_in = jax.random.normal(ks[2], (L, D_MODEL, IN_PROJ_WIDTH), f32) * D_MODEL ** -0.5
    fourier_w = jax.random.normal(ks[3], (L, FOURIER_GROUPS, FOURIER_GROUP_DIM, FOURIER_GROUP_DIM), f32) * FOURIER_GROUP_DIM ** -0.5
    ssm_conv_w = jax.random.normal(ks[4], (L, SSM_CONV, XBC_WIDTH), f32) * SSM_CONV ** -0.5
    ssm_conv_b = 0.01 * jax.random.normal(ks[5], (L, XBC_WIDTH), f32)
    u_f = jax.random.uniform(ks[6], (L, SSM_HEADS), f32)
    dt_f = jnp.exp(u_f * (np.log(DT_MAX) - np.log(DT_MIN)) + np.log(DT_MIN))
    dt_bias_fwd = dt_f + jnp.log(-jnp.expm1(-dt_f))
    u_b = jax.random.uniform(ks[7], (L, SSM_HEADS), f32)
    dt_b = jnp.exp(u_b * (np.log(DT_MAX) - np.log(DT_MIN)) + np.log(DT_MIN))
    dt_bias_bwd = dt_b + jnp.log(-jnp.expm1(-dt_b))
    a_log_fwd = jnp.log(jax.random.uniform(ks[8], (L, SSM_HEADS), f32, 1.0, 16.0))
    a_log_bwd = jnp.log(jax.random.uniform(ks[9], (L, SSM_HEADS), f32, 1.0, 16.0))
    ssm_d = 1.0 + 0.02 * jax.random.normal(ks[10], (L, SSM_HEADS), f32)
    ssm_norm_w = 1.0 + 0.02 * jax.random.normal(ks[11], (L, SSM_WIDTH), f32)
    w_out = jax.random.normal(ks[12], (L, MIX_WIDTH, D_MODEL), f32) * MIX_WIDTH ** -0.5
    norm_ffn_w = 1.0 + 0.02 * jax.random.normal(ks[13], (L, D_MODEL), f32)
    w_up = jax.random.normal(ks[14], (L, D_MODEL, 2 * FFN_DIM), f32) * D_MODEL ** -0.5
    ffn_conv_w = jax.random.normal(ks[15], (L, FFN_CONV, 2 * FFN_DIM), f32) * FFN_CONV ** -0.5
    ffn_conv_b = 0.01 * jax.random.normal(ks[16], (L, 2 * FFN_DIM), f32)
    w_down = jax.random.normal(ks[17], (L, FFN_DIM, D_MODEL), f32) * FFN_DIM ** -0.5
    norm_final_w = 1.0 + 0.02 * jax.random.normal(ks[18], (D_MODEL,), f32)
    return {"x": x, "norm_mix_w": norm_mix_w, "w_in": w_in, "fourier_w": fourier_w,
            "ssm_conv_w": ssm_conv_w, "ssm_conv_b": ssm_conv_b,
            "dt_bias_fwd": dt_bias_fwd, "a_log_fwd": a_log_fwd,
            "dt_bias_bwd": dt_bias_bwd, "a_log_bwd": a_log_bwd,
            "ssm_d": ssm_d, "ssm_norm_w": ssm_norm_w, "w_out": w_out,
            "norm_ffn_w": norm_ffn_w, "w_up": w_up, "ffn_conv_w": ffn_conv_w,
            "ffn_conv_b": ffn_conv_b, "w_down": w_down, "norm_final_w": norm_final_w}


def reference(x, norm_mix_w, w_in, fourier_w, ssm_conv_w, ssm_conv_b, dt_bias_fwd, a_log_fwd,
              dt_bias_bwd, a_log_bwd, ssm_d, ssm_norm_w, w_out, norm_ffn_w, w_up,
              ffn_conv_w, ffn_conv_b, w_down, norm_final_w):
    o_z = FOURIER_WIDTH
    o_x = o_z + SSM_WIDTH
    o_dt = o_x + XBC_WIDTH
    for l in range(DEPTH):
        h = rms_norm(x, norm_mix_w[l])
        proj = h @ w_in[l].astype(h.dtype)
        u = proj[..., :o_z]
        z = proj[..., o_z:o_x]
        xbc = proj[..., o_x:o_dt]
        dt_raw = proj[..., o_dt:]
        a_out = fourier_mixer(u, fourier_w[l])
        b_out = bidirectional_ssd_mixer(z, xbc, dt_raw, ssm_conv_w[l], ssm_conv_b[l],
                                        dt_bias_fwd[l], a_log_fwd[l], dt_bias_bwd[l],
                                        a_log_bwd[l], ssm_d[l], ssm_norm_w[l])
        mixed = jnp.concatenate([a_out, b_out.astype(a_out.dtype)], axis=-1)
        x = x + (mixed @ w_out[l].astype(mixed.dtype)).astype(x.dtype)
        h = rms_norm(x, norm_ffn_w[l])
        up = depthwise_conv(h @ w_up[l].astype(h.dtype), ffn_conv_w[l], ffn_conv_b[l])
        gate, val = up[..., :FFN_DIM], up[..., FFN_DIM:]
        x = x + ((jax.nn.silu(gate) * val) @ w_down[l].astype(up.dtype)).astype(x.dtype)
    return rms_norm(x, norm_final_w)
```

```python
import os
from contextlib import ExitStack
import numpy as np
import ml_dtypes
import concourse.bass as bass
import concourse.mybir as mybir
from concourse.bass_utils import run_bass_kernel_spmd

F32 = mybir.dt.float32
BF16 = mybir.dt.bfloat16
ALU = mybir.AluOpType
AF = mybir.ActivationFunctionType
AX = mybir.AxisListType

D = 2048
S = 4096
NLOC = 2176
NOWN = 2048
NCH_LOC = 17
FW = 1024
SW = 3072
XBC = 5120
INW = 9264
FFN = 5632
EPS = 1e-5
SAME_ENGINE_SYNC = True
ENGS = ("pe", "act", "dve", "pool", "sp")


class Buf:
    __slots__ = ("name", "lw", "rd", "dsem", "dcnt", "dbase")

    def __init__(self, name):
        self.name = name
        self.lw = None
        self.rd = []
        self.dsem = None
        self.dcnt = 0
        self.dbase = 0


class Op:
    __slots__ = ("fn", "waits", "signal", "dma_buf")

    def __init__(self, fn):
        self.fn = fn
        self.waits = []
        self.signal = False
        self.dma_buf = None


class SemPool:
    def __init__(self, nc, es, n_dma, n_eng):
        self.dma = [[es.enter_context(nc.semaphore(f"dq{i}")), 0] for i in range(n_dma)]
        self.eng = [es.enter_context(nc.semaphore(f"eq{i}")) for i in range(n_eng)]
        self.eng_next = 0
        self.free = list(range(n_dma))

    def take_eng(self):
        s = self.eng[self.eng_next]
        self.eng_next += 1
        return s


class Sched:
    def __init__(self, nc, pool):
        self.nc = nc
        self.pool = pool
        self.q = {e: [] for e in ENGS}
        self.seen_eng = {e: {} for e in ENGS}
        self.seen_dma = {e: {} for e in ENGS}
        self.bufs = []
        self.dma_bufs = []

    def buf(self, name):
        b = Buf(name)
        self.bufs.append(b)
        return b

    def bufs_n(self, name, n):
        return [self.buf(f"{name}{i}") for i in range(n)]

    def _add_dep(self, eng, op, dep):
        if dep[0] == "eng":
            _, e2, idx = dep
            if e2 == eng and (eng == "pe" or not SAME_ENGINE_SYNC):
                return
            if self.seen_eng[eng].get(e2, -1) >= idx:
                return
            self.seen_eng[eng][e2] = idx
            self.q[e2][idx].signal = True
            op.waits.append(dep)
        else:
            _, b, cnt = dep
            if self.seen_dma[eng].get(id(b), -1) >= cnt:
                return
            self.seen_dma[eng][id(b)] = cnt
            op.waits.append(dep)

    def op(self, eng, fn, reads=(), writes=()):
        o = Op(fn)
        idx = len(self.q[eng])
        for b in reads:
            if b.lw is not None:
                self._add_dep(eng, o, b.lw)
        for b in writes:
            if b.lw is not None:
                self._add_dep(eng, o, b.lw)
            for r in b.rd:
                self._add_dep(eng, o, r)
        me = ("eng", eng, idx)
        for b in reads:
            b.rd.append(me)
        for b in writes:
            b.lw = me
            b.rd = []
        self.q[eng].append(o)
        return o

    def _dsem(self, b):
        if b.dsem is None:
            i = self.pool.free.pop()
            b.dsem = i
            b.dbase = self.pool.dma[i][1]
            self.dma_bufs.append(b)
        return b

    def dma(self, eng, fn, src=None, dst=None):
        o = Op(fn)
        if src is not None and src.lw is not None:
            self._add_dep(eng, o, src.lw)
        if dst is not None:
            if dst.lw is not None:
                self._add_dep(eng, o, dst.lw)
            for r in dst.rd:
                self._add_dep(eng, o, r)
        b = dst if dst is not None else src
        self._dsem(b)
        b.dcnt += 1
        me = ("dma", b, b.dcnt)
        if dst is not None:
            dst.lw = me
            dst.rd = []
        else:
            src.rd.append(me)
        o.dma_buf = b
        self.q[eng].append(o)
        return o

    def emit(self):
        nc = self.nc
        pool = self.pool
        Sched.n_emit = getattr(Sched, "n_emit", -1) + 1
        if str(Sched.n_emit) in os.environ.get("KDBG_SKIP_EMITS", "").split(","):
            for b in self.dma_bufs:
                pool.free.append(b.dsem)
                b.dsem = None
            return
        last = {}
        for e in ENGS:
            if e == "sp":
                continue
            idxs = [i for i, o in enumerate(self.q[e]) if o.dma_buf is None]
            if idxs:
                last[e] = idxs[-1]
        for e2, idx in last.items():
            self.q[e2][idx].signal = True
        esem = {e: pool.take_eng() for e in ENGS if e != "sp"}
        val = {}
        for e in esem:
            c = 0
            for i, o in enumerate(self.q[e]):
                if o.signal:
                    c += 1
                val[(e, i)] = c
        dma_final = [(pool.dma[b.dsem][0], 16 * (b.dbase + b.dcnt)) for b in self.dma_bufs]

        def resolve(w):
            if w[0] == "eng":
                return esem[w[1]], val[(w[1], w[2])]
            return pool.dma[w[1].dsem][0], 16 * (w[1].dbase + w[2])

        with nc.Block() as block:
            decos = {"pe": block.tensor, "act": block.scalar, "dve": block.vector,
                     "pool": block.gpsimd, "sp": block.sync}
            for eng in ENGS:
                ops = self.q[eng]

                def body(e, ops=ops, eng=eng):
                    for o in ops:
                        for w in o.waits:
                            s, v = resolve(w)
                            e.wait_ge(s, v)
                        inst = o.fn(e)
                        if o.dma_buf is not None:
                            inst.then_inc(pool.dma[o.dma_buf.dsem][0], 16)
                        elif o.signal:
                            inst.then_inc(esem[eng], 1)
                    for e2, idx in last.items():
                        e.wait_ge(esem[e2], val[(e2, idx)])
                    for s, v in dma_final:
                        e.wait_ge(s, v)

                decos[eng](body)
        for b in self.dma_bufs:
            pool.dma[b.dsem][1] = b.dbase + b.dcnt
            pool.free.append(b.dsem)
            b.dsem = None


def build_program(stop_after=99, dump=None):
    Sched.n_emit = -1
    nc = bass.Bass("TRN2", target_bir_lowering=False)

    def din(name, shape, dt=F32):
        return nc.dram_tensor(name, list(shape), dt, kind="ExternalInput").ap()

    def dscr(name, shape, dt):
        kind = "ExternalOutput" if dump == name else "Internal"
        return nc.dram_tensor(name, list(shape), dt, kind=kind).ap()

    x_d = din("x", [S, D])
    g1_d = din("g1", [D])
    w_in_d = din("w_in", [D, INW])
    fw_d = din("fw", [8, 128, 128])
    ccs_d = din("ccs", [2, 128, 128], BF16)
    tab_d = din("tab", [2, 128, 32, NLOC], BF16)
    cw_d = din("cw", [128, 40, 5])
    cb_d = din("cb", [128, 40])
    ssp_d = din("ssp", [5, 48])
    nw_d = din("nw", [128, 24])
    cst_d = din("cst", [4, 128, 128])
    idb_d = din("idb", [128, 128], BF16)
    nmk_d = din("nmk", [2, 128, 128], BF16)
    w_out_d = din("w_out", [4096, D])
    g2_d = din("g2", [D])
    w_up_d = din("w_up", [D, 2 * FFN])
    fcw_d = din("fcw", [128, 88, 3])
    fcb_d = din("fcb", [128, 88])
    w_dn_d = din("w_dn", [FFN, D])
    g3_d = din("g3", [D])
    out_d = nc.dram_tensor("out", [NOWN, D], F32, kind="ExternalOutput").ap()

    uT_d = dscr("uT", [FW, S], BF16)
    zsT_d = dscr("zsT", [SW, NLOC], BF16)
    xT_d = dscr("xT", [SW, S], BF16)
    BT_d = dscr("BT", [1024, S], BF16)
    CT_d = dscr("CT", [1024, NLOC], BF16)
    V_d = dscr("V", [8, 128, 32, 256], BF16)
    mixT_d = dscr("mixT", [4096, NLOC], BF16)
    yb_d = dscr("yb", [NLOC, SW], F32)
    x1_d = dscr("x1", [NLOC, D], F32)
    h2T_d = dscr("h2T", [D, NLOC], BF16)
    actT_d = dscr("actT", [FFN, NOWN], BF16)
    x2_d = dscr("x2", [NOWN, D], F32)

    es = ExitStack()
    with es:
        es.enter_context(nc.allow_low_precision("bf16 matmul operands, fp32 accumulate"))
        es.enter_context(nc.allow_non_contiguous_dma("tiled layouts"))
        pool = SemPool(nc, es, 56, 40)

        uid = [0]

        def sb(st, name, shape, dt):
            uid[0] += 1
            return st.enter_context(nc.sbuf_tensor(f"s{uid[0]}_{name}", list(shape), dt))

        def ps(st, name, shape, dt=F32):
            uid[0] += 1
            return st.enter_context(nc.psum_tensor(f"p{uid[0]}_{name}", list(shape), dt))

        dtraw = sb(es, "dtraw", [128, 32, 48], F32)
        cst = sb(es, "cst", [128, 4, 128], F32)
        idb = sb(es, "idb", [128, 128], BF16)
        epsc = sb(es, "epsc", [128, 1], F32)
        Umat, Lmat, ones_f, id_f = (cst[:, i, :] for i in range(4))

        def rms_rows(sc, src, src_b, ssb, ss_ap, rstd_ap, gain, gain_b, hb, hb_b, junk, junk_b, eps_b=None):
            sc.op("act", lambda e: e.activation(out=junk[:], in_=src, func=AF.Square, accum_out=ss_ap),
                  reads=[src_b], writes=[junk_b, ssb])
            sc.op("act", lambda e: e.activation(out=rstd_ap, in_=ss_ap, func=AF.Sqrt, bias=epsc[:], scale=1.0 / D),
                  reads=[ssb] + ([eps_b] if eps_b is not None else []), writes=[ssb])
            sc.op("dve", lambda e: e.reciprocal(out=rstd_ap, in_=rstd_ap), reads=[ssb], writes=[ssb])
            sc.op("dve", lambda e: e.scalar_tensor_tensor(out=hb[:], in0=src, scalar=rstd_ap, in1=gain[:],
                                                          op0=ALU.mult, op1=ALU.mult),
                  reads=[src_b, ssb, gain_b], writes=[hb_b])

        if os.environ.get("KDBG_INIT"):
            with ExitStack() as st:
                sc = Sched(nc, pool)
                zt = sb(st, "zt", [128, S], BF16)
                zt_b = sc.buf("zt")
                sc.op("pool", lambda e: e.memset(zt[:], 0.5), writes=[zt_b])
                for t_, nm in ((dtraw, "a"), (epsc, "d")):
                    sc.op("pool", lambda e, t_=t_: e.memset(t_[:], 0.25), writes=[sc.buf(nm)])
                sc.dma("sp", lambda e: e.dma_start(out=cst[:], in_=cst_d.rearrange("c p f -> p c f")), dst=sc.buf("b"))
                sc.dma("sp", lambda e: e.dma_start(out=idb[:], in_=idb_d), dst=sc.buf("c"))
                for dten, rows, cols in ((xT_d, SW, S), (BT_d, 1024, S), (CT_d, 1024, NLOC), (zsT_d, SW, NLOC), (uT_d, FW, S)):
                    for r0 in range(0, rows, 128):
                        sc.dma("sp", lambda e, dten=dten, r0=r0, cols=cols: e.dma_start(out=dten[r0:r0 + 128, :], in_=zt[:, :cols]), src=zt_b)
                sc.emit()
        with ExitStack() as s12:
            hT = sb(s12, "hT", [128, 16, S], BF16)
            with ExitStack() as st:
                sc = Sched(nc, pool)
                xt = [sb(st, f"xt{i}", [128, D], F32) for i in range(2)]
                hb = [sb(st, f"hb{i}", [128, D], BF16) for i in range(2)]
                junk = sb(st, "junk", [128, D], BF16)
                gain = sb(st, "gain1", [128, D], F32)
                ss = sb(st, "ss", [128, 64], F32)
                pt = [ps(st, f"pt{i}", [128, 1024], BF16)[:, 0:512] for i in range(4)]
                xt_b, hb_b, pt_b = sc.bufs_n("xt", 2), sc.bufs_n("hb", 2), sc.bufs_n("pt", 4)
                junk_b, gain_b, cst_b, idb_b = sc.buf("junk"), sc.buf("gain"), sc.buf("cst"), sc.buf("idb")
                hT_b = sc.buf("hT")
                epsc_b = sc.buf("epsc")
                sc.op("pool", lambda e: e.memset(epsc[:], EPS), writes=[epsc_b])
                sc.dma("sp", lambda e: e.dma_start(out=gain[:], in_=g1_d.partition_broadcast(128)), dst=gain_b)
                sc.dma("sp", lambda e: e.dma_start(out=cst[:], in_=cst_d.rearrange("c p f -> p c f")), dst=cst_b)
                sc.dma("sp", lambda e: e.dma_start(out=idb[:], in_=idb_d), dst=idb_b)
                for i in range(32):
                    a = i % 2
                    ssb = sc.buf(f"ss{i}")
                    sc.dma("sp", lambda e, i=i, a=a: e.dma_start(out=xt[a][:], in_=x_d[i * 128:(i + 1) * 128, :]),
                           dst=xt_b[a])
                    rms_rows(sc, xt[a][:], xt_b[a], ssb, ss[:, 2 * i:2 * i + 1], ss[:, 2 * i + 1:2 * i + 2],
                             gain, gain_b, hb[a], hb_b[a], junk, junk_b, eps_b=epsc_b)
                    for q in range(4):
                        pi = (4 * i + q) % 4
                        for r in range(4):
                            k = 4 * q + r
                            sc.op("pe", lambda e, a=a, k=k, pi=pi, r=r: e.transpose(
                                out=pt[pi][:, r * 128:(r + 1) * 128], in_=hb[a][:, k * 128:(k + 1) * 128],
                                identity=idb[:]), reads=[hb_b[a], idb_b], writes=[pt_b[pi]])
                        sc.op("act", lambda e, i=i, q=q, pi=pi: e.activation(
                            out=hT[:, 4 * q:4 * q + 4, i * 128:(i + 1) * 128],
                            in_=pt[pi].rearrange("p (a b) -> p a b", a=4), func=AF.Copy),
                            reads=[pt_b[pi]], writes=[hT_b])
                sc.emit()
            if stop_after <= 1:
                return nc
            with ExitStack() as st:
                sc = Sched(nc, pool)
                wt = [sb(st, f"wt{i}", [128, 16, 128], BF16) for i in range(2)]
                stage = [sb(st, f"stage{i}", [128, S + 4], BF16) for i in range(2)]
                acc = sb(st, "acc", [128, S], F32)
                osb = [sb(st, f"osb{i}", [128, S], BF16) for i in range(2)]
                cw = sb(st, "cw", [128, 40, 5], F32)
                cb = sb(st, "cb", [128, 40], F32)
                pbank = [ps(st, f"pb{i}", [128, 512], F32) for i in range(8)]
                wt_b, stage_b, osb_b, pb_b = sc.bufs_n("wt", 2), sc.bufs_n("stage", 2), sc.bufs_n("osb", 2), sc.bufs_n("pb", 8)
                acc_b, cw_b, dtraw_b = sc.buf("acc"), sc.buf("cw"), sc.buf("dtraw")
                sc.dma("sp", lambda e: e.dma_start(out=cw[:], in_=cw_d), dst=cw_b)
                sc.dma("sp", lambda e: e.dma_start(out=cb[:], in_=cb_d), dst=cw_b)
                for i in range(2):
                    sc.op("pool", lambda e, i=i: e.memset(stage[i][:], 0.0), writes=[stage_b[i]])
                bank = 0
                w_in_v = w_in_d.rearrange("(k p) c -> p k c", p=128)
                NT = int(os.environ.get("KDBG_NT", "73"))
                for j in ([int(v) for v in os.environ["KDBG_TILES"].split(",")] if os.environ.get("KDBG_TILES") else (range(NT) if "KDBG_TILES" not in os.environ else [])):
                    a = j % 2
                    ncols = 128 if j < 72 else 48
                    sc.dma("pool", lambda e, j=j, a=a, ncols=ncols: e.dma_start(
                        out=wt[a][:, :, :ncols], in_=w_in_v[:, :, j * 128:j * 128 + ncols]), dst=wt_b[a])
                    if j == 72:
                        for i in range(32):
                            b_ = bank % 8
                            bank += 1
                            for k in range(16):
                                sc.op("pe", lambda e, a=a, k=k, i=i, b_=b_: e.matmul(
                                    pbank[b_][:, :48], lhsT=hT[:, k, i * 128:(i + 1) * 128], rhs=wt[a][:, k, :48],
                                    start=(k == 0), stop=(k == 15)), reads=[wt_b[a]], writes=[pb_b[b_]])
                            sc.op("act", lambda e, i=i, b_=b_: e.activation(out=dtraw[:, i, :], in_=pbank[b_][:, :48],
                                                                           func=AF.Copy),
                                  reads=[pb_b[b_]], writes=[dtraw_b])
                        continue
                    kind = "u" if j < 8 else "z" if j < 32 else "x" if j < 56 else "B" if j < 64 else "C"
                    if kind in ("u", "x", "B"):
                        blocks = [(b * 512, 512) for b in range(8)]
                    elif kind == "z":
                        blocks = [(b * 512, 512) for b in range(4)] + [(2048, 128)]
                    else:
                        blocks = [(b * 512, 512) for b in range(4)] + [(2048, 256)]
                    conv = kind in ("x", "B", "C")
                    sa = j % 2
                    oa = j % 2
                    for (t0, n) in blocks:
                        b_ = bank % 8
                        bank += 1
                        for k in range(16):
                            sc.op("pe", lambda e, a=a, k=k, t0=t0, n=n, b_=b_: e.matmul(
                                pbank[b_][:, :n], lhsT=wt[a][:, k, :], rhs=hT[:, k, t0:t0 + n],
                                start=(k == 0), stop=(k == 15)), reads=[wt_b[a]], writes=[pb_b[b_]])
                        if conv:
                            sc.op("act", lambda e, sa=sa, t0=t0, n=n, b_=b_: e.activation(
                                out=stage[sa][:, 2 + t0:2 + t0 + n], in_=pbank[b_][:, :n], func=AF.Copy),
                                reads=[pb_b[b_]], writes=[stage_b[sa]])
                        else:
                            fn = AF.Copy if kind == "u" else AF.Silu
                            sc.op("act", lambda e, oa=oa, t0=t0, n=n, b_=b_, fn=fn: e.activation(
                                out=osb[oa][:, t0:t0 + n], in_=pbank[b_][:, :n], func=fn),
                                reads=[pb_b[b_]], writes=[osb_b[oa]])
                    if kind == "u":
                        T, dst = S, uT_d[j * 128:(j + 1) * 128, :]
                    elif kind == "z":
                        T, dst = NLOC, zsT_d[(j - 8) * 128:(j - 7) * 128, :]
                    elif kind == "x":
                        T, dst = S, xT_d[(j - 32) * 128:(j - 31) * 128, :]
                    elif kind == "B":
                        T, dst = S, BT_d[(j - 56) * 128:(j - 55) * 128, :]
                    else:
                        T, dst = NLOC, CT_d[(j - 64) * 128:(j - 63) * 128, :]
                    if conv:
                        jj = j - 32
                        sc.op("dve", lambda e, sa=sa, jj=jj, T=T: e.tensor_scalar(
                            out=acc[:, :T], in0=stage[sa][:, 0:T], scalar1=cw[:, jj, 0:1], scalar2=cb[:, jj:jj + 1],
                            op0=ALU.mult, op1=ALU.add), reads=[stage_b[sa], cw_b], writes=[acc_b])
                        for tap in range(1, 5):
                            eng = "dve"
                            sc.op(eng, lambda e, sa=sa, jj=jj, T=T, tap=tap: e.scalar_tensor_tensor(
                                out=acc[:, :T], in0=stage[sa][:, tap:tap + T], scalar=cw[:, jj, tap:tap + 1],
                                in1=acc[:, :T], op0=ALU.mult, op1=ALU.add),
                                reads=[stage_b[sa], cw_b, acc_b], writes=[acc_b])
                        sc.op("act", lambda e, oa=oa, T=T: e.activation(out=osb[oa][:, :T], in_=acc[:, :T], func=AF.Silu),
                              reads=[acc_b], writes=[osb_b[oa]])
                    sc.dma("sp", lambda e, oa=oa, T=T, dst=dst: e.dma_start(out=dst, in_=osb[oa][:, :T]), src=osb_b[oa])
                sc.emit()
        if stop_after <= 2:
            return nc

        with ExitStack() as st:
            sc = Sched(nc, pool)
            wm = sb(st, "wm", [128, 8, 128], BF16)
            ccs = sb(st, "ccs", [128, 2, 128], BF16)
            Mg = sb(st, "Mg", [128, 8, 256], BF16)
            uTg = [sb(st, f"uTg{i}", [128, S], BF16) for i in range(2)]
            Vsb = [sb(st, f"Vsb{i}", [128, 32, 256], BF16) for i in range(2)]
            pM = [ps(st, f"pM{i}", [128, 512], F32) for i in range(4)]
            wm_b, ccs_b, Mg_b = sc.buf("wm"), sc.buf("ccs"), sc.buf("Mg")
            uTg_b, Vsb_b, pM_b = sc.bufs_n("uTg", 2), sc.bufs_n("Vsb", 2), sc.bufs_n("pM", 4)
            sc.dma("pool", lambda e: e.dma_start(out=wm[:], in_=fw_d.rearrange("g c d -> c g d")), dst=wm_b)
            sc.dma("sp", lambda e: e.dma_start(out=ccs[:], in_=ccs_d.rearrange("q c d -> c q d")), dst=ccs_b)
            for g in range(8):
                b_ = g % 4
                for q in range(2):
                    sc.op("pe", lambda e, g=g, q=q, b_=b_: e.matmul(pM[b_][:, q * 128:(q + 1) * 128], lhsT=ccs[:, q, :],
                                                                  rhs=wm[:, g, :], start=True, stop=True),
                          reads=[wm_b, ccs_b], writes=[pM_b[b_]])
                sc.op("act", lambda e, g=g, b_=b_: e.activation(out=Mg[:, g, :], in_=pM[b_][:, :256], func=AF.Copy),
                      reads=[pM_b[b_]], writes=[Mg_b])
            cnt = 0
            for g in range(8):
                a = g % 2
                sc.dma("sp", lambda e, g=g, a=a: e.dma_start(out=uTg[a][:], in_=uT_d[g * 128:(g + 1) * 128, :]),
                       dst=uTg_b[a])
                for stl in range(32):
                    b_ = cnt % 4
                    cnt += 1
                    sc.op("pe", lambda e, g=g, a=a, stl=stl, b_=b_: e.matmul(
                        pM[b_][:, :256], lhsT=uTg[a][:, stl * 128:(stl + 1) * 128], rhs=Mg[:, g, :],
                        start=True, stop=True), reads=[uTg_b[a], Mg_b], writes=[pM_b[b_]])
                    eng = "act" if stl % 2 == 0 else "dve"
                    if eng == "act":
                        sc.op("act", lambda e, a=a, stl=stl, b_=b_: e.activation(out=Vsb[a][:, stl, :], in_=pM[b_][:, :256],
                                                                              func=AF.Copy),
                              reads=[pM_b[b_]], writes=[Vsb_b[a]])
                    else:
                        sc.op("dve", lambda e, a=a, stl=stl, b_=b_: e.tensor_copy(out=Vsb[a][:, stl, :], in_=pM[b_][:, :256]),
                              reads=[pM_b[b_]], writes=[Vsb_b[a]])
                sc.dma("sp", lambda e, g=g, a=a: e.dma_start(out=V_d[g], in_=Vsb[a][:]), src=Vsb_b[a])
            sc.emit()
        if stop_after <= 3:
            return nc
        with ExitStack() as st:
            sc = Sched(nc, pool)
            tabs = [sb(st, f"tab{i}", [128, 2, 32, 512], BF16) for i in range(2)]
            Vg = [sb(st, f"Vg{i}", [128, 32, 256], BF16) for i in range(2)]
            aT = [sb(st, f"aT{i}", [128, 512], BF16) for i in range(2)]
            pF = [ps(st, f"pF{i}", [128, 512], F32) for i in range(4)]
            tab_b, Vg_b, aT_b, pF_b = sc.bufs_n("tab", 2), sc.bufs_n("Vg", 2), sc.bufs_n("aT", 2), sc.bufs_n("pF", 4)
            cnt = 0
            kblocks = [(b * 512, 512) for b in range(4)] + [(2048, 128)]
            for kb, (k0, n) in enumerate(kblocks):
                ta = kb % 2
                for q in range(2):
                    sc.dma("sp", lambda e, ta=ta, q=q, k0=k0, n=n: e.dma_start(
                        out=tabs[ta][:, q, :, :n], in_=tab_d[q][:, :, k0:k0 + n]),
                        dst=tab_b[ta])
                for g in range(8):
                    a = cnt % 2
                    b_ = cnt % 4
                    cnt += 1
                    sc.dma("sp", lambda e, g=g, a=a: e.dma_start(out=Vg[a][:], in_=V_d[g]), dst=Vg_b[a])
                    for stl in range(32):
                        for q in range(2):
                            sc.op("pe", lambda e, a=a, ta=ta, stl=stl, q=q, n=n, b_=b_: e.matmul(
                                pF[b_][:, :n], lhsT=Vg[a][:, stl, q * 128:(q + 1) * 128], rhs=tabs[ta][:, q, stl, :n],
                                start=(stl == 0 and q == 0), stop=(stl == 31 and q == 1)),
                                reads=[Vg_b[a], tab_b[ta]], writes=[pF_b[b_]])
                    sc.op("act", lambda e, a=a, n=n, b_=b_: e.activation(out=aT[a][:, :n], in_=pF[b_][:, :n], func=AF.Copy),
                          reads=[pF_b[b_]], writes=[aT_b[a]])
                    sc.dma("sp", lambda e, g=g, a=a, k0=k0, n=n: e.dma_start(out=mixT_d[g * 128:(g + 1) * 128, k0:k0 + n],
                                                                        in_=aT[a][:, :n]), src=aT_b[a])
            sc.emit()
        if stop_after <= 4:
            return nc

        with ExitStack() as s4:
            prm = sb(s4, "prm", [128, 5, 48], F32)
            dts = sb(s4, "dts", [128, 2, 32, 48], F32)
            adt = sb(s4, "adt", [128, 2, 32, 48], F32)
            nw = sb(s4, "nw", [128, 24], F32)
            nmk = sb(s4, "nmk", [128, 2, 128], BF16)
            stateT = sb(s4, "stateT", [128, SW], F32)
            prevb = sb(s4, "prevb", [128, SW], BF16)
            xTc = [sb(s4, f"xTc{i}", [128, 24, 128], BF16) for i in range(2)]
            BTc = [sb(s4, f"BTc{i}", [128, 8, 128], BF16) for i in range(2)]
            CTc = [sb(s4, f"CTc{i}", [128, 8, 128], BF16) for i in range(2)]
            xtok = sb(s4, "xtok", [128, SW], BF16)
            Btok = sb(s4, "Btok", [128, 1024], BF16)
            xd2 = [sb(s4, f"xd{i}", [128, SW], BF16) for i in range(2)]
            xdw2 = [sb(s4, f"xdw{i}", [128, SW], BF16) for i in range(2)]
            sm2 = sb(s4, "sm", [128, 2, 8, 48], F32)
            CBm = [sb(s4, f"CBm{i}", [128, 128], F32) for i in range(2)]
            Eb = [sb(s4, f"Eb{i}", [128, 128], F32) for i in range(3)]
            MT = [sb(s4, f"MT{i}", [128, 128], BF16) for i in range(3)]
            yoff = [sb(s4, f"yoff{i}", [128, 384], F32) for i in range(2)]
            ysb = sb(s4, "ysb", [128, SW], F32)
            ybc = sb(s4, "ybc", [128, SW], F32)
            zsc = sb(s4, "zsc", [128, 24, 128], BF16)
            yg = sb(s4, "yg", [128, 24, 128], F32)
            sq = sb(s4, "sq", [128, 24, 128], F32)
            rsg = [sb(s4, f"rsg{i}", [128, 128], F32) for i in range(2)]
            osb4 = sb(s4, "osb4", [128, 24, 128], BF16)
            pTr = [ps(s4, f"pTr{i}", [128, 1024], BF16)[:, 0:512] for i in range(2)]
            pA = [ps(s4, f"pA{i}", [128, 512], F32) for i in range(2)]
            pY = [ps(s4, f"pY{i}", [128, 512], F32) for i in range(2)]
            pO = [ps(s4, f"pO{i}", [128, 512], F32) for i in range(2)]

            for direction in ("b", "f"):
                sc = Sched(nc, pool)
                di = 1 if direction == "b" else 0
                sm = xd = xdw = None
                Tm = Lmat if direction == "b" else Umat
                prm_b, dts_b, adt_b, nw_b = sc.buf("prm"), sc.buf("dts"), sc.buf("adt"), sc.buf("nw")
                state_b = sc.bufs_n("state", 8)
                prev_b = sc.bufs_n("prev", 8)
                xTc_b, BTc_b, CTc_b = sc.bufs_n("xTc", 2), sc.bufs_n("BTc", 2), sc.bufs_n("CTc", 2)
                xtok_b, Btok_b = sc.buf("xtok"), sc.buf("Btok")
                xd_b2, xdw_b2, sm_b2 = sc.bufs_n("xd", 2), sc.bufs_n("xdw", 2), sc.bufs_n("sm", 2)
                CBm_b, Eb_b, MT_b, yoff_b = sc.bufs_n("CBm", 2), sc.bufs_n("Eb", 3), sc.bufs_n("MT", 3), sc.bufs_n("yoff", 2)
                ysb_g = sc.bufs_n("ysb", 8)
                ybc_b, zsc_b, yg_b, sq_b, osb4_b = sc.buf("ybc"), sc.buf("zsc"), sc.buf("yg"), sc.buf("sq"), sc.buf("osb4")
                rsg_b = sc.bufs_n("rsg", 2)
                pTr_b, pY_b, pO_b = sc.bufs_n("pTr", 2), sc.bufs_n("pY", 2), sc.bufs_n("pO", 2)
                pA_b = sc.bufs_n("pA", 2)
                pSm, pSm_b = pO[1][:, 384:512], pO_b[1]
                if direction == "b":
                    sc.dma("sp", lambda e, sm=sm, xd=xd, xdw=xdw: e.dma_start(out=prm[:].rearrange("p a h -> p (a h)"),
                                                       in_=ssp_d.rearrange("a h -> (a h)").partition_broadcast(128)), dst=prm_b)
                    sc.dma("sp", lambda e, sm=sm, xd=xd, xdw=xdw: e.dma_start(out=nw[:], in_=nw_d), dst=nw_b)
                    sc.dma("sp", lambda e: e.dma_start(out=nmk[:], in_=nmk_d.rearrange("q p f -> p q f")), dst=nw_b)
                    for d2 in range(2):
                        bias = prm[:, 2 * d2, :].unsqueeze(1).to_broadcast([128, 32, 48])
                        alog = prm[:, 2 * d2 + 1, :]
                        sc.op("dve", lambda e, d2=d2, bias=bias, sm=sm, xd=xd, xdw=xdw: e.tensor_tensor(out=dts[:, d2], in0=dtraw[:], in1=bias, op=ALU.add),
                              reads=[prm_b], writes=[dts_b])
                        sc.op("act", lambda e, d2=d2, sm=sm, xd=xd, xdw=xdw: e.activation(out=dts[:, d2], in_=dts[:, d2], func=AF.Exp),
                              reads=[dts_b], writes=[dts_b])
                        sc.op("act", lambda e, d2=d2, sm=sm, xd=xd, xdw=xdw: e.activation(out=dts[:, d2], in_=dts[:, d2], func=AF.Ln, bias=1.0),
                              reads=[dts_b], writes=[dts_b])
                        sc.op("act", lambda e, alog=alog, sm=sm, xd=xd, xdw=xdw: e.activation(out=alog, in_=alog, func=AF.Exp),
                              reads=[prm_b], writes=[prm_b])
                        sc.op("dve", lambda e, alog=alog, sm=sm, xd=xd, xdw=xdw: e.tensor_scalar(out=alog, in0=alog, scalar1=-1.0, scalar2=None, op0=ALU.mult),
                              reads=[prm_b], writes=[prm_b])
                        sc.op("dve", lambda e, d2=d2, alog=alog, sm=sm, xd=xd, xdw=xdw: e.tensor_tensor(
                            out=adt[:, d2], in0=dts[:, d2], in1=alog.unsqueeze(1).to_broadcast([128, 32, 48]), op=ALU.mult),
                            reads=[prm_b, dts_b], writes=[adt_b])
                sc.op("pool", lambda e, sm=sm, xd=xd, xdw=xdw: e.memset(stateT[:], 0.0), writes=state_b)
                sc.op("pool", lambda e, sm=sm, xd=xd, xdw=xdw: e.memset(prevb[:], 0.0), writes=prev_b)
                chunks = list(range(31, -1, -1)) if direction == "b" else list(range(NCH_LOC))
                if "KDBG4_LIST" in os.environ:
                    chunks = [int(v) for v in os.environ["KDBG4_LIST"].split(",")]
                if "KDBG4_CHUNKS" in os.environ:
                    chunks = chunks[:int(os.environ["KDBG4_CHUNKS"])]
                hcnt = 0
                gcnt = 0
                for ci, c in enumerate(chunks):
                    local = c < NCH_LOC and os.environ.get("KDBG4_LOCAL", "1") == "1"
                    a = ci % 2
                    t0 = c * 128
                    sm, sm_b = sm2[:, a], sm_b2[a]
                    xd, xd_b, xdw, xdw_b = xd2[a], xd_b2[a], xdw2[a], xdw_b2[a]
                    sc.dma("sp", lambda e, a=a, t0=t0, sm=sm, xd=xd, xdw=xdw: e.dma_start(
                        out=xTc[a][:], in_=xT_d.rearrange("(j p) t -> p j t", p=128)[:, :, t0:t0 + 128]), dst=xTc_b[a])
                    sc.dma("sp", lambda e, a=a, t0=t0, sm=sm, xd=xd, xdw=xdw: e.dma_start(
                        out=BTc[a][:], in_=BT_d.rearrange("(j p) t -> p j t", p=128)[:, :, t0:t0 + 128]), dst=BTc_b[a])
                    if local:
                        sc.dma("sp", lambda e, a=a, t0=t0, sm=sm, xd=xd, xdw=xdw: e.dma_start(
                            out=CTc[a][:], in_=CT_d.rearrange("(j p) t -> p j t", p=128)[:, :, t0:t0 + 128]), dst=CTc_b[a])
                    for q in range(8):
                        pi = q % 2
                        for r in range(4):
                            j = 4 * q + r
                            src = xTc[a][:, j, :] if j < 24 else BTc[a][:, j - 24, :]
                            sc.op("pe", lambda e, src=src, pi=pi, r=r, sm=sm, xd=xd, xdw=xdw: e.transpose(out=pTr[pi][:, r * 128:(r + 1) * 128],
                                                                                  in_=src, identity=idb[:]),
                                  reads=[xTc_b[a], BTc_b[a]], writes=[pTr_b[pi]])
                        if q < 6:
                            sc.op("act", lambda e, q=q, pi=pi, sm=sm, xd=xd, xdw=xdw: e.activation(out=xtok[:, q * 512:(q + 1) * 512], in_=pTr[pi],
                                                                           func=AF.Copy), reads=[pTr_b[pi]], writes=[xtok_b])
                        else:
                            sc.op("act", lambda e, q=q, pi=pi, sm=sm, xd=xd, xdw=xdw: e.activation(out=Btok[:, (q - 6) * 512:(q - 5) * 512], in_=pTr[pi],
                                                                           func=AF.Copy), reads=[pTr_b[pi]], writes=[Btok_b])
                    sc.op("pe", lambda e, c=c, Tm=Tm, di=di, sm=sm, xd=xd, xdw=xdw: e.matmul(pSm[:, 0:48], lhsT=Tm, rhs=adt[:, di, c, :], start=True, stop=True),
                          reads=[adt_b], writes=[pSm_b])
                    sc.op("pe", lambda e, c=c, di=di, sm=sm, xd=xd, xdw=xdw: e.matmul(pSm[:, 48:96], lhsT=ones_f, rhs=adt[:, di, c, :], start=True, stop=True),
                          reads=[adt_b], writes=[pSm_b])
                    sc.op("act", lambda e, sm=sm, xd=xd, xdw=xdw: e.activation(out=sm[:, 0:2, :].rearrange("p a h -> p (a h)"), in_=pSm[:, 0:96], func=AF.Copy),
                          reads=[pSm_b], writes=[sm_b])
                    sc.op("dve", lambda e, sm=sm, xd=xd, xdw=xdw: e.tensor_tensor(out=sm[:, 2, :], in0=sm[:, 1, :], in1=sm[:, 0, :], op=ALU.subtract),
                          reads=[sm_b], writes=[sm_b])
                    sc.op("act", lambda e, sm=sm, xd=xd, xdw=xdw: e.activation(out=sm[:, 3, :], in_=sm[:, 2, :], func=AF.Exp), reads=[sm_b], writes=[sm_b])
                    sc.op("act", lambda e, sm=sm, xd=xd, xdw=xdw: e.activation(out=sm[:, 6, :], in_=sm[:, 1, :], func=AF.Exp), reads=[sm_b], writes=[sm_b])
                    if local:
                        sc.op("act", lambda e, sm=sm, xd=xd, xdw=xdw: e.activation(out=sm[:, 4, :], in_=sm[:, 0, :], func=AF.Exp), reads=[sm_b], writes=[sm_b])
                        sc.op("dve", lambda e, sm=sm, xd=xd, xdw=xdw: e.tensor_scalar(out=sm[:, 5, :], in0=sm[:, 0, :], scalar1=-1.0, scalar2=None, op0=ALU.mult),
                              reads=[sm_b], writes=[sm_b])
                    sc.op("dve", lambda e, c=c, di=di, sm=sm, xd=xd, xdw=xdw: e.tensor_tensor(out=sm[:, 7, :], in0=dts[:, di, c, :], in1=sm[:, 3, :], op=ALU.mult),
                          reads=[sm_b, dts_b], writes=[sm_b])
                    x3 = xtok[:].rearrange("p (h q) -> p h q", q=64)
                    sc.op("pool", lambda e, x3=x3, sm=sm, xd=xd, xdw=xdw: e.tensor_tensor(
                        out=xdw[:].rearrange("p (h q) -> p h q", q=64), in0=x3,
                        in1=sm[:, 7, :].unsqueeze(2).to_broadcast([128, 48, 64]), op=ALU.mult),
                        reads=[xtok_b, sm_b], writes=[xdw_b])
                    if local:
                        sc.op("dve", lambda e, x3=x3, c=c, di=di, sm=sm, xd=xd, xdw=xdw: e.tensor_tensor(
                            out=xd[:].rearrange("p (h q) -> p h q", q=64), in0=x3,
                            in1=dts[:, di, c, :].unsqueeze(2).to_broadcast([128, 48, 64]), op=ALU.mult),
                            reads=[xtok_b, dts_b], writes=[xd_b])
                    for g in range(8):
                        gs = slice(g * 384, (g + 1) * 384)
                        ga = gcnt % 2
                        gcnt += 1
                        if local:
                            pa0 = pO[ga]
                            sc.op("pe", lambda e, a=a, g=g, pa0=pa0, sm=sm, xd=xd, xdw=xdw: e.matmul(pa0[:, 384:512], lhsT=BTc[a][:, g, :], rhs=CTc[a][:, g, :],
                                                                             start=True, stop=True),
                                  reads=[BTc_b[a], CTc_b[a]], writes=[pO_b[ga]])
                            sc.op("dve", lambda e, ga=ga, pa0=pa0: e.tensor_copy(out=CBm[ga][:], in_=pa0[:, 384:512]),
                                  reads=[pO_b[ga]], writes=[CBm_b[ga]])
                            for r in range(6):
                                h = 6 * g + r
                                ha = hcnt % 3
                                hcnt += 1
                                pb_ = hcnt % 2
                                sc.op("pe", lambda e, h=h, pb_=pb_, sm=sm, xd=xd, xdw=xdw: e.matmul(
                                    pA[pb_][:, 0:128], lhsT=sm[:, 0, h:h + 1].to_broadcast([128, 128]),
                                    rhs=id_f, start=True, stop=False), reads=[sm_b], writes=[pA_b[pb_]])
                                sc.op("pe", lambda e, pb_=pb_, di=di: e.matmul(pA[pb_][:, 0:128], lhsT=idb[:], rhs=nmk[:, di, :],
                                                                              start=False, stop=True), reads=[nw_b], writes=[pA_b[pb_]])
                                sc.op("act", lambda e, h=h, pb_=pb_, ha=ha, sm=sm, xd=xd, xdw=xdw: e.activation(
                                    out=Eb[ha][:], in_=pA[pb_][:, 0:128], func=AF.Exp,
                                    bias=sm[:, 5, h:h + 1], scale=1.0), reads=[pA_b[pb_], sm_b], writes=[Eb_b[ha]])
                                sc.op("dve", lambda e, ha=ha, ga=ga: e.tensor_tensor(
                                    out=MT[ha][:], in0=Eb[ha][:], in1=CBm[ga][:], op=ALU.mult),
                                    reads=[Eb_b[ha], CBm_b[ga]], writes=[MT_b[ha]])
                                sc.op("pe", lambda e, ha=ha, ga=ga, r=r, h=h, sm=sm, xd=xd, xdw=xdw: e.matmul(
                                    pY[ga][:, r * 64:(r + 1) * 64], lhsT=MT[ha][:], rhs=xd[:, h * 64:(h + 1) * 64],
                                    start=True, stop=True), reads=[MT_b[ha], xd_b], writes=[pY_b[ga]])
                            sc.op("pe", lambda e, a=a, g=g, ga=ga, gs=gs, sm=sm, xd=xd, xdw=xdw: e.matmul(pO[ga][:, 0:384], lhsT=CTc[a][:, g, :], rhs=prevb[:, gs],
                                                                                   start=True, stop=True),
                                  reads=[CTc_b[a], prev_b[g]], writes=[pO_b[ga]])
                            for r in range(6):
                                h = 6 * g + r
                                sc.op("act", lambda e, ga=ga, r=r, h=h, sm=sm, xd=xd, xdw=xdw: e.activation(
                                    out=yoff[ga][:, r * 64:(r + 1) * 64], in_=pO[ga][:, r * 64:(r + 1) * 64], func=AF.Copy,
                                    scale=sm[:, 4, h:h + 1]), reads=[pO_b[ga], sm_b], writes=[yoff_b[ga]])
                            sc.op("dve", lambda e, ga=ga, gs=gs, sm=sm, xd=xd, xdw=xdw: e.tensor_tensor(out=ysb[:, gs], in0=pY[ga][:, 0:384], in1=yoff[ga][:], op=ALU.add),
                                  reads=[pY_b[ga], yoff_b[ga]], writes=[ysb_g[g]])
                    for g in range(8):
                        gs = slice(g * 384, (g + 1) * 384)
                        ga = g % 2
                        sc.op("pe", lambda e, g=g, ga=ga, gs=gs, sm=sm, xd=xd, xdw=xdw: e.matmul(pO[ga][:, 0:384],
                                                                          lhsT=Btok[:, g * 128:(g + 1) * 128], rhs=xdw[:, gs],
                                                                          start=True, stop=True),
                              reads=[Btok_b, xdw_b], writes=[pO_b[ga]])
                        st3 = stateT[:, gs].rearrange("p (h q) -> p h q", q=64)
                        sc.op("pool", lambda e, g=g, st3=st3, sm=sm, xd=xd, xdw=xdw: e.tensor_tensor(
                            out=st3, in0=st3, in1=sm[:, 6, 6 * g:6 * g + 6].unsqueeze(2).to_broadcast([128, 6, 64]), op=ALU.mult),
                            reads=[sm_b, state_b[g]], writes=[state_b[g]])
                        sc.op("dve", lambda e, ga=ga, gs=gs, sm=sm, xd=xd, xdw=xdw: e.tensor_tensor(out=stateT[:, gs], in0=pO[ga][:, 0:384], in1=stateT[:, gs], op=ALU.add),
                              reads=[pO_b[ga], state_b[g]], writes=[state_b[g]])
                        sc.op("act", lambda e, gs=gs, sm=sm, xd=xd, xdw=xdw: e.activation(out=prevb[:, gs], in_=stateT[:, gs], func=AF.Copy),
                              reads=[state_b[g]], writes=[prev_b[g]])
                    if local and direction == "b":
                        for g in range(8):
                            gs = slice(g * 384, (g + 1) * 384)
                            sc.dma("sp", lambda e, t0=t0, gs=gs, sm=sm, xd=xd, xdw=xdw: e.dma_start(out=yb_d[t0:t0 + 128, gs], in_=ysb[:, gs]), src=ysb_g[g])
                    if direction == "f":
                        sc.dma("sp", lambda e, t0=t0, sm=sm, xd=xd, xdw=xdw: e.dma_start(out=ybc[:], in_=yb_d[t0:t0 + 128, :]), dst=ybc_b)
                        sc.dma("sp", lambda e, t0=t0, sm=sm, xd=xd, xdw=xdw: e.dma_start(
                            out=zsc[:], in_=zsT_d.rearrange("(j p) t -> p j t", p=128)[:, :, t0:t0 + 128]), dst=zsc_b)
                        sc.op("pool", lambda e, sm=sm, xd=xd, xdw=xdw: e.tensor_tensor(out=ybc[:], in0=ybc[:], in1=ysb[:], op=ALU.add),
                              reads=[ybc_b] + ysb_g, writes=[ybc_b])
                        sc.op("dve", lambda e, x3=x3, sm=sm, xd=xd, xdw=xdw: e.tensor_tensor(
                            out=xd[:].rearrange("p (h q) -> p h q", q=64), in0=x3,
                            in1=prm[:, 4, :].unsqueeze(2).to_broadcast([128, 48, 64]), op=ALU.mult),
                            reads=[xtok_b, prm_b, xd_b], writes=[xd_b])
                        sc.op("dve", lambda e, sm=sm, xd=xd, xdw=xdw: e.tensor_tensor(out=ybc[:], in0=ybc[:], in1=xd[:], op=ALU.add),
                              reads=[ybc_b, xd_b], writes=[ybc_b])
                        for q in range(6):
                            pi = q % 2
                            for r in range(4):
                                j = 4 * q + r
                                sc.op("pe", lambda e, j=j, pi=pi, r=r, sm=sm, xd=xd, xdw=xdw: e.transpose(out=pA[pi][:, r * 128:(r + 1) * 128],
                                                                                  in_=ybc[:, j * 128:(j + 1) * 128], identity=id_f),
                                      reads=[ybc_b], writes=[pA_b[pi]])
                            sc.op("dve", lambda e, q=q, pi=pi, sm=sm, xd=xd, xdw=xdw: e.tensor_tensor(
                                out=yg[:, 4 * q:4 * q + 4, :], in0=pA[pi][:].rearrange("p (a b) -> p a b", a=4),
                                in1=zsc[:, 4 * q:4 * q + 4, :], op=ALU.mult), reads=[pA_b[pi], zsc_b], writes=[yg_b])
                        sc.op("act", lambda e, sm=sm, xd=xd, xdw=xdw: e.activation(out=sq[:], in_=yg[:], func=AF.Square), reads=[yg_b], writes=[sq_b])
                        for g in range(8):
                            ga = g % 2
                            for jj in range(3):
                                sc.op("pe", lambda e, g=g, jj=jj, ga=ga, sm=sm, xd=xd, xdw=xdw: e.matmul(pY[ga][:, 0:128], lhsT=ones_f, rhs=sq[:, 3 * g + jj, :],
                                                                                 start=(jj == 0), stop=(jj == 2)),
                                      reads=[sq_b], writes=[pY_b[ga]])
                            sc.op("act", lambda e, ga=ga: e.activation(out=rsg[ga][:], in_=pY[ga][:, 0:128], func=AF.Sqrt, bias=epsc[:],
                                                                      scale=1.0 / 384.0), reads=[pY_b[ga]], writes=[rsg_b[ga]])
                            sc.op("dve", lambda e, ga=ga: e.reciprocal(out=rsg[ga][:], in_=rsg[ga][:]), reads=[rsg_b[ga]], writes=[rsg_b[ga]])
                            for jj in range(3):
                                j = 3 * g + jj
                                sc.op("dve", lambda e, j=j, ga=ga, sm=sm, xd=xd, xdw=xdw: e.scalar_tensor_tensor(
                                    out=osb4[:, j, :], in0=yg[:, j, :], scalar=nw[:, j:j + 1], in1=rsg[ga][:],
                                    op0=ALU.mult, op1=ALU.mult), reads=[yg_b, rsg_b[ga], nw_b], writes=[osb4_b])
                        sc.dma("sp", lambda e, t0=t0, sm=sm, xd=xd, xdw=xdw: e.dma_start(
                            out=mixT_d[1024:4096, :].rearrange("(j p) t -> p j t", p=128)[:, :, t0:t0 + 128], in_=osb4[:]), src=osb4_b)
                sc.emit()
                if direction == "b" and stop_after <= 5:
                    return nc
        if stop_after <= 6:
            return nc

        with ExitStack() as st:
            sc = Sched(nc, pool)
            wout = sb(st, "wout", [128, 32, D], BF16)
            mt = [sb(st, f"mt{i}", [128, 32, 128], BF16) for i in range(2)]
            xt = [sb(st, f"xt5{i}", [128, D], F32) for i in range(1)]
            wst = [sb(st, f"wst5{i}", [128, D], F32) for i in range(2)]
            x1 = sb(st, "x1s", [128, D], F32)
            hb = sb(st, "hb5", [128, D], BF16)
            junk = sb(st, "junk5", [128, D], BF16)
            h2s = [sb(st, f"h2s{i}", [128, 16, 128], BF16) for i in range(1)] * 2
            gain = sb(st, "gain2", [128, D], F32)
            ss = sb(st, "ss5", [128, 64], F32)
            pb = [ps(st, f"pb5{i}", [128, 512], F32) for i in range(4)]
            pt = [ps(st, f"pt5{i}", [128, 1024], BF16)[:, 0:512] for i in range(4)]
            wout_b = sc.bufs_n("wout", 32)
            wst_b = sc.bufs_n("wst", 2)
            mt_b, xt_b, h2s_b, pb_b, pt_b = sc.bufs_n("mt", 2), sc.bufs_n("xt", 1), sc.bufs_n("h2s", 1) * 2, sc.bufs_n("pb", 4), sc.bufs_n("pt", 4)
            x1_b, hb_b, junk_b, gain_b = sc.buf("x1"), sc.buf("hb"), sc.buf("junk"), sc.buf("gain")
            sc.dma("sp", lambda e: e.dma_start(out=gain[:], in_=g2_d.partition_broadcast(128)), dst=gain_b)
            for j in range(32):
                wa = j % 2
                sc.dma("sp", lambda e, j=j, wa=wa: e.dma_start(out=wst[wa][:], in_=w_out_d[j * 128:(j + 1) * 128, :]), dst=wst_b[wa])
                if j % 2 == 0:
                    sc.op("act", lambda e, j=j, wa=wa: e.activation(out=wout[:, j, :], in_=wst[wa][:], func=AF.Copy),
                          reads=[wst_b[wa]], writes=[wout_b[j]])
                else:
                    sc.op("pool", lambda e, j=j, wa=wa: e.tensor_copy(out=wout[:, j, :], in_=wst[wa][:]),
                          reads=[wst_b[wa]], writes=[wout_b[j]])
            for i in range(NCH_LOC):
                a = i % 2
                t0 = i * 128
                ssb = sc.buf(f"ss{i}")
                sc.dma("sp", lambda e, a=a, t0=t0: e.dma_start(
                    out=mt[a][:], in_=mixT_d.rearrange("(j p) t -> p j t", p=128)[:, :, t0:t0 + 128]), dst=mt_b[a])
                sc.dma("sp", lambda e, t0=t0: e.dma_start(out=xt[0][:], in_=x_d[t0:t0 + 128, :]), dst=xt_b[0])
                for j in range(32):
                    for db in range(4):
                        sc.op("pe", lambda e, a=a, db=db, j=j: e.matmul(pb[db][:], lhsT=mt[a][:, j, :], rhs=wout[:, j, db * 512:(db + 1) * 512],
                                                                       start=(j == 0), stop=(j == 31)),
                              reads=[mt_b[a], wout_b[j]], writes=[pb_b[db]])
                for db in range(4):
                    sc.op("dve", lambda e, db=db: e.tensor_tensor(out=x1[:, db * 512:(db + 1) * 512], in0=pb[db][:],
                                                                 in1=xt[0][:, db * 512:(db + 1) * 512], op=ALU.add),
                          reads=[pb_b[db], xt_b[0]], writes=[x1_b])
                rms_rows(sc, x1[:], x1_b, ssb, ss[:, 2 * i:2 * i + 1], ss[:, 2 * i + 1:2 * i + 2], gain, gain_b, hb, hb_b, junk, junk_b)
                sc.dma("sp", lambda e, t0=t0: e.dma_start(out=x1_d[t0:t0 + 128, :], in_=x1[:]), src=x1_b)
                for q in range(4):
                    pi = q
                    for r in range(4):
                        k = 4 * q + r
                        sc.op("pe", lambda e, k=k, pi=pi, r=r: e.transpose(out=pt[pi][:, r * 128:(r + 1) * 128],
                                                                          in_=hb[:, k * 128:(k + 1) * 128], identity=idb[:]),
                              reads=[hb_b], writes=[pt_b[pi]])
                    sc.op("act", lambda e, a=a, q=q, pi=pi: e.activation(out=h2s[a][:, 4 * q:4 * q + 4, :],
                                                                        in_=pt[pi].rearrange("p (a b) -> p a b", a=4), func=AF.Copy),
                          reads=[pt_b[pi]], writes=[h2s_b[a]])
                sc.dma("sp", lambda e, a=a, t0=t0: e.dma_start(
                    out=h2T_d.rearrange("(k p) t -> p k t", p=128)[:, :, t0:t0 + 128], in_=h2s[a][:]), src=h2s_b[a])
            sc.emit()
        if stop_after <= 7:
            return nc

        with ExitStack() as st:
            sc = Sched(nc, pool)
            h2r = sb(st, "h2r", [128, 16, NLOC], BF16)
            wup = [sb(st, f"wup{i}", [128, 16, 256], BF16) for i in range(2)]
            wst6 = [sb(st, f"wst6{i}", [128, 16, 128], F32) for i in range(4)]
            stg = [sb(st, f"stg{i}", [128, 2052], F32) for i in range(2)]
            accg = sb(st, "accg", [128, NOWN], F32)
            accv = sb(st, "accv", [128, NOWN], F32)
            sg = sb(st, "sg", [128, NOWN], F32)
            ao = [sb(st, f"ao{i}", [128, NOWN], BF16) for i in range(2)]
            fcw = sb(st, "fcw", [128, 88, 3], F32)
            fcb = sb(st, "fcb", [128, 88], F32)
            pb = [ps(st, f"pb6{i}", [128, 512], F32) for i in range(8)]
            h2r_b, fc_b = sc.buf("h2r"), sc.buf("fc")
            wup_b, stg_b, ao_b, pb_b = sc.bufs_n("wup", 4), sc.bufs_n("stg", 2), sc.bufs_n("ao", 2), sc.bufs_n("pb", 8)
            wst6_b = sc.bufs_n("wst6", 4)
            accg_b, accv_b, sg_b = sc.buf("accg"), sc.buf("accv"), sc.buf("sg")
            sc.dma("sp", lambda e: e.dma_start(out=h2r[:], in_=h2T_d.rearrange("(k p) t -> p k t", p=128)), dst=h2r_b)
            sc.dma("sp", lambda e: e.dma_start(out=fcw[:], in_=fcw_d), dst=fc_b)
            sc.dma("sp", lambda e: e.dma_start(out=fcb[:], in_=fcb_d), dst=fc_b)
            for i in range(2):
                sc.op("pool", lambda e, i=i: e.memset(stg[i][:], 0.0), writes=[stg_b[i]])
            wuv = w_up_d.rearrange("(k p) c -> p k c", p=128)
            bank = 0
            blocks = [(b * 512, 512) for b in range(4)] + [(2048, 1)]
            for f in range(44):
                a = f % 2
                for hv in range(2):
                    c0 = hv * FFN + f * 128
                    wa = 2 * a + hv
                    sc.dma("sp", lambda e, wa=wa, c0=c0: e.dma_start(out=wst6[wa][:], in_=wuv[:, :, c0:c0 + 128]), dst=wst6_b[wa])
                    if hv == 0:
                        sc.op("act", lambda e, a=a, hv=hv, wa=wa: e.activation(out=wup[a][:, :, hv * 128:(hv + 1) * 128], in_=wst6[wa][:],
                                                                              func=AF.Copy), reads=[wst6_b[wa]], writes=[wup_b[wa]])
                    else:
                        sc.op("pool", lambda e, a=a, hv=hv, wa=wa: e.tensor_copy(out=wup[a][:, :, hv * 128:(hv + 1) * 128], in_=wst6[wa][:]),
                              reads=[wst6_b[wa]], writes=[wup_b[wa]])
                for hv in range(2):
                    for (t0, n) in blocks:
                        b_ = bank % 8
                        bank += 1
                        for k in range(16):
                            sc.op("pe", lambda e, a=a, hv=hv, k=k, t0=t0, n=n, b_=b_: e.matmul(
                                pb[b_][:, :n], lhsT=wup[a][:, k, hv * 128:(hv + 1) * 128], rhs=h2r[:, k, t0:t0 + n],
                                start=(k == 0), stop=(k == 15)), reads=[wup_b[2 * a + hv], h2r_b], writes=[pb_b[b_]])
                        sc.op("act", lambda e, hv=hv, t0=t0, n=n, b_=b_: e.activation(out=stg[hv][:, 1 + t0:1 + t0 + n], in_=pb[b_][:, :n],
                                                                                     func=AF.Copy),
                              reads=[pb_b[b_]], writes=[stg_b[hv]])
                    jj = hv * 44 + f
                    eng = "dve" if hv == 0 else "pool"
                    acc_t, acc_tb = (accg, accg_b) if hv == 0 else (accv, accv_b)
                    sc.op(eng, lambda e, hv=hv, jj=jj, acc_t=acc_t: e.tensor_scalar(
                        out=acc_t[:], in0=stg[hv][:, 0:NOWN], scalar1=fcw[:, jj, 0:1], scalar2=fcb[:, jj:jj + 1],
                        op0=ALU.mult, op1=ALU.add), reads=[stg_b[hv], fc_b], writes=[acc_tb])
                    for tap in (1, 2):
                        sc.op("dve", lambda e, hv=hv, jj=jj, acc_t=acc_t, tap=tap: e.scalar_tensor_tensor(
                            out=acc_t[:], in0=stg[hv][:, tap:tap + NOWN], scalar=fcw[:, jj, tap:tap + 1], in1=acc_t[:],
                            op0=ALU.mult, op1=ALU.add), reads=[stg_b[hv], fc_b, acc_tb], writes=[acc_tb])
                sc.op("act", lambda e: e.activation(out=sg[:], in_=accg[:], func=AF.Silu), reads=[accg_b], writes=[sg_b])
                sc.op("dve", lambda e, a=a: e.tensor_tensor(out=ao[a][:], in0=sg[:], in1=accv[:], op=ALU.mult),
                      reads=[sg_b, accv_b], writes=[ao_b[a]])
                sc.dma("sp", lambda e, a=a, f=f: e.dma_start(out=actT_d[f * 128:(f + 1) * 128, :], in_=ao[a][:]), src=ao_b[a])
            sc.emit()
        if stop_after <= 8:
            return nc

        with ExitStack() as st:
            sc = Sched(nc, pool)
            wdn = [sb(st, f"wdn{i}", [128, 44, 512], BF16) for i in range(2)]
            wst7 = [sb(st, f"wst7{i}", [128, 4, 512], F32) for i in range(2)]
            at = [sb(st, f"at{i}", [128, 44, 128], BF16) for i in range(2)]
            x1t = [sb(st, f"x1t{i}", [128, 512], F32) for i in range(2)]
            x2s = [sb(st, f"x2s{i}", [128, 512], F32) for i in range(2)]
            junk = sb(st, "junk7", [128, 512], BF16)
            ssp = sb(st, "ssp", [128, 16, 4], F32)
            ss = sb(st, "ss7", [128, 32], F32)
            gain = sb(st, "gain3", [128, D], F32)
            xf = [sb(st, f"xf{i}", [128, D], F32) for i in range(2)]
            of = [sb(st, f"of{i}", [128, D], F32) for i in range(2)]
            pb = [ps(st, f"pb7{i}", [128, 512], F32) for i in range(4)]
            wdn_b, at_b, x1t_b, x2s_b, pb_b = sc.bufs_n("wdn", 22), sc.bufs_n("at", 2), sc.bufs_n("x1t", 2), sc.bufs_n("x2s", 2), sc.bufs_n("pb", 4)
            wst7_b = sc.bufs_n("wst7", 2)
            wcnt = 0
            junk_b, ssp_b, gain_b = sc.buf("junk"), sc.buf("ssp"), sc.buf("gain")
            xf_b, of_b = sc.bufs_n("xf", 2), sc.bufs_n("of", 2)
            sc.dma("sp", lambda e: e.dma_start(out=gain[:], in_=g3_d.partition_broadcast(128)), dst=gain_b)
            wdv = w_dn_d.rearrange("(j p) d -> p j d", p=128)
            cnt = 0
            for db in range(4):
                wa = db % 2
                for jg in range(11):
                    sa_ = wcnt % 2
                    wcnt += 1
                    sc.dma("sp", lambda e, sa_=sa_, jg=jg, db=db: e.dma_start(
                        out=wst7[sa_][:], in_=wdv[:, 4 * jg:4 * jg + 4, db * 512:(db + 1) * 512]), dst=wst7_b[sa_])
                    if jg % 2 == 0:
                        sc.op("act", lambda e, sa_=sa_, jg=jg, wa=wa: e.activation(out=wdn[wa][:, 4 * jg:4 * jg + 4, :], in_=wst7[sa_][:],
                                                                                 func=AF.Copy), reads=[wst7_b[sa_]], writes=[wdn_b[wa * 11 + jg]])
                    else:
                        sc.op("pool", lambda e, sa_=sa_, jg=jg, wa=wa: e.tensor_copy(out=wdn[wa][:, 4 * jg:4 * jg + 4, :], in_=wst7[sa_][:]),
                              reads=[wst7_b[sa_]], writes=[wdn_b[wa * 11 + jg]])
                for i in range(16):
                    a = cnt % 2
                    b_ = cnt % 4
                    cnt += 1
                    t0 = i * 128
                    sc.dma("sp", lambda e, a=a, t0=t0: e.dma_start(
                        out=at[a][:], in_=actT_d.rearrange("(j p) t -> p j t", p=128)[:, :, t0:t0 + 128]), dst=at_b[a])
                    sc.dma("sp", lambda e, a=a, t0=t0, db=db: e.dma_start(out=x1t[a][:], in_=x1_d[t0:t0 + 128, db * 512:(db + 1) * 512]),
                           dst=x1t_b[a])
                    for j in range(44):
                        sc.op("pe", lambda e, a=a, wa=wa, j=j, b_=b_: e.matmul(pb[b_][:], lhsT=at[a][:, j, :], rhs=wdn[wa][:, j, :],
                                                                              start=(j == 0), stop=(j == 43)),
                              reads=[at_b[a], wdn_b[wa * 11 + j // 4]], writes=[pb_b[b_]])
                    sc.op("dve", lambda e, a=a, b_=b_: e.tensor_tensor(out=x2s[a][:], in0=pb[b_][:], in1=x1t[a][:], op=ALU.add),
                          reads=[pb_b[b_], x1t_b[a]], writes=[x2s_b[a]])
                    sc.op("act", lambda e, a=a, i=i, db=db: e.activation(out=junk[:], in_=x2s[a][:], func=AF.Square,
                                                                        accum_out=ssp[:, i, db:db + 1]),
                          reads=[x2s_b[a]], writes=[junk_b, ssp_b])
                    sc.dma("sp", lambda e, a=a, t0=t0, db=db: e.dma_start(out=x2_d[t0:t0 + 128, db * 512:(db + 1) * 512], in_=x2s[a][:]),
                           src=x2s_b[a])
            sc.emit()
            sc = Sched(nc, pool)
            xf_b, of_b, ssp_b, gain_b = sc.bufs_n("xf", 2), sc.bufs_n("of", 2), sc.buf("ssp"), sc.buf("gain")
            for i in range(16):
                a = i % 2
                t0 = i * 128
                sc.dma("sp", lambda e, a=a, t0=t0: e.dma_start(out=xf[a][:], in_=x2_d[t0:t0 + 128, :]), dst=xf_b[a])
                sc.op("dve", lambda e, i=i: e.tensor_reduce(out=ss[:, 2 * i:2 * i + 1], in_=ssp[:, i, :], axis=AX.X, op=ALU.add),
                      reads=[ssp_b], writes=[ssp_b])
                sc.op("act", lambda e, i=i: e.activation(out=ss[:, 2 * i + 1:2 * i + 2], in_=ss[:, 2 * i:2 * i + 1], func=AF.Sqrt,
                                                        bias=epsc[:], scale=1.0 / D), reads=[ssp_b], writes=[ssp_b])
                sc.op("dve", lambda e, i=i: e.reciprocal(out=ss[:, 2 * i + 1:2 * i + 2], in_=ss[:, 2 * i + 1:2 * i + 2]),
                      reads=[ssp_b], writes=[ssp_b])
                sc.op("dve", lambda e, a=a, i=i: e.scalar_tensor_tensor(out=of[a][:], in0=xf[a][:], scalar=ss[:, 2 * i + 1:2 * i + 2],
                                                                    in1=gain[:], op0=ALU.mult, op1=ALU.mult),
                      reads=[xf_b[a], ssp_b, gain_b], writes=[of_b[a]])
                sc.dma("sp", lambda e, a=a, t0=t0: e.dma_start(out=out_d[t0:t0 + 128, :], in_=of[a][:]), src=of_b[a])
            sc.emit()
    return nc


_CONST = {}


def _consts():
    if _CONST:
        return _CONST
    bf = ml_dtypes.bfloat16
    c = np.arange(128, dtype=np.float64)
    ang = 2.0 * np.pi * np.outer(c, c) / 128.0
    sc = 1.0 / np.sqrt(float(S) * 128.0)
    _CONST["ccs"] = np.stack([np.cos(ang) * sc, np.sin(ang) * sc]).astype(np.float32).astype(bf)
    s = np.arange(S, dtype=np.int64)[:, None]
    k = np.arange(NLOC, dtype=np.int64)[None, :]
    tabs = []
    for flip in (0, 1):
        prod = ((s + flip) * (k + flip)) % S
        a = 2.0 * np.pi * prod.astype(np.float64) / S
        t2 = np.stack([np.cos(a), -np.sin(a)]).astype(np.float32).astype(bf)
        tabs.append(np.ascontiguousarray(t2.reshape(2, 32, 128, NLOC).transpose(0, 2, 1, 3)))
    _CONST["tab"] = tabs
    i = np.arange(128)
    U = (i[:, None] <= i[None, :]).astype(np.float32)
    L = (i[:, None] >= i[None, :]).astype(np.float32)
    _CONST["cst"] = np.stack([U, L, np.ones((128, 128), np.float32), np.eye(128, dtype=np.float32)])
    _CONST["idb"] = np.eye(128, dtype=np.float32).astype(bf)
    NEG = np.float32(-1.0e5)
    mf = np.where(i[None, :] < i[:, None], NEG, np.float32(0))
    mb = np.where(i[None, :] > i[:, None], NEG, np.float32(0))
    _CONST["nmk"] = np.stack([mf, mb]).astype(np.float32).astype(bf)
    return _CONST


def _in_maps(inp):
    cs = _consts()
    f = lambda a: np.ascontiguousarray(np.asarray(a, dtype=np.float32))
    x = f(inp["x"])
    w_in, w_out, w_up, w_dn = f(inp["w_in"][0]), f(inp["w_out"][0]), f(inp["w_up"][0]), f(inp["w_down"][0])
    fw = f(inp["fourier_w"][0])
    g1, g2, g3 = f(inp["norm_mix_w"][0]), f(inp["norm_ffn_w"][0]), f(inp["norm_final_w"])
    cw, cb = f(inp["ssm_conv_w"][0]), f(inp["ssm_conv_b"][0])
    fcw, fcb = f(inp["ffn_conv_w"][0]), f(inp["ffn_conv_b"][0])
    nw = f(inp["ssm_norm_w"][0]).reshape(24, 128).T.copy()
    cb_l = cb.reshape(40, 128).T.copy()
    fcb_l = fcb.reshape(88, 128).T.copy()
    prm = [f(inp[k][0]) for k in ("dt_bias_fwd", "a_log_fwd", "dt_bias_bwd", "a_log_bwd", "ssm_d")]
    maps = []
    for c in range(8):
        b, hf = c // 2, c % 2
        if hf == 0:
            xc, cwc, fcwc = x[b], cw, fcw
            ssp = np.stack([prm[0], prm[1], prm[2], prm[3], prm[4]])
        else:
            xc, cwc, fcwc = np.ascontiguousarray(x[b, ::-1]), cw[::-1], fcw[::-1]
            ssp = np.stack([prm[2], prm[3], prm[0], prm[1], prm[4]])
        cw_l = np.ascontiguousarray(cwc.reshape(5, 40, 128).transpose(2, 1, 0))
        fcw_l = np.ascontiguousarray(fcwc.reshape(3, 88, 128).transpose(2, 1, 0))
        maps.append({"x": xc, "g1": g1, "w_in": w_in, "fw": fw, "ccs": cs["ccs"], "tab": cs["tab"][hf],
                     "cw": cw_l, "cb": cb_l, "ssp": np.ascontiguousarray(ssp), "nw": nw, "cst": cs["cst"],
                     "idb": cs["idb"], "nmk": cs["nmk"], "w_out": w_out, "g2": g2, "w_up": w_up, "fcw": fcw_l, "fcb": fcb_l,
                     "w_dn": w_dn, "g3": g3})
    return maps


_NC = {}


def kernel(**inputs):
    if "nc" not in _NC:
        _NC["nc"] = build_program()
    maps = _in_maps(inputs)
    res = run_bass_kernel_spmd(_NC["nc"], maps, core_ids=list(range(8)))
    out = np.empty((4, S, D), np.float32)
    for c in range(8):
        b, hf = c // 2, c % 2
        o = np.asarray(res.results[c]["out"], dtype=np.float32)
        if hf == 0:
            out[b, :NOWN] = o
        else:
            out[b, NOWN:] = o[::-1]
    return out
```

```python
import os
from contextlib import ExitStack
import numpy as np
import ml_dtypes
import concourse.bass as bass
import concourse.mybir as mybir
from concourse.bass_utils import run_bass_kernel_spmd

F32 = mybir.dt.float32
BF16 = mybir.dt.bfloat16
ALU = mybir.AluOpType
AF = mybir.ActivationFunctionType
AX = mybir.AxisListType

D = 2048
S = 4096
NLOC = 2176
NOWN = 2048
NCH_LOC = 17
FW = 1024
SW = 3072
XBC = 5120
INW = 9264
FFN = 5632
EPS = 1e-5
SAME_ENGINE_SYNC = True
ENGS = ("pe", "act", "dve", "pool", "sp")


class Buf:
    __slots__ = ("name", "lw", "rd", "dsem", "dcnt", "dbase")

    def __init__(self, name):
        self.name = name
        self.lw = None
        self.rd = []
        self.dsem = None
        self.dcnt = 0
        self.dbase = 0


class Op:
    __slots__ = ("fn", "waits", "signal", "dma_buf")

    def __init__(self, fn):
        self.fn = fn
        self.waits = []
        self.signal = False
        self.dma_buf = None


class SemPool:
    def __init__(self, nc, es, n_dma, n_eng):
        self.dma = [[es.enter_context(nc.semaphore(f"dq{i}")), 0] for i in range(n_dma)]
        self.eng = [es.enter_context(nc.semaphore(f"eq{i}")) for i in range(n_eng)]
        self.eng_next = 0
        self.free = list(range(n_dma))

    def take_eng(self):
        s = self.eng[self.eng_next]
        self.eng_next += 1
        return s


class Sched:
    def __init__(self, nc, pool):
        self.nc = nc
        self.pool = pool
        self.q = {e: [] for e in ENGS}
        self.seen_eng = {e: {} for e in ENGS}
        self.seen_dma = {e: {} for e in ENGS}
        self.bufs = []
        self.dma_bufs = []

    def buf(self, name):
        b = Buf(name)
        self.bufs.append(b)
        return b

    def bufs_n(self, name, n):
        return [self.buf(f"{name}{i}") for i in range(n)]

    def _add_dep(self, eng, op, dep):
        if dep[0] == "eng":
            _, e2, idx = dep
            if e2 == eng and (eng == "pe" or not SAME_ENGINE_SYNC):
                return
            if self.seen_eng[eng].get(e2, -1) >= idx:
                return
            self.seen_eng[eng][e2] = idx
            self.q[e2][idx].signal = True
            op.waits.append(dep)
        else:
            _, b, cnt = dep
            if self.seen_dma[eng].get(id(b), -1) >= cnt:
                return
            self.seen_dma[eng][id(b)] = cnt
            op.waits.append(dep)

    def op(self, eng, fn, reads=(), writes=()):
        o = Op(fn)
        idx = len(self.q[eng])
        for b in reads:
            if b.lw is not None:
                self._add_dep(eng, o, b.lw)
        for b in writes:
            if b.lw is not None:
                self._add_dep(eng, o, b.lw)
            for r in b.rd:
                self._add_dep(eng, o, r)
        me = ("eng", eng, idx)
        for b in reads:
            b.rd.append(me)
        for b in writes:
            b.lw = me
            b.rd = []
        self.q[eng].append(o)
        return o

    def _dsem(self, b):
        if b.dsem is None:
            i = self.pool.free.pop()
            b.dsem = i
            b.dbase = self.pool.dma[i][1]
            self.dma_bufs.append(b)
        return b

    def dma(self, eng, fn, src=None, dst=None):
        o = Op(fn)
        if src is not None and src.lw is not None:
            self._add_dep(eng, o, src.lw)
        if dst is not None:
            if dst.lw is not None:
                self._add_dep(eng, o, dst.lw)
            for r in dst.rd:
                self._add_dep(eng, o, r)
        b = dst if dst is not None else src
        self._dsem(b)
        b.dcnt += 1
        me = ("dma", b, b.dcnt)
        if dst is not None:
            dst.lw = me
            dst.rd = []
        else:
            src.rd.append(me)
        o.dma_buf = b
        self.q[eng].append(o)
        return o

    def emit(self):
        nc = self.nc
        pool = self.pool
        Sched.n_emit = getattr(Sched, "n_emit", -1) + 1
        if str(Sched.n_emit) in os.environ.get("KDBG_SKIP_EMITS", "").split(","):
            for b in self.dma_bufs:
                pool.free.append(b.dsem)
                b.dsem = None
            return
        last = {}
        for e in ENGS:
            if e == "sp":
                continue
            idxs = [i for i, o in enumerate(self.q[e]) if o.dma_buf is None]
            if idxs:
                last[e] = idxs[-1]
        for e2, idx in last.items():
            self.q[e2][idx].signal = True
        esem = {e: pool.take_eng() for e in ENGS if e != "sp"}
        val = {}
        for e in esem:
            c = 0
            for i, o in enumerate(self.q[e]):
                if o.signal:
                    c += 1
                val[(e, i)] = c
        dma_final = [(pool.dma[b.dsem][0], 16 * (b.dbase + b.dcnt)) for b in self.dma_bufs]

        def resolve(w):
            if w[0] == "eng":
                return esem[w[1]], val[(w[1], w[2])]
            return pool.dma[w[1].dsem][0], 16 * (w[1].dbase + w[2])

        with nc.Block() as block:
            decos = {"pe": block.tensor, "act": block.scalar, "dve": block.vector,
                     "pool": block.gpsimd, "sp": block.sync}
            for eng in ENGS:
                ops = self.q[eng]

                def body(e, ops=ops, eng=eng):
                    for o in ops:
                        for w in o.waits:
                            s, v = resolve(w)
                            e.wait_ge(s, v)
                        inst = o.fn(e)
                        if o.dma_buf is not None:
                            inst.then_inc(pool.dma[o.dma_buf.dsem][0], 16)
                        elif o.signal:
                            inst.then_inc(esem[eng], 1)
                    for e2, idx in last.items():
                        e.wait_ge(esem[e2], val[(e2, idx)])
                    for s, v in dma_final:
                        e.wait_ge(s, v)

                decos[eng](body)
        for b in self.dma_bufs:
            pool.dma[b.dsem][1] = b.dbase + b.dcnt
            pool.free.append(b.dsem)
            b.dsem = None


def build_program(stop_after=99, dump=None):
    Sched.n_emit = -1
    nc = bass.Bass("TRN2", target_bir_lowering=False)

    def din(name, shape, dt=F32):
        return nc.dram_tensor(name, list(shape), dt, kind="ExternalInput").ap()

    def dscr(name, shape, dt):
        kind = "ExternalOutput" if dump == name else "Internal"
        return nc.dram_tensor(name, list(shape), dt, kind=kind).ap()

    x_d = din("x", [S, D])
    g1_d = din("g1", [D])
    w_in_d = din("w_in", [D, INW])
    fw_d = din("fw", [8, 128, 128])
    ccs_d = din("ccs", [2, 128, 128], BF16)
    tab_d = din("tab", [2, 128, 32, NLOC], BF16)
    cw_d = din("cw", [128, 40, 5])
    cb_d = din("cb", [128, 40])
    ssp_d = din("ssp", [5, 48])
    nw_d = din("nw", [128, 24])
    cst_d = din("cst", [4, 128, 128])
    idb_d = din("idb", [128, 128], BF16)
    nmk_d = din("nmk", [2, 128, 128], BF16)
    w_out_d = din("w_out", [4096, D])
    g2_d = din("g2", [D])
    w_up_d = din("w_up", [D, 2 * FFN])
    fcw_d = din("fcw", [128, 88, 3])
    fcb_d = din("fcb", [128, 88])
    w_dn_d = din("w_dn", [FFN, D])
    g3_d = din("g3", [D])
    out_d = nc.dram_tensor("out", [NOWN, D], F32, kind="ExternalOutput").ap()

    uT_d = dscr("uT", [FW, S], BF16)
    zsT_d = dscr("zsT", [SW, NLOC], BF16)
    xT_d = dscr("xT", [SW, S], BF16)
    BT_d = dscr("BT", [1024, S], BF16)
    CT_d = dscr("CT", [1024, NLOC], BF16)
    V_d = dscr("V", [8, 128, 32, 256], BF16)
    mixT_d = dscr("mixT", [4096, NLOC], BF16)
    yb_d = dscr("yb", [NLOC, SW], F32)
    x1_d = dscr("x1", [NLOC, D], F32)
    h2T_d = dscr("h2T", [D, NLOC], BF16)
    actT_d = dscr("actT", [FFN, NOWN], BF16)
    x2_d = dscr("x2", [NOWN, D], F32)

    es = ExitStack()
    with es:
        es.enter_context(nc.allow_low_precision("bf16 matmul operands, fp32 accumulate"))
        es.enter_context(nc.allow_non_contiguous_dma("tiled layouts"))
        pool = SemPool(nc, es, 56, 40)

        uid = [0]

        def sb(st, name, shape, dt):
            uid[0] += 1
            return st.enter_context(nc.sbuf_tensor(f"s{uid[0]}_{name}", list(shape), dt))

        def ps(st, name, shape, dt=F32):
            uid[0] += 1
            return st.enter_context(nc.psum_tensor(f"p{uid[0]}_{name}", list(shape), dt))

        dtraw = sb(es, "dtraw", [128, 32, 48], F32)
        cst = sb(es, "cst", [128, 4, 128], F32)
        idb = sb(es, "idb", [128, 128], BF16)
        epsc = sb(es, "epsc", [128, 1], F32)
        Umat, Lmat, ones_f, id_f = (cst[:, i, :] for i in range(4))

        def rms_rows(sc, src, src_b, ssb, ss_ap, rstd_ap, gain, gain_b, hb, hb_b, junk, junk_b, eps_b=None):
            sc.op("act", lambda e: e.activation(out=junk[:], in_=src, func=AF.Square, accum_out=ss_ap),
                  reads=[src_b], writes=[junk_b, ssb])
            sc.op("act", lambda e: e.activation(out=rstd_ap, in_=ss_ap, func=AF.Sqrt, bias=epsc[:], scale=1.0 / D),
                  reads=[ssb] + ([eps_b] if eps_b is not None else []), writes=[ssb])
            sc.op("dve", lambda e: e.reciprocal(out=rstd_ap, in_=rstd_ap), reads=[ssb], writes=[ssb])
            sc.op("dve", lambda e: e.scalar_tensor_tensor(out=hb[:], in0=src, scalar=rstd_ap, in1=gain[:],
                                                          op0=ALU.mult, op1=ALU.mult),
                  reads=[src_b, ssb, gain_b], writes=[hb_b])

        if os.environ.get("KDBG_INIT"):
            with ExitStack() as st:
                sc = Sched(nc, pool)
                zt = sb(st, "zt", [128, S], BF16)
                zt_b = sc.buf("zt")
                sc.op("pool", lambda e: e.memset(zt[:], 0.5), writes=[zt_b])
                for t_, nm in ((dtraw, "a"), (epsc, "d")):
                    sc.op("pool", lambda e, t_=t_: e.memset(t_[:], 0.25), writes=[sc.buf(nm)])
                sc.dma("sp", lambda e: e.dma_start(out=cst[:], in_=cst_d.rearrange("c p f -> p c f")), dst=sc.buf("b"))
                sc.dma("sp", lambda e: e.dma_start(out=idb[:], in_=idb_d), dst=sc.buf("c"))
                for dten, rows, cols in ((xT_d, SW, S), (BT_d, 1024, S), (CT_d, 1024, NLOC), (zsT_d, SW, NLOC), (uT_d, FW, S)):
                    for r0 in range(0, rows, 128):
                        sc.dma("sp", lambda e, dten=dten, r0=r0, cols=cols: e.dma_start(out=dten[r0:r0 + 128, :], in_=zt[:, :cols]), src=zt_b)
                sc.emit()
        with ExitStack() as s12:
            hT = sb(s12, "hT", [128, 16, S], BF16)
            with ExitStack() as st:
                sc = Sched(nc, pool)
                xt = [sb(st, f"xt{i}", [128, D], F32) for i in range(2)]
                hb = [sb(st, f"hb{i}", [128, D], BF16) for i in range(2)]
                junk = sb(st, "junk", [128, D], BF16)
                gain = sb(st, "gain1", [128, D], F32)
                ss = sb(st, "ss", [128, 64], F32)
                pt = [ps(st, f"pt{i}", [128, 1024], BF16)[:, 0:512] for i in range(4)]
                xt_b, hb_b, pt_b = sc.bufs_n("xt", 2), sc.bufs_n("hb", 2), sc.bufs_n("pt", 4)
                junk_b, gain_b, cst_b, idb_b = sc.buf("junk"), sc.buf("gain"), sc.buf("cst"), sc.buf("idb")
                hT_b = sc.buf("hT")
                epsc_b = sc.buf("epsc")
                sc.op("pool", lambda e: e.memset(epsc[:], EPS), writes=[epsc_b])
                sc.dma("sp", lambda e: e.dma_start(out=gain[:], in_=g1_d.partition_broadcast(128)), dst=gain_b)
                sc.dma("sp", lambda e: e.dma_start(out=cst[:], in_=cst_d.rearrange("c p f -> p c f")), dst=cst_b)
                sc.dma("sp", lambda e: e.dma_start(out=idb[:], in_=idb_d), dst=idb_b)
                for i in range(32):
                    a = i % 2
                    ssb = sc.buf(f"ss{i}")
                    sc.dma("sp", lambda e, i=i, a=a: e.dma_start(out=xt[a][:], in_=x_d[i * 128:(i + 1) * 128, :]),
                           dst=xt_b[a])
                    rms_rows(sc, xt[a][:], xt_b[a], ssb, ss[:, 2 * i:2 * i + 1], ss[:, 2 * i + 1:2 * i + 2],
                             gain, gain_b, hb[a], hb_b[a], junk, junk_b, eps_b=epsc_b)
                    for q in range(4):
                        pi = (4 * i + q) % 4
                        for r in range(4):
                            k = 4 * q + r
                            sc.op("pe", lambda e, a=a, k=k, pi=pi, r=r: e.transpose(
                                out=pt[pi][:, r * 128:(r + 1) * 128], in_=hb[a][:, k * 128:(k + 1) * 128],
                                identity=idb[:]), reads=[hb_b[a], idb_b], writes=[pt_b[pi]])
                        sc.op("act", lambda e, i=i, q=q, pi=pi: e.activation(
                            out=hT[:, 4 * q:4 * q + 4, i * 128:(i + 1) * 128],
                            in_=pt[pi].rearrange("p (a b) -> p a b", a=4), func=AF.Copy),
                            reads=[pt_b[pi]], writes=[hT_b])
                sc.emit()
            if stop_after <= 1:
                return nc
            with ExitStack() as st:
                sc = Sched(nc, pool)
                wt = [sb(st, f"wt{i}", [128, 16, 128], BF16) for i in range(2)]
                stage = [sb(st, f"stage{i}", [128, S + 4], BF16) for i in range(2)]
                acc = sb(st, "acc", [128, S], F32)
                osb = [sb(st, f"osb{i}", [128, S], BF16) for i in range(2)]
                cw = sb(st, "cw", [128, 40, 5], F32)
                cb = sb(st, "cb", [128, 40], F32)
                pbank = [ps(st, f"pb{i}", [128, 512], F32) for i in range(8)]
                wt_b, stage_b, osb_b, pb_b = sc.bufs_n("wt", 2), sc.bufs_n("stage", 2), sc.bufs_n("osb", 2), sc.bufs_n("pb", 8)
                acc_b, cw_b, dtraw_b = sc.buf("acc"), sc.buf("cw"), sc.buf("dtraw")
                sc.dma("sp", lambda e: e.dma_start(out=cw[:], in_=cw_d), dst=cw_b)
                sc.dma("sp", lambda e: e.dma_start(out=cb[:], in_=cb_d), dst=cw_b)
                for i in range(2):
                    sc.op("pool", lambda e, i=i: e.memset(stage[i][:], 0.0), writes=[stage_b[i]])
                bank = 0
                w_in_v = w_in_d.rearrange("(k p) c -> p k c", p=128)
                NT = int(os.environ.get("KDBG_NT", "73"))
                for j in ([int(v) for v in os.environ["KDBG_TILES"].split(",")] if os.environ.get("KDBG_TILES") else (range(NT) if "KDBG_TILES" not in os.environ else [])):
                    a = j % 2
                    ncols = 128 if j < 72 else 48
                    sc.dma("pool", lambda e, j=j, a=a, ncols=ncols: e.dma_start(
                        out=wt[a][:, :, :ncols], in_=w_in_v[:, :, j * 128:j * 128 + ncols]), dst=wt_b[a])
                    if j == 72:
                        for i in range(32):
                            b_ = bank % 8
                            bank += 1
                            for k in range(16):
                                sc.op("pe", lambda e, a=a, k=k, i=i, b_=b_: e.matmul(
                                    pbank[b_][:, :48], lhsT=hT[:, k, i * 128:(i + 1) * 128], rhs=wt[a][:, k, :48],
                                    start=(k == 0), stop=(k == 15)), reads=[wt_b[a]], writes=[pb_b[b_]])
                            sc.op("act", lambda e, i=i, b_=b_: e.activation(out=dtraw[:, i, :], in_=pbank[b_][:, :48],
                                                                           func=AF.Copy),
                                  reads=[pb_b[b_]], writes=[dtraw_b])
                        continue
                    kind = "u" if j < 8 else "z" if j < 32 else "x" if j < 56 else "B" if j < 64 else "C"
                    if kind in ("u", "x", "B"):
                        blocks = [(b * 512, 512) for b in range(8)]
                    elif kind == "z":
                        blocks = [(b * 512, 512) for b in range(4)] + [(2048, 128)]
                    else:
                        blocks = [(b * 512, 512) for b in range(4)] + [(2048, 256)]
                    conv = kind in ("x", "B", "C")
                    sa = j % 2
                    oa = j % 2
                    for (t0, n) in blocks:
                        b_ = bank % 8
                        bank += 1
                        for k in range(16):
                            sc.op("pe", lambda e, a=a, k=k, t0=t0, n=n, b_=b_: e.matmul(
                                pbank[b_][:, :n], lhsT=wt[a][:, k, :], rhs=hT[:, k, t0:t0 + n],
                                start=(k == 0), stop=(k == 15)), reads=[wt_b[a]], writes=[pb_b[b_]])
                        if conv:
                            sc.op("act", lambda e, sa=sa, t0=t0, n=n, b_=b_: e.activation(
                                out=stage[sa][:, 2 + t0:2 + t0 + n], in_=pbank[b_][:, :n], func=AF.Copy),
                                reads=[pb_b[b_]], writes=[stage_b[sa]])
                        else:
                            fn = AF.Copy if kind == "u" else AF.Silu
                            sc.op("act", lambda e, oa=oa, t0=t0, n=n, b_=b_, fn=fn: e.activation(
                                out=osb[oa][:, t0:t0 + n], in_=pbank[b_][:, :n], func=fn),
                                reads=[pb_b[b_]], writes=[osb_b[oa]])
                    if kind == "u":
                        T, dst = S, uT_d[j * 128:(j + 1) * 128, :]
                    elif kind == "z":
                        T, dst = NLOC, zsT_d[(j - 8) * 128:(j - 7) * 128, :]
                    elif kind == "x":
                        T, dst = S, xT_d[(j - 32) * 128:(j - 31) * 128, :]
                    elif kind == "B":
                        T, dst = S, BT_d[(j - 56) * 128:(j - 55) * 128, :]
                    else:
                        T, dst = NLOC, CT_d[(j - 64) * 128:(j - 63) * 128, :]
                    if conv:
                        jj = j - 32
                        sc.op("dve", lambda e, sa=sa, jj=jj, T=T: e.tensor_scalar(
                            out=acc[:, :T], in0=stage[sa][:, 0:T], scalar1=cw[:, jj, 0:1], scalar2=cb[:, jj:jj + 1],
                            op0=ALU.mult, op1=ALU.add), reads=[stage_b[sa], cw_b], writes=[acc_b])
                        for tap in range(1, 5):
                            eng = "dve"
                            sc.op(eng, lambda e, sa=sa, jj=jj, T=T, tap=tap: e.scalar_tensor_tensor(
                                out=acc[:, :T], in0=stage[sa][:, tap:tap + T], scalar=cw[:, jj, tap:tap + 1],
                                in1=acc[:, :T], op0=ALU.mult, op1=ALU.add),
                                reads=[stage_b[sa], cw_b, acc_b], writes=[acc_b])
                        sc.op("act", lambda e, oa=oa, T=T: e.activation(out=osb[oa][:, :T], in_=acc[:, :T], func=AF.Silu),
                              reads=[acc_b], writes=[osb_b[oa]])
                    sc.dma("act", lambda e, oa=oa, T=T, dst=dst: e.dma_start(out=dst, in_=osb[oa][:, :T]), src=osb_b[oa])
                sc.emit()
        if stop_after <= 2:
            return nc

        with ExitStack() as st:
            sc = Sched(nc, pool)
            wm = sb(st, "wm", [128, 8, 128], BF16)
            ccs = sb(st, "ccs", [128, 2, 128], BF16)
            Mg = sb(st, "Mg", [128, 8, 256], BF16)
            uTg = [sb(st, f"uTg{i}", [128, S], BF16) for i in range(2)]
            Vsb = [sb(st, f"Vsb{i}", [128, 32, 256], BF16) for i in range(2)]
            pM = [ps(st, f"pM{i}", [128, 512], F32) for i in range(4)]
            wm_b, ccs_b, Mg_b = sc.buf("wm"), sc.buf("ccs"), sc.buf("Mg")
            uTg_b, Vsb_b, pM_b = sc.bufs_n("uTg", 2), sc.bufs_n("Vsb", 2), sc.bufs_n("pM", 4)
            sc.dma("pool", lambda e: e.dma_start(out=wm[:], in_=fw_d.rearrange("g c d -> c g d")), dst=wm_b)
            sc.dma("sp", lambda e: e.dma_start(out=ccs[:], in_=ccs_d.rearrange("q c d -> c q d")), dst=ccs_b)
            for g in range(8):
                b_ = g % 4
                for q in range(2):
                    sc.op("pe", lambda e, g=g, q=q, b_=b_: e.matmul(pM[b_][:, q * 128:(q + 1) * 128], lhsT=ccs[:, q, :],
                                                                  rhs=wm[:, g, :], start=True, stop=True),
                          reads=[wm_b, ccs_b], writes=[pM_b[b_]])
                sc.op("act", lambda e, g=g, b_=b_: e.activation(out=Mg[:, g, :], in_=pM[b_][:, :256], func=AF.Copy),
                      reads=[pM_b[b_]], writes=[Mg_b])
            cnt = 0
            for g in range(8):
                a = g % 2
                sc.dma("sp", lambda e, g=g, a=a: e.dma_start(out=uTg[a][:], in_=uT_d[g * 128:(g + 1) * 128, :]),
                       dst=uTg_b[a])
                for stl in range(32):
                    b_ = cnt % 4
                    cnt += 1
                    sc.op("pe", lambda e, g=g, a=a, stl=stl, b_=b_: e.matmul(
                        pM[b_][:, :256], lhsT=uTg[a][:, stl * 128:(stl + 1) * 128], rhs=Mg[:, g, :],
                        start=True, stop=True), reads=[uTg_b[a], Mg_b], writes=[pM_b[b_]])
                    eng = "act" if stl % 2 == 0 else "dve"
                    if eng == "act":
                        sc.op("act", lambda e, a=a, stl=stl, b_=b_: e.activation(out=Vsb[a][:, stl, :], in_=pM[b_][:, :256],
                                                                              func=AF.Copy),
                              reads=[pM_b[b_]], writes=[Vsb_b[a]])
                    else:
                        sc.op("dve", lambda e, a=a, stl=stl, b_=b_: e.tensor_copy(out=Vsb[a][:, stl, :], in_=pM[b_][:, :256]),
                              reads=[pM_b[b_]], writes=[Vsb_b[a]])
                sc.dma("pool", lambda e, g=g, a=a: e.dma_start(out=V_d[g], in_=Vsb[a][:]), src=Vsb_b[a])
            sc.emit()
        if stop_after <= 3:
            return nc
        with ExitStack() as st:
            sc = Sched(nc, pool)
            tabs = [sb(st, f"tab{i}", [128, 2, 32, 512], BF16) for i in range(2)]
            Vg = [sb(st, f"Vg{i}", [128, 32, 256], BF16) for i in range(2)]
            aT = [sb(st, f"aT{i}", [128, 512], BF16) for i in range(2)]
            pF = [ps(st, f"pF{i}", [128, 512], F32) for i in range(4)]
            tab_b, Vg_b, aT_b, pF_b = sc.bufs_n("tab", 2), sc.bufs_n("Vg", 2), sc.bufs_n("aT", 2), sc.bufs_n("pF", 4)
            cnt = 0
            kblocks = [(b * 512, 512) for b in range(4)] + [(2048, 128)]
            for kb, (k0, n) in enumerate(kblocks):
                ta = kb % 2
                for q in range(2):
                    sc.dma("sp", lambda e, ta=ta, q=q, k0=k0, n=n: e.dma_start(
                        out=tabs[ta][:, q, :, :n], in_=tab_d[q][:, :, k0:k0 + n]),
                        dst=tab_b[ta])
                for g in range(8):
                    a = cnt % 2
                    b_ = cnt % 4
                    cnt += 1
                    sc.dma("sp", lambda e, g=g, a=a: e.dma_start(out=Vg[a][:], in_=V_d[g]), dst=Vg_b[a])
                    for stl in range(32):
                        for q in range(2):
                            sc.op("pe", lambda e, a=a, ta=ta, stl=stl, q=q, n=n, b_=b_: e.matmul(
                                pF[b_][:, :n], lhsT=Vg[a][:, stl, q * 128:(q + 1) * 128], rhs=tabs[ta][:, q, stl, :n],
                                start=(stl == 0 and q == 0), stop=(stl == 31 and q == 1)),
                                reads=[Vg_b[a], tab_b[ta]], writes=[pF_b[b_]])
                    sc.op("act", lambda e, a=a, n=n, b_=b_: e.activation(out=aT[a][:, :n], in_=pF[b_][:, :n], func=AF.Copy),
                          reads=[pF_b[b_]], writes=[aT_b[a]])
                    sc.dma("act", lambda e, g=g, a=a, k0=k0, n=n: e.dma_start(out=mixT_d[g * 128:(g + 1) * 128, k0:k0 + n],
                                                                        in_=aT[a][:, :n]), src=aT_b[a])
            sc.emit()
        if stop_after <= 4:
            return nc

        with ExitStack() as s4:
            prm = sb(s4, "prm", [128, 5, 48], F32)
            dts = sb(s4, "dts", [128, 2, 32, 48], F32)
            adt = sb(s4, "adt", [128, 2, 32, 48], F32)
            nw = sb(s4, "nw", [128, 24], F32)
            nmk = sb(s4, "nmk", [128, 2, 128], BF16)
            stateT = sb(s4, "stateT", [128, SW], F32)
            prevb = sb(s4, "prevb", [128, SW], BF16)
            xTc = [sb(s4, f"xTc{i}", [128, 24, 128], BF16) for i in range(2)]
            BTc = [sb(s4, f"BTc{i}", [128, 8, 128], BF16) for i in range(2)]
            CTc = [sb(s4, f"CTc{i}", [128, 8, 128], BF16) for i in range(2)]
            xtok = sb(s4, "xtok", [128, SW], BF16)
            Btok = sb(s4, "Btok", [128, 1024], BF16)
            xd2 = [sb(s4, f"xd{i}", [128, SW], BF16) for i in range(2)]
            xdw2 = [sb(s4, f"xdw{i}", [128, SW], BF16) for i in range(2)]
            sm2 = sb(s4, "sm", [128, 2, 8, 48], F32)
            CBm = [sb(s4, f"CBm{i}", [128, 128], F32) for i in range(2)]
            Eb = [sb(s4, f"Eb{i}", [128, 128], F32) for i in range(3)]
            MT = [sb(s4, f"MT{i}", [128, 128], BF16) for i in range(3)]
            yoff = [sb(s4, f"yoff{i}", [128, 384], F32) for i in range(2)]
            ysb = sb(s4, "ysb", [128, SW], F32)
            ybc = sb(s4, "ybc", [128, SW], F32)
            zsc = sb(s4, "zsc", [128, 24, 128], BF16)
            yg = sb(s4, "yg", [128, 24, 128], F32)
            sq = sb(s4, "sq", [128, 24, 128], F32)
            rsg = [sb(s4, f"rsg{i}", [128, 128], F32) for i in range(2)]
            osb4 = sb(s4, "osb4", [128, 24, 128], BF16)
            pTr = [ps(s4, f"pTr{i}", [128, 1024], BF16)[:, 0:512] for i in range(2)]
            pA = [ps(s4, f"pA{i}", [128, 512], F32) for i in range(2)]
            pY = [ps(s4, f"pY{i}", [128, 512], F32) for i in range(2)]
            pO = [ps(s4, f"pO{i}", [128, 512], F32) for i in range(2)]

            for direction in ("b", "f"):
                sc = Sched(nc, pool)
                di = 1 if direction == "b" else 0
                sm = xd = xdw = None
                Tm = Lmat if direction == "b" else Umat
                prm_b, dts_b, adt_b, nw_b = sc.buf("prm"), sc.buf("dts"), sc.buf("adt"), sc.buf("nw")
                state_b = sc.bufs_n("state", 8)
                prev_b = sc.bufs_n("prev", 8)
                xTc_b, BTc_b, CTc_b = sc.bufs_n("xTc", 2), sc.bufs_n("BTc", 2), sc.bufs_n("CTc", 2)
                xtok_b, Btok_b = sc.buf("xtok"), sc.buf("Btok")
                xd_b2, xdw_b2, sm_b2 = sc.bufs_n("xd", 2), sc.bufs_n("xdw", 2), sc.bufs_n("sm", 2)
                CBm_b, Eb_b, MT_b, yoff_b = sc.bufs_n("CBm", 2), sc.bufs_n("Eb", 3), sc.bufs_n("MT", 3), sc.bufs_n("yoff", 2)
                ysb_g = sc.bufs_n("ysb", 8)
                ybc_b, zsc_b, yg_b, sq_b, osb4_b = sc.buf("ybc"), sc.buf("zsc"), sc.buf("yg"), sc.buf("sq"), sc.buf("osb4")
                rsg_b = sc.bufs_n("rsg", 2)
                pTr_b, pY_b, pO_b = sc.bufs_n("pTr", 2), sc.bufs_n("pY", 2), sc.bufs_n("pO", 2)
                pA_b = sc.bufs_n("pA", 2)
                pSm, pSm_b = pO[1][:, 384:512], pO_b[1]
                if direction == "b":
                    sc.dma("sp", lambda e, sm=sm, xd=xd, xdw=xdw: e.dma_start(out=prm[:].rearrange("p a h -> p (a h)"),
                                                       in_=ssp_d.rearrange("a h -> (a h)").partition_broadcast(128)), dst=prm_b)
                    sc.dma("sp", lambda e, sm=sm, xd=xd, xdw=xdw: e.dma_start(out=nw[:], in_=nw_d), dst=nw_b)
                    sc.dma("sp", lambda e: e.dma_start(out=nmk[:], in_=nmk_d.rearrange("q p f -> p q f")), dst=nw_b)
                    for d2 in range(2):
                        bias = prm[:, 2 * d2, :].unsqueeze(1).to_broadcast([128, 32, 48])
                        alog = prm[:, 2 * d2 + 1, :]
                        sc.op("dve", lambda e, d2=d2, bias=bias, sm=sm, xd=xd, xdw=xdw: e.tensor_tensor(out=dts[:, d2], in0=dtraw[:], in1=bias, op=ALU.add),
                              reads=[prm_b], writes=[dts_b])
                        sc.op("act", lambda e, d2=d2, sm=sm, xd=xd, xdw=xdw: e.activation(out=dts[:, d2], in_=dts[:, d2], func=AF.Exp),
                              reads=[dts_b], writes=[dts_b])
                        sc.op("act", lambda e, d2=d2, sm=sm, xd=xd, xdw=xdw: e.activation(out=dts[:, d2], in_=dts[:, d2], func=AF.Ln, bias=1.0),
                              reads=[dts_b], writes=[dts_b])
                        sc.op("act", lambda e, alog=alog, sm=sm, xd=xd, xdw=xdw: e.activation(out=alog, in_=alog, func=AF.Exp),
                              reads=[prm_b], writes=[prm_b])
                        sc.op("dve", lambda e, alog=alog, sm=sm, xd=xd, xdw=xdw: e.tensor_scalar(out=alog, in0=alog, scalar1=-1.0, scalar2=None, op0=ALU.mult),
                              reads=[prm_b], writes=[prm_b])
                        sc.op("dve", lambda e, d2=d2, alog=alog, sm=sm, xd=xd, xdw=xdw: e.tensor_tensor(
                            out=adt[:, d2], in0=dts[:, d2], in1=alog.unsqueeze(1).to_broadcast([128, 32, 48]), op=ALU.mult),
                            reads=[prm_b, dts_b], writes=[adt_b])
                sc.op("pool", lambda e, sm=sm, xd=xd, xdw=xdw: e.memset(stateT[:], 0.0), writes=state_b)
                sc.op("pool", lambda e, sm=sm, xd=xd, xdw=xdw: e.memset(prevb[:], 0.0), writes=prev_b)
                chunks = list(range(31, -1, -1)) if direction == "b" else list(range(NCH_LOC))
                if "KDBG4_LIST" in os.environ:
                    chunks = [int(v) for v in os.environ["KDBG4_LIST"].split(",")]
                if "KDBG4_CHUNKS" in os.environ:
                    chunks = chunks[:int(os.environ["KDBG4_CHUNKS"])]
                hcnt = 0
                gcnt = 0
                for ci, c in enumerate(chunks):
                    local = c < NCH_LOC and os.environ.get("KDBG4_LOCAL", "1") == "1"
                    a = ci % 2
                    t0 = c * 128
                    sm, sm_b = sm2[:, a], sm_b2[a]
                    xd, xd_b, xdw, xdw_b = xd2[a], xd_b2[a], xdw2[a], xdw_b2[a]
                    sc.dma("sp", lambda e, a=a, t0=t0, sm=sm, xd=xd, xdw=xdw: e.dma_start(
                        out=xTc[a][:], in_=xT_d.rearrange("(j p) t -> p j t", p=128)[:, :, t0:t0 + 128]), dst=xTc_b[a])
                    sc.dma("sp", lambda e, a=a, t0=t0, sm=sm, xd=xd, xdw=xdw: e.dma_start(
                        out=BTc[a][:], in_=BT_d.rearrange("(j p) t -> p j t", p=128)[:, :, t0:t0 + 128]), dst=BTc_b[a])
                    if local:
                        sc.dma("sp", lambda e, a=a, t0=t0, sm=sm, xd=xd, xdw=xdw: e.dma_start(
                            out=CTc[a][:], in_=CT_d.rearrange("(j p) t -> p j t", p=128)[:, :, t0:t0 + 128]), dst=CTc_b[a])
                    for q in range(8):
                        pi = q % 2
                        for r in range(4):
                            j = 4 * q + r
                            src = xTc[a][:, j, :] if j < 24 else BTc[a][:, j - 24, :]
                            sc.op("pe", lambda e, src=src, pi=pi, r=r, sm=sm, xd=xd, xdw=xdw: e.transpose(out=pTr[pi][:, r * 128:(r + 1) * 128],
                                                                                  in_=src, identity=idb[:]),
                                  reads=[xTc_b[a], BTc_b[a]], writes=[pTr_b[pi]])
                        if q < 6:
                            sc.op("act", lambda e, q=q, pi=pi, sm=sm, xd=xd, xdw=xdw: e.activation(out=xtok[:, q * 512:(q + 1) * 512], in_=pTr[pi],
                                                                           func=AF.Copy), reads=[pTr_b[pi]], writes=[xtok_b])
                        else:
                            sc.op("act", lambda e, q=q, pi=pi, sm=sm, xd=xd, xdw=xdw: e.activation(out=Btok[:, (q - 6) * 512:(q - 5) * 512], in_=pTr[pi],
                                                                           func=AF.Copy), reads=[pTr_b[pi]], writes=[Btok_b])
                    sc.op("pe", lambda e, c=c, Tm=Tm, di=di, sm=sm, xd=xd, xdw=xdw: e.matmul(pSm[:, 0:48], lhsT=Tm, rhs=adt[:, di, c, :], start=True, stop=True),
                          reads=[adt_b], writes=[pSm_b])
                    sc.op("pe", lambda e, c=c, di=di, sm=sm, xd=xd, xdw=xdw: e.matmul(pSm[:, 48:96], lhsT=ones_f, rhs=adt[:, di, c, :], start=True, stop=True),
                          reads=[adt_b], writes=[pSm_b])
                    sc.op("act", lambda e, sm=sm, xd=xd, xdw=xdw: e.activation(out=sm[:, 0:2, :].rearrange("p a h -> p (a h)"), in_=pSm[:, 0:96], func=AF.Copy),
                          reads=[pSm_b], writes=[sm_b])
                    sc.op("dve", lambda e, sm=sm, xd=xd, xdw=xdw: e.tensor_tensor(out=sm[:, 2, :], in0=sm[:, 1, :], in1=sm[:, 0, :], op=ALU.subtract),
                          reads=[sm_b], writes=[sm_b])
                    sc.op("act", lambda e, sm=sm, xd=xd, xdw=xdw: e.activation(out=sm[:, 3, :], in_=sm[:, 2, :], func=AF.Exp), reads=[sm_b], writes=[sm_b])
                    sc.op("act", lambda e, sm=sm, xd=xd, xdw=xdw: e.activation(out=sm[:, 6, :], in_=sm[:, 1, :], func=AF.Exp), reads=[sm_b], writes=[sm_b])
                    if local:
                        sc.op("act", lambda e, sm=sm, xd=xd, xdw=xdw: e.activation(out=sm[:, 4, :], in_=sm[:, 0, :], func=AF.Exp), reads=[sm_b], writes=[sm_b])
                        sc.op("dve", lambda e, sm=sm, xd=xd, xdw=xdw: e.tensor_scalar(out=sm[:, 5, :], in0=sm[:, 0, :], scalar1=-1.0, scalar2=None, op0=ALU.mult),
                              reads=[sm_b], writes=[sm_b])
                    sc.op("dve", lambda e, c=c, di=di, sm=sm, xd=xd, xdw=xdw: e.tensor_tensor(out=sm[:, 7, :], in0=dts[:, di, c, :], in1=sm[:, 3, :], op=ALU.mult),
                          reads=[sm_b, dts_b], writes=[sm_b])
                    x3 = xtok[:].rearrange("p (h q) -> p h q", q=64)
                    sc.op("pool", lambda e, x3=x3, sm=sm, xd=xd, xdw=xdw: e.tensor_tensor(
                        out=xdw[:].rearrange("p (h q) -> p h q", q=64), in0=x3,
                        in1=sm[:, 7, :].unsqueeze(2).to_broadcast([128, 48, 64]), op=ALU.mult),
                        reads=[xtok_b, sm_b], writes=[xdw_b])
                    if local:
                        sc.op("dve", lambda e, x3=x3, c=c, di=di, sm=sm, xd=xd, xdw=xdw: e.tensor_tensor(
                            out=xd[:].rearrange("p (h q) -> p h q", q=64), in0=x3,
                            in1=dts[:, di, c, :].unsqueeze(2).to_broadcast([128, 48, 64]), op=ALU.mult),
                            reads=[xtok_b, dts_b], writes=[xd_b])
                    for g in range(8):
                        gs = slice(g * 384, (g + 1) * 384)
                        ga = gcnt % 2
                        gcnt += 1
                        if local:
                            pa0 = pO[ga]
                            sc.op("pe", lambda e, a=a, g=g, pa0=pa0, sm=sm, xd=xd, xdw=xdw: e.matmul(pa0[:, 384:512], lhsT=BTc[a][:, g, :], rhs=CTc[a][:, g, :],
                                                                             start=True, stop=True),
                                  reads=[BTc_b[a], CTc_b[a]], writes=[pO_b[ga]])
                            sc.op("dve", lambda e, ga=ga, pa0=pa0: e.tensor_copy(out=CBm[ga][:], in_=pa0[:, 384:512]),
                                  reads=[pO_b[ga]], writes=[CBm_b[ga]])
                            for r in range(6):
                                h = 6 * g + r
                                ha = hcnt % 3
                                hcnt += 1
                                pb_ = hcnt % 2
                                sc.op("pe", lambda e, h=h, pb_=pb_, sm=sm, xd=xd, xdw=xdw: e.matmul(
                                    pA[pb_][:, 0:128], lhsT=sm[:, 0, h:h + 1].to_broadcast([128, 128]),
                                    rhs=id_f, start=True, stop=False), reads=[sm_b], writes=[pA_b[pb_]])
                                sc.op("pe", lambda e, pb_=pb_, di=di: e.matmul(pA[pb_][:, 0:128], lhsT=idb[:], rhs=nmk[:, di, :],
                                                                              start=False, stop=True), reads=[nw_b], writes=[pA_b[pb_]])
                                sc.op("act", lambda e, h=h, pb_=pb_, ha=ha, sm=sm, xd=xd, xdw=xdw: e.activation(
                                    out=Eb[ha][:], in_=pA[pb_][:, 0:128], func=AF.Exp,
                                    bias=sm[:, 5, h:h + 1], scale=1.0), reads=[pA_b[pb_], sm_b], writes=[Eb_b[ha]])
                                sc.op("dve", lambda e, ha=ha, ga=ga: e.tensor_tensor(
                                    out=MT[ha][:], in0=Eb[ha][:], in1=CBm[ga][:], op=ALU.mult),
                                    reads=[Eb_b[ha], CBm_b[ga]], writes=[MT_b[ha]])
                                sc.op("pe", lambda e, ha=ha, ga=ga, r=r, h=h, sm=sm, xd=xd, xdw=xdw: e.matmul(
                                    pY[ga][:, r * 64:(r + 1) * 64], lhsT=MT[ha][:], rhs=xd[:, h * 64:(h + 1) * 64],
                                    start=True, stop=True), reads=[MT_b[ha], xd_b], writes=[pY_b[ga]])
                            sc.op("pe", lambda e, a=a, g=g, ga=ga, gs=gs, sm=sm, xd=xd, xdw=xdw: e.matmul(pO[ga][:, 0:384], lhsT=CTc[a][:, g, :], rhs=prevb[:, gs],
                                                                                   start=True, stop=True),
                                  reads=[CTc_b[a], prev_b[g]], writes=[pO_b[ga]])
                            for r in range(6):
                                h = 6 * g + r
                                sc.op("act", lambda e, ga=ga, r=r, h=h, sm=sm, xd=xd, xdw=xdw: e.activation(
                                    out=yoff[ga][:, r * 64:(r + 1) * 64], in_=pO[ga][:, r * 64:(r + 1) * 64], func=AF.Copy,
                                    scale=sm[:, 4, h:h + 1]), reads=[pO_b[ga], sm_b], writes=[yoff_b[ga]])
                            sc.op("dve", lambda e, ga=ga, gs=gs, sm=sm, xd=xd, xdw=xdw: e.tensor_tensor(out=ysb[:, gs], in0=pY[ga][:, 0:384], in1=yoff[ga][:], op=ALU.add),
                                  reads=[pY_b[ga], yoff_b[ga]], writes=[ysb_g[g]])
                    for g in range(8):
                        gs = slice(g * 384, (g + 1) * 384)
                        ga = g % 2
                        sc.op("pe", lambda e, g=g, ga=ga, gs=gs, sm=sm, xd=xd, xdw=xdw: e.matmul(pO[ga][:, 0:384],
                                                                          lhsT=Btok[:, g * 128:(g + 1) * 128], rhs=xdw[:, gs],
                                                                          start=True, stop=True),
                              reads=[Btok_b, xdw_b], writes=[pO_b[ga]])
                        st3 = stateT[:, gs].rearrange("p (h q) -> p h q", q=64)
                        sc.op("pool", lambda e, g=g, st3=st3, sm=sm, xd=xd, xdw=xdw: e.tensor_tensor(
                            out=st3, in0=st3, in1=sm[:, 6, 6 * g:6 * g + 6].unsqueeze(2).to_broadcast([128, 6, 64]), op=ALU.mult),
                            reads=[sm_b, state_b[g]], writes=[state_b[g]])
                        sc.op("dve", lambda e, ga=ga, gs=gs, sm=sm, xd=xd, xdw=xdw: e.tensor_tensor(out=stateT[:, gs], in0=pO[ga][:, 0:384], in1=stateT[:, gs], op=ALU.add),
                              reads=[pO_b[ga], state_b[g]], writes=[state_b[g]])
                        sc.op("act", lambda e, gs=gs, sm=sm, xd=xd, xdw=xdw: e.activation(out=prevb[:, gs], in_=stateT[:, gs], func=AF.Copy),
                              reads=[state_b[g]], writes=[prev_b[g]])
                    if local and direction == "b":
                        for g in range(8):
                            gs = slice(g * 384, (g + 1) * 384)
                            sc.dma("sp", lambda e, t0=t0, gs=gs, sm=sm, xd=xd, xdw=xdw: e.dma_start(out=yb_d[t0:t0 + 128, gs], in_=ysb[:, gs]), src=ysb_g[g])
                    if direction == "f":
                        sc.dma("sp", lambda e, t0=t0, sm=sm, xd=xd, xdw=xdw: e.dma_start(out=ybc[:], in_=yb_d[t0:t0 + 128, :]), dst=ybc_b)
                        sc.dma("sp", lambda e, t0=t0, sm=sm, xd=xd, xdw=xdw: e.dma_start(
                            out=zsc[:], in_=zsT_d.rearrange("(j p) t -> p j t", p=128)[:, :, t0:t0 + 128]), dst=zsc_b)
                        sc.op("pool", lambda e, sm=sm, xd=xd, xdw=xdw: e.tensor_tensor(out=ybc[:], in0=ybc[:], in1=ysb[:], op=ALU.add),
                              reads=[ybc_b] + ysb_g, writes=[ybc_b])
                        sc.op("dve", lambda e, x3=x3, sm=sm, xd=xd, xdw=xdw: e.tensor_tensor(
                            out=xd[:].rearrange("p (h q) -> p h q", q=64), in0=x3,
                            in1=prm[:, 4, :].unsqueeze(2).to_broadcast([128, 48, 64]), op=ALU.mult),
                            reads=[xtok_b, prm_b, xd_b], writes=[xd_b])
                        sc.op("dve", lambda e, sm=sm, xd=xd, xdw=xdw: e.tensor_tensor(out=ybc[:], in0=ybc[:], in1=xd[:], op=ALU.add),
                              reads=[ybc_b, xd_b], writes=[ybc_b])
                        for q in range(6):
                            pi = q % 2
                            for r in range(4):
                                j = 4 * q + r
                                sc.op("pe", lambda e, j=j, pi=pi, r=r, sm=sm, xd=xd, xdw=xdw: e.transpose(out=pA[pi][:, r * 128:(r + 1) * 128],
                                                                                  in_=ybc[:, j * 128:(j + 1) * 128], identity=id_f),
                                      reads=[ybc_b], writes=[pA_b[pi]])
                            sc.op("dve", lambda e, q=q, pi=pi, sm=sm, xd=xd, xdw=xdw: e.tensor_tensor(
                                out=yg[:, 4 * q:4 * q + 4, :], in0=pA[pi][:].rearrange("p (a b) -> p a b", a=4),
                                in1=zsc[:, 4 * q:4 * q + 4, :], op=ALU.mult), reads=[pA_b[pi], zsc_b], writes=[yg_b])
                        sc.op("act", lambda e, sm=sm, xd=xd, xdw=xdw: e.activation(out=sq[:], in_=yg[:], func=AF.Square), reads=[yg_b], writes=[sq_b])
                        for g in range(8):
                            ga = g % 2
                            for jj in range(3):
                                sc.op("pe", lambda e, g=g, jj=jj, ga=ga, sm=sm, xd=xd, xdw=xdw: e.matmul(pY[ga][:, 0:128], lhsT=ones_f, rhs=sq[:, 3 * g + jj, :],
                                                                                 start=(jj == 0), stop=(jj == 2)),
                                      reads=[sq_b], writes=[pY_b[ga]])
                            sc.op("act", lambda e, ga=ga: e.activation(out=rsg[ga][:], in_=pY[ga][:, 0:128], func=AF.Sqrt, bias=epsc[:],
                                                                      scale=1.0 / 384.0), reads=[pY_b[ga]], writes=[rsg_b[ga]])
                            sc.op("dve", lambda e, ga=ga: e.reciprocal(out=rsg[ga][:], in_=rsg[ga][:]), reads=[rsg_b[ga]], writes=[rsg_b[ga]])
                            for jj in range(3):
                                j = 3 * g + jj
                                sc.op("dve", lambda e, j=j, ga=ga, sm=sm, xd=xd, xdw=xdw: e.scalar_tensor_tensor(
                                    out=osb4[:, j, :], in0=yg[:, j, :], scalar=nw[:, j:j + 1], in1=rsg[ga][:],
                                    op0=ALU.mult, op1=ALU.mult), reads=[yg_b, rsg_b[ga], nw_b], writes=[osb4_b])
                        sc.dma("sp", lambda e, t0=t0, sm=sm, xd=xd, xdw=xdw: e.dma_start(
                            out=mixT_d[1024:4096, :].rearrange("(j p) t -> p j t", p=128)[:, :, t0:t0 + 128], in_=osb4[:]), src=osb4_b)
                sc.emit()
                if direction == "b" and stop_after <= 5:
                    return nc
        if stop_after <= 6:
            return nc

        with ExitStack() as st:
            sc = Sched(nc, pool)
            wout = sb(st, "wout", [128, 32, D], BF16)
            mt = [sb(st, f"mt{i}", [128, 32, 128], BF16) for i in range(2)]
            xt = [sb(st, f"xt5{i}", [128, D], F32) for i in range(1)]
            wst = [sb(st, f"wst5{i}", [128, D], F32) for i in range(2)]
            x1 = sb(st, "x1s", [128, D], F32)
            hb = sb(st, "hb5", [128, D], BF16)
            junk = sb(st, "junk5", [128, D], BF16)
            h2s = [sb(st, f"h2s{i}", [128, 16, 128], BF16) for i in range(1)] * 2
            gain = sb(st, "gain2", [128, D], F32)
            ss = sb(st, "ss5", [128, 64], F32)
            pb = [ps(st, f"pb5{i}", [128, 512], F32) for i in range(4)]
            pt = [ps(st, f"pt5{i}", [128, 1024], BF16)[:, 0:512] for i in range(4)]
            wout_b = sc.bufs_n("wout", 32)
            wst_b = sc.bufs_n("wst", 2)
            mt_b, xt_b, h2s_b, pb_b, pt_b = sc.bufs_n("mt", 2), sc.bufs_n("xt", 1), sc.bufs_n("h2s", 1) * 2, sc.bufs_n("pb", 4), sc.bufs_n("pt", 4)
            x1_b, hb_b, junk_b, gain_b = sc.buf("x1"), sc.buf("hb"), sc.buf("junk"), sc.buf("gain")
            sc.dma("sp", lambda e: e.dma_start(out=gain[:], in_=g2_d.partition_broadcast(128)), dst=gain_b)
            for j in range(32):
                wa = j % 2
                sc.dma("sp", lambda e, j=j, wa=wa: e.dma_start(out=wst[wa][:], in_=w_out_d[j * 128:(j + 1) * 128, :]), dst=wst_b[wa])
                if j % 2 == 0:
                    sc.op("act", lambda e, j=j, wa=wa: e.activation(out=wout[:, j, :], in_=wst[wa][:], func=AF.Copy),
                          reads=[wst_b[wa]], writes=[wout_b[j]])
                else:
                    sc.op("dve", lambda e, j=j, wa=wa: e.tensor_copy(out=wout[:, j, :], in_=wst[wa][:]),
                          reads=[wst_b[wa]], writes=[wout_b[j]])
            for i in range(NCH_LOC):
                a = i % 2
                t0 = i * 128
                ssb = sc.buf(f"ss{i}")
                sc.dma("sp", lambda e, a=a, t0=t0: e.dma_start(
                    out=mt[a][:], in_=mixT_d.rearrange("(j p) t -> p j t", p=128)[:, :, t0:t0 + 128]), dst=mt_b[a])
                sc.dma("sp", lambda e, t0=t0: e.dma_start(out=xt[0][:], in_=x_d[t0:t0 + 128, :]), dst=xt_b[0])
                for j in range(32):
                    for db in range(4):
                        sc.op("pe", lambda e, a=a, db=db, j=j: e.matmul(pb[db][:], lhsT=mt[a][:, j, :], rhs=wout[:, j, db * 512:(db + 1) * 512],
                                                                       start=(j == 0), stop=(j == 31)),
                              reads=[mt_b[a], wout_b[j]], writes=[pb_b[db]])
                for db in range(4):
                    sc.op("dve", lambda e, db=db: e.tensor_tensor(out=x1[:, db * 512:(db + 1) * 512], in0=pb[db][:],
                                                                 in1=xt[0][:, db * 512:(db + 1) * 512], op=ALU.add),
                          reads=[pb_b[db], xt_b[0]], writes=[x1_b])
                rms_rows(sc, x1[:], x1_b, ssb, ss[:, 2 * i:2 * i + 1], ss[:, 2 * i + 1:2 * i + 2], gain, gain_b, hb, hb_b, junk, junk_b)
                sc.dma("pool", lambda e, t0=t0: e.dma_start(out=x1_d[t0:t0 + 128, :], in_=x1[:]), src=x1_b)
                for q in range(4):
                    pi = q
                    for r in range(4):
                        k = 4 * q + r
                        sc.op("pe", lambda e, k=k, pi=pi, r=r: e.transpose(out=pt[pi][:, r * 128:(r + 1) * 128],
                                                                          in_=hb[:, k * 128:(k + 1) * 128], identity=idb[:]),
                              reads=[hb_b], writes=[pt_b[pi]])
                    sc.op("act", lambda e, a=a, q=q, pi=pi: e.activation(out=h2s[a][:, 4 * q:4 * q + 4, :],
                                                                        in_=pt[pi].rearrange("p (a b) -> p a b", a=4), func=AF.Copy),
                          reads=[pt_b[pi]], writes=[h2s_b[a]])
                sc.dma("act", lambda e, a=a, t0=t0: e.dma_start(
                    out=h2T_d.rearrange("(k p) t -> p k t", p=128)[:, :, t0:t0 + 128], in_=h2s[a][:]), src=h2s_b[a])
            sc.emit()
        if stop_after <= 7:
            return nc

        with ExitStack() as st:
            sc = Sched(nc, pool)
            h2r = sb(st, "h2r", [128, 16, NLOC], BF16)
            wup = [sb(st, f"wup{i}", [128, 16, 256], BF16) for i in range(2)]
            wst6 = [sb(st, f"wst6{i}", [128, 16, 128], F32) for i in range(4)]
            stg = [sb(st, f"stg{i}", [128, 2052], F32) for i in range(2)]
            accg = sb(st, "accg", [128, NOWN], F32)
            accv = sb(st, "accv", [128, NOWN], F32)
            sg = sb(st, "sg", [128, NOWN], F32)
            ao = [sb(st, f"ao{i}", [128, NOWN], BF16) for i in range(2)]
            fcw = sb(st, "fcw", [128, 88, 3], F32)
            fcb = sb(st, "fcb", [128, 88], F32)
            pb = [ps(st, f"pb6{i}", [128, 512], F32) for i in range(8)]
            h2r_b, fc_b = sc.buf("h2r"), sc.buf("fc")
            wup_b, stg_b, ao_b, pb_b = sc.bufs_n("wup", 4), sc.bufs_n("stg", 2), sc.bufs_n("ao", 2), sc.bufs_n("pb", 8)
            wst6_b = sc.bufs_n("wst6", 4)
            accg_b, accv_b, sg_b = sc.buf("accg"), sc.buf("accv"), sc.buf("sg")
            sc.dma("sp", lambda e: e.dma_start(out=h2r[:], in_=h2T_d.rearrange("(k p) t -> p k t", p=128)), dst=h2r_b)
            sc.dma("sp", lambda e: e.dma_start(out=fcw[:], in_=fcw_d), dst=fc_b)
            sc.dma("sp", lambda e: e.dma_start(out=fcb[:], in_=fcb_d), dst=fc_b)
            for i in range(2):
                sc.op("pool", lambda e, i=i: e.memset(stg[i][:], 0.0), writes=[stg_b[i]])
            wuv = w_up_d.rearrange("(k p) c -> p k c", p=128)
            bank = 0
            blocks = [(b * 512, 512) for b in range(4)] + [(2048, 1)]
            for f in range(44):
                a = f % 2
                for hv in range(2):
                    c0 = hv * FFN + f * 128
                    wa = 2 * a + hv
                    sc.dma("sp", lambda e, wa=wa, c0=c0: e.dma_start(out=wst6[wa][:], in_=wuv[:, :, c0:c0 + 128]), dst=wst6_b[wa])
                    if hv == 0:
                        sc.op("act", lambda e, a=a, hv=hv, wa=wa: e.activation(out=wup[a][:, :, hv * 128:(hv + 1) * 128], in_=wst6[wa][:],
                                                                              func=AF.Copy), reads=[wst6_b[wa]], writes=[wup_b[wa]])
                    else:
                        sc.op("act", lambda e, a=a, hv=hv, wa=wa: e.activation(out=wup[a][:, :, hv * 128:(hv + 1) * 128], in_=wst6[wa][:],
                                                                              func=AF.Copy), reads=[wst6_b[wa]], writes=[wup_b[wa]])
                for hv in range(2):
                    for (t0, n) in blocks:
                        b_ = bank % 8
                        bank += 1
                        for k in range(16):
                            sc.op("pe", lambda e, a=a, hv=hv, k=k, t0=t0, n=n, b_=b_: e.matmul(
                                pb[b_][:, :n], lhsT=wup[a][:, k, hv * 128:(hv + 1) * 128], rhs=h2r[:, k, t0:t0 + n],
                                start=(k == 0), stop=(k == 15)), reads=[wup_b[2 * a + hv], h2r_b], writes=[pb_b[b_]])
                        sc.op("act", lambda e, hv=hv, t0=t0, n=n, b_=b_: e.activation(out=stg[hv][:, 1 + t0:1 + t0 + n], in_=pb[b_][:, :n],
                                                                                     func=AF.Copy),
                              reads=[pb_b[b_]], writes=[stg_b[hv]])
                    jj = hv * 44 + f
                    eng = "dve"
                    acc_t, acc_tb = (accg, accg_b) if hv == 0 else (accv, accv_b)
                    sc.op(eng, lambda e, hv=hv, jj=jj, acc_t=acc_t: e.tensor_scalar(
                        out=acc_t[:], in0=stg[hv][:, 0:NOWN], scalar1=fcw[:, jj, 0:1], scalar2=fcb[:, jj:jj + 1],
                        op0=ALU.mult, op1=ALU.add), reads=[stg_b[hv], fc_b], writes=[acc_tb])
                    for tap in (1, 2):
                        sc.op("dve", lambda e, hv=hv, jj=jj, acc_t=acc_t, tap=tap: e.scalar_tensor_tensor(
                            out=acc_t[:], in0=stg[hv][:, tap:tap + NOWN], scalar=fcw[:, jj, tap:tap + 1], in1=acc_t[:],
                            op0=ALU.mult, op1=ALU.add), reads=[stg_b[hv], fc_b, acc_tb], writes=[acc_tb])
                sc.op("act", lambda e: e.activation(out=sg[:], in_=accg[:], func=AF.Silu), reads=[accg_b], writes=[sg_b])
                sc.op("dve", lambda e, a=a: e.tensor_tensor(out=ao[a][:], in0=sg[:], in1=accv[:], op=ALU.mult),
                      reads=[sg_b, accv_b], writes=[ao_b[a]])
                sc.dma("pool", lambda e, a=a, f=f: e.dma_start(out=actT_d[f * 128:(f + 1) * 128, :], in_=ao[a][:]), src=ao_b[a])
            sc.emit()
        if stop_after <= 8:
            return nc

        with ExitStack() as st:
            sc = Sched(nc, pool)
            wdn = [sb(st, f"wdn{i}", [128, 44, 512], BF16) for i in range(2)]
            wst7 = [sb(st, f"wst7{i}", [128, 4, 512], F32) for i in range(2)]
            at = [sb(st, f"at{i}", [128, 44, 128], BF16) for i in range(2)]
            x1t = [sb(st, f"x1t{i}", [128, 512], F32) for i in range(2)]
            x2s = [sb(st, f"x2s{i}", [128, 512], F32) for i in range(2)]
            junk = sb(st, "junk7", [128, 512], BF16)
            ssp = sb(st, "ssp", [128, 16, 4], F32)
            ss = sb(st, "ss7", [128, 32], F32)
            gain = sb(st, "gain3", [128, D], F32)
            xf = [sb(st, f"xf{i}", [128, D], F32) for i in range(2)]
            of = [sb(st, f"of{i}", [128, D], F32) for i in range(2)]
            pb = [ps(st, f"pb7{i}", [128, 512], F32) for i in range(4)]
            wdn_b, at_b, x1t_b, x2s_b, pb_b = sc.bufs_n("wdn", 22), sc.bufs_n("at", 2), sc.bufs_n("x1t", 2), sc.bufs_n("x2s", 2), sc.bufs_n("pb", 4)
            wst7_b = sc.bufs_n("wst7", 2)
            wcnt = 0
            junk_b, ssp_b, gain_b = sc.buf("junk"), sc.buf("ssp"), sc.buf("gain")
            xf_b, of_b = sc.bufs_n("xf", 2), sc.bufs_n("of", 2)
            sc.dma("sp", lambda e: e.dma_start(out=gain[:], in_=g3_d.partition_broadcast(128)), dst=gain_b)
            wdv = w_dn_d.rearrange("(j p) d -> p j d", p=128)
            cnt = 0
            for db in range(4):
                wa = db % 2
                for jg in range(11):
                    sa_ = wcnt % 2
                    wcnt += 1
                    sc.dma("sp", lambda e, sa_=sa_, jg=jg, db=db: e.dma_start(
                        out=wst7[sa_][:], in_=wdv[:, 4 * jg:4 * jg + 4, db * 512:(db + 1) * 512]), dst=wst7_b[sa_])
                    if jg % 2 == 0:
                        sc.op("act", lambda e, sa_=sa_, jg=jg, wa=wa: e.activation(out=wdn[wa][:, 4 * jg:4 * jg + 4, :], in_=wst7[sa_][:],
                                                                                 func=AF.Copy), reads=[wst7_b[sa_]], writes=[wdn_b[wa * 11 + jg]])
                    else:
                        sc.op("act", lambda e, sa_=sa_, jg=jg, wa=wa: e.activation(out=wdn[wa][:, 4 * jg:4 * jg + 4, :], in_=wst7[sa_][:],
                                                                                 func=AF.Copy), reads=[wst7_b[sa_]], writes=[wdn_b[wa * 11 + jg]])
                for i in range(16):
                    a = cnt % 2
                    b_ = cnt % 4
                    cnt += 1
                    t0 = i * 128
                    sc.dma("sp", lambda e, a=a, t0=t0: e.dma_start(
                        out=at[a][:], in_=actT_d.rearrange("(j p) t -> p j t", p=128)[:, :, t0:t0 + 128]), dst=at_b[a])
                    sc.dma("sp", lambda e, a=a, t0=t0, db=db: e.dma_start(out=x1t[a][:], in_=x1_d[t0:t0 + 128, db * 512:(db + 1) * 512]),
                           dst=x1t_b[a])
                    for j in range(44):
                        sc.op("pe", lambda e, a=a, wa=wa, j=j, b_=b_: e.matmul(pb[b_][:], lhsT=at[a][:, j, :], rhs=wdn[wa][:, j, :],
                                                                              start=(j == 0), stop=(j == 43)),
                              reads=[at_b[a], wdn_b[wa * 11 + j // 4]], writes=[pb_b[b_]])
                    sc.op("dve", lambda e, a=a, b_=b_: e.tensor_tensor(out=x2s[a][:], in0=pb[b_][:], in1=x1t[a][:], op=ALU.add),
                          reads=[pb_b[b_], x1t_b[a]], writes=[x2s_b[a]])
                    sc.op("act", lambda e, a=a, i=i, db=db: e.activation(out=junk[:], in_=x2s[a][:], func=AF.Square,
                                                                        accum_out=ssp[:, i, db:db + 1]),
                          reads=[x2s_b[a]], writes=[junk_b, ssp_b])
                    sc.dma("pool", lambda e, a=a, t0=t0, db=db: e.dma_start(out=x2_d[t0:t0 + 128, db * 512:(db + 1) * 512], in_=x2s[a][:]),
                           src=x2s_b[a])
            sc.emit()
            sc = Sched(nc, pool)
            xf_b, of_b, ssp_b, gain_b = sc.bufs_n("xf", 2), sc.bufs_n("of", 2), sc.buf("ssp"), sc.buf("gain")
            for i in range(16):
                a = i % 2
                t0 = i * 128
                sc.dma("sp", lambda e, a=a, t0=t0: e.dma_start(out=xf[a][:], in_=x2_d[t0:t0 + 128, :]), dst=xf_b[a])
                sc.op("dve", lambda e, i=i: e.tensor_reduce(out=ss[:, 2 * i:2 * i + 1], in_=ssp[:, i, :], axis=AX.X, op=ALU.add),
                      reads=[ssp_b], writes=[ssp_b])
                sc.op("act", lambda e, i=i: e.activation(out=ss[:, 2 * i + 1:2 * i + 2], in_=ss[:, 2 * i:2 * i + 1], func=AF.Sqrt,
                                                        bias=epsc[:], scale=1.0 / D), reads=[ssp_b], writes=[ssp_b])
                sc.op("dve", lambda e, i=i: e.reciprocal(out=ss[:, 2 * i + 1:2 * i + 2], in_=ss[:, 2 * i + 1:2 * i + 2]),
                      reads=[ssp_b], writes=[ssp_b])
                sc.op("dve", lambda e, a=a, i=i: e.scalar_tensor_tensor(out=of[a][:], in0=xf[a][:], scalar=ss[:, 2 * i + 1:2 * i + 2],
                                                                    in1=gain[:], op0=ALU.mult, op1=ALU.mult),
                      reads=[xf_b[a], ssp_b, gain_b], writes=[of_b[a]])
                sc.dma("pool", lambda e, a=a, t0=t0: e.dma_start(out=out_d[t0:t0 + 128, :], in_=of[a][:]), src=of_b[a])
            sc.emit()
    return nc


_CONST = {}


def _consts():
    if _CONST:
        return _CONST
    bf = ml_dtypes.bfloat16
    c = np.arange(128, dtype=np.float64)
    ang = 2.0 * np.pi * np.outer(c, c) / 128.0
    sc = 1.0 / np.sqrt(float(S) * 128.0)
    _CONST["ccs"] = np.stack([np.cos(ang) * sc, np.sin(ang) * sc]).astype(np.float32).astype(bf)
    s = np.arange(S, dtype=np.int64)[:, None]
    k = np.arange(NLOC, dtype=np.int64)[None, :]
    tabs = []
    for flip in (0, 1):
        prod = ((s + flip) * (k + flip)) % S
        a = 2.0 * np.pi * prod.astype(np.float64) / S
        t2 = np.stack([np.cos(a), -np.sin(a)]).astype(np.float32).astype(bf)
        tabs.append(np.ascontiguousarray(t2.reshape(2, 32, 128, NLOC).transpose(0, 2, 1, 3)))
    _CONST["tab"] = tabs
    i = np.arange(128)
    U = (i[:, None] <= i[None, :]).astype(np.float32)
    L = (i[:, None] >= i[None, :]).astype(np.float32)
    _CONST["cst"] = np.stack([U, L, np.ones((128, 128), np.float32), np.eye(128, dtype=np.float32)])
    _CONST["idb"] = np.eye(128, dtype=np.float32).astype(bf)
    NEG = np.float32(-1.0e5)
    mf = np.where(i[None, :] < i[:, None], NEG, np.float32(0))
    mb = np.where(i[None, :] > i[:, None], NEG, np.float32(0))
    _CONST["nmk"] = np.stack([mf, mb]).astype(np.float32).astype(bf)
    return _CONST


def _in_maps(inp):
    cs = _consts()
    f = lambda a: np.ascontiguousarray(np.asarray(a, dtype=np.float32))
    x = f(inp["x"])
    w_in, w_out, w_up, w_dn = f(inp["w_in"][0]), f(inp["w_out"][0]), f(inp["w_up"][0]), f(inp["w_down"][0])
    fw = f(inp["fourier_w"][0])
    g1, g2, g3 = f(inp["norm_mix_w"][0]), f(inp["norm_ffn_w"][0]), f(inp["norm_final_w"])
    cw, cb = f(inp["ssm_conv_w"][0]), f(inp["ssm_conv_b"][0])
    fcw, fcb = f(inp["ffn_conv_w"][0]), f(inp["ffn_conv_b"][0])
    nw = f(inp["ssm_norm_w"][0]).reshape(24, 128).T.copy()
    cb_l = cb.reshape(40, 128).T.copy()
    fcb_l = fcb.reshape(88, 128).T.copy()
    prm = [f(inp[k][0]) for k in ("dt_bias_fwd", "a_log_fwd", "dt_bias_bwd", "a_log_bwd", "ssm_d")]
    maps = []
    for c in range(8):
        b, hf = c // 2, c % 2
        if hf == 0:
            xc, cwc, fcwc = x[b], cw, fcw
            ssp = np.stack([prm[0], prm[1], prm[2], prm[3], prm[4]])
        else:
            xc, cwc, fcwc = np.ascontiguousarray(x[b, ::-1]), cw[::-1], fcw[::-1]
            ssp = np.stack([prm[2], prm[3], prm[0], prm[1], prm[4]])
        cw_l = np.ascontiguousarray(cwc.reshape(5, 40, 128).transpose(2, 1, 0))
        fcw_l = np.ascontiguousarray(fcwc.reshape(3, 88, 128).transpose(2, 1, 0))
        maps.append({"x": xc, "g1": g1, "w_in": w_in, "fw": fw, "ccs": cs["ccs"], "tab": cs["tab"][hf],
                     "cw": cw_l, "cb": cb_l, "ssp": np.ascontiguousarray(ssp), "nw": nw, "cst": cs["cst"],
                     "idb": cs["idb"], "nmk": cs["nmk"], "w_out": w_out, "g2": g2, "w_up": w_up, "fcw": fcw_l, "fcb": fcb_l,
                     "w_dn": w_dn, "g3": g3})
    return maps


_NC = {}


def kernel(**inputs):
    if "nc" not in _NC:
        _NC["nc"] = build_program()
    maps = _in_maps(inputs)
    res = run_bass_kernel_spmd(_NC["nc"], maps, core_ids=list(range(8)))
    out = np.empty((4, S, D), np.float32)
    for c in range(8):
        b, hf = c // 2, c % 2
        o = np.asarray(res.results[c]["out"], dtype=np.float32)
        if hf == 0:
            out[b, :NOWN] = o
        else:
            out[b, NOWN:] = o[::-1]
    return out
```

```python
import os
from contextlib import ExitStack
import numpy as np
import ml_dtypes
import concourse.bass as bass
import concourse.mybir as mybir
from concourse.bass_utils import run_bass_kernel_spmd

F32 = mybir.dt.float32
BF16 = mybir.dt.bfloat16
ALU = mybir.AluOpType
AF = mybir.ActivationFunctionType
AX = mybir.AxisListType

D = 2048
S = 4096
NLOC = 2176
NOWN = 2048
NCH_LOC = 17
FW = 1024
SW = 3072
XBC = 5120
INW = 9264
FFN = 5632
EPS = 1e-5
SAME_ENGINE_SYNC = True
ENGS = ("pe", "act", "dve", "pool", "sp")


class Buf:
    __slots__ = ("name", "lw", "rd", "dsem", "dcnt", "dbase")

    def __init__(self, name):
        self.name = name
        self.lw = None
        self.rd = []
        self.dsem = None
        self.dcnt = 0
        self.dbase = 0


class Op:
    __slots__ = ("fn", "waits", "signal", "dma_buf")

    def __init__(self, fn):
        self.fn = fn
        self.waits = []
        self.signal = False
        self.dma_buf = None


class SemPool:
    def __init__(self, nc, es, n_dma, n_eng):
        self.dma = [[es.enter_context(nc.semaphore(f"dq{i}")), 0] for i in range(n_dma)]
        self.eng = [es.enter_context(nc.semaphore(f"eq{i}")) for i in range(n_eng)]
        self.eng_next = 0
        self.free = list(range(n_dma))

    def take_eng(self):
        s = self.eng[self.eng_next]
        self.eng_next += 1
        return s


class Sched:
    def __init__(self, nc, pool):
        self.nc = nc
        self.pool = pool
        self.q = {e: [] for e in ENGS}
        self.seen_eng = {e: {} for e in ENGS}
        self.seen_dma = {e: {} for e in ENGS}
        self.bufs = []
        self.dma_bufs = []

    def buf(self, name):
        b = Buf(name)
        self.bufs.append(b)
        return b

    def bufs_n(self, name, n):
        return [self.buf(f"{name}{i}") for i in range(n)]

    def _add_dep(self, eng, op, dep):
        if dep[0] == "eng":
            _, e2, idx = dep
            if e2 == eng and (eng == "pe" or not SAME_ENGINE_SYNC):
                return
            if self.seen_eng[eng].get(e2, -1) >= idx:
                return
            self.seen_eng[eng][e2] = idx
            self.q[e2][idx].signal = True
            op.waits.append(dep)
        else:
            _, b, cnt = dep
            if self.seen_dma[eng].get(id(b), -1) >= cnt:
                return
            self.seen_dma[eng][id(b)] = cnt
            op.waits.append(dep)

    def op(self, eng, fn, reads=(), writes=()):
        o = Op(fn)
        idx = len(self.q[eng])
        for b in reads:
            if b.lw is not None:
                self._add_dep(eng, o, b.lw)
        for b in writes:
            if b.lw is not None:
                self._add_dep(eng, o, b.lw)
            for r in b.rd:
                self._add_dep(eng, o, r)
        me = ("eng", eng, idx)
        for b in reads:
            b.rd.append(me)
        for b in writes:
            b.lw = me
            b.rd = []
        self.q[eng].append(o)
        return o

    def _dsem(self, b):
        if b.dsem is None:
            i = self.pool.free.pop()
            b.dsem = i
            b.dbase = self.pool.dma[i][1]
            self.dma_bufs.append(b)
        return b

    def dma(self, eng, fn, src=None, dst=None):
        o = Op(fn)
        if src is not None and src.lw is not None:
            self._add_dep(eng, o, src.lw)
        if dst is not None:
            if dst.lw is not None:
                self._add_dep(eng, o, dst.lw)
            for r in dst.rd:
                self._add_dep(eng, o, r)
        b = dst if dst is not None else src
        self._dsem(b)
        b.dcnt += 1
        me = ("dma", b, b.dcnt)
        if dst is not None:
            dst.lw = me
            dst.rd = []
        else:
            src.rd.append(me)
        o.dma_buf = b
        self.q[eng].append(o)
        return o

    def emit(self):
        nc = self.nc
        pool = self.pool
        Sched.n_emit = getattr(Sched, "n_emit", -1) + 1
        if str(Sched.n_emit) in os.environ.get("KDBG_SKIP_EMITS", "").split(","):
            for b in self.dma_bufs:
                pool.free.append(b.dsem)
                b.dsem = None
            return
        last = {}
        for e in ENGS:
            if e == "sp":
                continue
            idxs = [i for i, o in enumerate(self.q[e]) if o.dma_buf is None]
            if idxs:
                last[e] = idxs[-1]
        for e2, idx in last.items():
            self.q[e2][idx].signal = True
        esem = {e: pool.take_eng() for e in ENGS if e != "sp"}
        val = {}
        for e in esem:
            c = 0
            for i, o in enumerate(self.q[e]):
                if o.signal:
                    c += 1
                val[(e, i)] = c
        dma_final = [(pool.dma[b.dsem][0], 16 * (b.dbase + b.dcnt)) for b in self.dma_bufs]

        def resolve(w):
            if w[0] == "eng":
                return esem[w[1]], val[(w[1], w[2])]
            return pool.dma[w[1].dsem][0], 16 * (w[1].dbase + w[2])

        with nc.Block() as block:
            decos = {"pe": block.tensor, "act": block.scalar, "dve": block.vector,
                     "pool": block.gpsimd, "sp": block.sync}
            for eng in ENGS:
                ops = self.q[eng]

                def body(e, ops=ops, eng=eng):
                    for o in ops:
                        for w in o.waits:
                            s, v = resolve(w)
                            e.wait_ge(s, v)
                        inst = o.fn(e)
                        if o.dma_buf is not None:
                            inst.then_inc(pool.dma[o.dma_buf.dsem][0], 16)
                        elif o.signal:
                            inst.then_inc(esem[eng], 1)
                    for e2, idx in last.items():
                        e.wait_ge(esem[e2], val[(e2, idx)])
                    for s, v in dma_final:
                        e.wait_ge(s, v)

                decos[eng](body)
        for b in self.dma_bufs:
            pool.dma[b.dsem][1] = b.dbase + b.dcnt
            pool.free.append(b.dsem)
            b.dsem = None


def build_program(stop_after=99, dump=None):
    Sched.n_emit = -1
    nc = bass.Bass("TRN2", target_bir_lowering=False)

    def din(name, shape, dt=F32):
        return nc.dram_tensor(name, list(shape), dt, kind="ExternalInput").ap()

    def dscr(name, shape, dt):
        kind = "ExternalOutput" if dump == name else "Internal"
        return nc.dram_tensor(name, list(shape), dt, kind=kind).ap()

    x_d = din("x", [S, D])
    g1_d = din("g1", [D])
    w_in_d = din("w_in", [D, INW])
    fw_d = din("fw", [8, 128, 128])
    ccs_d = din("ccs", [2, 128, 128], BF16)
    tab_d = din("tab", [2, 128, 32, NLOC], BF16)
    cw_d = din("cw", [128, 40, 5])
    cb_d = din("cb", [128, 40])
    ssp_d = din("ssp", [5, 48])
    nw_d = din("nw", [128, 24])
    cst_d = din("cst", [4, 128, 128])
    idb_d = din("idb", [128, 128], BF16)
    nmk_d = din("nmk", [2, 128, 128], BF16)
    w_out_d = din("w_out", [4096, D])
    g2_d = din("g2", [D])
    w_up_d = din("w_up", [D, 2 * FFN])
    fcw_d = din("fcw", [128, 88, 3])
    fcb_d = din("fcb", [128, 88])
    w_dn_d = din("w_dn", [FFN, D])
    g3_d = din("g3", [D])
    out_d = nc.dram_tensor("out", [NOWN, D], F32, kind="ExternalOutput").ap()

    uT_d = dscr("uT", [FW, S], BF16)
    zsT_d = dscr("zsT", [SW, NLOC], BF16)
    xT_d = dscr("xT", [SW, S], BF16)
    BT_d = dscr("BT", [1024, S], BF16)
    CT_d = dscr("CT", [1024, NLOC], BF16)
    V_d = dscr("V", [8, 128, 32, 256], BF16)
    mixT_d = dscr("mixT", [4096, NLOC], BF16)
    yb_d = dscr("yb", [NLOC, SW], F32)
    x1_d = dscr("x1", [NLOC, D], F32)
    h2T_d = dscr("h2T", [D, NLOC], BF16)
    actT_d = dscr("actT", [FFN, NOWN], BF16)
    x2_d = dscr("x2", [NOWN, D], F32)

    es = ExitStack()
    with es:
        es.enter_context(nc.allow_low_precision("bf16 matmul operands, fp32 accumulate"))
        es.enter_context(nc.allow_non_contiguous_dma("tiled layouts"))
        pool = SemPool(nc, es, 56, 40)

        uid = [0]

        def sb(st, name, shape, dt):
            uid[0] += 1
            return st.enter_context(nc.sbuf_tensor(f"s{uid[0]}_{name}", list(shape), dt))

        def ps(st, name, shape, dt=F32):
            uid[0] += 1
            return st.enter_context(nc.psum_tensor(f"p{uid[0]}_{name}", list(shape), dt))

        dtraw = sb(es, "dtraw", [128, 32, 48], F32)
        cst = sb(es, "cst", [128, 4, 128], F32)
        idb = sb(es, "idb", [128, 128], BF16)
        epsc = sb(es, "epsc", [128, 1], F32)
        Umat, Lmat, ones_f, id_f = (cst[:, i, :] for i in range(4))

        def rms_rows(sc, src, src_b, ssb, ss_ap, rstd_ap, gain, gain_b, hb, hb_b, junk, junk_b, eps_b=None):
            sc.op("act", lambda e: e.activation(out=junk[:], in_=src, func=AF.Square, accum_out=ss_ap),
                  reads=[src_b], writes=[junk_b, ssb])
            sc.op("act", lambda e: e.activation(out=rstd_ap, in_=ss_ap, func=AF.Sqrt, bias=epsc[:], scale=1.0 / D),
                  reads=[ssb] + ([eps_b] if eps_b is not None else []), writes=[ssb])
            sc.op("dve", lambda e: e.reciprocal(out=rstd_ap, in_=rstd_ap), reads=[ssb], writes=[ssb])
            sc.op("dve", lambda e: e.scalar_tensor_tensor(out=hb[:], in0=src, scalar=rstd_ap, in1=gain[:],
                                                          op0=ALU.mult, op1=ALU.mult),
                  reads=[src_b, ssb, gain_b], writes=[hb_b])

        if os.environ.get("KDBG_INIT"):
            with ExitStack() as st:
                sc = Sched(nc, pool)
                zt = sb(st, "zt", [128, S], BF16)
                zt_b = sc.buf("zt")
                sc.op("pool", lambda e: e.memset(zt[:], 0.5), writes=[zt_b])
                for t_, nm in ((dtraw, "a"), (epsc, "d")):
                    sc.op("pool", lambda e, t_=t_: e.memset(t_[:], 0.25), writes=[sc.buf(nm)])
                sc.dma("sp", lambda e: e.dma_start(out=cst[:], in_=cst_d.rearrange("c p f -> p c f")), dst=sc.buf("b"))
                sc.dma("sp", lambda e: e.dma_start(out=idb[:], in_=idb_d), dst=sc.buf("c"))
                for dten, rows, cols in ((xT_d, SW, S), (BT_d, 1024, S), (CT_d, 1024, NLOC), (zsT_d, SW, NLOC), (uT_d, FW, S)):
                    for r0 in range(0, rows, 128):
                        sc.dma("sp", lambda e, dten=dten, r0=r0, cols=cols: e.dma_start(out=dten[r0:r0 + 128, :], in_=zt[:, :cols]), src=zt_b)
                sc.emit()
        with ExitStack() as s12:
            hT = sb(s12, "hT", [128, 16, S], BF16)
            with ExitStack() as st:
                sc = Sched(nc, pool)
                xt = [sb(st, f"xt{i}", [128, D], F32) for i in range(2)]
                hb = [sb(st, f"hb{i}", [128, D], BF16) for i in range(2)]
                junk = sb(st, "junk", [128, D], BF16)
                gain = sb(st, "gain1", [128, D], F32)
                ss = sb(st, "ss", [128, 64], F32)
                pt = [ps(st, f"pt{i}", [128, 1024], BF16)[:, 0:512] for i in range(4)]
                xt_b, hb_b, pt_b = sc.bufs_n("xt", 2), sc.bufs_n("hb", 2), sc.bufs_n("pt", 4)
                junk_b, gain_b, cst_b, idb_b = sc.buf("junk"), sc.buf("gain"), sc.buf("cst"), sc.buf("idb")
                hT_b = sc.buf("hT")
                epsc_b = sc.buf("epsc")
                sc.op("pool", lambda e: e.memset(epsc[:], EPS), writes=[epsc_b])
                sc.dma("sp", lambda e: e.dma_start(out=gain[:], in_=g1_d.partition_broadcast(128)), dst=gain_b)
                sc.dma("sp", lambda e: e.dma_start(out=cst[:], in_=cst_d.rearrange("c p f -> p c f")), dst=cst_b)
                sc.dma("sp", lambda e: e.dma_start(out=idb[:], in_=idb_d), dst=idb_b)
                for i in range(32):
                    a = i % 2
                    ssb = sc.buf(f"ss{i}")
                    sc.dma("sp", lambda e, i=i, a=a: e.dma_start(out=xt[a][:], in_=x_d[i * 128:(i + 1) * 128, :]),
                           dst=xt_b[a])
                    rms_rows(sc, xt[a][:], xt_b[a], ssb, ss[:, 2 * i:2 * i + 1], ss[:, 2 * i + 1:2 * i + 2],
                             gain, gain_b, hb[a], hb_b[a], junk, junk_b, eps_b=epsc_b)
                    for q in range(4):
                        pi = (4 * i + q) % 4
                        for r in range(4):
                            k = 4 * q + r
                            sc.op("pe", lambda e, a=a, k=k, pi=pi, r=r: e.transpose(
                                out=pt[pi][:, r * 128:(r + 1) * 128], in_=hb[a][:, k * 128:(k + 1) * 128],
                                identity=idb[:]), reads=[hb_b[a], idb_b], writes=[pt_b[pi]])
                        sc.op("act", lambda e, i=i, q=q, pi=pi: e.activation(
                            out=hT[:, 4 * q:4 * q + 4, i * 128:(i + 1) * 128],
                            in_=pt[pi].rearrange("p (a b) -> p a b", a=4), func=AF.Copy),
                            reads=[pt_b[pi]], writes=[hT_b])
                sc.emit()
            if stop_after <= 1:
                return nc
            with ExitStack() as st:
                sc = Sched(nc, pool)
                wt = [sb(st, f"wt{i}", [128, 16, 128], BF16) for i in range(2)]
                stage = [sb(st, f"stage{i}", [128, S + 4], BF16) for i in range(2)]
                acc = sb(st, "acc", [128, S], F32)
                osb = [sb(st, f"osb{i}", [128, S], BF16) for i in range(2)]
                cw = sb(st, "cw", [128, 40, 5], F32)
                cb = sb(st, "cb", [128, 40], F32)
                pbank = [ps(st, f"pb{i}", [128, 512], F32) for i in range(8)]
                wt_b, stage_b, osb_b, pb_b = sc.bufs_n("wt", 2), sc.bufs_n("stage", 2), sc.bufs_n("osb", 2), sc.bufs_n("pb", 8)
                acc_b, cw_b, dtraw_b = sc.buf("acc"), sc.buf("cw"), sc.buf("dtraw")
                sc.dma("sp", lambda e: e.dma_start(out=cw[:], in_=cw_d), dst=cw_b)
                sc.dma("sp", lambda e: e.dma_start(out=cb[:], in_=cb_d), dst=cw_b)
                for i in range(2):
                    sc.op("pool", lambda e, i=i: e.memset(stage[i][:], 0.0), writes=[stage_b[i]])
                bank = 0
                w_in_v = w_in_d.rearrange("(k p) c -> p k c", p=128)
                NT = int(os.environ.get("KDBG_NT", "73"))
                for j in ([int(v) for v in os.environ["KDBG_TILES"].split(",")] if os.environ.get("KDBG_TILES") else (range(NT) if "KDBG_TILES" not in os.environ else [])):
                    a = j % 2
                    ncols = 128 if j < 72 else 48
                    sc.dma("pool", lambda e, j=j, a=a, ncols=ncols: e.dma_start(
                        out=wt[a][:, :, :ncols], in_=w_in_v[:, :, j * 128:j * 128 + ncols]), dst=wt_b[a])
                    if j == 72:
                        for i in range(32):
                            b_ = bank % 8
                            bank += 1
                            for k in range(16):
                                sc.op("pe", lambda e, a=a, k=k, i=i, b_=b_: e.matmul(
                                    pbank[b_][:, :48], lhsT=hT[:, k, i * 128:(i + 1) * 128], rhs=wt[a][:, k, :48],
                                    start=(k == 0), stop=(k == 15)), reads=[wt_b[a]], writes=[pb_b[b_]])
                            sc.op("act", lambda e, i=i, b_=b_: e.activation(out=dtraw[:, i, :], in_=pbank[b_][:, :48],
                                                                           func=AF.Copy),
                                  reads=[pb_b[b_]], writes=[dtraw_b])
                        continue
                    kind = "u" if j < 8 else "z" if j < 32 else "x" if j < 56 else "B" if j < 64 else "C"
                    if kind in ("u", "x", "B"):
                        blocks = [(b * 512, 512) for b in range(8)]
                    elif kind == "z":
                        blocks = [(b * 512, 512) for b in range(4)] + [(2048, 128)]
                    else:
                        blocks = [(b * 512, 512) for b in range(4)] + [(2048, 256)]
                    conv = kind in ("x", "B", "C")
                    sa = j % 2
                    oa = j % 2
                    for (t0, n) in blocks:
                        b_ = bank % 8
                        bank += 1
                        for k in range(16):
                            sc.op("pe", lambda e, a=a, k=k, t0=t0, n=n, b_=b_: e.matmul(
                                pbank[b_][:, :n], lhsT=wt[a][:, k, :], rhs=hT[:, k, t0:t0 + n],
                                start=(k == 0), stop=(k == 15)), reads=[wt_b[a]], writes=[pb_b[b_]])
                        if conv:
                            sc.op("act", lambda e, sa=sa, t0=t0, n=n, b_=b_: e.activation(
                                out=stage[sa][:, 2 + t0:2 + t0 + n], in_=pbank[b_][:, :n], func=AF.Copy),
                                reads=[pb_b[b_]], writes=[stage_b[sa]])
                        else:
                            fn = AF.Copy if kind == "u" else AF.Silu
                            sc.op("act", lambda e, oa=oa, t0=t0, n=n, b_=b_, fn=fn: e.activation(
                                out=osb[oa][:, t0:t0 + n], in_=pbank[b_][:, :n], func=fn),
                                reads=[pb_b[b_]], writes=[osb_b[oa]])
                    if kind == "u":
                        T, dst = S, uT_d[j * 128:(j + 1) * 128, :]
                    elif kind == "z":
                        T, dst = NLOC, zsT_d[(j - 8) * 128:(j - 7) * 128, :]
                    elif kind == "x":
                        T, dst = S, xT_d[(j - 32) * 128:(j - 31) * 128, :]
                    elif kind == "B":
                        T, dst = S, BT_d[(j - 56) * 128:(j - 55) * 128, :]
                    else:
                        T, dst = NLOC, CT_d[(j - 64) * 128:(j - 63) * 128, :]
                    if conv:
                        jj = j - 32
                        sc.op("dve", lambda e, sa=sa, jj=jj, T=T: e.tensor_scalar(
                            out=acc[:, :T], in0=stage[sa][:, 0:T], scalar1=cw[:, jj, 0:1], scalar2=cb[:, jj:jj + 1],
                            op0=ALU.mult, op1=ALU.add), reads=[stage_b[sa], cw_b], writes=[acc_b])
                        for tap in range(1, 5):
                            eng = "dve"
                            sc.op(eng, lambda e, sa=sa, jj=jj, T=T, tap=tap: e.scalar_tensor_tensor(
                                out=acc[:, :T], in0=stage[sa][:, tap:tap + T], scalar=cw[:, jj, tap:tap + 1],
                                in1=acc[:, :T], op0=ALU.mult, op1=ALU.add),
                                reads=[stage_b[sa], cw_b, acc_b], writes=[acc_b])
                        sc.op("act", lambda e, oa=oa, T=T: e.activation(out=osb[oa][:, :T], in_=acc[:, :T], func=AF.Silu),
                              reads=[acc_b], writes=[osb_b[oa]])
                    sc.dma("act", lambda e, oa=oa, T=T, dst=dst: e.dma_start(out=dst, in_=osb[oa][:, :T]), src=osb_b[oa])
                sc.emit()
        if stop_after <= 2:
            return nc

        with ExitStack() as st:
            sc = Sched(nc, pool)
            wm = sb(st, "wm", [128, 8, 128], BF16)
            ccs = sb(st, "ccs", [128, 2, 128], BF16)
            Mg = sb(st, "Mg", [128, 8, 256], BF16)
            uTg = [sb(st, f"uTg{i}", [128, S], BF16) for i in range(2)]
            Vsb = [sb(st, f"Vsb{i}", [128, 32, 256], BF16) for i in range(2)]
            pM = [ps(st, f"pM{i}", [128, 512], F32) for i in range(4)]
            wm_b, ccs_b, Mg_b = sc.buf("wm"), sc.buf("ccs"), sc.buf("Mg")
            uTg_b, Vsb_b, pM_b = sc.bufs_n("uTg", 2), sc.bufs_n("Vsb", 2), sc.bufs_n("pM", 4)
            sc.dma("pool", lambda e: e.dma_start(out=wm[:], in_=fw_d.rearrange("g c d -> c g d")), dst=wm_b)
            sc.dma("sp", lambda e: e.dma_start(out=ccs[:], in_=ccs_d.rearrange("q c d -> c q d")), dst=ccs_b)
            for g in range(8):
                b_ = g % 4
                for q in range(2):
                    sc.op("pe", lambda e, g=g, q=q, b_=b_: e.matmul(pM[b_][:, q * 128:(q + 1) * 128], lhsT=ccs[:, q, :],
                                                                  rhs=wm[:, g, :], start=True, stop=True),
                          reads=[wm_b, ccs_b], writes=[pM_b[b_]])
                sc.op("act", lambda e, g=g, b_=b_: e.activation(out=Mg[:, g, :], in_=pM[b_][:, :256], func=AF.Copy),
                      reads=[pM_b[b_]], writes=[Mg_b])
            cnt = 0
            for g in range(8):
                a = g % 2
                sc.dma("sp", lambda e, g=g, a=a: e.dma_start(out=uTg[a][:], in_=uT_d[g * 128:(g + 1) * 128, :]),
                       dst=uTg_b[a])
                for stl in range(32):
                    b_ = cnt % 4
                    cnt += 1
                    sc.op("pe", lambda e, g=g, a=a, stl=stl, b_=b_: e.matmul(
                        pM[b_][:, :256], lhsT=uTg[a][:, stl * 128:(stl + 1) * 128], rhs=Mg[:, g, :],
                        start=True, stop=True), reads=[uTg_b[a], Mg_b], writes=[pM_b[b_]])
                    eng = "act" if stl % 2 == 0 else "dve"
                    if eng == "act":
                        sc.op("act", lambda e, a=a, stl=stl, b_=b_: e.activation(out=Vsb[a][:, stl, :], in_=pM[b_][:, :256],
                                                                              func=AF.Copy),
                              reads=[pM_b[b_]], writes=[Vsb_b[a]])
                    else:
                        sc.op("dve", lambda e, a=a, stl=stl, b_=b_: e.tensor_copy(out=Vsb[a][:, stl, :], in_=pM[b_][:, :256]),
                              reads=[pM_b[b_]], writes=[Vsb_b[a]])
                sc.dma("pool", lambda e, g=g, a=a: e.dma_start(out=V_d[g], in_=Vsb[a][:]), src=Vsb_b[a])
            sc.emit()
        if stop_after <= 3:
            return nc
        with ExitStack() as st:
            sc = Sched(nc, pool)
            tabs = [sb(st, f"tab{i}", [128, 2, 32, 512], BF16) for i in range(2)]
            Vg = [sb(st, f"Vg{i}", [128, 32, 256], BF16) for i in range(2)]
            aT = [sb(st, f"aT{i}", [128, 512], BF16) for i in range(2)]
            pF = [ps(st, f"pF{i}", [128, 512], F32) for i in range(4)]
            tab_b, Vg_b, aT_b, pF_b = sc.bufs_n("tab", 2), sc.bufs_n("Vg", 2), sc.bufs_n("aT", 2), sc.bufs_n("pF", 4)
            cnt = 0
            kblocks = [(b * 512, 512) for b in range(4)] + [(2048, 128)]
            for kb, (k0, n) in enumerate(kblocks):
                ta = kb % 2
                for q in range(2):
                    sc.dma("sp", lambda e, ta=ta, q=q, k0=k0, n=n: e.dma_start(
                        out=tabs[ta][:, q, :, :n], in_=tab_d[q][:, :, k0:k0 + n]),
                        dst=tab_b[ta])
                for g in range(8):
                    a = cnt % 2
                    b_ = cnt % 4
                    cnt += 1
                    sc.dma("sp", lambda e, g=g, a=a: e.dma_start(out=Vg[a][:], in_=V_d[g]), dst=Vg_b[a])
                    for stl in range(32):
                        for q in range(2):
                            sc.op("pe", lambda e, a=a, ta=ta, stl=stl, q=q, n=n, b_=b_: e.matmul(
                                pF[b_][:, :n], lhsT=Vg[a][:, stl, q * 128:(q + 1) * 128], rhs=tabs[ta][:, q, stl, :n],
                                start=(stl == 0 and q == 0), stop=(stl == 31 and q == 1)),
                                reads=[Vg_b[a], tab_b[ta]], writes=[pF_b[b_]])
                    sc.op("act", lambda e, a=a, n=n, b_=b_: e.activation(out=aT[a][:, :n], in_=pF[b_][:, :n], func=AF.Copy),
                          reads=[pF_b[b_]], writes=[aT_b[a]])
                    sc.dma("act", lambda e, g=g, a=a, k0=k0, n=n: e.dma_start(out=mixT_d[g * 128:(g + 1) * 128, k0:k0 + n],
                                                                        in_=aT[a][:, :n]), src=aT_b[a])
            sc.emit()
        if stop_after <= 4:
            return nc

        with ExitStack() as s4:
            prm = sb(s4, "prm", [128, 5, 48], F32)
            dts = sb(s4, "dts", [128, 2, 32, 48], F32)
            adt = sb(s4, "adt", [128, 2, 32, 48], F32)
            nw = sb(s4, "nw", [128, 24], F32)
            nmk = sb(s4, "nmk", [128, 2, 128], BF16)
            stateT = sb(s4, "stateT", [128, SW], F32)
            prevb = sb(s4, "prevb", [128, SW], BF16)
            xTc = [sb(s4, f"xTc{i}", [128, 24, 128], BF16) for i in range(2)]
            BTc = [sb(s4, f"BTc{i}", [128, 8, 128], BF16) for i in range(2)]
            CTc = [sb(s4, f"CTc{i}", [128, 8, 128], BF16) for i in range(2)]
            xtok = sb(s4, "xtok", [128, SW], BF16)
            Btok = sb(s4, "Btok", [128, 1024], BF16)
            xd2 = [sb(s4, f"xd{i}", [128, SW], BF16) for i in range(2)]
            xdw2 = [sb(s4, f"xdw{i}", [128, SW], BF16) for i in range(2)]
            sm2 = sb(s4, "sm", [128, 2, 8, 48], F32)
            CBm = [sb(s4, f"CBm{i}", [128, 128], F32) for i in range(2)]
            Eb = [sb(s4, f"Eb{i}", [128, 128], F32) for i in range(3)]
            MT = [sb(s4, f"MT{i}", [128, 128], BF16) for i in range(3)]
            yoff = [sb(s4, f"yoff{i}", [128, 384], F32) for i in range(2)]
            ysb = sb(s4, "ysb", [128, SW], F32)
            ybc = sb(s4, "ybc", [128, SW], F32)
            zsc = sb(s4, "zsc", [128, 24, 128], BF16)
            yg = sb(s4, "yg", [128, 24, 128], F32)
            sq = sb(s4, "sq", [128, 24, 128], F32)
            rsg = [sb(s4, f"rsg{i}", [128, 128], F32) for i in range(2)]
            osb4 = sb(s4, "osb4", [128, 24, 128], BF16)
            pTr = [ps(s4, f"pTr{i}", [128, 1024], BF16)[:, 0:512] for i in range(2)]
            pA = [ps(s4, f"pA{i}", [128, 512], F32) for i in range(2)]
            pY = [ps(s4, f"pY{i}", [128, 512], F32) for i in range(2)]
            pO = [ps(s4, f"pO{i}", [128, 512], F32) for i in range(2)]

            LAGH = 2
            for direction in ("b", "f"):
                sc = Sched(nc, pool)
                di = 1 if direction == "b" else 0
                sm = xd = xdw = None
                Tm = Lmat if direction == "b" else Umat
                prm_b, dts_b, adt_b, nw_b = sc.buf("prm"), sc.buf("dts"), sc.buf("adt"), sc.buf("nw")
                state_b = sc.bufs_n("state", 8)
                prev_b = sc.bufs_n("prev", 8)
                xTc_b, BTc_b, CTc_b = sc.bufs_n("xTc", 2), sc.bufs_n("BTc", 2), sc.bufs_n("CTc", 2)
                xtok_b, Btok_b = sc.buf("xtok"), sc.buf("Btok")
                xd_b2, xdw_b2, sm_b2 = sc.bufs_n("xd", 2), sc.bufs_n("xdw", 2), sc.bufs_n("sm", 2)
                CBm_b, Eb_b, MT_b, yoff_b = sc.bufs_n("CBm", 2), sc.bufs_n("Eb", 3), sc.bufs_n("MT", 3), sc.bufs_n("yoff", 2)
                ysb_g = sc.bufs_n("ysb", 8)
                ybc_b, zsc_b, yg_b, sq_b, osb4_b = sc.buf("ybc"), sc.buf("zsc"), sc.buf("yg"), sc.buf("sq"), sc.buf("osb4")
                rsg_b = sc.bufs_n("rsg", 2)
                pTr_b, pY_b, pO_b = sc.bufs_n("pTr", 2), sc.bufs_n("pY", 2), sc.bufs_n("pO", 2)
                pA_b = sc.bufs_n("pA", 2)
                pSm, pSm_b = pO[1][:, 384:512], pO_b[1]
                if direction == "b":
                    sc.dma("sp", lambda e, sm=sm, xd=xd, xdw=xdw: e.dma_start(out=prm[:].rearrange("p a h -> p (a h)"),
                                                       in_=ssp_d.rearrange("a h -> (a h)").partition_broadcast(128)), dst=prm_b)
                    sc.dma("sp", lambda e, sm=sm, xd=xd, xdw=xdw: e.dma_start(out=nw[:], in_=nw_d), dst=nw_b)
                    sc.dma("sp", lambda e: e.dma_start(out=nmk[:], in_=nmk_d.rearrange("q p f -> p q f")), dst=nw_b)
                    for d2 in range(2):
                        bias = prm[:, 2 * d2, :].unsqueeze(1).to_broadcast([128, 32, 48])
                        alog = prm[:, 2 * d2 + 1, :]
                        sc.op("dve", lambda e, d2=d2, bias=bias, sm=sm, xd=xd, xdw=xdw: e.tensor_tensor(out=dts[:, d2], in0=dtraw[:], in1=bias, op=ALU.add),
                              reads=[prm_b], writes=[dts_b])
                        sc.op("act", lambda e, d2=d2, sm=sm, xd=xd, xdw=xdw: e.activation(out=dts[:, d2], in_=dts[:, d2], func=AF.Exp),
                              reads=[dts_b], writes=[dts_b])
                        sc.op("act", lambda e, d2=d2, sm=sm, xd=xd, xdw=xdw: e.activation(out=dts[:, d2], in_=dts[:, d2], func=AF.Ln, bias=1.0),
                              reads=[dts_b], writes=[dts_b])
                        sc.op("act", lambda e, alog=alog, sm=sm, xd=xd, xdw=xdw: e.activation(out=alog, in_=alog, func=AF.Exp),
                              reads=[prm_b], writes=[prm_b])
                        sc.op("dve", lambda e, alog=alog, sm=sm, xd=xd, xdw=xdw: e.tensor_scalar(out=alog, in0=alog, scalar1=-1.0, scalar2=None, op0=ALU.mult),
                              reads=[prm_b], writes=[prm_b])
                        sc.op("dve", lambda e, d2=d2, alog=alog, sm=sm, xd=xd, xdw=xdw: e.tensor_tensor(
                            out=adt[:, d2], in0=dts[:, d2], in1=alog.unsqueeze(1).to_broadcast([128, 32, 48]), op=ALU.mult),
                            reads=[prm_b, dts_b], writes=[adt_b])
                sc.op("pool", lambda e, sm=sm, xd=xd, xdw=xdw: e.memset(stateT[:], 0.0), writes=state_b)
                sc.op("pool", lambda e, sm=sm, xd=xd, xdw=xdw: e.memset(prevb[:], 0.0), writes=prev_b)
                chunks = list(range(31, -1, -1)) if direction == "b" else list(range(NCH_LOC))
                if "KDBG4_LIST" in os.environ:
                    chunks = [int(v) for v in os.environ["KDBG4_LIST"].split(",")]
                if "KDBG4_CHUNKS" in os.environ:
                    chunks = chunks[:int(os.environ["KDBG4_CHUNKS"])]
                hcnt = 0
                gcnt = 0
                for ci, c in enumerate(chunks):
                    local = c < NCH_LOC and os.environ.get("KDBG4_LOCAL", "1") == "1"
                    a = ci % 2
                    t0 = c * 128
                    sm, sm_b = sm2[:, a], sm_b2[a]
                    xd, xd_b, xdw, xdw_b = xd2[a], xd_b2[a], xdw2[a], xdw_b2[a]
                    sc.dma("sp", lambda e, a=a, t0=t0, sm=sm, xd=xd, xdw=xdw: e.dma_start(
                        out=xTc[a][:], in_=xT_d.rearrange("(j p) t -> p j t", p=128)[:, :, t0:t0 + 128]), dst=xTc_b[a])
                    sc.dma("sp", lambda e, a=a, t0=t0, sm=sm, xd=xd, xdw=xdw: e.dma_start(
                        out=BTc[a][:], in_=BT_d.rearrange("(j p) t -> p j t", p=128)[:, :, t0:t0 + 128]), dst=BTc_b[a])
                    if local:
                        sc.dma("sp", lambda e, a=a, t0=t0, sm=sm, xd=xd, xdw=xdw: e.dma_start(
                            out=CTc[a][:], in_=CT_d.rearrange("(j p) t -> p j t", p=128)[:, :, t0:t0 + 128]), dst=CTc_b[a])
                    for q in range(8):
                        pi = q % 2
                        for r in range(4):
                            j = 4 * q + r
                            src = xTc[a][:, j, :] if j < 24 else BTc[a][:, j - 24, :]
                            sc.op("pe", lambda e, src=src, pi=pi, r=r, sm=sm, xd=xd, xdw=xdw: e.transpose(out=pTr[pi][:, r * 128:(r + 1) * 128],
                                                                                  in_=src, identity=idb[:]),
                                  reads=[xTc_b[a], BTc_b[a]], writes=[pTr_b[pi]])
                        if q < 6:
                            sc.op("act", lambda e, q=q, pi=pi, sm=sm, xd=xd, xdw=xdw: e.activation(out=xtok[:, q * 512:(q + 1) * 512], in_=pTr[pi],
                                                                           func=AF.Copy), reads=[pTr_b[pi]], writes=[xtok_b])
                        else:
                            sc.op("act", lambda e, q=q, pi=pi, sm=sm, xd=xd, xdw=xdw: e.activation(out=Btok[:, (q - 6) * 512:(q - 5) * 512], in_=pTr[pi],
                                                                           func=AF.Copy), reads=[pTr_b[pi]], writes=[Btok_b])
                    sc.op("pe", lambda e, c=c, Tm=Tm, di=di, sm=sm, xd=xd, xdw=xdw: e.matmul(pSm[:, 0:48], lhsT=Tm, rhs=adt[:, di, c, :], start=True, stop=True),
                          reads=[adt_b], writes=[pSm_b])
                    sc.op("pe", lambda e, c=c, di=di, sm=sm, xd=xd, xdw=xdw: e.matmul(pSm[:, 48:96], lhsT=ones_f, rhs=adt[:, di, c, :], start=True, stop=True),
                          reads=[adt_b], writes=[pSm_b])
                    sc.op("act", lambda e, sm=sm, xd=xd, xdw=xdw: e.activation(out=sm[:, 0:2, :].rearrange("p a h -> p (a h)"), in_=pSm[:, 0:96], func=AF.Copy),
                          reads=[pSm_b], writes=[sm_b])
                    sc.op("dve", lambda e, sm=sm, xd=xd, xdw=xdw: e.tensor_tensor(out=sm[:, 2, :], in0=sm[:, 1, :], in1=sm[:, 0, :], op=ALU.subtract),
                          reads=[sm_b], writes=[sm_b])
                    sc.op("act", lambda e, sm=sm, xd=xd, xdw=xdw: e.activation(out=sm[:, 3, :], in_=sm[:, 2, :], func=AF.Exp), reads=[sm_b], writes=[sm_b])
                    sc.op("act", lambda e, sm=sm, xd=xd, xdw=xdw: e.activation(out=sm[:, 6, :], in_=sm[:, 1, :], func=AF.Exp), reads=[sm_b], writes=[sm_b])
                    if local:
                        sc.op("act", lambda e, sm=sm, xd=xd, xdw=xdw: e.activation(out=sm[:, 4, :], in_=sm[:, 0, :], func=AF.Exp), reads=[sm_b], writes=[sm_b])
                        sc.op("dve", lambda e, sm=sm, xd=xd, xdw=xdw: e.tensor_scalar(out=sm[:, 5, :], in0=sm[:, 0, :], scalar1=-1.0, scalar2=None, op0=ALU.mult),
                              reads=[sm_b], writes=[sm_b])
                    sc.op("dve", lambda e, c=c, di=di, sm=sm, xd=xd, xdw=xdw: e.tensor_tensor(out=sm[:, 7, :], in0=dts[:, di, c, :], in1=sm[:, 3, :], op=ALU.mult),
                          reads=[sm_b, dts_b], writes=[sm_b])
                    x3 = xtok[:].rearrange("p (h q) -> p h q", q=64)
                    sc.op("pool", lambda e, x3=x3, sm=sm, xd=xd, xdw=xdw: e.tensor_tensor(
                        out=xdw[:].rearrange("p (h q) -> p h q", q=64), in0=x3,
                        in1=sm[:, 7, :].unsqueeze(2).to_broadcast([128, 48, 64]), op=ALU.mult),
                        reads=[xtok_b, sm_b], writes=[xdw_b])
                    if local:
                        sc.op("dve", lambda e, x3=x3, c=c, di=di, sm=sm, xd=xd, xdw=xdw: e.tensor_tensor(
                            out=xd[:].rearrange("p (h q) -> p h q", q=64), in0=x3,
                            in1=dts[:, di, c, :].unsqueeze(2).to_broadcast([128, 48, 64]), op=ALU.mult),
                            reads=[xtok_b, dts_b], writes=[xd_b])
                    for g in range(8):
                        gs = slice(g * 384, (g + 1) * 384)
                        ga = gcnt % 2
                        gcnt += 1
                        if local:
                            pa0 = pO[ga]
                            sc.op("pe", lambda e, a=a, g=g, pa0=pa0, sm=sm, xd=xd, xdw=xdw: e.matmul(pa0[:, 384:512], lhsT=BTc[a][:, g, :], rhs=CTc[a][:, g, :],
                                                                             start=True, stop=True),
                                  reads=[BTc_b[a], CTc_b[a]], writes=[pO_b[ga]])
                            sc.op("dve", lambda e, ga=ga, pa0=pa0: e.tensor_copy(out=CBm[ga][:], in_=pa0[:, 384:512]),
                                  reads=[pO_b[ga]], writes=[CBm_b[ga]])
                            def headA(r, g=g, ga=ga, sm=sm, sm_b=sm_b):
                                nonlocal hcnt
                                h = 6 * g + r
                                ha = hcnt % 3
                                pb_ = hcnt % 2
                                hcnt += 1
                                sc.op("pe", lambda e: e.matmul(
                                    pA[pb_][:, 0:128], lhsT=sm[:, 0, h:h + 1].to_broadcast([128, 128]),
                                    rhs=id_f, start=True, stop=False), reads=[sm_b], writes=[pA_b[pb_]])
                                sc.op("pe", lambda e: e.matmul(pA[pb_][:, 0:128], lhsT=idb[:], rhs=nmk[:, di, :],
                                                               start=False, stop=True), reads=[nw_b], writes=[pA_b[pb_]])
                                sc.op("act", lambda e: e.activation(
                                    out=Eb[ha][:], in_=pA[pb_][:, 0:128], func=AF.Exp,
                                    bias=sm[:, 5, h:h + 1], scale=1.0), reads=[pA_b[pb_], sm_b], writes=[Eb_b[ha]])
                                sc.op("dve", lambda e: e.tensor_tensor(
                                    out=MT[ha][:], in0=Eb[ha][:], in1=CBm[ga][:], op=ALU.mult),
                                    reads=[Eb_b[ha], CBm_b[ga]], writes=[MT_b[ha]])
                                return ha

                            def headY(r, ha, g=g, ga=ga, xd=xd, xd_b=xd_b):
                                h = 6 * g + r
                                sc.op("pe", lambda e: e.matmul(
                                    pY[ga][:, r * 64:(r + 1) * 64], lhsT=MT[ha][:], rhs=xd[:, h * 64:(h + 1) * 64],
                                    start=True, stop=True), reads=[MT_b[ha], xd_b], writes=[pY_b[ga]])

                            has = {}
                            for idx in range(6 + LAGH):
                                if idx < 6:
                                    has[idx] = headA(idx)
                                if idx >= LAGH:
                                    headY(idx - LAGH, has[idx - LAGH])
                            sc.op("pe", lambda e, a=a, g=g, ga=ga, gs=gs, sm=sm, xd=xd, xdw=xdw: e.matmul(pO[ga][:, 0:384], lhsT=CTc[a][:, g, :], rhs=prevb[:, gs],
                                                                                   start=True, stop=True),
                                  reads=[CTc_b[a], prev_b[g]], writes=[pO_b[ga]])
                            for r in range(6):
                                h = 6 * g + r
                                sc.op("act", lambda e, ga=ga, r=r, h=h, sm=sm, xd=xd, xdw=xdw: e.activation(
                                    out=yoff[ga][:, r * 64:(r + 1) * 64], in_=pO[ga][:, r * 64:(r + 1) * 64], func=AF.Copy,
                                    scale=sm[:, 4, h:h + 1]), reads=[pO_b[ga], sm_b], writes=[yoff_b[ga]])
                            sc.op("dve", lambda e, ga=ga, gs=gs, sm=sm, xd=xd, xdw=xdw: e.tensor_tensor(out=ysb[:, gs], in0=pY[ga][:, 0:384], in1=yoff[ga][:], op=ALU.add),
                                  reads=[pY_b[ga], yoff_b[ga]], writes=[ysb_g[g]])
                    for g in range(8):
                        gs = slice(g * 384, (g + 1) * 384)
                        ga = g % 2
                        sc.op("pe", lambda e, g=g, ga=ga, gs=gs, sm=sm, xd=xd, xdw=xdw: e.matmul(pO[ga][:, 0:384],
                                                                          lhsT=Btok[:, g * 128:(g + 1) * 128], rhs=xdw[:, gs],
                                                                          start=True, stop=True),
                              reads=[Btok_b, xdw_b], writes=[pO_b[ga]])
                        st3 = stateT[:, gs].rearrange("p (h q) -> p h q", q=64)
                        sc.op("pool", lambda e, g=g, st3=st3, sm=sm, xd=xd, xdw=xdw: e.tensor_tensor(
                            out=st3, in0=st3, in1=sm[:, 6, 6 * g:6 * g + 6].unsqueeze(2).to_broadcast([128, 6, 64]), op=ALU.mult),
                            reads=[sm_b, state_b[g]], writes=[state_b[g]])
                        sc.op("dve", lambda e, ga=ga, gs=gs, sm=sm, xd=xd, xdw=xdw: e.tensor_tensor(out=stateT[:, gs], in0=pO[ga][:, 0:384], in1=stateT[:, gs], op=ALU.add),
                              reads=[pO_b[ga], state_b[g]], writes=[state_b[g]])
                        sc.op("act", lambda e, gs=gs, sm=sm, xd=xd, xdw=xdw: e.activation(out=prevb[:, gs], in_=stateT[:, gs], func=AF.Copy),
                              reads=[state_b[g]], writes=[prev_b[g]])
                    if local and direction == "b":
                        for g in range(8):
                            gs = slice(g * 384, (g + 1) * 384)
                            sc.dma("sp", lambda e, t0=t0, gs=gs, sm=sm, xd=xd, xdw=xdw: e.dma_start(out=yb_d[t0:t0 + 128, gs], in_=ysb[:, gs]), src=ysb_g[g])
                    if direction == "f":
                        sc.dma("sp", lambda e, t0=t0, sm=sm, xd=xd, xdw=xdw: e.dma_start(out=ybc[:], in_=yb_d[t0:t0 + 128, :]), dst=ybc_b)
                        sc.dma("sp", lambda e, t0=t0, sm=sm, xd=xd, xdw=xdw: e.dma_start(
                            out=zsc[:], in_=zsT_d.rearrange("(j p) t -> p j t", p=128)[:, :, t0:t0 + 128]), dst=zsc_b)
                        sc.op("pool", lambda e, sm=sm, xd=xd, xdw=xdw: e.tensor_tensor(out=ybc[:], in0=ybc[:], in1=ysb[:], op=ALU.add),
                              reads=[ybc_b] + ysb_g, writes=[ybc_b])
                        sc.op("dve", lambda e, x3=x3, sm=sm, xd=xd, xdw=xdw: e.tensor_tensor(
                            out=xd[:].rearrange("p (h q) -> p h q", q=64), in0=x3,
                            in1=prm[:, 4, :].unsqueeze(2).to_broadcast([128, 48, 64]), op=ALU.mult),
                            reads=[xtok_b, prm_b, xd_b], writes=[xd_b])
                        sc.op("dve", lambda e, sm=sm, xd=xd, xdw=xdw: e.tensor_tensor(out=ybc[:], in0=ybc[:], in1=xd[:], op=ALU.add),
                              reads=[ybc_b, xd_b], writes=[ybc_b])
                        for q in range(6):
                            pi = q % 2
                            for r in range(4):
                                j = 4 * q + r
                                sc.op("pe", lambda e, j=j, pi=pi, r=r, sm=sm, xd=xd, xdw=xdw: e.transpose(out=pA[pi][:, r * 128:(r + 1) * 128],
                                                                                  in_=ybc[:, j * 128:(j + 1) * 128], identity=id_f),
                                      reads=[ybc_b], writes=[pA_b[pi]])
                            sc.op("dve", lambda e, q=q, pi=pi, sm=sm, xd=xd, xdw=xdw: e.tensor_tensor(
                                out=yg[:, 4 * q:4 * q + 4, :], in0=pA[pi][:].rearrange("p (a b) -> p a b", a=4),
                                in1=zsc[:, 4 * q:4 * q + 4, :], op=ALU.mult), reads=[pA_b[pi], zsc_b], writes=[yg_b])
                        sc.op("act", lambda e, sm=sm, xd=xd, xdw=xdw: e.activation(out=sq[:], in_=yg[:], func=AF.Square), reads=[yg_b], writes=[sq_b])
                        for g in range(8):
                            ga = g % 2
                            for jj in range(3):
                                sc.op("pe", lambda e, g=g, jj=jj, ga=ga, sm=sm, xd=xd, xdw=xdw: e.matmul(pY[ga][:, 0:128], lhsT=ones_f, rhs=sq[:, 3 * g + jj, :],
                                                                                 start=(jj == 0), stop=(jj == 2)),
                                      reads=[sq_b], writes=[pY_b[ga]])
                            sc.op("act", lambda e, ga=ga: e.activation(out=rsg[ga][:], in_=pY[ga][:, 0:128], func=AF.Sqrt, bias=epsc[:],
                                                                      scale=1.0 / 384.0), reads=[pY_b[ga]], writes=[rsg_b[ga]])
                            sc.op("dve", lambda e, ga=ga: e.reciprocal(out=rsg[ga][:], in_=rsg[ga][:]), reads=[rsg_b[ga]], writes=[rsg_b[ga]])
                            for jj in range(3):
                                j = 3 * g + jj
                                sc.op("dve", lambda e, j=j, ga=ga, sm=sm, xd=xd, xdw=xdw: e.scalar_tensor_tensor(
                                    out=osb4[:, j, :], in0=yg[:, j, :], scalar=nw[:, j:j + 1], in1=rsg[ga][:],
                                    op0=ALU.mult, op1=ALU.mult), reads=[yg_b, rsg_b[ga], nw_b], writes=[osb4_b])
                        sc.dma("sp", lambda e, t0=t0, sm=sm, xd=xd, xdw=xdw: e.dma_start(
                            out=mixT_d[1024:4096, :].rearrange("(j p) t -> p j t", p=128)[:, :, t0:t0 + 128], in_=osb4[:]), src=osb4_b)
                sc.emit()
                if direction == "b" and stop_after <= 5:
                    return nc
        if stop_after <= 6:
            return nc

        with ExitStack() as st:
            sc = Sched(nc, pool)
            wout = sb(st, "wout", [128, 32, D], BF16)
            mt = [sb(st, f"mt{i}", [128, 32, 128], BF16) for i in range(2)]
            xt = [sb(st, f"xt5{i}", [128, D], F32) for i in range(1)]
            wst = [sb(st, f"wst5{i}", [128, D], F32) for i in range(2)]
            x1 = sb(st, "x1s", [128, D], F32)
            hb = sb(st, "hb5", [128, D], BF16)
            junk = sb(st, "junk5", [128, D], BF16)
            h2s = [sb(st, f"h2s{i}", [128, 16, 128], BF16) for i in range(1)] * 2
            gain = sb(st, "gain2", [128, D], F32)
            ss = sb(st, "ss5", [128, 64], F32)
            pb = [ps(st, f"pb5{i}", [128, 512], F32) for i in range(4)]
            pt = [ps(st, f"pt5{i}", [128, 1024], BF16)[:, 0:512] for i in range(4)]
            wout_b = sc.bufs_n("wout", 32)
            wst_b = sc.bufs_n("wst", 2)
            mt_b, xt_b, h2s_b, pb_b, pt_b = sc.bufs_n("mt", 2), sc.bufs_n("xt", 1), sc.bufs_n("h2s", 1) * 2, sc.bufs_n("pb", 4), sc.bufs_n("pt", 4)
            x1_b, hb_b, junk_b, gain_b = sc.buf("x1"), sc.buf("hb"), sc.buf("junk"), sc.buf("gain")
            sc.dma("sp", lambda e: e.dma_start(out=gain[:], in_=g2_d.partition_broadcast(128)), dst=gain_b)
            for j in range(32):
                wa = j % 2
                sc.dma("sp", lambda e, j=j, wa=wa: e.dma_start(out=wst[wa][:], in_=w_out_d[j * 128:(j + 1) * 128, :]), dst=wst_b[wa])
                if j % 2 == 0:
                    sc.op("act", lambda e, j=j, wa=wa: e.activation(out=wout[:, j, :], in_=wst[wa][:], func=AF.Copy),
                          reads=[wst_b[wa]], writes=[wout_b[j]])
                else:
                    sc.op("dve", lambda e, j=j, wa=wa: e.tensor_copy(out=wout[:, j, :], in_=wst[wa][:]),
                          reads=[wst_b[wa]], writes=[wout_b[j]])
            for i in range(NCH_LOC):
                a = i % 2
                t0 = i * 128
                ssb = sc.buf(f"ss{i}")
                sc.dma("sp", lambda e, a=a, t0=t0: e.dma_start(
                    out=mt[a][:], in_=mixT_d.rearrange("(j p) t -> p j t", p=128)[:, :, t0:t0 + 128]), dst=mt_b[a])
                sc.dma("sp", lambda e, t0=t0: e.dma_start(out=xt[0][:], in_=x_d[t0:t0 + 128, :]), dst=xt_b[0])
                for j in range(32):
                    for db in range(4):
                        sc.op("pe", lambda e, a=a, db=db, j=j: e.matmul(pb[db][:], lhsT=mt[a][:, j, :], rhs=wout[:, j, db * 512:(db + 1) * 512],
                                                                       start=(j == 0), stop=(j == 31)),
                              reads=[mt_b[a], wout_b[j]], writes=[pb_b[db]])
                for db in range(4):
                    sc.op("dve", lambda e, db=db: e.tensor_tensor(out=x1[:, db * 512:(db + 1) * 512], in0=pb[db][:],
                                                                 in1=xt[0][:, db * 512:(db + 1) * 512], op=ALU.add),
                          reads=[pb_b[db], xt_b[0]], writes=[x1_b])
                rms_rows(sc, x1[:], x1_b, ssb, ss[:, 2 * i:2 * i + 1], ss[:, 2 * i + 1:2 * i + 2], gain, gain_b, hb, hb_b, junk, junk_b)
                sc.dma("pool", lambda e, t0=t0: e.dma_start(out=x1_d[t0:t0 + 128, :], in_=x1[:]), src=x1_b)
                for q in range(4):
                    pi = q
                    for r in range(4):
                        k = 4 * q + r
                        sc.op("pe", lambda e, k=k, pi=pi, r=r: e.transpose(out=pt[pi][:, r * 128:(r + 1) * 128],
                                                                          in_=hb[:, k * 128:(k + 1) * 128], identity=idb[:]),
                              reads=[hb_b], writes=[pt_b[pi]])
                    sc.op("act", lambda e, a=a, q=q, pi=pi: e.activation(out=h2s[a][:, 4 * q:4 * q + 4, :],
                                                                        in_=pt[pi].rearrange("p (a b) -> p a b", a=4), func=AF.Copy),
                          reads=[pt_b[pi]], writes=[h2s_b[a]])
                sc.dma("act", lambda e, a=a, t0=t0: e.dma_start(
                    out=h2T_d.rearrange("(k p) t -> p k t", p=128)[:, :, t0:t0 + 128], in_=h2s[a][:]), src=h2s_b[a])
            sc.emit()
        if stop_after <= 7:
            return nc

        with ExitStack() as st:
            sc = Sched(nc, pool)
            h2r = sb(st, "h2r", [128, 16, NLOC], BF16)
            wup = [sb(st, f"wup{i}", [128, 16, 256], BF16) for i in range(2)]
            wst6 = [sb(st, f"wst6{i}", [128, 16, 128], F32) for i in range(4)]
            stg = [sb(st, f"stg{i}", [128, 2052], F32) for i in range(2)]
            accg = sb(st, "accg", [128, NOWN], F32)
            accv = sb(st, "accv", [128, NOWN], F32)
            sg = sb(st, "sg", [128, NOWN], F32)
            ao = [sb(st, f"ao{i}", [128, NOWN], BF16) for i in range(2)]
            fcw = sb(st, "fcw", [128, 88, 3], F32)
            fcb = sb(st, "fcb", [128, 88], F32)
            pb = [ps(st, f"pb6{i}", [128, 512], F32) for i in range(8)]
            h2r_b, fc_b = sc.buf("h2r"), sc.buf("fc")
            wup_b, stg_b, ao_b, pb_b = sc.bufs_n("wup", 4), sc.bufs_n("stg", 2), sc.bufs_n("ao", 2), sc.bufs_n("pb", 8)
            wst6_b = sc.bufs_n("wst6", 4)
            accg_b, accv_b, sg_b = sc.buf("accg"), sc.buf("accv"), sc.buf("sg")
            sc.dma("sp", lambda e: e.dma_start(out=h2r[:], in_=h2T_d.rearrange("(k p) t -> p k t", p=128)), dst=h2r_b)
            sc.dma("sp", lambda e: e.dma_start(out=fcw[:], in_=fcw_d), dst=fc_b)
            sc.dma("sp", lambda e: e.dma_start(out=fcb[:], in_=fcb_d), dst=fc_b)
            for i in range(2):
                sc.op("pool", lambda e, i=i: e.memset(stg[i][:], 0.0), writes=[stg_b[i]])
            wuv = w_up_d.rearrange("(k p) c -> p k c", p=128)
            bank = 0
            blocks = [(b * 512, 512) for b in range(4)] + [(2048, 1)]
            for f in range(44):
                a = f % 2
                for hv in range(2):
                    c0 = hv * FFN + f * 128
                    wa = 2 * a + hv
                    sc.dma("sp", lambda e, wa=wa, c0=c0: e.dma_start(out=wst6[wa][:], in_=wuv[:, :, c0:c0 + 128]), dst=wst6_b[wa])
                    if hv == 0:
                        sc.op("act", lambda e, a=a, hv=hv, wa=wa: e.activation(out=wup[a][:, :, hv * 128:(hv + 1) * 128], in_=wst6[wa][:],
                                                                              func=AF.Copy), reads=[wst6_b[wa]], writes=[wup_b[wa]])
                    else:
                        sc.op("act", lambda e, a=a, hv=hv, wa=wa: e.activation(out=wup[a][:, :, hv * 128:(hv + 1) * 128], in_=wst6[wa][:],
                                                                              func=AF.Copy), reads=[wst6_b[wa]], writes=[wup_b[wa]])
                for hv in range(2):
                    for (t0, n) in blocks:
                        b_ = bank % 8
                        bank += 1
                        for k in range(16):
                            sc.op("pe", lambda e, a=a, hv=hv, k=k, t0=t0, n=n, b_=b_: e.matmul(
                                pb[b_][:, :n], lhsT=wup[a][:, k, hv * 128:(hv + 1) * 128], rhs=h2r[:, k, t0:t0 + n],
                                start=(k == 0), stop=(k == 15)), reads=[wup_b[2 * a + hv], h2r_b], writes=[pb_b[b_]])
                        sc.op("act", lambda e, hv=hv, t0=t0, n=n, b_=b_: e.activation(out=stg[hv][:, 1 + t0:1 + t0 + n], in_=pb[b_][:, :n],
                                                                                     func=AF.Copy),
                              reads=[pb_b[b_]], writes=[stg_b[hv]])
                    jj = hv * 44 + f
                    eng = "dve"
                    acc_t, acc_tb = (accg, accg_b) if hv == 0 else (accv, accv_b)
                    sc.op(eng, lambda e, hv=hv, jj=jj, acc_t=acc_t: e.tensor_scalar(
                        out=acc_t[:], in0=stg[hv][:, 0:NOWN], scalar1=fcw[:, jj, 0:1], scalar2=fcb[:, jj:jj + 1],
                        op0=ALU.mult, op1=ALU.add), reads=[stg_b[hv], fc_b], writes=[acc_tb])
                    for tap in (1, 2):
                        sc.op("dve", lambda e, hv=hv, jj=jj, acc_t=acc_t, tap=tap: e.scalar_tensor_tensor(
                            out=acc_t[:], in0=stg[hv][:, tap:tap + NOWN], scalar=fcw[:, jj, tap:tap + 1], in1=acc_t[:],
                            op0=ALU.mult, op1=ALU.add), reads=[stg_b[hv], fc_b, acc_tb], writes=[acc_tb])
                sc.op("act", lambda e: e.activation(out=sg[:], in_=accg[:], func=AF.Silu), reads=[accg_b], writes=[sg_b])
                sc.op("dve", lambda e, a=a: e.tensor_tensor(out=ao[a][:], in0=sg[:], in1=accv[:], op=ALU.mult),
                      reads=[sg_b, accv_b], writes=[ao_b[a]])
                sc.dma("pool", lambda e, a=a, f=f: e.dma_start(out=actT_d[f * 128:(f + 1) * 128, :], in_=ao[a][:]), src=ao_b[a])
            sc.emit()
        if stop_after <= 8:
            return nc

        with ExitStack() as st:
            sc = Sched(nc, pool)
            wdn = [sb(st, f"wdn{i}", [128, 44, 512], BF16) for i in range(2)]
            wst7 = [sb(st, f"wst7{i}", [128, 4, 512], F32) for i in range(2)]
            at = [sb(st, f"at{i}", [128, 44, 128], BF16) for i in range(2)]
            x1t = [sb(st, f"x1t{i}", [128, 512], F32) for i in range(2)]
            x2s = [sb(st, f"x2s{i}", [128, 512], F32) for i in range(2)]
            junk = sb(st, "junk7", [128, 512], BF16)
            ssp = sb(st, "ssp", [128, 16, 4], F32)
            ss = sb(st, "ss7", [128, 32], F32)
            gain = sb(st, "gain3", [128, D], F32)
            xf = [sb(st, f"xf{i}", [128, D], F32) for i in range(2)]
            of = [sb(st, f"of{i}", [128, D], F32) for i in range(2)]
            pb = [ps(st, f"pb7{i}", [128, 512], F32) for i in range(4)]
            wdn_b, at_b, x1t_b, x2s_b, pb_b = sc.bufs_n("wdn", 22), sc.bufs_n("at", 2), sc.bufs_n("x1t", 2), sc.bufs_n("x2s", 2), sc.bufs_n("pb", 4)
            wst7_b = sc.bufs_n("wst7", 2)
            wcnt = 0
            junk_b, ssp_b, gain_b = sc.buf("junk"), sc.buf("ssp"), sc.buf("gain")
            xf_b, of_b = sc.bufs_n("xf", 2), sc.bufs_n("of", 2)
            sc.dma("sp", lambda e: e.dma_start(out=gain[:], in_=g3_d.partition_broadcast(128)), dst=gain_b)
            wdv = w_dn_d.rearrange("(j p) d -> p j d", p=128)
            cnt = 0
            for db in range(4):
                wa = db % 2
                for jg in range(11):
                    sa_ = wcnt % 2
                    wcnt += 1
                    sc.dma("sp", lambda e, sa_=sa_, jg=jg, db=db: e.dma_start(
                        out=wst7[sa_][:], in_=wdv[:, 4 * jg:4 * jg + 4, db * 512:(db + 1) * 512]), dst=wst7_b[sa_])
                    if jg % 2 == 0:
                        sc.op("act", lambda e, sa_=sa_, jg=jg, wa=wa: e.activation(out=wdn[wa][:, 4 * jg:4 * jg + 4, :], in_=wst7[sa_][:],
                                                                                 func=AF.Copy), reads=[wst7_b[sa_]], writes=[wdn_b[wa * 11 + jg]])
                    else:
                        sc.op("act", lambda e, sa_=sa_, jg=jg, wa=wa: e.activation(out=wdn[wa][:, 4 * jg:4 * jg + 4, :], in_=wst7[sa_][:],
                                                                                 func=AF.Copy), reads=[wst7_b[sa_]], writes=[wdn_b[wa * 11 + jg]])
                for i in range(16):
                    a = cnt % 2
                    b_ = cnt % 4
                    cnt += 1
                    t0 = i * 128
                    sc.dma("sp", lambda e, a=a, t0=t0: e.dma_start(
                        out=at[a][:], in_=actT_d.rearrange("(j p) t -> p j t", p=128)[:, :, t0:t0 + 128]), dst=at_b[a])
                    sc.dma("sp", lambda e, a=a, t0=t0, db=db: e.dma_start(out=x1t[a][:], in_=x1_d[t0:t0 + 128, db * 512:(db + 1) * 512]),
                           dst=x1t_b[a])
                    for j in range(44):
                        sc.op("pe", lambda e, a=a, wa=wa, j=j, b_=b_: e.matmul(pb[b_][:], lhsT=at[a][:, j, :], rhs=wdn[wa][:, j, :],
                                                                              start=(j == 0), stop=(j == 43)),
                              reads=[at_b[a], wdn_b[wa * 11 + j // 4]], writes=[pb_b[b_]])
                    sc.op("dve", lambda e, a=a, b_=b_: e.tensor_tensor(out=x2s[a][:], in0=pb[b_][:], in1=x1t[a][:], op=ALU.add),
                          reads=[pb_b[b_], x1t_b[a]], writes=[x2s_b[a]])
                    sc.op("act", lambda e, a=a, i=i, db=db: e.activation(out=junk[:], in_=x2s[a][:], func=AF.Square,
                                                                        accum_out=ssp[:, i, db:db + 1]),
                          reads=[x2s_b[a]], writes=[junk_b, ssp_b])
                    sc.dma("pool", lambda e, a=a, t0=t0, db=db: e.dma_start(out=x2_d[t0:t0 + 128, db * 512:(db + 1) * 512], in_=x2s[a][:]),
                           src=x2s_b[a])
            sc.emit()
            sc = Sched(nc, pool)
            xf_b, of_b, ssp_b, gain_b = sc.bufs_n("xf", 2), sc.bufs_n("of", 2), sc.buf("ssp"), sc.buf("gain")
            for i in range(16):
                a = i % 2
                t0 = i * 128
                sc.dma("sp", lambda e, a=a, t0=t0: e.dma_start(out=xf[a][:], in_=x2_d[t0:t0 + 128, :]), dst=xf_b[a])
                sc.op("dve", lambda e, i=i: e.tensor_reduce(out=ss[:, 2 * i:2 * i + 1], in_=ssp[:, i, :], axis=AX.X, op=ALU.add),
                      reads=[ssp_b], writes=[ssp_b])
                sc.op("act", lambda e, i=i: e.activation(out=ss[:, 2 * i + 1:2 * i + 2], in_=ss[:, 2 * i:2 * i + 1], func=AF.Sqrt,
                                                        bias=epsc[:], scale=1.0 / D), reads=[ssp_b], writes=[ssp_b])
                sc.op("dve", lambda e, i=i: e.reciprocal(out=ss[:, 2 * i + 1:2 * i + 2], in_=ss[:, 2 * i + 1:2 * i + 2]),
                      reads=[ssp_b], writes=[ssp_b])
                sc.op("dve", lambda e, a=a, i=i: e.scalar_tensor_tensor(out=of[a][:], in0=xf[a][:], scalar=ss[:, 2 * i + 1:2 * i + 2],
                                                                    in1=gain[:], op0=ALU.mult, op1=ALU.mult),
                      reads=[xf_b[a], ssp_b, gain_b], writes=[of_b[a]])
                sc.dma("pool", lambda e, a=a, t0=t0: e.dma_start(out=out_d[t0:t0 + 128, :], in_=of[a][:]), src=of_b[a])
            sc.emit()
    return nc


_CONST = {}


def _consts():
    if _CONST:
        return _CONST
    bf = ml_dtypes.bfloat16
    c = np.arange(128, dtype=np.float64)
    ang = 2.0 * np.pi * np.outer(c, c) / 128.0
    sc = 1.0 / np.sqrt(float(S) * 128.0)
    _CONST["ccs"] = np.stack([np.cos(ang) * sc, np.sin(ang) * sc]).astype(np.float32).astype(bf)
    s = np.arange(S, dtype=np.int64)[:, None]
    k = np.arange(NLOC, dtype=np.int64)[None, :]
    tabs = []
    for flip in (0, 1):
        prod = ((s + flip) * (k + flip)) % S
        a = 2.0 * np.pi * prod.astype(np.float64) / S
        t2 = np.stack([np.cos(a), -np.sin(a)]).astype(np.float32).astype(bf)
        tabs.append(np.ascontiguousarray(t2.reshape(2, 32, 128, NLOC).transpose(0, 2, 1, 3)))
    _CONST["tab"] = tabs
    i = np.arange(128)
    U = (i[:, None] <= i[None, :]).astype(np.float32)
    L = (i[:, None] >= i[None, :]).astype(np.float32)
    _CONST["cst"] = np.stack([U, L, np.ones((128, 128), np.float32), np.eye(128, dtype=np.float32)])
    _CONST["idb"] = np.eye(128, dtype=np.float32).astype(bf)
    NEG = np.float32(-1.0e5)
    mf = np.where(i[None, :] < i[:, None], NEG, np.float32(0))
    mb = np.where(i[None, :] > i[:, None], NEG, np.float32(0))
    _CONST["nmk"] = np.stack([mf, mb]).astype(np.float32).astype(bf)
    return _CONST


def _in_maps(inp):
    cs = _consts()
    f = lambda a: np.ascontiguousarray(np.asarray(a, dtype=np.float32))
    x = f(inp["x"])
    w_in, w_out, w_up, w_dn = f(inp["w_in"][0]), f(inp["w_out"][0]), f(inp["w_up"][0]), f(inp["w_down"][0])
    fw = f(inp["fourier_w"][0])
    g1, g2, g3 = f(inp["norm_mix_w"][0]), f(inp["norm_ffn_w"][0]), f(inp["norm_final_w"])
    cw, cb = f(inp["ssm_conv_w"][0]), f(inp["ssm_conv_b"][0])
    fcw, fcb = f(inp["ffn_conv_w"][0]), f(inp["ffn_conv_b"][0])
    nw = f(inp["ssm_norm_w"][0]).reshape(24, 128).T.copy()
    cb_l = cb.reshape(40, 128).T.copy()
    fcb_l = fcb.reshape(88, 128).T.copy()
    prm = [f(inp[k][0]) for k in ("dt_bias_fwd", "a_log_fwd", "dt_bias_bwd", "a_log_bwd", "ssm_d")]
    maps = []
    for c in range(8):
        b, hf = c // 2, c % 2
        if hf == 0:
            xc, cwc, fcwc = x[b], cw, fcw
            ssp = np.stack([prm[0], prm[1], prm[2], prm[3], prm[4]])
        else:
            xc, cwc, fcwc = np.ascontiguousarray(x[b, ::-1]), cw[::-1], fcw[::-1]
            ssp = np.stack([prm[2], prm[3], prm[0], prm[1], prm[4]])
        cw_l = np.ascontiguousarray(cwc.reshape(5, 40, 128).transpose(2, 1, 0))
        fcw_l = np.ascontiguousarray(fcwc.reshape(3, 88, 128).transpose(2, 1, 0))
        maps.append({"x": xc, "g1": g1, "w_in": w_in, "fw": fw, "ccs": cs["ccs"], "tab": cs["tab"][hf],
                     "cw": cw_l, "cb": cb_l, "ssp": np.ascontiguousarray(ssp), "nw": nw, "cst": cs["cst"],
                     "idb": cs["idb"], "nmk": cs["nmk"], "w_out": w_out, "g2": g2, "w_up": w_up, "fcw": fcw_l, "fcb": fcb_l,
                     "w_dn": w_dn, "g3": g3})
    return maps


_NC = {}


def kernel(**inputs):
    if "nc" not in _NC:
        _NC["nc"] = build_program()
    maps = _in_maps(inputs)
    res = run_bass_kernel_spmd(_NC["nc"], maps, core_ids=list(range(8)))
    out = np.empty((4, S, D), np.float32)
    for c in range(8):
        b, hf = c // 2, c % 2
        o = np.asarray(res.results[c]["out"], dtype=np.float32)
        if hf == 0:
            out[b, :NOWN] = o
        else:
            out[b, NOWN:] = o[::-1]
    return out
```

```python
import os
from contextlib import ExitStack
import numpy as np
import ml_dtypes
import concourse.bass as bass
import concourse.mybir as mybir
from concourse.bass_utils import run_bass_kernel_spmd

F32 = mybir.dt.float32
BF16 = mybir.dt.bfloat16
ALU = mybir.AluOpType
AF = mybir.ActivationFunctionType
AX = mybir.AxisListType

D = 2048
S = 4096
NLOC = 2176
NOWN = 2048
NCH_LOC = 17
FW = 1024
SW = 3072
XBC = 5120
INW = 9264
FFN = 5632
EPS = 1e-5
SAME_ENGINE_SYNC = True
ENGS = ("pe", "act", "dve", "pool", "sp")


class Buf:
    __slots__ = ("name", "lw", "rd", "dsem", "dcnt", "dbase")

    def __init__(self, name):
        self.name = name
        self.lw = None
        self.rd = []
        self.dsem = None
        self.dcnt = 0
        self.dbase = 0


class Op:
    __slots__ = ("fn", "waits", "signal", "dma_buf")

    def __init__(self, fn):
        self.fn = fn
        self.waits = []
        self.signal = False
        self.dma_buf = None


class SemPool:
    def __init__(self, nc, es, n_dma, n_eng):
        self.dma = [[es.enter_context(nc.semaphore(f"dq{i}")), 0] for i in range(n_dma)]
        self.eng = [es.enter_context(nc.semaphore(f"eq{i}")) for i in range(n_eng)]
        self.eng_next = 0
        self.free = list(range(n_dma))

    def take_eng(self):
        s = self.eng[self.eng_next]
        self.eng_next += 1
        return s


class Sched:
    def __init__(self, nc, pool):
        self.nc = nc
        self.pool = pool
        self.q = {e: [] for e in ENGS}
        self.seen_eng = {e: {} for e in ENGS}
        self.seen_dma = {e: {} for e in ENGS}
        self.bufs = []
        self.dma_bufs = []

    def buf(self, name):
        b = Buf(name)
        self.bufs.append(b)
        return b

    def bufs_n(self, name, n):
        return [self.buf(f"{name}{i}") for i in range(n)]

    def _add_dep(self, eng, op, dep):
        if dep[0] == "eng":
            _, e2, idx = dep
            if e2 == eng and (eng == "pe" or not SAME_ENGINE_SYNC):
                return
            if self.seen_eng[eng].get(e2, -1) >= idx:
                return
            self.seen_eng[eng][e2] = idx
            self.q[e2][idx].signal = True
            op.waits.append(dep)
        else:
            _, b, cnt = dep
            if self.seen_dma[eng].get(id(b), -1) >= cnt:
                return
            self.seen_dma[eng][id(b)] = cnt
            op.waits.append(dep)

    def op(self, eng, fn, reads=(), writes=()):
        o = Op(fn)
        idx = len(self.q[eng])
        for b in reads:
            if b.lw is not None:
                self._add_dep(eng, o, b.lw)
        for b in writes:
            if b.lw is not None:
                self._add_dep(eng, o, b.lw)
            for r in b.rd:
                self._add_dep(eng, o, r)
        me = ("eng", eng, idx)
        for b in reads:
            b.rd.append(me)
        for b in writes:
            b.lw = me
            b.rd = []
        self.q[eng].append(o)
        return o

    def _dsem(self, b):
        if b.dsem is None:
            i = self.pool.free.pop()
            b.dsem = i
            b.dbase = self.pool.dma[i][1]
            self.dma_bufs.append(b)
        return b

    def dma(self, eng, fn, src=None, dst=None):
        o = Op(fn)
        if src is not None and src.lw is not None:
            self._add_dep(eng, o, src.lw)
        if dst is not None:
            if dst.lw is not None:
                self._add_dep(eng, o, dst.lw)
            for r in dst.rd:
                self._add_dep(eng, o, r)
        b = dst if dst is not None else src
        self._dsem(b)
        b.dcnt += 1
        me = ("dma", b, b.dcnt)
        if dst is not None:
            dst.lw = me
            dst.rd = []
        else:
            src.rd.append(me)
        o.dma_buf = b
        self.q[eng].append(o)
        return o

    def emit(self):
        nc = self.nc
        pool = self.pool
        Sched.n_emit = getattr(Sched, "n_emit", -1) + 1
        if str(Sched.n_emit) in os.environ.get("KDBG_SKIP_EMITS", "").split(","):
            for b in self.dma_bufs:
                pool.free.append(b.dsem)
                b.dsem = None
            return
        last = {}
        for e in ENGS:
            if e == "sp":
                continue
            idxs = [i for i, o in enumerate(self.q[e]) if o.dma_buf is None]
            if idxs:
                last[e] = idxs[-1]
        for e2, idx in last.items():
            self.q[e2][idx].signal = True
        esem = {e: pool.take_eng() for e in ENGS if e != "sp"}
        val = {}
        for e in esem:
            c = 0
            for i, o in enumerate(self.q[e]):
                if o.signal:
                    c += 1
                val[(e, i)] = c
        dma_final = [(pool.dma[b.dsem][0], 16 * (b.dbase + b.dcnt)) for b in self.dma_bufs]

        def resolve(w):
            if w[0] == "eng":
                return esem[w[1]], val[(w[1], w[2])]
            return pool.dma[w[1].dsem][0], 16 * (w[1].dbase + w[2])

        with nc.Block() as block:
            decos = {"pe": block.tensor, "act": block.scalar, "dve": block.vector,
                     "pool": block.gpsimd, "sp": block.sync}
            for eng in ENGS:
                ops = self.q[eng]

                def body(e, ops=ops, eng=eng):
                    for o in ops:
                        for w in o.waits:
                            s, v = resolve(w)
                            e.wait_ge(s, v)
                        inst = o.fn(e)
                        if o.dma_buf is not None:
                            inst.then_inc(pool.dma[o.dma_buf.dsem][0], 16)
                        elif o.signal:
                            inst.then_inc(esem[eng], 1)
                    for e2, idx in last.items():
                        e.wait_ge(esem[e2], val[(e2, idx)])
                    for s, v in dma_final:
                        e.wait_ge(s, v)

                decos[eng](body)
        for b in self.dma_bufs:
            pool.dma[b.dsem][1] = b.dbase + b.dcnt
            pool.free.append(b.dsem)
            b.dsem = None


def build_program(stop_after=99, dump=None):
    Sched.n_emit = -1
    nc = bass.Bass("TRN2", target_bir_lowering=False)

    def din(name, shape, dt=F32):
        return nc.dram_tensor(name, list(shape), dt, kind="ExternalInput").ap()

    def dscr(name, shape, dt):
        kind = "ExternalOutput" if dump == name else "Internal"
        return nc.dram_tensor(name, list(shape), dt, kind=kind).ap()

    x_d = din("x", [S, D])
    g1_d = din("g1", [D])
    w_in_d = din("w_in", [D, INW])
    fw_d = din("fw", [8, 128, 128])
    ccs_d = din("ccs", [2, 128, 128], BF16)
    tab_d = din("tab", [2, 128, 32, NLOC], BF16)
    cw_d = din("cw", [128, 40, 5])
    cb_d = din("cb", [128, 40])
    ssp_d = din("ssp", [5, 48])
    nw_d = din("nw", [128, 24])
    cst_d = din("cst", [4, 128, 128])
    idb_d = din("idb", [128, 128], BF16)
    nmk_d = din("nmk", [2, 128, 128], BF16)
    w_out_d = din("w_out", [4096, D])
    g2_d = din("g2", [D])
    w_up_d = din("w_up", [D, 2 * FFN])
    fcw_d = din("fcw", [128, 88, 3])
    fcb_d = din("fcb", [128, 88])
    w_dn_d = din("w_dn", [FFN, D])
    g3_d = din("g3", [D])
    out_d = nc.dram_tensor("out", [NOWN, D], F32, kind="ExternalOutput").ap()

    uT_d = dscr("uT", [FW, S], BF16)
    zsT_d = dscr("zsT", [SW, NLOC], BF16)
    xT_d = dscr("xT", [SW, S], BF16)
    BT_d = dscr("BT", [1024, S], BF16)
    CT_d = dscr("CT", [1024, NLOC], BF16)
    V_d = dscr("V", [8, 128, 32, 256], BF16)
    mixT_d = dscr("mixT", [4096, NLOC], BF16)
    yb_d = dscr("yb", [NLOC, SW], F32)
    x1_d = dscr("x1", [NLOC, D], F32)
    h2T_d = dscr("h2T", [D, NLOC], BF16)
    actT_d = dscr("actT", [FFN, NOWN], BF16)
    x2_d = dscr("x2", [NOWN, D], F32)

    es = ExitStack()
    with es:
        es.enter_context(nc.allow_low_precision("bf16 matmul operands, fp32 accumulate"))
        es.enter_context(nc.allow_non_contiguous_dma("tiled layouts"))
        pool = SemPool(nc, es, 56, 40)

        uid = [0]

        def sb(st, name, shape, dt):
            uid[0] += 1
            return st.enter_context(nc.sbuf_tensor(f"s{uid[0]}_{name}", list(shape), dt))

        def ps(st, name, shape, dt=F32):
            uid[0] += 1
            return st.enter_context(nc.psum_tensor(f"p{uid[0]}_{name}", list(shape), dt))

        dtraw = sb(es, "dtraw", [128, 32, 48], F32)
        cst = sb(es, "cst", [128, 4, 128], F32)
        idb = sb(es, "idb", [128, 128], BF16)
        epsc = sb(es, "epsc", [128, 1], F32)
        Umat, Lmat, ones_f, id_f = (cst[:, i, :] for i in range(4))

        def rms_rows(sc, src, src_b, ssb, ss_ap, rstd_ap, gain, gain_b, hb, hb_b, junk, junk_b, eps_b=None):
            sc.op("act", lambda e: e.activation(out=junk[:], in_=src, func=AF.Square, accum_out=ss_ap),
                  reads=[src_b], writes=[junk_b, ssb])
            sc.op("act", lambda e: e.activation(out=rstd_ap, in_=ss_ap, func=AF.Sqrt, bias=epsc[:], scale=1.0 / D),
                  reads=[ssb] + ([eps_b] if eps_b is not None else []), writes=[ssb])
            sc.op("dve", lambda e: e.reciprocal(out=rstd_ap, in_=rstd_ap), reads=[ssb], writes=[ssb])
            sc.op("dve", lambda e: e.scalar_tensor_tensor(out=hb[:], in0=src, scalar=rstd_ap, in1=gain[:],
                                                          op0=ALU.mult, op1=ALU.mult),
                  reads=[src_b, ssb, gain_b], writes=[hb_b])

        if os.environ.get("KDBG_INIT"):
            with ExitStack() as st:
                sc = Sched(nc, pool)
                zt = sb(st, "zt", [128, S], BF16)
                zt_b = sc.buf("zt")
                sc.op("pool", lambda e: e.memset(zt[:], 0.5), writes=[zt_b])
                for t_, nm in ((dtraw, "a"), (epsc, "d")):
                    sc.op("pool", lambda e, t_=t_: e.memset(t_[:], 0.25), writes=[sc.buf(nm)])
                sc.dma("sp", lambda e: e.dma_start(out=cst[:], in_=cst_d.rearrange("c p f -> p c f")), dst=sc.buf("b"))
                sc.dma("sp", lambda e: e.dma_start(out=idb[:], in_=idb_d), dst=sc.buf("c"))
                for dten, rows, cols in ((xT_d, SW, S), (BT_d, 1024, S), (CT_d, 1024, NLOC), (zsT_d, SW, NLOC), (uT_d, FW, S)):
                    for r0 in range(0, rows, 128):
                        sc.dma("sp", lambda e, dten=dten, r0=r0, cols=cols: e.dma_start(out=dten[r0:r0 + 128, :], in_=zt[:, :cols]), src=zt_b)
                sc.emit()
        with ExitStack() as s12:
            hT = sb(s12, "hT", [128, 16, S], BF16)
            with ExitStack() as st:
                sc = Sched(nc, pool)
                xt = [sb(st, f"xt{i}", [128, D], F32) for i in range(2)]
                hb = [sb(st, f"hb{i}", [128, D], BF16) for i in range(2)]
                junk = sb(st, "junk", [128, D], BF16)
                gain = sb(st, "gain1", [128, D], F32)
                ss = sb(st, "ss", [128, 64], F32)
                pt = [ps(st, f"pt{i}", [128, 1024], BF16)[:, 0:512] for i in range(4)]
                xt_b, hb_b, pt_b = sc.bufs_n("xt", 2), sc.bufs_n("hb", 2), sc.bufs_n("pt", 4)
                junk_b, gain_b, cst_b, idb_b = sc.buf("junk"), sc.buf("gain"), sc.buf("cst"), sc.buf("idb")
                hT_b = sc.buf("hT")
                epsc_b = sc.buf("epsc")
                sc.op("pool", lambda e: e.memset(epsc[:], EPS), writes=[epsc_b])
                sc.dma("sp", lambda e: e.dma_start(out=gain[:], in_=g1_d.partition_broadcast(128)), dst=gain_b)
                sc.dma("sp", lambda e: e.dma_start(out=cst[:], in_=cst_d.rearrange("c p f -> p c f")), dst=cst_b)
                sc.dma("sp", lambda e: e.dma_start(out=idb[:], in_=idb_d), dst=idb_b)
                for i in range(32):
                    a = i % 2
                    ssb = sc.buf(f"ss{i}")
                    sc.dma("sp", lambda e, i=i, a=a: e.dma_start(out=xt[a][:], in_=x_d[i * 128:(i + 1) * 128, :]),
                           dst=xt_b[a])
                    rms_rows(sc, xt[a][:], xt_b[a], ssb, ss[:, 2 * i:2 * i + 1], ss[:, 2 * i + 1:2 * i + 2],
                             gain, gain_b, hb[a], hb_b[a], junk, junk_b, eps_b=epsc_b)
                    for q in range(4):
                        pi = (4 * i + q) % 4
                        for r in range(4):
                            k = 4 * q + r
                            sc.op("pe", lambda e, a=a, k=k, pi=pi, r=r: e.transpose(
                                out=pt[pi][:, r * 128:(r + 1) * 128], in_=hb[a][:, k * 128:(k + 1) * 128],
                                identity=idb[:]), reads=[hb_b[a], idb_b], writes=[pt_b[pi]])
                        sc.op("act", lambda e, i=i, q=q, pi=pi: e.activation(
                            out=hT[:, 4 * q:4 * q + 4, i * 128:(i + 1) * 128],
                            in_=pt[pi].rearrange("p (a b) -> p a b", a=4), func=AF.Copy),
                            reads=[pt_b[pi]], writes=[hT_b])
                sc.emit()
            if stop_after <= 1:
                return nc
            with ExitStack() as st:
                sc = Sched(nc, pool)
                wt = [sb(st, f"wt{i}", [128, 16, 128], BF16) for i in range(2)]
                stage = [sb(st, f"stage{i}", [128, S + 4], BF16) for i in range(2)]
                acc = sb(st, "acc", [128, S], F32)
                osb = [sb(st, f"osb{i}", [128, S], BF16) for i in range(2)]
                cw = sb(st, "cw", [128, 40, 5], F32)
                cb = sb(st, "cb", [128, 40], F32)
                pbank = [ps(st, f"pb{i}", [128, 512], F32) for i in range(8)]
                wt_b, stage_b, osb_b, pb_b = sc.bufs_n("wt", 2), sc.bufs_n("stage", 2), sc.bufs_n("osb", 2), sc.bufs_n("pb", 8)
                acc_b, cw_b, dtraw_b = sc.buf("acc"), sc.buf("cw"), sc.buf("dtraw")
                sc.dma("sp", lambda e: e.dma_start(out=cw[:], in_=cw_d), dst=cw_b)
                sc.dma("sp", lambda e: e.dma_start(out=cb[:], in_=cb_d), dst=cw_b)
                for i in range(2):
                    sc.op("pool", lambda e, i=i: e.memset(stage[i][:], 0.0), writes=[stage_b[i]])
                bank = 0
                w_in_v = w_in_d.rearrange("(k p) c -> p k c", p=128)
                NT = int(os.environ.get("KDBG_NT", "73"))
                for j in ([int(v) for v in os.environ["KDBG_TILES"].split(",")] if os.environ.get("KDBG_TILES") else (range(NT) if "KDBG_TILES" not in os.environ else [])):
                    a = j % 2
                    ncols = 128 if j < 72 else 48
                    sc.dma("pool", lambda e, j=j, a=a, ncols=ncols: e.dma_start(
                        out=wt[a][:, :, :ncols], in_=w_in_v[:, :, j * 128:j * 128 + ncols]), dst=wt_b[a])
                    if j == 72:
                        for i in range(32):
                            b_ = bank % 8
                            bank += 1
                            for k in range(16):
                                sc.op("pe", lambda e, a=a, k=k, i=i, b_=b_: e.matmul(
                                    pbank[b_][:, :48], lhsT=hT[:, k, i * 128:(i + 1) * 128], rhs=wt[a][:, k, :48],
                                    start=(k == 0), stop=(k == 15)), reads=[wt_b[a]], writes=[pb_b[b_]])
                            sc.op("act", lambda e, i=i, b_=b_: e.activation(out=dtraw[:, i, :], in_=pbank[b_][:, :48],
                                                                           func=AF.Copy),
                                  reads=[pb_b[b_]], writes=[dtraw_b])
                        continue
                    kind = "u" if j < 8 else "z" if j < 32 else "x" if j < 56 else "B" if j < 64 else "C"
                    if kind in ("u", "x", "B"):
                        blocks = [(b * 512, 512) for b in range(8)]
                    elif kind == "z":
                        blocks = [(b * 512, 512) for b in range(4)] + [(2048, 128)]
                    else:
                        blocks = [(b * 512, 512) for b in range(4)] + [(2048, 256)]
                    conv = kind in ("x", "B", "C")
                    sa = j % 2
                    oa = j % 2
                    for (t0, n) in blocks:
                        b_ = bank % 8
                        bank += 1
                        for k in range(16):
                            sc.op("pe", lambda e, a=a, k=k, t0=t0, n=n, b_=b_: e.matmul(
                                pbank[b_][:, :n], lhsT=wt[a][:, k, :], rhs=hT[:, k, t0:t0 + n],
                                start=(k == 0), stop=(k == 15)), reads=[wt_b[a]], writes=[pb_b[b_]])
                        if conv:
                            sc.op("act", lambda e, sa=sa, t0=t0, n=n, b_=b_: e.activation(
                                out=stage[sa][:, 2 + t0:2 + t0 + n], in_=pbank[b_][:, :n], func=AF.Copy),
                                reads=[pb_b[b_]], writes=[stage_b[sa]])
                        else:
                            fn = AF.Copy if kind == "u" else AF.Silu
                            sc.op("act", lambda e, oa=oa, t0=t0, n=n, b_=b_, fn=fn: e.activation(
                                out=osb[oa][:, t0:t0 + n], in_=pbank[b_][:, :n], func=fn),
                                reads=[pb_b[b_]], writes=[osb_b[oa]])
                    if kind == "u":
                        T, dst = S, uT_d[j * 128:(j + 1) * 128, :]
                    elif kind == "z":
                        T, dst = NLOC, zsT_d[(j - 8) * 128:(j - 7) * 128, :]
                    elif kind == "x":
                        T, dst = S, xT_d[(j - 32) * 128:(j - 31) * 128, :]
                    elif kind == "B":
                        T, dst = S, BT_d[(j - 56) * 128:(j - 55) * 128, :]
                    else:
                        T, dst = NLOC, CT_d[(j - 64) * 128:(j - 63) * 128, :]
                    if conv:
                        jj = j - 32
                        sc.op("dve", lambda e, sa=sa, jj=jj, T=T: e.tensor_scalar(
                            out=acc[:, :T], in0=stage[sa][:, 0:T], scalar1=cw[:, jj, 0:1], scalar2=cb[:, jj:jj + 1],
                            op0=ALU.mult, op1=ALU.add), reads=[stage_b[sa], cw_b], writes=[acc_b])
                        for tap in range(1, 5):
                            eng = "dve"
                            sc.op(eng, lambda e, sa=sa, jj=jj, T=T, tap=tap: e.scalar_tensor_tensor(
                                out=acc[:, :T], in0=stage[sa][:, tap:tap + T], scalar=cw[:, jj, tap:tap + 1],
                                in1=acc[:, :T], op0=ALU.mult, op1=ALU.add),
                                reads=[stage_b[sa], cw_b, acc_b], writes=[acc_b])
                        sc.op("act", lambda e, oa=oa, T=T: e.activation(out=osb[oa][:, :T], in_=acc[:, :T], func=AF.Silu),
                              reads=[acc_b], writes=[osb_b[oa]])
                    sc.dma("act", lambda e, oa=oa, T=T, dst=dst: e.dma_start(out=dst, in_=osb[oa][:, :T]), src=osb_b[oa])
                sc.emit()
        if stop_after <= 2:
            return nc

        with ExitStack() as st:
            sc = Sched(nc, pool)
            wm = sb(st, "wm", [128, 8, 128], BF16)
            ccs = sb(st, "ccs", [128, 2, 128], BF16)
            Mg = sb(st, "Mg", [128, 8, 256], BF16)
            uTg = [sb(st, f"uTg{i}", [128, S], BF16) for i in range(2)]
            Vsb = [sb(st, f"Vsb{i}", [128, 32, 256], BF16) for i in range(2)]
            pM = [ps(st, f"pM{i}", [128, 512], F32) for i in range(4)]
            wm_b, ccs_b, Mg_b = sc.buf("wm"), sc.buf("ccs"), sc.buf("Mg")
            uTg_b, Vsb_b, pM_b = sc.bufs_n("uTg", 2), sc.bufs_n("Vsb", 2), sc.bufs_n("pM", 4)
            sc.dma("pool", lambda e: e.dma_start(out=wm[:], in_=fw_d.rearrange("g c d -> c g d")), dst=wm_b)
            sc.dma("sp", lambda e: e.dma_start(out=ccs[:], in_=ccs_d.rearrange("q c d -> c q d")), dst=ccs_b)
            for g in range(8):
                b_ = g % 4
                for q in range(2):
                    sc.op("pe", lambda e, g=g, q=q, b_=b_: e.matmul(pM[b_][:, q * 128:(q + 1) * 128], lhsT=ccs[:, q, :],
                                                                  rhs=wm[:, g, :], start=True, stop=True),
                          reads=[wm_b, ccs_b], writes=[pM_b[b_]])
                sc.op("act", lambda e, g=g, b_=b_: e.activation(out=Mg[:, g, :], in_=pM[b_][:, :256], func=AF.Copy),
                      reads=[pM_b[b_]], writes=[Mg_b])
            cnt = 0
            for g in range(8):
                a = g % 2
                sc.dma("sp", lambda e, g=g, a=a: e.dma_start(out=uTg[a][:], in_=uT_d[g * 128:(g + 1) * 128, :]),
                       dst=uTg_b[a])
                for stl in range(32):
                    b_ = cnt % 4
                    cnt += 1
                    sc.op("pe", lambda e, g=g, a=a, stl=stl, b_=b_: e.matmul(
                        pM[b_][:, :256], lhsT=uTg[a][:, stl * 128:(stl + 1) * 128], rhs=Mg[:, g, :],
                        start=True, stop=True), reads=[uTg_b[a], Mg_b], writes=[pM_b[b_]])
                    eng = "act" if stl % 2 == 0 else "dve"
                    if eng == "act":
                        sc.op("act", lambda e, a=a, stl=stl, b_=b_: e.activation(out=Vsb[a][:, stl, :], in_=pM[b_][:, :256],
                                                                              func=AF.Copy),
                              reads=[pM_b[b_]], writes=[Vsb_b[a]])
                    else:
                        sc.op("dve", lambda e, a=a, stl=stl, b_=b_: e.tensor_copy(out=Vsb[a][:, stl, :], in_=pM[b_][:, :256]),
                              reads=[pM_b[b_]], writes=[Vsb_b[a]])
                sc.dma("pool", lambda e, g=g, a=a: e.dma_start(out=V_d[g], in_=Vsb[a][:]), src=Vsb_b[a])
            sc.emit()
        if stop_after <= 3:
            return nc
        with ExitStack() as st:
            sc = Sched(nc, pool)
            tabs = [sb(st, f"tab{i}", [128, 2, 32, 512], BF16) for i in range(2)]
            Vg = [sb(st, f"Vg{i}", [128, 32, 256], BF16) for i in range(2)]
            aT = [sb(st, f"aT{i}", [128, 512], BF16) for i in range(2)]
            pF = [ps(st, f"pF{i}", [128, 512], F32) for i in range(4)]
            tab_b, Vg_b, aT_b, pF_b = sc.bufs_n("tab", 2), sc.bufs_n("Vg", 2), sc.bufs_n("aT", 2), sc.bufs_n("pF", 4)
            cnt = 0
            kblocks = [(b * 512, 512) for b in range(4)] + [(2048, 128)]
            for kb, (k0, n) in enumerate(kblocks):
                ta = kb % 2
                for q in range(2):
                    sc.dma("sp", lambda e, ta=ta, q=q, k0=k0, n=n: e.dma_start(
                        out=tabs[ta][:, q, :, :n], in_=tab_d[q][:, :, k0:k0 + n]),
                        dst=tab_b[ta])
                for g in range(8):
                    a = cnt % 2
                    b_ = cnt % 4
                    cnt += 1
                    sc.dma("sp", lambda e, g=g, a=a: e.dma_start(out=Vg[a][:], in_=V_d[g]), dst=Vg_b[a])
                    for stl in range(32):
                        for q in range(2):
                            sc.op("pe", lambda e, a=a, ta=ta, stl=stl, q=q, n=n, b_=b_: e.matmul(
                                pF[b_][:, :n], lhsT=Vg[a][:, stl, q * 128:(q + 1) * 128], rhs=tabs[ta][:, q, stl, :n],
                                start=(stl == 0 and q == 0), stop=(stl == 31 and q == 1)),
                                reads=[Vg_b[a], tab_b[ta]], writes=[pF_b[b_]])
                    sc.op("act", lambda e, a=a, n=n, b_=b_: e.activation(out=aT[a][:, :n], in_=pF[b_][:, :n], func=AF.Copy),
                          reads=[pF_b[b_]], writes=[aT_b[a]])
                    sc.dma("act", lambda e, g=g, a=a, k0=k0, n=n: e.dma_start(out=mixT_d[g * 128:(g + 1) * 128, k0:k0 + n],
                                                                        in_=aT[a][:, :n]), src=aT_b[a])
            sc.emit()
        if stop_after <= 4:
            return nc

        with ExitStack() as s4:
            prm = sb(s4, "prm", [128, 5, 48], F32)
            dts = sb(s4, "dts", [128, 2, 32, 48], F32)
            adt = sb(s4, "adt", [128, 2, 32, 48], F32)
            nw = sb(s4, "nw", [128, 24], F32)
            nmk = sb(s4, "nmk", [128, 2, 128], BF16)
            stateT = sb(s4, "stateT", [128, SW], F32)
            prevb = sb(s4, "prevb", [128, SW], BF16)
            xTc = [sb(s4, f"xTc{i}", [128, 24, 128], BF16) for i in range(2)]
            BTc = [sb(s4, f"BTc{i}", [128, 8, 128], BF16) for i in range(2)]
            CTc = [sb(s4, f"CTc{i}", [128, 8, 128], BF16) for i in range(2)]
            xtok2 = [sb(s4, f"xtok{i}", [128, SW], BF16) for i in range(2)]
            Btok2 = [sb(s4, f"Btok{i}", [128, 1024], BF16) for i in range(2)]
            xd2 = [sb(s4, f"xd{i}", [128, SW], BF16) for i in range(2)]
            xdw2 = [sb(s4, f"xdw{i}", [128, SW], BF16) for i in range(2)]
            sm2 = sb(s4, "sm", [128, 2, 8, 48], F32)
            CBm = [sb(s4, f"CBm{i}", [128, 128], F32) for i in range(2)]
            Eb = [sb(s4, f"Eb{i}", [128, 128], F32) for i in range(3)]
            MT = [sb(s4, f"MT{i}", [128, 128], BF16) for i in range(3)]
            yoff = [sb(s4, f"yoff{i}", [128, 384], F32) for i in range(2)]
            ysb = sb(s4, "ysb", [128, SW], F32)
            ybc = sb(s4, "ybc", [128, SW], F32)
            zsc = sb(s4, "zsc", [128, 24, 128], BF16)
            yg = sb(s4, "yg", [128, 24, 128], F32)
            sq = sb(s4, "sq", [128, 24, 128], F32)
            rsg = [sb(s4, f"rsg{i}", [128, 128], F32) for i in range(2)]
            osb4 = sb(s4, "osb4", [128, 24, 128], BF16)
            pTr = [ps(s4, f"pTr{i}", [128, 1024], BF16)[:, 0:512] for i in range(2)]
            pA = [ps(s4, f"pA{i}", [128, 512], F32) for i in range(2)]
            pY = [ps(s4, f"pY{i}", [128, 512], F32) for i in range(2)]
            pO = [ps(s4, f"pO{i}", [128, 512], F32) for i in range(2)]

            LAGH = 2
            for direction in ("b", "f"):
                sc = Sched(nc, pool)
                di = 1 if direction == "b" else 0
                sm = xd = xdw = None
                Tm = Lmat if direction == "b" else Umat
                prm_b, dts_b, adt_b, nw_b = sc.buf("prm"), sc.buf("dts"), sc.buf("adt"), sc.buf("nw")
                state_b = sc.bufs_n("state", 8)
                prev_b = sc.bufs_n("prev", 8)
                xTc_b, BTc_b, CTc_b = sc.bufs_n("xTc", 2), sc.bufs_n("BTc", 2), sc.bufs_n("CTc", 2)
                xtok_b2, Btok_b2 = sc.bufs_n("xtok", 2), sc.bufs_n("Btok", 2)
                xd_b2, xdw_b2, sm_b2 = sc.bufs_n("xd", 2), sc.bufs_n("xdw", 2), sc.bufs_n("sm", 2)
                CBm_b, Eb_b, MT_b, yoff_b = sc.bufs_n("CBm", 2), sc.bufs_n("Eb", 3), sc.bufs_n("MT", 3), sc.bufs_n("yoff", 2)
                ysb_g = sc.bufs_n("ysb", 8)
                ybc_b, zsc_b, yg_b, sq_b, osb4_b = sc.buf("ybc"), sc.buf("zsc"), sc.buf("yg"), sc.buf("sq"), sc.buf("osb4")
                rsg_b = sc.bufs_n("rsg", 2)
                pTr_b, pY_b, pO_b = sc.bufs_n("pTr", 2), sc.bufs_n("pY", 2), sc.bufs_n("pO", 2)
                pA_b = sc.bufs_n("pA", 2)
                pSm, pSm_b = pO[1][:, 384:512], pO_b[1]
                if direction == "b":
                    sc.dma("sp", lambda e, sm=sm, xd=xd, xdw=xdw: e.dma_start(out=prm[:].rearrange("p a h -> p (a h)"),
                                                       in_=ssp_d.rearrange("a h -> (a h)").partition_broadcast(128)), dst=prm_b)
                    sc.dma("sp", lambda e, sm=sm, xd=xd, xdw=xdw: e.dma_start(out=nw[:], in_=nw_d), dst=nw_b)
                    sc.dma("sp", lambda e: e.dma_start(out=nmk[:], in_=nmk_d.rearrange("q p f -> p q f")), dst=nw_b)
                    for d2 in range(2):
                        bias = prm[:, 2 * d2, :].unsqueeze(1).to_broadcast([128, 32, 48])
                        alog = prm[:, 2 * d2 + 1, :]
                        sc.op("dve", lambda e, d2=d2, bias=bias, sm=sm, xd=xd, xdw=xdw: e.tensor_tensor(out=dts[:, d2], in0=dtraw[:], in1=bias, op=ALU.add),
                              reads=[prm_b], writes=[dts_b])
                        sc.op("act", lambda e, d2=d2, sm=sm, xd=xd, xdw=xdw: e.activation(out=dts[:, d2], in_=dts[:, d2], func=AF.Exp),
                              reads=[dts_b], writes=[dts_b])
                        sc.op("act", lambda e, d2=d2, sm=sm, xd=xd, xdw=xdw: e.activation(out=dts[:, d2], in_=dts[:, d2], func=AF.Ln, bias=1.0),
                              reads=[dts_b], writes=[dts_b])
                        sc.op("act", lambda e, alog=alog, sm=sm, xd=xd, xdw=xdw: e.activation(out=alog, in_=alog, func=AF.Exp),
                              reads=[prm_b], writes=[prm_b])
                        sc.op("dve", lambda e, alog=alog, sm=sm, xd=xd, xdw=xdw: e.tensor_scalar(out=alog, in0=alog, scalar1=-1.0, scalar2=None, op0=ALU.mult),
                              reads=[prm_b], writes=[prm_b])
                        sc.op("dve", lambda e, d2=d2, alog=alog, sm=sm, xd=xd, xdw=xdw: e.tensor_tensor(
                            out=adt[:, d2], in0=dts[:, d2], in1=alog.unsqueeze(1).to_broadcast([128, 32, 48]), op=ALU.mult),
                            reads=[prm_b, dts_b], writes=[adt_b])
                sc.op("pool", lambda e, sm=sm, xd=xd, xdw=xdw: e.memset(stateT[:], 0.0), writes=state_b)
                sc.op("pool", lambda e, sm=sm, xd=xd, xdw=xdw: e.memset(prevb[:], 0.0), writes=prev_b)
                chunks = list(range(31, -1, -1)) if direction == "b" else list(range(NCH_LOC))
                if "KDBG4_LIST" in os.environ:
                    chunks = [int(v) for v in os.environ["KDBG4_LIST"].split(",")]
                if "KDBG4_CHUNKS" in os.environ:
                    chunks = chunks[:int(os.environ["KDBG4_CHUNKS"])]
                hcnt = 0
                gcnt = 0
                def prologue(ci):
                        c = chunks[ci]
                        local = c < NCH_LOC and os.environ.get("KDBG4_LOCAL", "1") == "1"
                        a = ci % 2
                        t0 = c * 128
                        sm, sm_b = sm2[:, a], sm_b2[a]
                        xd, xd_b, xdw, xdw_b = xd2[a], xd_b2[a], xdw2[a], xdw_b2[a]
                        xtok, xtok_b, Btok, Btok_b = xtok2[a], xtok_b2[a], Btok2[a], Btok_b2[a]
                        sc.dma("sp", lambda e, a=a, t0=t0, sm=sm, xd=xd, xdw=xdw: e.dma_start(
                            out=xTc[a][:], in_=xT_d.rearrange("(j p) t -> p j t", p=128)[:, :, t0:t0 + 128]), dst=xTc_b[a])
                        sc.dma("sp", lambda e, a=a, t0=t0, sm=sm, xd=xd, xdw=xdw: e.dma_start(
                            out=BTc[a][:], in_=BT_d.rearrange("(j p) t -> p j t", p=128)[:, :, t0:t0 + 128]), dst=BTc_b[a])
                        if local:
                            sc.dma("sp", lambda e, a=a, t0=t0, sm=sm, xd=xd, xdw=xdw: e.dma_start(
                                out=CTc[a][:], in_=CT_d.rearrange("(j p) t -> p j t", p=128)[:, :, t0:t0 + 128]), dst=CTc_b[a])
                        for q in range(8):
                            pi = q % 2
                            for r in range(4):
                                j = 4 * q + r
                                src = xTc[a][:, j, :] if j < 24 else BTc[a][:, j - 24, :]
                                sc.op("pe", lambda e, src=src, pi=pi, r=r, sm=sm, xd=xd, xdw=xdw: e.transpose(out=pTr[pi][:, r * 128:(r + 1) * 128],
                                                                                      in_=src, identity=idb[:]),
                                      reads=[xTc_b[a], BTc_b[a]], writes=[pTr_b[pi]])
                            if q < 6:
                                sc.op("act", lambda e, q=q, pi=pi, sm=sm, xd=xd, xdw=xdw: e.activation(out=xtok[:, q * 512:(q + 1) * 512], in_=pTr[pi],
                                                                               func=AF.Copy), reads=[pTr_b[pi]], writes=[xtok_b])
                            else:
                                sc.op("act", lambda e, q=q, pi=pi, sm=sm, xd=xd, xdw=xdw: e.activation(out=Btok[:, (q - 6) * 512:(q - 5) * 512], in_=pTr[pi],
                                                                               func=AF.Copy), reads=[pTr_b[pi]], writes=[Btok_b])
                        sc.op("pe", lambda e, c=c, Tm=Tm, di=di, sm=sm, xd=xd, xdw=xdw: e.matmul(pSm[:, 0:48], lhsT=Tm, rhs=adt[:, di, c, :], start=True, stop=True),
                              reads=[adt_b], writes=[pSm_b])
                        sc.op("pe", lambda e, c=c, di=di, sm=sm, xd=xd, xdw=xdw: e.matmul(pSm[:, 48:96], lhsT=ones_f, rhs=adt[:, di, c, :], start=True, stop=True),
                              reads=[adt_b], writes=[pSm_b])
                        sc.op("act", lambda e, sm=sm, xd=xd, xdw=xdw: e.activation(out=sm[:, 0:2, :].rearrange("p a h -> p (a h)"), in_=pSm[:, 0:96], func=AF.Copy),
                              reads=[pSm_b], writes=[sm_b])
                        sc.op("dve", lambda e, sm=sm, xd=xd, xdw=xdw: e.tensor_tensor(out=sm[:, 2, :], in0=sm[:, 1, :], in1=sm[:, 0, :], op=ALU.subtract),
                              reads=[sm_b], writes=[sm_b])
                        sc.op("act", lambda e, sm=sm, xd=xd, xdw=xdw: e.activation(out=sm[:, 3, :], in_=sm[:, 2, :], func=AF.Exp), reads=[sm_b], writes=[sm_b])
                        sc.op("act", lambda e, sm=sm, xd=xd, xdw=xdw: e.activation(out=sm[:, 6, :], in_=sm[:, 1, :], func=AF.Exp), reads=[sm_b], writes=[sm_b])
                        if local:
                            sc.op("act", lambda e, sm=sm, xd=xd, xdw=xdw: e.activation(out=sm[:, 4, :], in_=sm[:, 0, :], func=AF.Exp), reads=[sm_b], writes=[sm_b])
                            sc.op("dve", lambda e, sm=sm, xd=xd, xdw=xdw: e.tensor_scalar(out=sm[:, 5, :], in0=sm[:, 0, :], scalar1=-1.0, scalar2=None, op0=ALU.mult),
                                  reads=[sm_b], writes=[sm_b])
                        sc.op("dve", lambda e, c=c, di=di, sm=sm, xd=xd, xdw=xdw: e.tensor_tensor(out=sm[:, 7, :], in0=dts[:, di, c, :], in1=sm[:, 3, :], op=ALU.mult),
                              reads=[sm_b, dts_b], writes=[sm_b])
                        x3 = xtok[:].rearrange("p (h q) -> p h q", q=64)
                        sc.op("pool", lambda e, x3=x3, sm=sm, xd=xd, xdw=xdw: e.tensor_tensor(
                            out=xdw[:].rearrange("p (h q) -> p h q", q=64), in0=x3,
                            in1=sm[:, 7, :].unsqueeze(2).to_broadcast([128, 48, 64]), op=ALU.mult),
                            reads=[xtok_b, sm_b], writes=[xdw_b])
                        if local:
                            sc.op("dve", lambda e, x3=x3, c=c, di=di, sm=sm, xd=xd, xdw=xdw: e.tensor_tensor(
                                out=xd[:].rearrange("p (h q) -> p h q", q=64), in0=x3,
                                in1=dts[:, di, c, :].unsqueeze(2).to_broadcast([128, 48, 64]), op=ALU.mult),
                                reads=[xtok_b, dts_b], writes=[xd_b])

                if chunks:
                    prologue(0)
                for ci, c in enumerate(chunks):
                    local = c < NCH_LOC and os.environ.get("KDBG4_LOCAL", "1") == "1"
                    a = ci % 2
                    t0 = c * 128
                    sm, sm_b = sm2[:, a], sm_b2[a]
                    xd, xd_b, xdw, xdw_b = xd2[a], xd_b2[a], xdw2[a], xdw_b2[a]
                    xtok, xtok_b, Btok, Btok_b = xtok2[a], xtok_b2[a], Btok2[a], Btok_b2[a]
                    x3 = xtok[:].rearrange("p (h q) -> p h q", q=64)
                    did_next = False
                    for g in range(8):
                        gs = slice(g * 384, (g + 1) * 384)
                        ga = gcnt % 2
                        if local and g == 4 and ci + 1 < len(chunks) and not did_next:
                            prologue(ci + 1)
                            did_next = True
                        gcnt += 1
                        if local:
                            pa0 = pO[ga]
                            sc.op("pe", lambda e, a=a, g=g, pa0=pa0, sm=sm, xd=xd, xdw=xdw: e.matmul(pa0[:, 384:512], lhsT=BTc[a][:, g, :], rhs=CTc[a][:, g, :],
                                                                             start=True, stop=True),
                                  reads=[BTc_b[a], CTc_b[a]], writes=[pO_b[ga]])
                            sc.op("dve", lambda e, ga=ga, pa0=pa0: e.tensor_copy(out=CBm[ga][:], in_=pa0[:, 384:512]),
                                  reads=[pO_b[ga]], writes=[CBm_b[ga]])
                            def headA(r, g=g, ga=ga, sm=sm, sm_b=sm_b):
                                nonlocal hcnt
                                h = 6 * g + r
                                ha = hcnt % 3
                                pb_ = hcnt % 2
                                hcnt += 1
                                sc.op("pe", lambda e: e.matmul(
                                    pA[pb_][:, 0:128], lhsT=sm[:, 0, h:h + 1].to_broadcast([128, 128]),
                                    rhs=id_f, start=True, stop=False), reads=[sm_b], writes=[pA_b[pb_]])
                                sc.op("pe", lambda e: e.matmul(pA[pb_][:, 0:128], lhsT=idb[:], rhs=nmk[:, di, :],
                                                               start=False, stop=True), reads=[nw_b], writes=[pA_b[pb_]])
                                sc.op("act", lambda e: e.activation(
                                    out=Eb[ha][:], in_=pA[pb_][:, 0:128], func=AF.Exp,
                                    bias=sm[:, 5, h:h + 1], scale=1.0), reads=[pA_b[pb_], sm_b], writes=[Eb_b[ha]])
                                sc.op("dve", lambda e: e.tensor_tensor(
                                    out=MT[ha][:], in0=Eb[ha][:], in1=CBm[ga][:], op=ALU.mult),
                                    reads=[Eb_b[ha], CBm_b[ga]], writes=[MT_b[ha]])
                                return ha

                            def headY(r, ha, g=g, ga=ga, xd=xd, xd_b=xd_b):
                                h = 6 * g + r
                                sc.op("pe", lambda e: e.matmul(
                                    pY[ga][:, r * 64:(r + 1) * 64], lhsT=MT[ha][:], rhs=xd[:, h * 64:(h + 1) * 64],
                                    start=True, stop=True), reads=[MT_b[ha], xd_b], writes=[pY_b[ga]])

                            has = {}
                            for idx in range(6 + LAGH):
                                if idx < 6:
                                    has[idx] = headA(idx)
                                if idx >= LAGH:
                                    headY(idx - LAGH, has[idx - LAGH])
                            sc.op("pe", lambda e, a=a, g=g, ga=ga, gs=gs, sm=sm, xd=xd, xdw=xdw: e.matmul(pO[ga][:, 0:384], lhsT=CTc[a][:, g, :], rhs=prevb[:, gs],
                                                                                   start=True, stop=True),
                                  reads=[CTc_b[a], prev_b[g]], writes=[pO_b[ga]])
                            for r in range(6):
                                h = 6 * g + r
                                sc.op("act", lambda e, ga=ga, r=r, h=h, sm=sm, xd=xd, xdw=xdw: e.activation(
                                    out=yoff[ga][:, r * 64:(r + 1) * 64], in_=pO[ga][:, r * 64:(r + 1) * 64], func=AF.Copy,
                                    scale=sm[:, 4, h:h + 1]), reads=[pO_b[ga], sm_b], writes=[yoff_b[ga]])
                            sc.op("dve", lambda e, ga=ga, gs=gs, sm=sm, xd=xd, xdw=xdw: e.tensor_tensor(out=ysb[:, gs], in0=pY[ga][:, 0:384], in1=yoff[ga][:], op=ALU.add),
                                  reads=[pY_b[ga], yoff_b[ga]], writes=[ysb_g[g]])
                    if ci + 1 < len(chunks) and not did_next:
                        prologue(ci + 1)
                        did_next = True
                    for g in range(8):
                        gs = slice(g * 384, (g + 1) * 384)
                        ga = g % 2
                        sc.op("pe", lambda e, g=g, ga=ga, gs=gs, sm=sm, xd=xd, xdw=xdw, Btok=Btok: e.matmul(pO[ga][:, 0:384],
                                                                          lhsT=Btok[:, g * 128:(g + 1) * 128], rhs=xdw[:, gs],
                                                                          start=True, stop=True),
                              reads=[Btok_b, xdw_b], writes=[pO_b[ga]])
                        st3 = stateT[:, gs].rearrange("p (h q) -> p h q", q=64)
                        sc.op("pool", lambda e, g=g, st3=st3, sm=sm, xd=xd, xdw=xdw: e.tensor_tensor(
                            out=st3, in0=st3, in1=sm[:, 6, 6 * g:6 * g + 6].unsqueeze(2).to_broadcast([128, 6, 64]), op=ALU.mult),
                            reads=[sm_b, state_b[g]], writes=[state_b[g]])
                        sc.op("dve", lambda e, ga=ga, gs=gs, sm=sm, xd=xd, xdw=xdw: e.tensor_tensor(out=stateT[:, gs], in0=pO[ga][:, 0:384], in1=stateT[:, gs], op=ALU.add),
                              reads=[pO_b[ga], state_b[g]], writes=[state_b[g]])
                        sc.op("act", lambda e, gs=gs, sm=sm, xd=xd, xdw=xdw: e.activation(out=prevb[:, gs], in_=stateT[:, gs], func=AF.Copy),
                              reads=[state_b[g]], writes=[prev_b[g]])
                    if local and direction == "b":
                        for g in range(8):
                            gs = slice(g * 384, (g + 1) * 384)
                            sc.dma("sp", lambda e, t0=t0, gs=gs, sm=sm, xd=xd, xdw=xdw: e.dma_start(out=yb_d[t0:t0 + 128, gs], in_=ysb[:, gs]), src=ysb_g[g])
                    if direction == "f":
                        sc.dma("sp", lambda e, t0=t0, sm=sm, xd=xd, xdw=xdw: e.dma_start(out=ybc[:], in_=yb_d[t0:t0 + 128, :]), dst=ybc_b)
                        sc.dma("sp", lambda e, t0=t0, sm=sm, xd=xd, xdw=xdw: e.dma_start(
                            out=zsc[:], in_=zsT_d.rearrange("(j p) t -> p j t", p=128)[:, :, t0:t0 + 128]), dst=zsc_b)
                        sc.op("pool", lambda e, sm=sm, xd=xd, xdw=xdw: e.tensor_tensor(out=ybc[:], in0=ybc[:], in1=ysb[:], op=ALU.add),
                              reads=[ybc_b] + ysb_g, writes=[ybc_b])
                        sc.op("dve", lambda e, x3=x3, sm=sm, xd=xd, xdw=xdw: e.tensor_tensor(
                            out=xd[:].rearrange("p (h q) -> p h q", q=64), in0=x3,
                            in1=prm[:, 4, :].unsqueeze(2).to_broadcast([128, 48, 64]), op=ALU.mult),
                            reads=[xtok_b, prm_b, xd_b], writes=[xd_b])
                        sc.op("dve", lambda e, sm=sm, xd=xd, xdw=xdw: e.tensor_tensor(out=ybc[:], in0=ybc[:], in1=xd[:], op=ALU.add),
                              reads=[ybc_b, xd_b], writes=[ybc_b])
                        for q in range(6):
                            pi = q % 2
                            for r in range(4):
                                j = 4 * q + r
                                sc.op("pe", lambda e, j=j, pi=pi, r=r, sm=sm, xd=xd, xdw=xdw: e.transpose(out=pA[pi][:, r * 128:(r + 1) * 128],
                                                                                  in_=ybc[:, j * 128:(j + 1) * 128], identity=id_f),
                                      reads=[ybc_b], writes=[pA_b[pi]])
                            sc.op("dve", lambda e, q=q, pi=pi, sm=sm, xd=xd, xdw=xdw: e.tensor_tensor(
                                out=yg[:, 4 * q:4 * q + 4, :], in0=pA[pi][:].rearrange("p (a b) -> p a b", a=4),
                                in1=zsc[:, 4 * q:4 * q + 4, :], op=ALU.mult), reads=[pA_b[pi], zsc_b], writes=[yg_b])
                        sc.op("act", lambda e, sm=sm, xd=xd, xdw=xdw: e.activation(out=sq[:], in_=yg[:], func=AF.Square), reads=[yg_b], writes=[sq_b])
                        for g in range(8):
                            ga = g % 2
                            for jj in range(3):
                                sc.op("pe", lambda e, g=g, jj=jj, ga=ga, sm=sm, xd=xd, xdw=xdw: e.matmul(pY[ga][:, 0:128], lhsT=ones_f, rhs=sq[:, 3 * g + jj, :],
                                                                                 start=(jj == 0), stop=(jj == 2)),
                                      reads=[sq_b], writes=[pY_b[ga]])
                            sc.op("act", lambda e, ga=ga: e.activation(out=rsg[ga][:], in_=pY[ga][:, 0:128], func=AF.Sqrt, bias=epsc[:],
                                                                      scale=1.0 / 384.0), reads=[pY_b[ga]], writes=[rsg_b[ga]])
                            sc.op("dve", lambda e, ga=ga: e.reciprocal(out=rsg[ga][:], in_=rsg[ga][:]), reads=[rsg_b[ga]], writes=[rsg_b[ga]])
                            for jj in range(3):
                                j = 3 * g + jj
                                sc.op("dve", lambda e, j=j, ga=ga, sm=sm, xd=xd, xdw=xdw: e.scalar_tensor_tensor(
                                    out=osb4[:, j, :], in0=yg[:, j, :], scalar=nw[:, j:j + 1], in1=rsg[ga][:],
                                    op0=ALU.mult, op1=ALU.mult), reads=[yg_b, rsg_b[ga], nw_b], writes=[osb4_b])
                        sc.dma("sp", lambda e, t0=t0, sm=sm, xd=xd, xdw=xdw: e.dma_start(
                            out=mixT_d[1024:4096, :].rearrange("(j p) t -> p j t", p=128)[:, :, t0:t0 + 128], in_=osb4[:]), src=osb4_b)
                sc.emit()
                if direction == "b" and stop_after <= 5:
                    return nc
        if stop_after <= 6:
            return nc

        with ExitStack() as st:
            sc = Sched(nc, pool)
            wout = sb(st, "wout", [128, 32, D], BF16)
            mt = [sb(st, f"mt{i}", [128, 32, 128], BF16) for i in range(2)]
            xt = [sb(st, f"xt5{i}", [128, D], F32) for i in range(1)]
            wst = [sb(st, f"wst5{i}", [128, D], F32) for i in range(2)]
            x1 = sb(st, "x1s", [128, D], F32)
            hb = sb(st, "hb5", [128, D], BF16)
            junk = sb(st, "junk5", [128, D], BF16)
            h2s = [sb(st, f"h2s{i}", [128, 16, 128], BF16) for i in range(1)] * 2
            gain = sb(st, "gain2", [128, D], F32)
            ss = sb(st, "ss5", [128, 64], F32)
            pb = [ps(st, f"pb5{i}", [128, 512], F32) for i in range(4)]
            pt = [ps(st, f"pt5{i}", [128, 1024], BF16)[:, 0:512] for i in range(4)]
            wout_b = sc.bufs_n("wout", 32)
            wst_b = sc.bufs_n("wst", 2)
            mt_b, xt_b, h2s_b, pb_b, pt_b = sc.bufs_n("mt", 2), sc.bufs_n("xt", 1), sc.bufs_n("h2s", 1) * 2, sc.bufs_n("pb", 4), sc.bufs_n("pt", 4)
            x1_b, hb_b, junk_b, gain_b = sc.buf("x1"), sc.buf("hb"), sc.buf("junk"), sc.buf("gain")
            sc.dma("sp", lambda e: e.dma_start(out=gain[:], in_=g2_d.partition_broadcast(128)), dst=gain_b)
            for j in range(32):
                wa = j % 2
                sc.dma("sp", lambda e, j=j, wa=wa: e.dma_start(out=wst[wa][:], in_=w_out_d[j * 128:(j + 1) * 128, :]), dst=wst_b[wa])
                if j % 2 == 0:
                    sc.op("act", lambda e, j=j, wa=wa: e.activation(out=wout[:, j, :], in_=wst[wa][:], func=AF.Copy),
                          reads=[wst_b[wa]], writes=[wout_b[j]])
                else:
                    sc.op("dve", lambda e, j=j, wa=wa: e.tensor_copy(out=wout[:, j, :], in_=wst[wa][:]),
                          reads=[wst_b[wa]], writes=[wout_b[j]])
            for i in range(NCH_LOC):
                a = i % 2
                t0 = i * 128
                ssb = sc.buf(f"ss{i}")
                sc.dma("sp", lambda e, a=a, t0=t0: e.dma_start(
                    out=mt[a][:], in_=mixT_d.rearrange("(j p) t -> p j t", p=128)[:, :, t0:t0 + 128]), dst=mt_b[a])
                sc.dma("sp", lambda e, t0=t0: e.dma_start(out=xt[0][:], in_=x_d[t0:t0 + 128, :]), dst=xt_b[0])
                for j in range(32):
                    for db in range(4):
                        sc.op("pe", lambda e, a=a, db=db, j=j: e.matmul(pb[db][:], lhsT=mt[a][:, j, :], rhs=wout[:, j, db * 512:(db + 1) * 512],
                                                                       start=(j == 0), stop=(j == 31)),
                              reads=[mt_b[a], wout_b[j]], writes=[pb_b[db]])
                for db in range(4):
                    sc.op("dve", lambda e, db=db: e.tensor_tensor(out=x1[:, db * 512:(db + 1) * 512], in0=pb[db][:],
                                                                 in1=xt[0][:, db * 512:(db + 1) * 512], op=ALU.add),
                          reads=[pb_b[db], xt_b[0]], writes=[x1_b])
                rms_rows(sc, x1[:], x1_b, ssb, ss[:, 2 * i:2 * i + 1], ss[:, 2 * i + 1:2 * i + 2], gain, gain_b, hb, hb_b, junk, junk_b)
                sc.dma("pool", lambda e, t0=t0: e.dma_start(out=x1_d[t0:t0 + 128, :], in_=x1[:]), src=x1_b)
                for q in range(4):
                    pi = q
                    for r in range(4):
                        k = 4 * q + r
                        sc.op("pe", lambda e, k=k, pi=pi, r=r: e.transpose(out=pt[pi][:, r * 128:(r + 1) * 128],
                                                                          in_=hb[:, k * 128:(k + 1) * 128], identity=idb[:]),
                              reads=[hb_b], writes=[pt_b[pi]])
                    sc.op("act", lambda e, a=a, q=q, pi=pi: e.activation(out=h2s[a][:, 4 * q:4 * q + 4, :],
                                                                        in_=pt[pi].rearrange("p (a b) -> p a b", a=4), func=AF.Copy),
                          reads=[pt_b[pi]], writes=[h2s_b[a]])
                sc.dma("act", lambda e, a=a, t0=t0: e.dma_start(
                    out=h2T_d.rearrange("(k p) t -> p k t", p=128)[:, :, t0:t0 + 128], in_=h2s[a][:]), src=h2s_b[a])
            sc.emit()
        if stop_after <= 7:
            return nc

        with ExitStack() as st:
            sc = Sched(nc, pool)
            h2r = sb(st, "h2r", [128, 16, NLOC], BF16)
            wup = [sb(st, f"wup{i}", [128, 16, 256], BF16) for i in range(2)]
            wst6 = [sb(st, f"wst6{i}", [128, 16, 128], F32) for i in range(4)]
            stg = [sb(st, f"stg{i}", [128, 2052], F32) for i in range(2)]
            accg = sb(st, "accg", [128, NOWN], F32)
            accv = sb(st, "accv", [128, NOWN], F32)
            sg = sb(st, "sg", [128, NOWN], F32)
            ao = [sb(st, f"ao{i}", [128, NOWN], BF16) for i in range(2)]
            fcw = sb(st, "fcw", [128, 88, 3], F32)
            fcb = sb(st, "fcb", [128, 88], F32)
            pb = [ps(st, f"pb6{i}", [128, 512], F32) for i in range(8)]
            h2r_b, fc_b = sc.buf("h2r"), sc.buf("fc")
            wup_b, stg_b, ao_b, pb_b = sc.bufs_n("wup", 4), sc.bufs_n("stg", 2), sc.bufs_n("ao", 2), sc.bufs_n("pb", 8)
            wst6_b = sc.bufs_n("wst6", 4)
            accg_b, accv_b, sg_b = sc.buf("accg"), sc.buf("accv"), sc.buf("sg")
            sc.dma("sp", lambda e: e.dma_start(out=h2r[:], in_=h2T_d.rearrange("(k p) t -> p k t", p=128)), dst=h2r_b)
            sc.dma("sp", lambda e: e.dma_start(out=fcw[:], in_=fcw_d), dst=fc_b)
            sc.dma("sp", lambda e: e.dma_start(out=fcb[:], in_=fcb_d), dst=fc_b)
            for i in range(2):
                sc.op("pool", lambda e, i=i: e.memset(stg[i][:], 0.0), writes=[stg_b[i]])
            wuv = w_up_d.rearrange("(k p) c -> p k c", p=128)
            bank = 0
            blocks = [(b * 512, 512) for b in range(4)] + [(2048, 1)]
            for f in range(44):
                a = f % 2
                for hv in range(2):
                    c0 = hv * FFN + f * 128
                    wa = 2 * a + hv
                    sc.dma("sp", lambda e, wa=wa, c0=c0: e.dma_start(out=wst6[wa][:], in_=wuv[:, :, c0:c0 + 128]), dst=wst6_b[wa])
                    if hv == 0:
                        sc.op("act", lambda e, a=a, hv=hv, wa=wa: e.activation(out=wup[a][:, :, hv * 128:(hv + 1) * 128], in_=wst6[wa][:],
                                                                              func=AF.Copy), reads=[wst6_b[wa]], writes=[wup_b[wa]])
                    else:
                        sc.op("act", lambda e, a=a, hv=hv, wa=wa: e.activation(out=wup[a][:, :, hv * 128:(hv + 1) * 128], in_=wst6[wa][:],
                                                                              func=AF.Copy), reads=[wst6_b[wa]], writes=[wup_b[wa]])
                for hv in range(2):
                    for (t0, n) in blocks:
                        b_ = bank % 8
                        bank += 1
                        for k in range(16):
                            sc.op("pe", lambda e, a=a, hv=hv, k=k, t0=t0, n=n, b_=b_: e.matmul(
                                pb[b_][:, :n], lhsT=wup[a][:, k, hv * 128:(hv + 1) * 128], rhs=h2r[:, k, t0:t0 + n],
                                start=(k == 0), stop=(k == 15)), reads=[wup_b[2 * a + hv], h2r_b], writes=[pb_b[b_]])
                        sc.op("act", lambda e, hv=hv, t0=t0, n=n, b_=b_: e.activation(out=stg[hv][:, 1 + t0:1 + t0 + n], in_=pb[b_][:, :n],
                                                                                     func=AF.Copy),
                              reads=[pb_b[b_]], writes=[stg_b[hv]])
                    jj = hv * 44 + f
                    eng = "dve"
                    acc_t, acc_tb = (accg, accg_b) if hv == 0 else (accv, accv_b)
                    sc.op(eng, lambda e, hv=hv, jj=jj, acc_t=acc_t: e.tensor_scalar(
                        out=acc_t[:], in0=stg[hv][:, 0:NOWN], scalar1=fcw[:, jj, 0:1], scalar2=fcb[:, jj:jj + 1],
                        op0=ALU.mult, op1=ALU.add), reads=[stg_b[hv], fc_b], writes=[acc_tb])
                    for tap in (1, 2):
                        sc.op("dve", lambda e, hv=hv, jj=jj, acc_t=acc_t, tap=tap: e.scalar_tensor_tensor(
                            out=acc_t[:], in0=stg[hv][:, tap:tap + NOWN], scalar=fcw[:, jj, tap:tap + 1], in1=acc_t[:],
                            op0=ALU.mult, op1=ALU.add), reads=[stg_b[hv], fc_b, acc_tb], writes=[acc_tb])
                sc.op("act", lambda e: e.activation(out=sg[:], in_=accg[:], func=AF.Silu), reads=[accg_b], writes=[sg_b])
                sc.op("dve", lambda e, a=a: e.tensor_tensor(out=ao[a][:], in0=sg[:], in1=accv[:], op=ALU.mult),
                      reads=[sg_b, accv_b], writes=[ao_b[a]])
                sc.dma("pool", lambda e, a=a, f=f: e.dma_start(out=actT_d[f * 128:(f + 1) * 128, :], in_=ao[a][:]), src=ao_b[a])
            sc.emit()
        if stop_after <= 8:
            return nc

        with ExitStack() as st:
            sc = Sched(nc, pool)
            wdn = [sb(st, f"wdn{i}", [128, 44, 512], BF16) for i in range(2)]
            wst7 = [sb(st, f"wst7{i}", [128, 4, 512], F32) for i in range(2)]
            at = [sb(st, f"at{i}", [128, 44, 128], BF16) for i in range(2)]
            x1t = [sb(st, f"x1t{i}", [128, 512], F32) for i in range(2)]
            x2s = [sb(st, f"x2s{i}", [128, 512], F32) for i in range(2)]
            junk = sb(st, "junk7", [128, 512], BF16)
            ssp = sb(st, "ssp", [128, 16, 4], F32)
            ss = sb(st, "ss7", [128, 32], F32)
            gain = sb(st, "gain3", [128, D], F32)
            xf = [sb(st, f"xf{i}", [128, D], F32) for i in range(2)]
            of = [sb(st, f"of{i}", [128, D], F32) for i in range(2)]
            pb = [ps(st, f"pb7{i}", [128, 512], F32) for i in range(4)]
            wdn_b, at_b, x1t_b, x2s_b, pb_b = sc.bufs_n("wdn", 22), sc.bufs_n("at", 2), sc.bufs_n("x1t", 2), sc.bufs_n("x2s", 2), sc.bufs_n("pb", 4)
            wst7_b = sc.bufs_n("wst7", 2)
            wcnt = 0
            junk_b, ssp_b, gain_b = sc.buf("junk"), sc.buf("ssp"), sc.buf("gain")
            xf_b, of_b = sc.bufs_n("xf", 2), sc.bufs_n("of", 2)
            sc.dma("sp", lambda e: e.dma_start(out=gain[:], in_=g3_d.partition_broadcast(128)), dst=gain_b)
            wdv = w_dn_d.rearrange("(j p) d -> p j d", p=128)
            cnt = 0
            for db in range(4):
                wa = db % 2
                for jg in range(11):
                    sa_ = wcnt % 2
                    wcnt += 1
                    sc.dma("sp", lambda e, sa_=sa_, jg=jg, db=db: e.dma_start(
                        out=wst7[sa_][:], in_=wdv[:, 4 * jg:4 * jg + 4, db * 512:(db + 1) * 512]), dst=wst7_b[sa_])
                    if jg % 2 == 0:
                        sc.op("act", lambda e, sa_=sa_, jg=jg, wa=wa: e.activation(out=wdn[wa][:, 4 * jg:4 * jg + 4, :], in_=wst7[sa_][:],
                                                                                 func=AF.Copy), reads=[wst7_b[sa_]], writes=[wdn_b[wa * 11 + jg]])
                    else:
                        sc.op("act", lambda e, sa_=sa_, jg=jg, wa=wa: e.activation(out=wdn[wa][:, 4 * jg:4 * jg + 4, :], in_=wst7[sa_][:],
                                                                                 func=AF.Copy), reads=[wst7_b[sa_]], writes=[wdn_b[wa * 11 + jg]])
                for i in range(16):
                    a = cnt % 2
                    b_ = cnt % 4
                    cnt += 1
                    t0 = i * 128
                    sc.dma("sp", lambda e, a=a, t0=t0: e.dma_start(
                        out=at[a][:], in_=actT_d.rearrange("(j p) t -> p j t", p=128)[:, :, t0:t0 + 128]), dst=at_b[a])
                    sc.dma("sp", lambda e, a=a, t0=t0, db=db: e.dma_start(out=x1t[a][:], in_=x1_d[t0:t0 + 128, db * 512:(db + 1) * 512]),
                           dst=x1t_b[a])
                    for j in range(44):
                        sc.op("pe", lambda e, a=a, wa=wa, j=j, b_=b_: e.matmul(pb[b_][:], lhsT=at[a][:, j, :], rhs=wdn[wa][:, j, :],
                                                                              start=(j == 0), stop=(j == 43)),
                              reads=[at_b[a], wdn_b[wa * 11 + j // 4]], writes=[pb_b[b_]])
                    sc.op("dve", lambda e, a=a, b_=b_: e.tensor_tensor(out=x2s[a][:], in0=pb[b_][:], in1=x1t[a][:], op=ALU.add),
                          reads=[pb_b[b_], x1t_b[a]], writes=[x2s_b[a]])
                    sc.op("act", lambda e, a=a, i=i, db=db: e.activation(out=junk[:], in_=x2s[a][:], func=AF.Square,
                                                                        accum_out=ssp[:, i, db:db + 1]),
                          reads=[x2s_b[a]], writes=[junk_b, ssp_b])
                    sc.dma("pool", lambda e, a=a, t0=t0, db=db: e.dma_start(out=x2_d[t0:t0 + 128, db * 512:(db + 1) * 512], in_=x2s[a][:]),
                           src=x2s_b[a])
            sc.emit()
            sc = Sched(nc, pool)
            xf_b, of_b, ssp_b, gain_b = sc.bufs_n("xf", 2), sc.bufs_n("of", 2), sc.buf("ssp"), sc.buf("gain")
            for i in range(16):
                a = i % 2
                t0 = i * 128
                sc.dma("sp", lambda e, a=a, t0=t0: e.dma_start(out=xf[a][:], in_=x2_d[t0:t0 + 128, :]), dst=xf_b[a])
                sc.op("dve", lambda e, i=i: e.tensor_reduce(out=ss[:, 2 * i:2 * i + 1], in_=ssp[:, i, :], axis=AX.X, op=ALU.add),
                      reads=[ssp_b], writes=[ssp_b])
                sc.op("act", lambda e, i=i: e.activation(out=ss[:, 2 * i + 1:2 * i + 2], in_=ss[:, 2 * i:2 * i + 1], func=AF.Sqrt,
                                                        bias=epsc[:], scale=1.0 / D), reads=[ssp_b], writes=[ssp_b])
                sc.op("dve", lambda e, i=i: e.reciprocal(out=ss[:, 2 * i + 1:2 * i + 2], in_=ss[:, 2 * i + 1:2 * i + 2]),
                      reads=[ssp_b], writes=[ssp_b])
                sc.op("dve", lambda e, a=a, i=i: e.scalar_tensor_tensor(out=of[a][:], in0=xf[a][:], scalar=ss[:, 2 * i + 1:2 * i + 2],
                                                                    in1=gain[:], op0=ALU.mult, op1=ALU.mult),
                      reads=[xf_b[a], ssp_b, gain_b], writes=[of_b[a]])
                sc.dma("pool", lambda e, a=a, t0=t0: e.dma_start(out=out_d[t0:t0 + 128, :], in_=of[a][:]), src=of_b[a])
            sc.emit()
    return nc


_CONST = {}


def _consts():
    if _CONST:
        return _CONST
    bf = ml_dtypes.bfloat16
    c = np.arange(128, dtype=np.float64)
    ang = 2.0 * np.pi * np.outer(c, c) / 128.0
    sc = 1.0 / np.sqrt(float(S) * 128.0)
    _CONST["ccs"] = np.stack([np.cos(ang) * sc, np.sin(ang) * sc]).astype(np.float32).astype(bf)
    s = np.arange(S, dtype=np.int64)[:, None]
    k = np.arange(NLOC, dtype=np.int64)[None, :]
    tabs = []
    for flip in (0, 1):
        prod = ((s + flip) * (k + flip)) % S
        a = 2.0 * np.pi * prod.astype(np.float64) / S
        t2 = np.stack([np.cos(a), -np.sin(a)]).astype(np.float32).astype(bf)
        tabs.append(np.ascontiguousarray(t2.reshape(2, 32, 128, NLOC).transpose(0, 2, 1, 3)))
    _CONST["tab"] = tabs
    i = np.arange(128)
    U = (i[:, None] <= i[None, :]).astype(np.float32)
    L = (i[:, None] >= i[None, :]).astype(np.float32)
    _CONST["cst"] = np.stack([U, L, np.ones((128, 128), np.float32), np.eye(128, dtype=np.float32)])
    _CONST["idb"] = np.eye(128, dtype=np.float32).astype(bf)
    NEG = np.float32(-1.0e5)
    mf = np.where(i[None, :] < i[:, None], NEG, np.float32(0))
    mb = np.where(i[None, :] > i[:, None], NEG, np.float32(0))
    _CONST["nmk"] = np.stack([mf, mb]).astype(np.float32).astype(bf)
    return _CONST


def _in_maps(inp):
    cs = _consts()
    f = lambda a: np.ascontiguousarray(np.asarray(a, dtype=np.float32))
    x = f(inp["x"])
    w_in, w_out, w_up, w_dn = f(inp["w_in"][0]), f(inp["w_out"][0]), f(inp["w_up"][0]), f(inp["w_down"][0])
    fw = f(inp["fourier_w"][0])
    g1, g2, g3 = f(inp["norm_mix_w"][0]), f(inp["norm_ffn_w"][0]), f(inp["norm_final_w"])
    cw, cb = f(inp["ssm_conv_w"][0]), f(inp["ssm_conv_b"][0])
    fcw, fcb = f(inp["ffn_conv_w"][0]), f(inp["ffn_conv_b"][0])
    nw = f(inp["ssm_norm_w"][0]).reshape(24, 128).T.copy()
    cb_l = cb.reshape(40, 128).T.copy()
    fcb_l = fcb.reshape(88, 128).T.copy()
    prm = [f(inp[k][0]) for k in ("dt_bias_fwd", "a_log_fwd", "dt_bias_bwd", "a_log_bwd", "ssm_d")]
    maps = []
    for c in range(8):
        b, hf = c // 2, c % 2
        if hf == 0:
            xc, cwc, fcwc = x[b], cw, fcw
            ssp = np.stack([prm[0], prm[1], prm[2], prm[3], prm[4]])
        else:
            xc, cwc, fcwc = np.ascontiguousarray(x[b, ::-1]), cw[::-1], fcw[::-1]
            ssp = np.stack([prm[2], prm[3], prm[0], prm[1], prm[4]])
        cw_l = np.ascontiguousarray(cwc.reshape(5, 40, 128).transpose(2, 1, 0))
        fcw_l = np.ascontiguousarray(fcwc.reshape(3, 88, 128).transpose(2, 1, 0))
        maps.append({"x": xc, "g1": g1, "w_in": w_in, "fw": fw, "ccs": cs["ccs"], "tab": cs["tab"][hf],
                     "cw": cw_l, "cb": cb_l, "ssp": np.ascontiguousarray(ssp), "nw": nw, "cst": cs["cst"],
                     "idb": cs["idb"], "nmk": cs["nmk"], "w_out": w_out, "g2": g2, "w_up": w_up, "fcw": fcw_l, "fcb": fcb_l,
                     "w_dn": w_dn, "g3": g3})
    return maps


_NC = {}


def kernel(**inputs):
    if "nc" not in _NC:
        _NC["nc"] = build_program()
    maps = _in_maps(inputs)
    res = run_bass_kernel_spmd(_NC["nc"], maps, core_ids=list(range(8)))
    out = np.empty((4, S, D), np.float32)
    for c in range(8):
        b, hf = c // 2, c % 2
        o = np.asarray(res.results[c]["out"], dtype=np.float32)
        if hf == 0:
            out[b, :NOWN] = o
        else:
            out[b, NOWN:] = o[::-1]
    return out
```

```python
import os
from contextlib import ExitStack
import numpy as np
import ml_dtypes
import concourse.bass as bass
import concourse.mybir as mybir
from concourse.bass_utils import run_bass_kernel_spmd

F32 = mybir.dt.float32
BF16 = mybir.dt.bfloat16
ALU = mybir.AluOpType
AF = mybir.ActivationFunctionType
AX = mybir.AxisListType

D = 2048
S = 4096
NLOC = 2176
NOWN = 2048
NCH_LOC = 17
FW = 1024
SW = 3072
XBC = 5120
INW = 9264
FFN = 5632
EPS = 1e-5
SAME_ENGINE_SYNC = True
ENGS = ("pe", "act", "dve", "pool", "sp")


class Buf:
    __slots__ = ("name", "lw", "rd", "dsem", "dcnt", "dbase")

    def __init__(self, name):
        self.name = name
        self.lw = None
        self.rd = []
        self.dsem = None
        self.dcnt = 0
        self.dbase = 0


class Op:
    __slots__ = ("fn", "waits", "signal", "dma_buf")

    def __init__(self, fn):
        self.fn = fn
        self.waits = []
        self.signal = False
        self.dma_buf = None


class SemPool:
    def __init__(self, nc, es, n_dma, n_eng):
        self.dma = [[es.enter_context(nc.semaphore(f"dq{i}")), 0] for i in range(n_dma)]
        self.eng = [es.enter_context(nc.semaphore(f"eq{i}")) for i in range(n_eng)]
        self.eng_next = 0
        self.free = list(range(n_dma))

    def take_eng(self):
        s = self.eng[self.eng_next]
        self.eng_next += 1
        return s


class Sched:
    def __init__(self, nc, pool):
        self.nc = nc
        self.pool = pool
        self.q = {e: [] for e in ENGS}
        self.seen_eng = {e: {} for e in ENGS}
        self.seen_dma = {e: {} for e in ENGS}
        self.bufs = []
        self.dma_bufs = []

    def buf(self, name):
        b = Buf(name)
        self.bufs.append(b)
        return b

    def bufs_n(self, name, n):
        return [self.buf(f"{name}{i}") for i in range(n)]

    def _add_dep(self, eng, op, dep):
        if dep[0] == "eng":
            _, e2, idx = dep
            if e2 == eng and (eng == "pe" or not SAME_ENGINE_SYNC):
                return
            if self.seen_eng[eng].get(e2, -1) >= idx:
                return
            self.seen_eng[eng][e2] = idx
            self.q[e2][idx].signal = True
            op.waits.append(dep)
        else:
            _, b, cnt = dep
            if self.seen_dma[eng].get(id(b), -1) >= cnt:
                return
            self.seen_dma[eng][id(b)] = cnt
            op.waits.append(dep)

    def op(self, eng, fn, reads=(), writes=()):
        o = Op(fn)
        idx = len(self.q[eng])
        for b in reads:
            if b.lw is not None:
                self._add_dep(eng, o, b.lw)
        for b in writes:
            if b.lw is not None:
                self._add_dep(eng, o, b.lw)
            for r in b.rd:
                self._add_dep(eng, o, r)
        me = ("eng", eng, idx)
        for b in reads:
            b.rd.append(me)
        for b in writes:
            b.lw = me
            b.rd = []
        self.q[eng].append(o)
        return o

    def _dsem(self, b):
        if b.dsem is None:
            i = self.pool.free.pop()
            b.dsem = i
            b.dbase = self.pool.dma[i][1]
            self.dma_bufs.append(b)
        return b

    def dma(self, eng, fn, src=None, dst=None):
        o = Op(fn)
        if src is not None and src.lw is not None:
            self._add_dep(eng, o, src.lw)
        if dst is not None:
            if dst.lw is not None:
                self._add_dep(eng, o, dst.lw)
            for r in dst.rd:
                self._add_dep(eng, o, r)
        b = dst if dst is not None else src
        self._dsem(b)
        b.dcnt += 1
        me = ("dma", b, b.dcnt)
        if dst is not None:
            dst.lw = me
            dst.rd = []
        else:
            src.rd.append(me)
        o.dma_buf = b
        self.q[eng].append(o)
        return o

    def emit(self):
        nc = self.nc
        pool = self.pool
        Sched.n_emit = getattr(Sched, "n_emit", -1) + 1
        if str(Sched.n_emit) in os.environ.get("KDBG_SKIP_EMITS", "").split(","):
            for b in self.dma_bufs:
                pool.free.append(b.dsem)
                b.dsem = None
            return
        last = {}
        for e in ENGS:
            if e == "sp":
                continue
            idxs = [i for i, o in enumerate(self.q[e]) if o.dma_buf is None]
            if idxs:
                last[e] = idxs[-1]
        for e2, idx in last.items():
            self.q[e2][idx].signal = True
        esem = {e: pool.take_eng() for e in ENGS if e != "sp"}
        val = {}
        for e in esem:
            c = 0
            for i, o in enumerate(self.q[e]):
                if o.signal:
                    c += 1
                val[(e, i)] = c
        dma_final = [(pool.dma[b.dsem][0], 16 * (b.dbase + b.dcnt)) for b in self.dma_bufs]

        def resolve(w):
            if w[0] == "eng":
                return esem[w[1]], val[(w[1], w[2])]
            return pool.dma[w[1].dsem][0], 16 * (w[1].dbase + w[2])

        with nc.Block() as block:
            decos = {"pe": block.tensor, "act": block.scalar, "dve": block.vector,
                     "pool": block.gpsimd, "sp": block.sync}
            for eng in ENGS:
                ops = self.q[eng]

                def body(e, ops=ops, eng=eng):
                    for o in ops:
                        for w in o.waits:
                            s, v = resolve(w)
                            e.wait_ge(s, v)
                        inst = o.fn(e)
                        if o.dma_buf is not None:
                            inst.then_inc(pool.dma[o.dma_buf.dsem][0], 16)
                        elif o.signal:
                            inst.then_inc(esem[eng], 1)
                    for e2, idx in last.items():
                        e.wait_ge(esem[e2], val[(e2, idx)])
                    for s, v in dma_final:
                        e.wait_ge(s, v)

                decos[eng](body)
        for b in self.dma_bufs:
            pool.dma[b.dsem][1] = b.dbase + b.dcnt
            pool.free.append(b.dsem)
            b.dsem = None


def build_program(stop_after=99, dump=None):
    Sched.n_emit = -1
    nc = bass.Bass("TRN2", target_bir_lowering=False)

    def din(name, shape, dt=F32):
        return nc.dram_tensor(name, list(shape), dt, kind="ExternalInput").ap()

    def dscr(name, shape, dt):
        kind = "ExternalOutput" if dump == name else "Internal"
        return nc.dram_tensor(name, list(shape), dt, kind=kind).ap()

    x_d = din("x", [S, D])
    g1_d = din("g1", [D])
    w_in_d = din("w_in", [D, INW])
    fw_d = din("fw", [8, 128, 128])
    ccs_d = din("ccs", [2, 128, 128], BF16)
    tab_d = din("tab", [2, 128, 32, NLOC], BF16)
    cw_d = din("cw", [128, 40, 5])
    cb_d = din("cb", [128, 40])
    ssp_d = din("ssp", [5, 48])
    nw_d = din("nw", [128, 24])
    cst_d = din("cst", [4, 128, 128])
    idb_d = din("idb", [128, 128], BF16)
    nmk_d = din("nmk", [2, 128, 128], BF16)
    w_out_d = din("w_out", [4096, D])
    g2_d = din("g2", [D])
    w_up_d = din("w_up", [D, 2 * FFN])
    fcw_d = din("fcw", [128, 88, 3])
    fcb_d = din("fcb", [128, 88])
    w_dn_d = din("w_dn", [FFN, D])
    g3_d = din("g3", [D])
    out_d = nc.dram_tensor("out", [NOWN, D], F32, kind="ExternalOutput").ap()

    uT_d = dscr("uT", [FW, S], BF16)
    zsT_d = dscr("zsT", [SW, NLOC], BF16)
    xT_d = dscr("xT", [SW, S], BF16)
    BT_d = dscr("BT", [1024, S], BF16)
    CT_d = dscr("CT", [1024, NLOC], BF16)
    V_d = dscr("V", [8, 128, 32, 256], BF16)
    mixT_d = dscr("mixT", [4096, NLOC], BF16)
    yb_d = dscr("yb", [NLOC, SW], F32)
    x1_d = dscr("x1", [NLOC, D], F32)
    h2T_d = dscr("h2T", [D, NLOC], BF16)
    actT_d = dscr("actT", [FFN, NOWN], BF16)
    x2_d = dscr("x2", [NOWN, D], F32)

    es = ExitStack()
    with es:
        es.enter_context(nc.allow_low_precision("bf16 matmul operands, fp32 accumulate"))
        es.enter_context(nc.allow_non_contiguous_dma("tiled layouts"))
        pool = SemPool(nc, es, 56, 40)

        uid = [0]

        def sb(st, name, shape, dt):
            uid[0] += 1
            return st.enter_context(nc.sbuf_tensor(f"s{uid[0]}_{name}", list(shape), dt))

        def ps(st, name, shape, dt=F32):
            uid[0] += 1
            return st.enter_context(nc.psum_tensor(f"p{uid[0]}_{name}", list(shape), dt))

        dtraw = sb(es, "dtraw", [128, 32, 48], F32)
        cst = sb(es, "cst", [128, 4, 128], F32)
        idb = sb(es, "idb", [128, 128], BF16)
        epsc = sb(es, "epsc", [128, 1], F32)
        Umat, Lmat, ones_f, id_f = (cst[:, i, :] for i in range(4))

        def rms_rows(sc, src, src_b, ssb, ss_ap, rstd_ap, gain, gain_b, hb, hb_b, junk, junk_b, eps_b=None):
            sc.op("act", lambda e: e.activation(out=junk[:], in_=src, func=AF.Square, accum_out=ss_ap),
                  reads=[src_b], writes=[junk_b, ssb])
            sc.op("act", lambda e: e.activation(out=rstd_ap, in_=ss_ap, func=AF.Sqrt, bias=epsc[:], scale=1.0 / D),
                  reads=[ssb] + ([eps_b] if eps_b is not None else []), writes=[ssb])
            sc.op("dve", lambda e: e.reciprocal(out=rstd_ap, in_=rstd_ap), reads=[ssb], writes=[ssb])
            sc.op("dve", lambda e: e.scalar_tensor_tensor(out=hb[:], in0=src, scalar=rstd_ap, in1=gain[:],
                                                          op0=ALU.mult, op1=ALU.mult),
                  reads=[src_b, ssb, gain_b], writes=[hb_b])

        if os.environ.get("KDBG_INIT"):
            with ExitStack() as st:
                sc = Sched(nc, pool)
                zt = sb(st, "zt", [128, S], BF16)
                zt_b = sc.buf("zt")
                sc.op("pool", lambda e: e.memset(zt[:], 0.5), writes=[zt_b])
                for t_, nm in ((dtraw, "a"), (epsc, "d")):
                    sc.op("pool", lambda e, t_=t_: e.memset(t_[:], 0.25), writes=[sc.buf(nm)])
                sc.dma("sp", lambda e: e.dma_start(out=cst[:], in_=cst_d.rearrange("c p f -> p c f")), dst=sc.buf("b"))
                sc.dma("sp", lambda e: e.dma_start(out=idb[:], in_=idb_d), dst=sc.buf("c"))
                for dten, rows, cols in ((xT_d, SW, S), (BT_d, 1024, S), (CT_d, 1024, NLOC), (zsT_d, SW, NLOC), (uT_d, FW, S)):
                    for r0 in range(0, rows, 128):
                        sc.dma("sp", lambda e, dten=dten, r0=r0, cols=cols: e.dma_start(out=dten[r0:r0 + 128, :], in_=zt[:, :cols]), src=zt_b)
                sc.emit()
        with ExitStack() as s12:
            hT = sb(s12, "hT", [128, 16, S], BF16)
            with ExitStack() as st:
                sc = Sched(nc, pool)
                xt = [sb(st, f"xt{i}", [128, D], F32) for i in range(2)]
                hb = [sb(st, f"hb{i}", [128, D], BF16) for i in range(2)]
                junk = sb(st, "junk", [128, D], BF16)
                gain = sb(st, "gain1", [128, D], F32)
                ss = sb(st, "ss", [128, 64], F32)
                pt = [ps(st, f"pt{i}", [128, 1024], BF16)[:, 0:512] for i in range(4)]
                xt_b, hb_b, pt_b = sc.bufs_n("xt", 2), sc.bufs_n("hb", 2), sc.bufs_n("pt", 4)
                junk_b, gain_b, cst_b, idb_b = sc.buf("junk"), sc.buf("gain"), sc.buf("cst"), sc.buf("idb")
                hT_b = sc.buf("hT")
                epsc_b = sc.buf("epsc")
                sc.op("pool", lambda e: e.memset(epsc[:], EPS), writes=[epsc_b])
                sc.dma("sp", lambda e: e.dma_start(out=gain[:], in_=g1_d.partition_broadcast(128)), dst=gain_b)
                sc.dma("sp", lambda e: e.dma_start(out=cst[:], in_=cst_d.rearrange("c p f -> p c f")), dst=cst_b)
                sc.dma("sp", lambda e: e.dma_start(out=idb[:], in_=idb_d), dst=idb_b)
                for i in range(32):
                    a = i % 2
                    ssb = sc.buf(f"ss{i}")
                    sc.dma("sp", lambda e, i=i, a=a: e.dma_start(out=xt[a][:], in_=x_d[i * 128:(i + 1) * 128, :]),
                           dst=xt_b[a])
                    rms_rows(sc, xt[a][:], xt_b[a], ssb, ss[:, 2 * i:2 * i + 1], ss[:, 2 * i + 1:2 * i + 2],
                             gain, gain_b, hb[a], hb_b[a], junk, junk_b, eps_b=epsc_b)
                    for q in range(4):
                        pi = (4 * i + q) % 4
                        for r in range(4):
                            k = 4 * q + r
                            sc.op("pe", lambda e, a=a, k=k, pi=pi, r=r: e.transpose(
                                out=pt[pi][:, r * 128:(r + 1) * 128], in_=hb[a][:, k * 128:(k + 1) * 128],
                                identity=idb[:]), reads=[hb_b[a], idb_b], writes=[pt_b[pi]])
                        sc.op("act", lambda e, i=i, q=q, pi=pi: e.activation(
                            out=hT[:, 4 * q:4 * q + 4, i * 128:(i + 1) * 128],
                            in_=pt[pi].rearrange("p (a b) -> p a b", a=4), func=AF.Copy),
                            reads=[pt_b[pi]], writes=[hT_b])
                sc.emit()
            if stop_after <= 1:
                return nc
            with ExitStack() as st:
                sc = Sched(nc, pool)
                wt = [sb(st, f"wt{i}", [128, 16, 128], BF16) for i in range(2)]
                stage = [sb(st, f"stage{i}", [128, S + 4], BF16) for i in range(2)]
                acc = sb(st, "acc", [128, S], F32)
                osb = [sb(st, f"osb{i}", [128, S], BF16) for i in range(2)]
                cw = sb(st, "cw", [128, 40, 5], F32)
                cb = sb(st, "cb", [128, 40], F32)
                pbank = [ps(st, f"pb{i}", [128, 512], F32) for i in range(8)]
                wt_b, stage_b, osb_b, pb_b = sc.bufs_n("wt", 2), sc.bufs_n("stage", 2), sc.bufs_n("osb", 2), sc.bufs_n("pb", 8)
                acc_b, cw_b, dtraw_b = sc.buf("acc"), sc.buf("cw"), sc.buf("dtraw")
                sc.dma("sp", lambda e: e.dma_start(out=cw[:], in_=cw_d), dst=cw_b)
                sc.dma("sp", lambda e: e.dma_start(out=cb[:], in_=cb_d), dst=cw_b)
                for i in range(2):
                    sc.op("pool", lambda e, i=i: e.memset(stage[i][:], 0.0), writes=[stage_b[i]])
                bank = 0
                w_in_v = w_in_d.rearrange("(k p) c -> p k c", p=128)
                NT = int(os.environ.get("KDBG_NT", "73"))
                for j in ([int(v) for v in os.environ["KDBG_TILES"].split(",")] if os.environ.get("KDBG_TILES") else (range(NT) if "KDBG_TILES" not in os.environ else [])):
                    a = j % 2
                    ncols = 128 if j < 72 else 48
                    sc.dma("pool", lambda e, j=j, a=a, ncols=ncols: e.dma_start(
                        out=wt[a][:, :, :ncols], in_=w_in_v[:, :, j * 128:j * 128 + ncols]), dst=wt_b[a])
                    if j == 72:
                        for i in range(32):
                            b_ = bank % 8
                            bank += 1
                            for k in range(16):
                                sc.op("pe", lambda e, a=a, k=k, i=i, b_=b_: e.matmul(
                                    pbank[b_][:, :48], lhsT=hT[:, k, i * 128:(i + 1) * 128], rhs=wt[a][:, k, :48],
                                    start=(k == 0), stop=(k == 15)), reads=[wt_b[a]], writes=[pb_b[b_]])
                            sc.op("act", lambda e, i=i, b_=b_: e.activation(out=dtraw[:, i, :], in_=pbank[b_][:, :48],
                                                                           func=AF.Copy),
                                  reads=[pb_b[b_]], writes=[dtraw_b])
                        continue
                    kind = "u" if j < 8 else "z" if j < 32 else "x" if j < 56 else "B" if j < 64 else "C"
                    if kind in ("u", "x", "B"):
                        blocks = [(b * 512, 512) for b in range(8)]
                    elif kind == "z":
                        blocks = [(b * 512, 512) for b in range(4)] + [(2048, 128)]
                    else:
                        blocks = [(b * 512, 512) for b in range(4)] + [(2048, 256)]
                    conv = kind in ("x", "B", "C")
                    sa = j % 2
                    oa = j % 2
                    for (t0, n) in blocks:
                        b_ = bank % 8
                        bank += 1
                        for k in range(16):
                            sc.op("pe", lambda e, a=a, k=k, t0=t0, n=n, b_=b_: e.matmul(
                                pbank[b_][:, :n], lhsT=wt[a][:, k, :], rhs=hT[:, k, t0:t0 + n],
                                start=(k == 0), stop=(k == 15)), reads=[wt_b[a]], writes=[pb_b[b_]])
                        if conv:
                            sc.op("act", lambda e, sa=sa, t0=t0, n=n, b_=b_: e.activation(
                                out=stage[sa][:, 2 + t0:2 + t0 + n], in_=pbank[b_][:, :n], func=AF.Copy),
                                reads=[pb_b[b_]], writes=[stage_b[sa]])
                        else:
                            fn = AF.Copy if kind == "u" else AF.Silu
                            sc.op("act", lambda e, oa=oa, t0=t0, n=n, b_=b_, fn=fn: e.activation(
                                out=osb[oa][:, t0:t0 + n], in_=pbank[b_][:, :n], func=fn),
                                reads=[pb_b[b_]], writes=[osb_b[oa]])
                    if kind == "u":
                        T, dst = S, uT_d[j * 128:(j + 1) * 128, :]
                    elif kind == "z":
                        T, dst = NLOC, zsT_d[(j - 8) * 128:(j - 7) * 128, :]
                    elif kind == "x":
                        T, dst = S, xT_d[(j - 32) * 128:(j - 31) * 128, :]
                    elif kind == "B":
                        T, dst = S, BT_d[(j - 56) * 128:(j - 55) * 128, :]
                    else:
                        T, dst = NLOC, CT_d[(j - 64) * 128:(j - 63) * 128, :]
                    if conv:
                        jj = j - 32
                        sc.op("dve", lambda e, sa=sa, jj=jj, T=T: e.tensor_scalar(
                            out=acc[:, :T], in0=stage[sa][:, 0:T], scalar1=cw[:, jj, 0:1], scalar2=cb[:, jj:jj + 1],
                            op0=ALU.mult, op1=ALU.add), reads=[stage_b[sa], cw_b], writes=[acc_b])
                        for tap in range(1, 5):
                            eng = "dve"
                            sc.op(eng, lambda e, sa=sa, jj=jj, T=T, tap=tap: e.scalar_tensor_tensor(
                                out=acc[:, :T], in0=stage[sa][:, tap:tap + T], scalar=cw[:, jj, tap:tap + 1],
                                in1=acc[:, :T], op0=ALU.mult, op1=ALU.add),
                                reads=[stage_b[sa], cw_b, acc_b], writes=[acc_b])
                        sc.op("act", lambda e, oa=oa, T=T: e.activation(out=osb[oa][:, :T], in_=acc[:, :T], func=AF.Silu),
                              reads=[acc_b], writes=[osb_b[oa]])
                    sc.dma("act", lambda e, oa=oa, T=T, dst=dst: e.dma_start(out=dst, in_=osb[oa][:, :T]), src=osb_b[oa])
                sc.emit()
        if stop_after <= 2:
            return nc

        with ExitStack() as st:
            sc = Sched(nc, pool)
            wm = sb(st, "wm", [128, 8, 128], BF16)
            ccs = sb(st, "ccs", [128, 2, 128], BF16)
            Mg = sb(st, "Mg", [128, 8, 256], BF16)
            uTg = [sb(st, f"uTg{i}", [128, S], BF16) for i in range(2)]
            Vsb = [sb(st, f"Vsb{i}", [128, 32, 256], BF16) for i in range(2)]
            pM = [ps(st, f"pM{i}", [128, 512], F32) for i in range(4)]
            wm_b, ccs_b, Mg_b = sc.buf("wm"), sc.buf("ccs"), sc.buf("Mg")
            uTg_b, Vsb_b, pM_b = sc.bufs_n("uTg", 2), sc.bufs_n("Vsb", 2), sc.bufs_n("pM", 4)
            sc.dma("pool", lambda e: e.dma_start(out=wm[:], in_=fw_d.rearrange("g c d -> c g d")), dst=wm_b)
            sc.dma("sp", lambda e: e.dma_start(out=ccs[:], in_=ccs_d.rearrange("q c d -> c q d")), dst=ccs_b)
            for g in range(8):
                b_ = g % 4
                for q in range(2):
                    sc.op("pe", lambda e, g=g, q=q, b_=b_: e.matmul(pM[b_][:, q * 128:(q + 1) * 128], lhsT=ccs[:, q, :],
                                                                  rhs=wm[:, g, :], start=True, stop=True),
                          reads=[wm_b, ccs_b], writes=[pM_b[b_]])
                sc.op("act", lambda e, g=g, b_=b_: e.activation(out=Mg[:, g, :], in_=pM[b_][:, :256], func=AF.Copy),
                      reads=[pM_b[b_]], writes=[Mg_b])
            cnt = 0
            for g in range(8):
                a = g % 2
                sc.dma("sp", lambda e, g=g, a=a: e.dma_start(out=uTg[a][:], in_=uT_d[g * 128:(g + 1) * 128, :]),
                       dst=uTg_b[a])
                for stl in range(32):
                    b_ = cnt % 4
                    cnt += 1
                    sc.op("pe", lambda e, g=g, a=a, stl=stl, b_=b_: e.matmul(
                        pM[b_][:, :256], lhsT=uTg[a][:, stl * 128:(stl + 1) * 128], rhs=Mg[:, g, :],
                        start=True, stop=True), reads=[uTg_b[a], Mg_b], writes=[pM_b[b_]])
                    eng = "act" if stl % 2 == 0 else "dve"
                    if eng == "act":
                        sc.op("act", lambda e, a=a, stl=stl, b_=b_: e.activation(out=Vsb[a][:, stl, :], in_=pM[b_][:, :256],
                                                                              func=AF.Copy),
                              reads=[pM_b[b_]], writes=[Vsb_b[a]])
                    else:
                        sc.op("dve", lambda e, a=a, stl=stl, b_=b_: e.tensor_copy(out=Vsb[a][:, stl, :], in_=pM[b_][:, :256]),
                              reads=[pM_b[b_]], writes=[Vsb_b[a]])
                sc.dma("pool", lambda e, g=g, a=a: e.dma_start(out=V_d[g], in_=Vsb[a][:]), src=Vsb_b[a])
            sc.emit()
        if stop_after <= 3:
            return nc
        with ExitStack() as st:
            sc = Sched(nc, pool)
            tabs = [sb(st, f"tab{i}", [128, 2, 32, 512], BF16) for i in range(2)]
            Vg = [sb(st, f"Vg{i}", [128, 32, 256], BF16) for i in range(2)]
            aT = [sb(st, f"aT{i}", [128, 512], BF16) for i in range(2)]
            pF = [ps(st, f"pF{i}", [128, 512], F32) for i in range(4)]
            tab_b, Vg_b, aT_b, pF_b = sc.bufs_n("tab", 2), sc.bufs_n("Vg", 2), sc.bufs_n("aT", 2), sc.bufs_n("pF", 4)
            cnt = 0
            kblocks = [(b * 512, 512) for b in range(4)] + [(2048, 128)]
            for kb, (k0, n) in enumerate(kblocks):
                ta = kb % 2
                for q in range(2):
                    sc.dma("sp", lambda e, ta=ta, q=q, k0=k0, n=n: e.dma_start(
                        out=tabs[ta][:, q, :, :n], in_=tab_d[q][:, :, k0:k0 + n]),
                        dst=tab_b[ta])
                for g in range(8):
                    a = cnt % 2
                    b_ = cnt % 4
                    cnt += 1
                    sc.dma("sp", lambda e, g=g, a=a: e.dma_start(out=Vg[a][:], in_=V_d[g]), dst=Vg_b[a])
                    for stl in range(32):
                        for q in range(2):
                            sc.op("pe", lambda e, a=a, ta=ta, stl=stl, q=q, n=n, b_=b_: e.matmul(
                                pF[b_][:, :n], lhsT=Vg[a][:, stl, q * 128:(q + 1) * 128], rhs=tabs[ta][:, q, stl, :n],
                                start=(stl == 0 and q == 0), stop=(stl == 31 and q == 1)),
                                reads=[Vg_b[a], tab_b[ta]], writes=[pF_b[b_]])
                    sc.op("act", lambda e, a=a, n=n, b_=b_: e.activation(out=aT[a][:, :n], in_=pF[b_][:, :n], func=AF.Copy),
                          reads=[pF_b[b_]], writes=[aT_b[a]])
                    sc.dma("act", lambda e, g=g, a=a, k0=k0, n=n: e.dma_start(out=mixT_d[g * 128:(g + 1) * 128, k0:k0 + n],
                                                                        in_=aT[a][:, :n]), src=aT_b[a])
            sc.emit()
        if stop_after <= 4:
            return nc

        with ExitStack() as s4:
            prm = sb(s4, "prm", [128, 5, 48], F32)
            dts = sb(s4, "dts", [128, 2, 32, 48], F32)
            adt = sb(s4, "adt", [128, 2, 32, 48], F32)
            nw = sb(s4, "nw", [128, 24], F32)
            nmk = sb(s4, "nmk", [128, 2, 128], BF16)
            stateT = sb(s4, "stateT", [128, SW], F32)
            prevb = sb(s4, "prevb", [128, SW], BF16)
            xTc = [sb(s4, f"xTc{i}", [128, 24, 128], BF16) for i in range(2)]
            BTc = [sb(s4, f"BTc{i}", [128, 8, 128], BF16) for i in range(2)]
            CTc = [sb(s4, f"CTc{i}", [128, 8, 128], BF16) for i in range(2)]
            xtok2 = [sb(s4, f"xtok{i}", [128, SW], BF16) for i in range(2)]
            Btok2 = [sb(s4, f"Btok{i}", [128, 1024], BF16) for i in range(2)]
            xd2 = [sb(s4, f"xd{i}", [128, SW], BF16) for i in range(2)]
            xdw2 = [sb(s4, f"xdw{i}", [128, SW], BF16) for i in range(2)]
            sm2 = sb(s4, "sm", [128, 2, 8, 48], F32)
            CBm8 = [sb(s4, f"CBm{i}", [128, 128], F32) for i in range(8)]
            Eb = [sb(s4, f"Eb{i}", [128, 128], F32) for i in range(3)]
            MT = [sb(s4, f"MT{i}", [128, 128], BF16) for i in range(3)]
            yoff = [sb(s4, f"yoff{i}", [128, 384], F32) for i in range(2)]
            ysb = sb(s4, "ysb", [128, SW], F32)
            ybc = sb(s4, "ybc", [128, SW], F32)
            zsc = sb(s4, "zsc", [128, 24, 128], BF16)
            yg = sb(s4, "yg", [128, 24, 128], F32)
            sq = sb(s4, "sq", [128, 24, 128], F32)
            rsg = [sb(s4, f"rsg{i}", [128, 128], F32) for i in range(2)]
            osb4 = sb(s4, "osb4", [128, 24, 128], BF16)
            pTr = [ps(s4, f"pTr{i}", [128, 1024], BF16)[:, 0:512] for i in range(2)]
            pA = [ps(s4, f"pA{i}", [128, 512], F32) for i in range(2)]
            pY = [ps(s4, f"pY{i}", [128, 512], F32) for i in range(2)]
            pO = [ps(s4, f"pO{i}", [128, 512], F32) for i in range(2)]

            LAGH = 2
            for direction in ("b", "f"):
                sc = Sched(nc, pool)
                di = 1 if direction == "b" else 0
                sm = xd = xdw = None
                Tm = Lmat if direction == "b" else Umat
                prm_b, dts_b, adt_b, nw_b = sc.buf("prm"), sc.buf("dts"), sc.buf("adt"), sc.buf("nw")
                state_b = sc.bufs_n("state", 8)
                prev_b = sc.bufs_n("prev", 8)
                xTc_b, BTc_b, CTc_b = sc.bufs_n("xTc", 2), sc.bufs_n("BTc", 2), sc.bufs_n("CTc", 2)
                xtok_b2, Btok_b2 = sc.bufs_n("xtok", 2), sc.bufs_n("Btok", 2)
                xd_b2, xdw_b2, sm_b2 = sc.bufs_n("xd", 2), sc.bufs_n("xdw", 2), sc.bufs_n("sm", 2)
                CBm8_b, Eb_b, MT_b, yoff_b = sc.bufs_n("CBm", 8), sc.bufs_n("Eb", 3), sc.bufs_n("MT", 3), sc.bufs_n("yoff", 2)
                ysb_g = sc.bufs_n("ysb", 8)
                ybc_b, zsc_b, yg_b, sq_b, osb4_b = sc.buf("ybc"), sc.buf("zsc"), sc.buf("yg"), sc.buf("sq"), sc.buf("osb4")
                rsg_b = sc.bufs_n("rsg", 2)
                pTr_b, pY_b, pO_b = sc.bufs_n("pTr", 2), sc.bufs_n("pY", 2), sc.bufs_n("pO", 2)
                pA_b = sc.bufs_n("pA", 2)
                pSm, pSm_b = pO[1][:, 384:512], pO_b[1]
                if direction == "b":
                    sc.dma("sp", lambda e, sm=sm, xd=xd, xdw=xdw: e.dma_start(out=prm[:].rearrange("p a h -> p (a h)"),
                                                       in_=ssp_d.rearrange("a h -> (a h)").partition_broadcast(128)), dst=prm_b)
                    sc.dma("sp", lambda e, sm=sm, xd=xd, xdw=xdw: e.dma_start(out=nw[:], in_=nw_d), dst=nw_b)
                    sc.dma("sp", lambda e: e.dma_start(out=nmk[:], in_=nmk_d.rearrange("q p f -> p q f")), dst=nw_b)
                    for d2 in range(2):
                        bias = prm[:, 2 * d2, :].unsqueeze(1).to_broadcast([128, 32, 48])
                        alog = prm[:, 2 * d2 + 1, :]
                        sc.op("dve", lambda e, d2=d2, bias=bias, sm=sm, xd=xd, xdw=xdw: e.tensor_tensor(out=dts[:, d2], in0=dtraw[:], in1=bias, op=ALU.add),
                              reads=[prm_b], writes=[dts_b])
                        sc.op("act", lambda e, d2=d2, sm=sm, xd=xd, xdw=xdw: e.activation(out=dts[:, d2], in_=dts[:, d2], func=AF.Exp),
                              reads=[dts_b], writes=[dts_b])
                        sc.op("act", lambda e, d2=d2, sm=sm, xd=xd, xdw=xdw: e.activation(out=dts[:, d2], in_=dts[:, d2], func=AF.Ln, bias=1.0),
                              reads=[dts_b], writes=[dts_b])
                        sc.op("act", lambda e, alog=alog, sm=sm, xd=xd, xdw=xdw: e.activation(out=alog, in_=alog, func=AF.Exp),
                              reads=[prm_b], writes=[prm_b])
                        sc.op("dve", lambda e, alog=alog, sm=sm, xd=xd, xdw=xdw: e.tensor_scalar(out=alog, in0=alog, scalar1=-1.0, scalar2=None, op0=ALU.mult),
                              reads=[prm_b], writes=[prm_b])
                        sc.op("dve", lambda e, d2=d2, alog=alog, sm=sm, xd=xd, xdw=xdw: e.tensor_tensor(
                            out=adt[:, d2], in0=dts[:, d2], in1=alog.unsqueeze(1).to_broadcast([128, 32, 48]), op=ALU.mult),
                            reads=[prm_b, dts_b], writes=[adt_b])
                sc.op("pool", lambda e, sm=sm, xd=xd, xdw=xdw: e.memset(stateT[:], 0.0), writes=state_b)
                sc.op("pool", lambda e, sm=sm, xd=xd, xdw=xdw: e.memset(prevb[:], 0.0), writes=prev_b)
                chunks = list(range(31, -1, -1)) if direction == "b" else list(range(NCH_LOC))
                if "KDBG4_LIST" in os.environ:
                    chunks = [int(v) for v in os.environ["KDBG4_LIST"].split(",")]
                if "KDBG4_CHUNKS" in os.environ:
                    chunks = chunks[:int(os.environ["KDBG4_CHUNKS"])]
                hcnt = 0
                gcnt = 0
                def prologue(ci):
                        c = chunks[ci]
                        local = c < NCH_LOC and os.environ.get("KDBG4_LOCAL", "1") == "1"
                        a = ci % 2
                        t0 = c * 128
                        sm, sm_b = sm2[:, a], sm_b2[a]
                        xd, xd_b, xdw, xdw_b = xd2[a], xd_b2[a], xdw2[a], xdw_b2[a]
                        xtok, xtok_b, Btok, Btok_b = xtok2[a], xtok_b2[a], Btok2[a], Btok_b2[a]
                        sc.dma("sp", lambda e, a=a, t0=t0, sm=sm, xd=xd, xdw=xdw: e.dma_start(
                            out=xTc[a][:], in_=xT_d.rearrange("(j p) t -> p j t", p=128)[:, :, t0:t0 + 128]), dst=xTc_b[a])
                        sc.dma("sp", lambda e, a=a, t0=t0, sm=sm, xd=xd, xdw=xdw: e.dma_start(
                            out=BTc[a][:], in_=BT_d.rearrange("(j p) t -> p j t", p=128)[:, :, t0:t0 + 128]), dst=BTc_b[a])
                        if local:
                            sc.dma("sp", lambda e, a=a, t0=t0, sm=sm, xd=xd, xdw=xdw: e.dma_start(
                                out=CTc[a][:], in_=CT_d.rearrange("(j p) t -> p j t", p=128)[:, :, t0:t0 + 128]), dst=CTc_b[a])
                        for q in range(8):
                            pi = q % 2
                            for r in range(4):
                                j = 4 * q + r
                                src = xTc[a][:, j, :] if j < 24 else BTc[a][:, j - 24, :]
                                sc.op("pe", lambda e, src=src, pi=pi, r=r, sm=sm, xd=xd, xdw=xdw: e.transpose(out=pTr[pi][:, r * 128:(r + 1) * 128],
                                                                                      in_=src, identity=idb[:]),
                                      reads=[xTc_b[a], BTc_b[a]], writes=[pTr_b[pi]])
                            if q < 6:
                                sc.op("act", lambda e, q=q, pi=pi, sm=sm, xd=xd, xdw=xdw: e.activation(out=xtok[:, q * 512:(q + 1) * 512], in_=pTr[pi],
                                                                               func=AF.Copy), reads=[pTr_b[pi]], writes=[xtok_b])
                            else:
                                sc.op("act", lambda e, q=q, pi=pi, sm=sm, xd=xd, xdw=xdw: e.activation(out=Btok[:, (q - 6) * 512:(q - 5) * 512], in_=pTr[pi],
                                                                               func=AF.Copy), reads=[pTr_b[pi]], writes=[Btok_b])
                        sc.op("pe", lambda e, c=c, Tm=Tm, di=di, sm=sm, xd=xd, xdw=xdw: e.matmul(pSm[:, 0:48], lhsT=Tm, rhs=adt[:, di, c, :], start=True, stop=True),
                              reads=[adt_b], writes=[pSm_b])
                        sc.op("pe", lambda e, c=c, di=di, sm=sm, xd=xd, xdw=xdw: e.matmul(pSm[:, 48:96], lhsT=ones_f, rhs=adt[:, di, c, :], start=True, stop=True),
                              reads=[adt_b], writes=[pSm_b])
                        sc.op("act", lambda e, sm=sm, xd=xd, xdw=xdw: e.activation(out=sm[:, 0:2, :].rearrange("p a h -> p (a h)"), in_=pSm[:, 0:96], func=AF.Copy),
                              reads=[pSm_b], writes=[sm_b])
                        sc.op("dve", lambda e, sm=sm, xd=xd, xdw=xdw: e.tensor_tensor(out=sm[:, 2, :], in0=sm[:, 1, :], in1=sm[:, 0, :], op=ALU.subtract),
                              reads=[sm_b], writes=[sm_b])
                        sc.op("act", lambda e, sm=sm, xd=xd, xdw=xdw: e.activation(out=sm[:, 3, :], in_=sm[:, 2, :], func=AF.Exp), reads=[sm_b], writes=[sm_b])
                        sc.op("act", lambda e, sm=sm, xd=xd, xdw=xdw: e.activation(out=sm[:, 6, :], in_=sm[:, 1, :], func=AF.Exp), reads=[sm_b], writes=[sm_b])
                        if local:
                            sc.op("act", lambda e, sm=sm, xd=xd, xdw=xdw: e.activation(out=sm[:, 4, :], in_=sm[:, 0, :], func=AF.Exp), reads=[sm_b], writes=[sm_b])
                            sc.op("dve", lambda e, sm=sm, xd=xd, xdw=xdw: e.tensor_scalar(out=sm[:, 5, :], in0=sm[:, 0, :], scalar1=-1.0, scalar2=None, op0=ALU.mult),
                                  reads=[sm_b], writes=[sm_b])
                        sc.op("dve", lambda e, c=c, di=di, sm=sm, xd=xd, xdw=xdw: e.tensor_tensor(out=sm[:, 7, :], in0=dts[:, di, c, :], in1=sm[:, 3, :], op=ALU.mult),
                              reads=[sm_b, dts_b], writes=[sm_b])
                        x3 = xtok[:].rearrange("p (h q) -> p h q", q=64)
                        sc.op("pool", lambda e, x3=x3, sm=sm, xd=xd, xdw=xdw: e.tensor_tensor(
                            out=xdw[:].rearrange("p (h q) -> p h q", q=64), in0=x3,
                            in1=sm[:, 7, :].unsqueeze(2).to_broadcast([128, 48, 64]), op=ALU.mult),
                            reads=[xtok_b, sm_b], writes=[xdw_b])
                        if local:
                            sc.op("dve", lambda e, x3=x3, c=c, di=di, sm=sm, xd=xd, xdw=xdw: e.tensor_tensor(
                                out=xd[:].rearrange("p (h q) -> p h q", q=64), in0=x3,
                                in1=dts[:, di, c, :].unsqueeze(2).to_broadcast([128, 48, 64]), op=ALU.mult),
                                reads=[xtok_b, dts_b], writes=[xd_b])

                if chunks:
                    prologue(0)
                for ci, c in enumerate(chunks):
                    local = c < NCH_LOC and os.environ.get("KDBG4_LOCAL", "1") == "1"
                    a = ci % 2
                    t0 = c * 128
                    sm, sm_b = sm2[:, a], sm_b2[a]
                    xd, xd_b, xdw, xdw_b = xd2[a], xd_b2[a], xdw2[a], xdw_b2[a]
                    xtok, xtok_b, Btok, Btok_b = xtok2[a], xtok_b2[a], Btok2[a], Btok_b2[a]
                    x3 = xtok[:].rearrange("p (h q) -> p h q", q=64)
                    did_next = False
                    if local:
                        for g in range(8):
                            ga = g % 2
                            sc.op("pe", lambda e, a=a, g=g, ga=ga: e.matmul(pO[ga][:, 384:512], lhsT=BTc[a][:, g, :], rhs=CTc[a][:, g, :],
                                                                           start=True, stop=True),
                                  reads=[BTc_b[a], CTc_b[a]], writes=[pO_b[ga]])
                            sc.op("dve", lambda e, g=g, ga=ga: e.tensor_copy(out=CBm8[g][:], in_=pO[ga][:, 384:512]),
                                  reads=[pO_b[ga]], writes=[CBm8_b[g]])

                        def headA(h, sm=sm, sm_b=sm_b):
                            nonlocal hcnt
                            g = h // 6
                            ha = hcnt % 3
                            pb_ = hcnt % 2
                            hcnt += 1
                            sc.op("pe", lambda e: e.matmul(
                                pA[pb_][:, 0:128], lhsT=sm[:, 0, h:h + 1].to_broadcast([128, 128]),
                                rhs=id_f, start=True, stop=False), reads=[sm_b], writes=[pA_b[pb_]])
                            sc.op("pe", lambda e: e.matmul(pA[pb_][:, 0:128], lhsT=idb[:], rhs=nmk[:, di, :],
                                                           start=False, stop=True), reads=[nw_b], writes=[pA_b[pb_]])
                            sc.op("act", lambda e: e.activation(
                                out=Eb[ha][:], in_=pA[pb_][:, 0:128], func=AF.Exp,
                                bias=sm[:, 5, h:h + 1], scale=1.0), reads=[pA_b[pb_], sm_b], writes=[Eb_b[ha]])
                            sc.op("dve", lambda e: e.tensor_tensor(
                                out=MT[ha][:], in0=Eb[ha][:], in1=CBm8[g][:], op=ALU.mult),
                                reads=[Eb_b[ha], CBm8_b[g]], writes=[MT_b[ha]])
                            return ha

                        def headY(h, ha, xd=xd, xd_b=xd_b):
                            g, r = h // 6, h % 6
                            ga = g % 2
                            sc.op("pe", lambda e: e.matmul(
                                pY[ga][:, r * 64:(r + 1) * 64], lhsT=MT[ha][:], rhs=xd[:, h * 64:(h + 1) * 64],
                                start=True, stop=True), reads=[MT_b[ha], xd_b], writes=[pY_b[ga]])

                        def gtail(g, a=a, sm=sm, sm_b=sm_b):
                            ga = g % 2
                            gs = slice(g * 384, (g + 1) * 384)
                            sc.op("pe", lambda e: e.matmul(pO[ga][:, 0:384], lhsT=CTc[a][:, g, :], rhs=prevb[:, gs], start=True, stop=True),
                                  reads=[CTc_b[a], prev_b[g]], writes=[pO_b[ga]])
                            for r in range(6):
                                h = 6 * g + r
                                sc.op("act", lambda e, r=r, h=h: e.activation(
                                    out=yoff[ga][:, r * 64:(r + 1) * 64], in_=pO[ga][:, r * 64:(r + 1) * 64], func=AF.Copy,
                                    scale=sm[:, 4, h:h + 1]), reads=[pO_b[ga], sm_b], writes=[yoff_b[ga]])
                            sc.op("dve", lambda e: e.tensor_tensor(out=ysb[:, gs], in0=pY[ga][:, 0:384], in1=yoff[ga][:], op=ALU.add),
                                  reads=[pY_b[ga], yoff_b[ga]], writes=[ysb_g[g]])

                        has = {}
                        for idx in range(48 + LAGH):
                            if idx < 48:
                                has[idx] = headA(idx)
                            if idx >= LAGH:
                                hh = idx - LAGH
                                headY(hh, has[hh])
                                if hh % 6 == 5:
                                    gtail(hh // 6)
                            if idx == 24 and ci + 1 < len(chunks) and not did_next:
                                prologue(ci + 1)
                                did_next = True
                    if ci + 1 < len(chunks) and not did_next:
                        prologue(ci + 1)
                        did_next = True
                    for g in range(8):
                        gs = slice(g * 384, (g + 1) * 384)
                        ga = g % 2
                        sc.op("pe", lambda e, g=g, ga=ga, gs=gs, sm=sm, xd=xd, xdw=xdw, Btok=Btok: e.matmul(pO[ga][:, 0:384],
                                                                          lhsT=Btok[:, g * 128:(g + 1) * 128], rhs=xdw[:, gs],
                                                                          start=True, stop=True),
                              reads=[Btok_b, xdw_b], writes=[pO_b[ga]])
                        st3 = stateT[:, gs].rearrange("p (h q) -> p h q", q=64)
                        sc.op("pool", lambda e, g=g, st3=st3, sm=sm, xd=xd, xdw=xdw: e.tensor_tensor(
                            out=st3, in0=st3, in1=sm[:, 6, 6 * g:6 * g + 6].unsqueeze(2).to_broadcast([128, 6, 64]), op=ALU.mult),
                            reads=[sm_b, state_b[g]], writes=[state_b[g]])
                        sc.op("dve", lambda e, ga=ga, gs=gs, sm=sm, xd=xd, xdw=xdw: e.tensor_tensor(out=stateT[:, gs], in0=pO[ga][:, 0:384], in1=stateT[:, gs], op=ALU.add),
                              reads=[pO_b[ga], state_b[g]], writes=[state_b[g]])
                        sc.op("act", lambda e, gs=gs, sm=sm, xd=xd, xdw=xdw: e.activation(out=prevb[:, gs], in_=stateT[:, gs], func=AF.Copy),
                              reads=[state_b[g]], writes=[prev_b[g]])
                    if local and direction == "b":
                        for g in range(8):
                            gs = slice(g * 384, (g + 1) * 384)
                            sc.dma("sp", lambda e, t0=t0, gs=gs, sm=sm, xd=xd, xdw=xdw: e.dma_start(out=yb_d[t0:t0 + 128, gs], in_=ysb[:, gs]), src=ysb_g[g])
                    if direction == "f":
                        sc.dma("sp", lambda e, t0=t0, sm=sm, xd=xd, xdw=xdw: e.dma_start(out=ybc[:], in_=yb_d[t0:t0 + 128, :]), dst=ybc_b)
                        sc.dma("sp", lambda e, t0=t0, sm=sm, xd=xd, xdw=xdw: e.dma_start(
                            out=zsc[:], in_=zsT_d.rearrange("(j p) t -> p j t", p=128)[:, :, t0:t0 + 128]), dst=zsc_b)
                        sc.op("pool", lambda e, sm=sm, xd=xd, xdw=xdw: e.tensor_tensor(out=ybc[:], in0=ybc[:], in1=ysb[:], op=ALU.add),
                              reads=[ybc_b] + ysb_g, writes=[ybc_b])
                        sc.op("dve", lambda e, x3=x3, sm=sm, xd=xd, xdw=xdw: e.tensor_tensor(
                            out=xd[:].rearrange("p (h q) -> p h q", q=64), in0=x3,
                            in1=prm[:, 4, :].unsqueeze(2).to_broadcast([128, 48, 64]), op=ALU.mult),
                            reads=[xtok_b, prm_b, xd_b], writes=[xd_b])
                        sc.op("dve", lambda e, sm=sm, xd=xd, xdw=xdw: e.tensor_tensor(out=ybc[:], in0=ybc[:], in1=xd[:], op=ALU.add),
                              reads=[ybc_b, xd_b], writes=[ybc_b])
                        for q in range(6):
                            pi = q % 2
                            for r in range(4):
                                j = 4 * q + r
                                sc.op("pe", lambda e, j=j, pi=pi, r=r, sm=sm, xd=xd, xdw=xdw: e.transpose(out=pA[pi][:, r * 128:(r + 1) * 128],
                                                                                  in_=ybc[:, j * 128:(j + 1) * 128], identity=id_f),
                                      reads=[ybc_b], writes=[pA_b[pi]])
                            sc.op("dve", lambda e, q=q, pi=pi, sm=sm, xd=xd, xdw=xdw: e.tensor_tensor(
                                out=yg[:, 4 * q:4 * q + 4, :], in0=pA[pi][:].rearrange("p (a b) -> p a b", a=4),
                                in1=zsc[:, 4 * q:4 * q + 4, :], op=ALU.mult), reads=[pA_b[pi], zsc_b], writes=[yg_b])
                        sc.op("act", lambda e, sm=sm, xd=xd, xdw=xdw: e.activation(out=sq[:], in_=yg[:], func=AF.Square), reads=[yg_b], writes=[sq_b])
                        for g in range(8):
                            ga = g % 2
                            for jj in range(3):
                                sc.op("pe", lambda e, g=g, jj=jj, ga=ga, sm=sm, xd=xd, xdw=xdw: e.matmul(pY[ga][:, 0:128], lhsT=ones_f, rhs=sq[:, 3 * g + jj, :],
                                                                                 start=(jj == 0), stop=(jj == 2)),
                                      reads=[sq_b], writes=[pY_b[ga]])
                            sc.op("act", lambda e, ga=ga: e.activation(out=rsg[ga][:], in_=pY[ga][:, 0:128], func=AF.Sqrt, bias=epsc[:],
                                                                      scale=1.0 / 384.0), reads=[pY_b[ga]], writes=[rsg_b[ga]])
                            sc.op("dve", lambda e, ga=ga: e.reciprocal(out=rsg[ga][:], in_=rsg[ga][:]), reads=[rsg_b[ga]], writes=[rsg_b[ga]])
                            for jj in range(3):
                                j = 3 * g + jj
                                sc.op("dve", lambda e, j=j, ga=ga, sm=sm, xd=xd, xdw=xdw: e.scalar_tensor_tensor(
                                    out=osb4[:, j, :], in0=yg[:, j, :], scalar=nw[:, j:j + 1], in1=rsg[ga][:],
                                    op0=ALU.mult, op1=ALU.mult), reads=[yg_b, rsg_b[ga], nw_b], writes=[osb4_b])
                        sc.dma("sp", lambda e, t0=t0, sm=sm, xd=xd, xdw=xdw: e.dma_start(
                            out=mixT_d[1024:4096, :].rearrange("(j p) t -> p j t", p=128)[:, :, t0:t0 + 128], in_=osb4[:]), src=osb4_b)
                sc.emit()
                if direction == "b" and stop_after <= 5:
                    return nc
        if stop_after <= 6:
            return nc

        with ExitStack() as st:
            sc = Sched(nc, pool)
            wout = sb(st, "wout", [128, 32, D], BF16)
            mt = [sb(st, f"mt{i}", [128, 32, 128], BF16) for i in range(2)]
            xt = [sb(st, f"xt5{i}", [128, D], F32) for i in range(1)]
            wst = [sb(st, f"wst5{i}", [128, D], F32) for i in range(2)]
            x1 = sb(st, "x1s", [128, D], F32)
            hb = sb(st, "hb5", [128, D], BF16)
            junk = sb(st, "junk5", [128, D], BF16)
            h2s = [sb(st, f"h2s{i}", [128, 16, 128], BF16) for i in range(1)] * 2
            gain = sb(st, "gain2", [128, D], F32)
            ss = sb(st, "ss5", [128, 64], F32)
            pb = [ps(st, f"pb5{i}", [128, 512], F32) for i in range(4)]
            pt = [ps(st, f"pt5{i}", [128, 1024], BF16)[:, 0:512] for i in range(4)]
            wout_b = sc.bufs_n("wout", 32)
            wst_b = sc.bufs_n("wst", 2)
            mt_b, xt_b, h2s_b, pb_b, pt_b = sc.bufs_n("mt", 2), sc.bufs_n("xt", 1), sc.bufs_n("h2s", 1) * 2, sc.bufs_n("pb", 4), sc.bufs_n("pt", 4)
            x1_b, hb_b, junk_b, gain_b = sc.buf("x1"), sc.buf("hb"), sc.buf("junk"), sc.buf("gain")
            sc.dma("sp", lambda e: e.dma_start(out=gain[:], in_=g2_d.partition_broadcast(128)), dst=gain_b)
            for j in range(32):
                wa = j % 2
                sc.dma("sp", lambda e, j=j, wa=wa: e.dma_start(out=wst[wa][:], in_=w_out_d[j * 128:(j + 1) * 128, :]), dst=wst_b[wa])
                if j % 2 == 0:
                    sc.op("act", lambda e, j=j, wa=wa: e.activation(out=wout[:, j, :], in_=wst[wa][:], func=AF.Copy),
                          reads=[wst_b[wa]], writes=[wout_b[j]])
                else:
                    sc.op("dve", lambda e, j=j, wa=wa: e.tensor_copy(out=wout[:, j, :], in_=wst[wa][:]),
                          reads=[wst_b[wa]], writes=[wout_b[j]])
            for i in range(NCH_LOC):
                a = i % 2
                t0 = i * 128
                ssb = sc.buf(f"ss{i}")
                sc.dma("sp", lambda e, a=a, t0=t0: e.dma_start(
                    out=mt[a][:], in_=mixT_d.rearrange("(j p) t -> p j t", p=128)[:, :, t0:t0 + 128]), dst=mt_b[a])
                sc.dma("sp", lambda e, t0=t0: e.dma_start(out=xt[0][:], in_=x_d[t0:t0 + 128, :]), dst=xt_b[0])
                for j in range(32):
                    for db in range(4):
                        sc.op("pe", lambda e, a=a, db=db, j=j: e.matmul(pb[db][:], lhsT=mt[a][:, j, :], rhs=wout[:, j, db * 512:(db + 1) * 512],
                                                                       start=(j == 0), stop=(j == 31)),
                              reads=[mt_b[a], wout_b[j]], writes=[pb_b[db]])
                for db in range(4):
                    sc.op("dve", lambda e, db=db: e.tensor_tensor(out=x1[:, db * 512:(db + 1) * 512], in0=pb[db][:],
                                                                 in1=xt[0][:, db * 512:(db + 1) * 512], op=ALU.add),
                          reads=[pb_b[db], xt_b[0]], writes=[x1_b])
                rms_rows(sc, x1[:], x1_b, ssb, ss[:, 2 * i:2 * i + 1], ss[:, 2 * i + 1:2 * i + 2], gain, gain_b, hb, hb_b, junk, junk_b)
                sc.dma("pool", lambda e, t0=t0: e.dma_start(out=x1_d[t0:t0 + 128, :], in_=x1[:]), src=x1_b)
                for q in range(4):
                    pi = q
                    for r in range(4):
                        k = 4 * q + r
                        sc.op("pe", lambda e, k=k, pi=pi, r=r: e.transpose(out=pt[pi][:, r * 128:(r + 1) * 128],
                                                                          in_=hb[:, k * 128:(k + 1) * 128], identity=idb[:]),
                              reads=[hb_b], writes=[pt_b[pi]])
                    sc.op("act", lambda e, a=a, q=q, pi=pi: e.activation(out=h2s[a][:, 4 * q:4 * q + 4, :],
                                                                        in_=pt[pi].rearrange("p (a b) -> p a b", a=4), func=AF.Copy),
                          reads=[pt_b[pi]], writes=[h2s_b[a]])
                sc.dma("act", lambda e, a=a, t0=t0: e.dma_start(
                    out=h2T_d.rearrange("(k p) t -> p k t", p=128)[:, :, t0:t0 + 128], in_=h2s[a][:]), src=h2s_b[a])
            sc.emit()
        if stop_after <= 7:
            return nc

        with ExitStack() as st:
            sc = Sched(nc, pool)
            h2r = sb(st, "h2r", [128, 16, NLOC], BF16)
            wup = [sb(st, f"wup{i}", [128, 16, 256], BF16) for i in range(2)]
            wst6 = [sb(st, f"wst6{i}", [128, 16, 128], F32) for i in range(4)]
            stg = [sb(st, f"stg{i}", [128, 2052], F32) for i in range(2)]
            accg = sb(st, "accg", [128, NOWN], F32)
            accv = sb(st, "accv", [128, NOWN], F32)
            sg = sb(st, "sg", [128, NOWN], F32)
            ao = [sb(st, f"ao{i}", [128, NOWN], BF16) for i in range(2)]
            fcw = sb(st, "fcw", [128, 88, 3], F32)
            fcb = sb(st, "fcb", [128, 88], F32)
            pb = [ps(st, f"pb6{i}", [128, 512], F32) for i in range(8)]
            h2r_b, fc_b = sc.buf("h2r"), sc.buf("fc")
            wup_b, stg_b, ao_b, pb_b = sc.bufs_n("wup", 4), sc.bufs_n("stg", 2), sc.bufs_n("ao", 2), sc.bufs_n("pb", 8)
            wst6_b = sc.bufs_n("wst6", 4)
            accg_b, accv_b, sg_b = sc.buf("accg"), sc.buf("accv"), sc.buf("sg")
            sc.dma("sp", lambda e: e.dma_start(out=h2r[:], in_=h2T_d.rearrange("(k p) t -> p k t", p=128)), dst=h2r_b)
            sc.dma("sp", lambda e: e.dma_start(out=fcw[:], in_=fcw_d), dst=fc_b)
            sc.dma("sp", lambda e: e.dma_start(out=fcb[:], in_=fcb_d), dst=fc_b)
            for i in range(2):
                sc.op("pool", lambda e, i=i: e.memset(stg[i][:], 0.0), writes=[stg_b[i]])
            wuv = w_up_d.rearrange("(k p) c -> p k c", p=128)
            bank = 0
            blocks = [(b * 512, 512) for b in range(4)] + [(2048, 1)]
            for f in range(44):
                a = f % 2
                for hv in range(2):
                    c0 = hv * FFN + f * 128
                    wa = 2 * a + hv
                    sc.dma("sp", lambda e, wa=wa, c0=c0: e.dma_start(out=wst6[wa][:], in_=wuv[:, :, c0:c0 + 128]), dst=wst6_b[wa])
                    if hv == 0:
                        sc.op("act", lambda e, a=a, hv=hv, wa=wa: e.activation(out=wup[a][:, :, hv * 128:(hv + 1) * 128], in_=wst6[wa][:],
                                                                              func=AF.Copy), reads=[wst6_b[wa]], writes=[wup_b[wa]])
                    else:
                        sc.op("act", lambda e, a=a, hv=hv, wa=wa: e.activation(out=wup[a][:, :, hv * 128:(hv + 1) * 128], in_=wst6[wa][:],
                                                                              func=AF.Copy), reads=[wst6_b[wa]], writes=[wup_b[wa]])
                for hv in range(2):
                    for (t0, n) in blocks:
                        b_ = bank % 8
                        bank += 1
                        for k in range(16):
                            sc.op("pe", lambda e, a=a, hv=hv, k=k, t0=t0, n=n, b_=b_: e.matmul(
                                pb[b_][:, :n], lhsT=wup[a][:, k, hv * 128:(hv + 1) * 128], rhs=h2r[:, k, t0:t0 + n],
                                start=(k == 0), stop=(k == 15)), reads=[wup_b[2 * a + hv], h2r_b], writes=[pb_b[b_]])
                        sc.op("act", lambda e, hv=hv, t0=t0, n=n, b_=b_: e.activation(out=stg[hv][:, 1 + t0:1 + t0 + n], in_=pb[b_][:, :n],
                                                                                     func=AF.Copy),
                              reads=[pb_b[b_]], writes=[stg_b[hv]])
                    jj = hv * 44 + f
                    eng = "dve"
                    acc_t, acc_tb = (accg, accg_b) if hv == 0 else (accv, accv_b)
                    sc.op(eng, lambda e, hv=hv, jj=jj, acc_t=acc_t: e.tensor_scalar(
                        out=acc_t[:], in0=stg[hv][:, 0:NOWN], scalar1=fcw[:, jj, 0:1], scalar2=fcb[:, jj:jj + 1],
                        op0=ALU.mult, op1=ALU.add), reads=[stg_b[hv], fc_b], writes=[acc_tb])
                    for tap in (1, 2):
                        sc.op("dve", lambda e, hv=hv, jj=jj, acc_t=acc_t, tap=tap: e.scalar_tensor_tensor(
                            out=acc_t[:], in0=stg[hv][:, tap:tap + NOWN], scalar=fcw[:, jj, tap:tap + 1], in1=acc_t[:],
                            op0=ALU.mult, op1=ALU.add), reads=[stg_b[hv], fc_b, acc_tb], writes=[acc_tb])
                sc.op("act", lambda e: e.activation(out=sg[:], in_=accg[:], func=AF.Silu), reads=[accg_b], writes=[sg_b])
                sc.op("dve", lambda e, a=a: e.tensor_tensor(out=ao[a][:], in0=sg[:], in1=accv[:], op=ALU.mult),
                      reads=[sg_b, accv_b], writes=[ao_b[a]])
                sc.dma("pool", lambda e, a=a, f=f: e.dma_start(out=actT_d[f * 128:(f + 1) * 128, :], in_=ao[a][:]), src=ao_b[a])
            sc.emit()
        if stop_after <= 8:
            return nc

        with ExitStack() as st:
            sc = Sched(nc, pool)
            wdn = [sb(st, f"wdn{i}", [128, 44, 512], BF16) for i in range(2)]
            wst7 = [sb(st, f"wst7{i}", [128, 4, 512], F32) for i in range(2)]
            at = [sb(st, f"at{i}", [128, 44, 128], BF16) for i in range(2)]
            x1t = [sb(st, f"x1t{i}", [128, 512], F32) for i in range(2)]
            x2s = [sb(st, f"x2s{i}", [128, 512], F32) for i in range(2)]
            junk = sb(st, "junk7", [128, 512], BF16)
            ssp = sb(st, "ssp", [128, 16, 4], F32)
            ss = sb(st, "ss7", [128, 32], F32)
            gain = sb(st, "gain3", [128, D], F32)
            xf = [sb(st, f"xf{i}", [128, D], F32) for i in range(2)]
            of = [sb(st, f"of{i}", [128, D], F32) for i in range(2)]
            pb = [ps(st, f"pb7{i}", [128, 512], F32) for i in range(4)]
            wdn_b, at_b, x1t_b, x2s_b, pb_b = sc.bufs_n("wdn", 22), sc.bufs_n("at", 2), sc.bufs_n("x1t", 2), sc.bufs_n("x2s", 2), sc.bufs_n("pb", 4)
            wst7_b = sc.bufs_n("wst7", 2)
            wcnt = 0
            junk_b, ssp_b, gain_b = sc.buf("junk"), sc.buf("ssp"), sc.buf("gain")
            xf_b, of_b = sc.bufs_n("xf", 2), sc.bufs_n("of", 2)
            sc.dma("sp", lambda e: e.dma_start(out=gain[:], in_=g3_d.partition_broadcast(128)), dst=gain_b)
            wdv = w_dn_d.rearrange("(j p) d -> p j d", p=128)
            cnt = 0
            for db in range(4):
                wa = db % 2
                for jg in range(11):
                    sa_ = wcnt % 2
                    wcnt += 1
                    sc.dma("sp", lambda e, sa_=sa_, jg=jg, db=db: e.dma_start(
                        out=wst7[sa_][:], in_=wdv[:, 4 * jg:4 * jg + 4, db * 512:(db + 1) * 512]), dst=wst7_b[sa_])
                    if jg % 2 == 0:
                        sc.op("act", lambda e, sa_=sa_, jg=jg, wa=wa: e.activation(out=wdn[wa][:, 4 * jg:4 * jg + 4, :], in_=wst7[sa_][:],
                                                                                 func=AF.Copy), reads=[wst7_b[sa_]], writes=[wdn_b[wa * 11 + jg]])
                    else:
                        sc.op("act", lambda e, sa_=sa_, jg=jg, wa=wa: e.activation(out=wdn[wa][:, 4 * jg:4 * jg + 4, :], in_=wst7[sa_][:],
                                                                                 func=AF.Copy), reads=[wst7_b[sa_]], writes=[wdn_b[wa * 11 + jg]])
                for i in range(16):
                    a = cnt % 2
                    b_ = cnt % 4
                    cnt += 1
                    t0 = i * 128
                    sc.dma("sp", lambda e, a=a, t0=t0: e.dma_start(
                        out=at[a][:], in_=actT_d.rearrange("(j p) t -> p j t", p=128)[:, :, t0:t0 + 128]), dst=at_b[a])
                    sc.dma("sp", lambda e, a=a, t0=t0, db=db: e.dma_start(out=x1t[a][:], in_=x1_d[t0:t0 + 128, db * 512:(db + 1) * 512]),
                           dst=x1t_b[a])
                    for j in range(44):
                        sc.op("pe", lambda e, a=a, wa=wa, j=j, b_=b_: e.matmul(pb[b_][:], lhsT=at[a][:, j, :], rhs=wdn[wa][:, j, :],
                                                                              start=(j == 0), stop=(j == 43)),
                              reads=[at_b[a], wdn_b[wa * 11 + j // 4]], writes=[pb_b[b_]])
                    sc.op("dve", lambda e, a=a, b_=b_: e.tensor_tensor(out=x2s[a][:], in0=pb[b_][:], in1=x1t[a][:], op=ALU.add),
                          reads=[pb_b[b_], x1t_b[a]], writes=[x2s_b[a]])
                    sc.op("act", lambda e, a=a, i=i, db=db: e.activation(out=junk[:], in_=x2s[a][:], func=AF.Square,
                                                                        accum_out=ssp[:, i, db:db + 1]),
                          reads=[x2s_b[a]], writes=[junk_b, ssp_b])
                    sc.dma("pool", lambda e, a=a, t0=t0, db=db: e.dma_start(out=x2_d[t0:t0 + 128, db * 512:(db + 1) * 512], in_=x2s[a][:]),
                           src=x2s_b[a])
            sc.emit()
            sc = Sched(nc, pool)
            xf_b, of_b, ssp_b, gain_b = sc.bufs_n("xf", 2), sc.bufs_n("of", 2), sc.buf("ssp"), sc.buf("gain")
            for i in range(16):
                a = i % 2
                t0 = i * 128
                sc.dma("sp", lambda e, a=a, t0=t0: e.dma_start(out=xf[a][:], in_=x2_d[t0:t0 + 128, :]), dst=xf_b[a])
                sc.op("dve", lambda e, i=i: e.tensor_reduce(out=ss[:, 2 * i:2 * i + 1], in_=ssp[:, i, :], axis=AX.X, op=ALU.add),
                      reads=[ssp_b], writes=[ssp_b])
                sc.op("act", lambda e, i=i: e.activation(out=ss[:, 2 * i + 1:2 * i + 2], in_=ss[:, 2 * i:2 * i + 1], func=AF.Sqrt,
                                                        bias=epsc[:], scale=1.0 / D), reads=[ssp_b], writes=[ssp_b])
                sc.op("dve", lambda e, i=i: e.reciprocal(out=ss[:, 2 * i + 1:2 * i + 2], in_=ss[:, 2 * i + 1:2 * i + 2]),
                      reads=[ssp_b], writes=[ssp_b])
                sc.op("dve", lambda e, a=a, i=i: e.scalar_tensor_tensor(out=of[a][:], in0=xf[a][:], scalar=ss[:, 2 * i + 1:2 * i + 2],
                                                                    in1=gain[:], op0=ALU.mult, op1=ALU.mult),
                      reads=[xf_b[a], ssp_b, gain_b], writes=[of_b[a]])
                sc.dma("pool", lambda e, a=a, t0=t0: e.dma_start(out=out_d[t0:t0 + 128, :], in_=of[a][:]), src=of_b[a])
            sc.emit()
    return nc


_CONST = {}


def _consts():
    if _CONST:
        return _CONST
    bf = ml_dtypes.bfloat16
    c = np.arange(128, dtype=np.float64)
    ang = 2.0 * np.pi * np.outer(c, c) / 128.0
    sc = 1.0 / np.sqrt(float(S) * 128.0)
    _CONST["ccs"] = np.stack([np.cos(ang) * sc, np.sin(ang) * sc]).astype(np.float32).astype(bf)
    s = np.arange(S, dtype=np.int64)[:, None]
    k = np.arange(NLOC, dtype=np.int64)[None, :]
    tabs = []
    for flip in (0, 1):
        prod = ((s + flip) * (k + flip)) % S
        a = 2.0 * np.pi * prod.astype(np.float64) / S
        t2 = np.stack([np.cos(a), -np.sin(a)]).astype(np.float32).astype(bf)
        tabs.append(np.ascontiguousarray(t2.reshape(2, 32, 128, NLOC).transpose(0, 2, 1, 3)))
    _CONST["tab"] = tabs
    i = np.arange(128)
    U = (i[:, None] <= i[None, :]).astype(np.float32)
    L = (i[:, None] >= i[None, :]).astype(np.float32)
    _CONST["cst"] = np.stack([U, L, np.ones((128, 128), np.float32), np.eye(128, dtype=np.float32)])
    _CONST["idb"] = np.eye(128, dtype=np.float32).astype(bf)
    NEG = np.float32(-1.0e5)
    mf = np.where(i[None, :] < i[:, None], NEG, np.float32(0))
    mb = np.where(i[None, :] > i[:, None], NEG, np.float32(0))
    _CONST["nmk"] = np.stack([mf, mb]).astype(np.float32).astype(bf)
    return _CONST


def _in_maps(inp):
    cs = _consts()
    f = lambda a: np.ascontiguousarray(np.asarray(a, dtype=np.float32))
    x = f(inp["x"])
    w_in, w_out, w_up, w_dn = f(inp["w_in"][0]), f(inp["w_out"][0]), f(inp["w_up"][0]), f(inp["w_down"][0])
    fw = f(inp["fourier_w"][0])
    g1, g2, g3 = f(inp["norm_mix_w"][0]), f(inp["norm_ffn_w"][0]), f(inp["norm_final_w"])
    cw, cb = f(inp["ssm_conv_w"][0]), f(inp["ssm_conv_b"][0])
    fcw, fcb = f(inp["ffn_conv_w"][0]), f(inp["ffn_conv_b"][0])
    nw = f(inp["ssm_norm_w"][0]).reshape(24, 128).T.copy()
    cb_l = cb.reshape(40, 128).T.copy()
    fcb_l = fcb.reshape(88, 128).T.copy()
    prm = [f(inp[k][0]) for k in ("dt_bias_fwd", "a_log_fwd", "dt_bias_bwd", "a_log_bwd", "ssm_d")]
    maps = []
    for c in range(8):
        b, hf = c // 2, c % 2
        if hf == 0:
            xc, cwc, fcwc = x[b], cw, fcw
            ssp = np.stack([prm[0], prm[1], prm[2], prm[3], prm[4]])
        else:
            xc, cwc, fcwc = np.ascontiguousarray(x[b, ::-1]), cw[::-1], fcw[::-1]
            ssp = np.stack([prm[2], prm[3], prm[0], prm[1], prm[4]])
        cw_l = np.ascontiguousarray(cwc.reshape(5, 40, 128).transpose(2, 1, 0))
        fcw_l = np.ascontiguousarray(fcwc.reshape(3, 88, 128).transpose(2, 1, 0))
        maps.append({"x": xc, "g1": g1, "w_in": w_in, "fw": fw, "ccs": cs["ccs"], "tab": cs["tab"][hf],
                     "cw": cw_l, "cb": cb_l, "ssp": np.ascontiguousarray(ssp), "nw": nw, "cst": cs["cst"],
                     "idb": cs["idb"], "nmk": cs["nmk"], "w_out": w_out, "g2": g2, "w_up": w_up, "fcw": fcw_l, "fcb": fcb_l,
                     "w_dn": w_dn, "g3": g3})
    return maps


_NC = {}


def kernel(**inputs):
    if "nc" not in _NC:
        _NC["nc"] = build_program()
    maps = _in_maps(inputs)
    res = run_bass_kernel_spmd(_NC["nc"], maps, core_ids=list(range(8)))
    out = np.empty((4, S, D), np.float32)
    for c in range(8):
        b, hf = c // 2, c % 2
        o = np.asarray(res.results[c]["out"], dtype=np.float32)
        if hf == 0:
            out[b, :NOWN] = o
        else:
            out[b, NOWN:] = o[::-1]
    return out
```

```python
import os
from contextlib import ExitStack
import numpy as np
import ml_dtypes
import concourse.bass as bass
import concourse.mybir as mybir
from concourse.bass_utils import run_bass_kernel_spmd

F32 = mybir.dt.float32
BF16 = mybir.dt.bfloat16
ALU = mybir.AluOpType
AF = mybir.ActivationFunctionType
AX = mybir.AxisListType

D = 2048
S = 4096
NLOC = 2176
NOWN = 2048
NCH_LOC = 17
FW = 1024
SW = 3072
XBC = 5120
INW = 9264
FFN = 5632
EPS = 1e-5
SAME_ENGINE_SYNC = True
ENGS = ("pe", "act", "dve", "pool", "sp")


class Buf:
    __slots__ = ("name", "lw", "rd", "dsem", "dcnt", "dbase")

    def __init__(self, name):
        self.name = name
        self.lw = None
        self.rd = []
        self.dsem = None
        self.dcnt = 0
        self.dbase = 0


class Op:
    __slots__ = ("fn", "waits", "signal", "dma_buf")

    def __init__(self, fn):
        self.fn = fn
        self.waits = []
        self.signal = False
        self.dma_buf = None


class SemPool:
    def __init__(self, nc, es, n_dma, n_eng):
        self.dma = [[es.enter_context(nc.semaphore(f"dq{i}")), 0] for i in range(n_dma)]
        self.eng = [es.enter_context(nc.semaphore(f"eq{i}")) for i in range(n_eng)]
        self.eng_next = 0
        self.free = list(range(n_dma))

    def take_eng(self):
        s = self.eng[self.eng_next]
        self.eng_next += 1
        return s


class Sched:
    def __init__(self, nc, pool):
        self.nc = nc
        self.pool = pool
        self.q = {e: [] for e in ENGS}
        self.seen_eng = {e: {} for e in ENGS}
        self.seen_dma = {e: {} for e in ENGS}
        self.bufs = []
        self.dma_bufs = []

    def buf(self, name):
        b = Buf(name)
        self.bufs.append(b)
        return b

    def bufs_n(self, name, n):
        return [self.buf(f"{name}{i}") for i in range(n)]

    def _add_dep(self, eng, op, dep):
        if dep[0] == "eng":
            _, e2, idx = dep
            if e2 == eng and (eng == "pe" or not SAME_ENGINE_SYNC):
                return
            if self.seen_eng[eng].get(e2, -1) >= idx:
                return
            self.seen_eng[eng][e2] = idx
            self.q[e2][idx].signal = True
            op.waits.append(dep)
        else:
            _, b, cnt = dep
            if self.seen_dma[eng].get(id(b), -1) >= cnt:
                return
            self.seen_dma[eng][id(b)] = cnt
            op.waits.append(dep)

    def op(self, eng, fn, reads=(), writes=()):
        o = Op(fn)
        idx = len(self.q[eng])
        for b in reads:
            if b.lw is not None:
                self._add_dep(eng, o, b.lw)
        for b in writes:
            if b.lw is not None:
                self._add_dep(eng, o, b.lw)
            for r in b.rd:
                self._add_dep(eng, o, r)
        me = ("eng", eng, idx)
        for b in reads:
            b.rd.append(me)
        for b in writes:
            b.lw = me
            b.rd = []
        self.q[eng].append(o)
        return o

    def _dsem(self, b):
        if b.dsem is None:
            i = self.pool.free.pop()
            b.dsem = i
            b.dbase = self.pool.dma[i][1]
            self.dma_bufs.append(b)
        return b

    def dma(self, eng, fn, src=None, dst=None):
        o = Op(fn)
        if src is not None and src.lw is not None:
            self._add_dep(eng, o, src.lw)
        if dst is not None:
            if dst.lw is not None:
                self._add_dep(eng, o, dst.lw)
            for r in dst.rd:
                self._add_dep(eng, o, r)
        b = dst if dst is not None else src
        self._dsem(b)
        b.dcnt += 1
        me = ("dma", b, b.dcnt)
        if dst is not None:
            dst.lw = me
            dst.rd = []
        else:
            src.rd.append(me)
        o.dma_buf = b
        self.q[eng].append(o)
        return o

    def emit(self):
        nc = self.nc
        pool = self.pool
        Sched.n_emit = getattr(Sched, "n_emit", -1) + 1
        if str(Sched.n_emit) in os.environ.get("KDBG_SKIP_EMITS", "").split(","):
            for b in self.dma_bufs:
                pool.free.append(b.dsem)
                b.dsem = None
            return
        last = {}
        for e in ENGS:
            if e == "sp":
                continue
            idxs = [i for i, o in enumerate(self.q[e]) if o.dma_buf is None]
            if idxs:
                last[e] = idxs[-1]
        for e2, idx in last.items():
            self.q[e2][idx].signal = True
        esem = {e: pool.take_eng() for e in ENGS if e != "sp"}
        val = {}
        for e in esem:
            c = 0
            for i, o in enumerate(self.q[e]):
                if o.signal:
                    c += 1
                val[(e, i)] = c
        dma_final = [(pool.dma[b.dsem][0], 16 * (b.dbase + b.dcnt)) for b in self.dma_bufs]

        def resolve(w):
            if w[0] == "eng":
                return esem[w[1]], val[(w[1], w[2])]
            return pool.dma[w[1].dsem][0], 16 * (w[1].dbase + w[2])

        with nc.Block() as block:
            decos = {"pe": block.tensor, "act": block.scalar, "dve": block.vector,
                     "pool": block.gpsimd, "sp": block.sync}
            for eng in ENGS:
                ops = self.q[eng]

                def body(e, ops=ops, eng=eng):
                    for o in ops:
                        for w in o.waits:
                            s, v = resolve(w)
                            e.wait_ge(s, v)
                        inst = o.fn(e)
                        if o.dma_buf is not None:
                            inst.then_inc(pool.dma[o.dma_buf.dsem][0], 16)
                        elif o.signal:
                            inst.then_inc(esem[eng], 1)
                    for e2, idx in last.items():
                        e.wait_ge(esem[e2], val[(e2, idx)])
                    for s, v in dma_final:
                        e.wait_ge(s, v)

                decos[eng](body)
        for b in self.dma_bufs:
            pool.dma[b.dsem][1] = b.dbase + b.dcnt
            pool.free.append(b.dsem)
            b.dsem = None


def build_program(stop_after=99, dump=None):
    Sched.n_emit = -1
    nc = bass.Bass("TRN2", target_bir_lowering=False)

    def din(name, shape, dt=F32):
        return nc.dram_tensor(name, list(shape), dt, kind="ExternalInput").ap()

    def dscr(name, shape, dt):
        kind = "ExternalOutput" if dump == name else "Internal"
        return nc.dram_tensor(name, list(shape), dt, kind=kind).ap()

    x_d = din("x", [S, D])
    g1_d = din("g1", [D])
    w_in_d = din("w_in", [D, INW])
    fw_d = din("fw", [8, 128, 128])
    ccs_d = din("ccs", [2, 128, 128], BF16)
    tab_d = din("tab", [2, 128, 32, NLOC], BF16)
    cw_d = din("cw", [128, 40, 5])
    cb_d = din("cb", [128, 40])
    ssp_d = din("ssp", [5, 48])
    nw_d = din("nw", [128, 24])
    cst_d = din("cst", [4, 128, 128])
    idb_d = din("idb", [128, 128], BF16)
    nmk_d = din("nmk", [2, 128, 128], BF16)
    w_out_d = din("w_out", [4096, D])
    g2_d = din("g2", [D])
    w_up_d = din("w_up", [D, 2 * FFN])
    fcw_d = din("fcw", [128, 88, 3])
    fcb_d = din("fcb", [128, 88])
    w_dn_d = din("w_dn", [FFN, D])
    g3_d = din("g3", [D])
    out_d = nc.dram_tensor("out", [NOWN, D], F32, kind="ExternalOutput").ap()

    uT_d = dscr("uT", [FW, S], BF16)
    zsT_d = dscr("zsT", [SW, NLOC], BF16)
    xT_d = dscr("xT", [SW, S], BF16)
    BT_d = dscr("BT", [1024, S], BF16)
    CT_d = dscr("CT", [1024, NLOC], BF16)
    V_d = dscr("V", [8, 128, 32, 256], BF16)
    mixT_d = dscr("mixT", [4096, NLOC], BF16)
    yb_d = dscr("yb", [NLOC, SW], F32)
    x1_d = dscr("x1", [NLOC, D], F32)
    h2T_d = dscr("h2T", [D, NLOC], BF16)
    actT_d = dscr("actT", [FFN, NOWN], BF16)
    x2_d = dscr("x2", [NOWN, D], F32)

    es = ExitStack()
    with es:
        es.enter_context(nc.allow_low_precision("bf16 matmul operands, fp32 accumulate"))
        es.enter_context(nc.allow_non_contiguous_dma("tiled layouts"))
        pool = SemPool(nc, es, 56, 40)

        uid = [0]

        def sb(st, name, shape, dt):
            uid[0] += 1
            return st.enter_context(nc.sbuf_tensor(f"s{uid[0]}_{name}", list(shape), dt))

        def ps(st, name, shape, dt=F32):
            uid[0] += 1
            return st.enter_context(nc.psum_tensor(f"p{uid[0]}_{name}", list(shape), dt))

        dtraw = sb(es, "dtraw", [128, 32, 48], F32)
        cst = sb(es, "cst", [128, 4, 128], F32)
        idb = sb(es, "idb", [128, 128], BF16)
        epsc = sb(es, "epsc", [128, 1], F32)
        Umat, Lmat, ones_f, id_f = (cst[:, i, :] for i in range(4))

        def rms_rows(sc, src, src_b, ssb, ss_ap, rstd_ap, gain, gain_b, hb, hb_b, junk, junk_b, eps_b=None):
            sc.op("act", lambda e: e.activation(out=junk[:], in_=src, func=AF.Square, accum_out=ss_ap),
                  reads=[src_b], writes=[junk_b, ssb])
            sc.op("act", lambda e: e.activation(out=rstd_ap, in_=ss_ap, func=AF.Sqrt, bias=epsc[:], scale=1.0 / D),
                  reads=[ssb] + ([eps_b] if eps_b is not None else []), writes=[ssb])
            sc.op("dve", lambda e: e.reciprocal(out=rstd_ap, in_=rstd_ap), reads=[ssb], writes=[ssb])
            sc.op("dve", lambda e: e.scalar_tensor_tensor(out=hb[:], in0=src, scalar=rstd_ap, in1=gain[:],
                                                          op0=ALU.mult, op1=ALU.mult),
                  reads=[src_b, ssb, gain_b], writes=[hb_b])

        if os.environ.get("KDBG_INIT"):
            with ExitStack() as st:
                sc = Sched(nc, pool)
                zt = sb(st, "zt", [128, S], BF16)
                zt_b = sc.buf("zt")
                sc.op("pool", lambda e: e.memset(zt[:], 0.5), writes=[zt_b])
                for t_, nm in ((dtraw, "a"), (epsc, "d")):
                    sc.op("pool", lambda e, t_=t_: e.memset(t_[:], 0.25), writes=[sc.buf(nm)])
                sc.dma("sp", lambda e: e.dma_start(out=cst[:], in_=cst_d.rearrange("c p f -> p c f")), dst=sc.buf("b"))
                sc.dma("sp", lambda e: e.dma_start(out=idb[:], in_=idb_d), dst=sc.buf("c"))
                for dten, rows, cols in ((xT_d, SW, S), (BT_d, 1024, S), (CT_d, 1024, NLOC), (zsT_d, SW, NLOC), (uT_d, FW, S)):
                    for r0 in range(0, rows, 128):
                        sc.dma("sp", lambda e, dten=dten, r0=r0, cols=cols: e.dma_start(out=dten[r0:r0 + 128, :], in_=zt[:, :cols]), src=zt_b)
                sc.emit()
        with ExitStack() as s12:
            hT = sb(s12, "hT", [128, 16, S], BF16)
            with ExitStack() as st:
                sc = Sched(nc, pool)
                xt = [sb(st, f"xt{i}", [128, D], F32) for i in range(2)]
                hb = [sb(st, f"hb{i}", [128, D], BF16) for i in range(2)]
                junk = sb(st, "junk", [128, D], BF16)
                gain = sb(st, "gain1", [128, D], F32)
                ss = sb(st, "ss", [128, 64], F32)
                pt = [ps(st, f"pt{i}", [128, 1024], BF16)[:, 0:512] for i in range(4)]
                xt_b, hb_b, pt_b = sc.bufs_n("xt", 2), sc.bufs_n("hb", 2), sc.bufs_n("pt", 4)
                junk_b, gain_b, cst_b, idb_b = sc.buf("junk"), sc.buf("gain"), sc.buf("cst"), sc.buf("idb")
                hT_b = sc.buf("hT")
                epsc_b = sc.buf("epsc")
                sc.op("pool", lambda e: e.memset(epsc[:], EPS), writes=[epsc_b])
                sc.dma("sp", lambda e: e.dma_start(out=gain[:], in_=g1_d.partition_broadcast(128)), dst=gain_b)
                sc.dma("sp", lambda e: e.dma_start(out=cst[:], in_=cst_d.rearrange("c p f -> p c f")), dst=cst_b)
                sc.dma("sp", lambda e: e.dma_start(out=idb[:], in_=idb_d), dst=idb_b)
                for i in range(32):
                    a = i % 2
                    ssb = sc.buf(f"ss{i}")
                    sc.dma("sp", lambda e, i=i, a=a: e.dma_start(out=xt[a][:], in_=x_d[i * 128:(i + 1) * 128, :]),
                           dst=xt_b[a])
                    rms_rows(sc, xt[a][:], xt_b[a], ssb, ss[:, 2 * i:2 * i + 1], ss[:, 2 * i + 1:2 * i + 2],
                             gain, gain_b, hb[a], hb_b[a], junk, junk_b, eps_b=epsc_b)
                    for q in range(4):
                        pi = (4 * i + q) % 4
                        for r in range(4):
                            k = 4 * q + r
                            sc.op("pe", lambda e, a=a, k=k, pi=pi, r=r: e.transpose(
                                out=pt[pi][:, r * 128:(r + 1) * 128], in_=hb[a][:, k * 128:(k + 1) * 128],
                                identity=idb[:]), reads=[hb_b[a], idb_b], writes=[pt_b[pi]])
                        sc.op("act", lambda e, i=i, q=q, pi=pi: e.activation(
                            out=hT[:, 4 * q:4 * q + 4, i * 128:(i + 1) * 128],
                            in_=pt[pi].rearrange("p (a b) -> p a b", a=4), func=AF.Copy),
                            reads=[pt_b[pi]], writes=[hT_b])
                sc.emit()
            if stop_after <= 1:
                return nc
            with ExitStack() as st:
                sc = Sched(nc, pool)
                wt = [sb(st, f"wt{i}", [128, 16, 128], BF16) for i in range(2)]
                stage = [sb(st, f"stage{i}", [128, S + 4], BF16) for i in range(2)]
                acc = sb(st, "acc", [128, S], F32)
                osb = [sb(st, f"osb{i}", [128, S], BF16) for i in range(2)]
                cw = sb(st, "cw", [128, 40, 5], F32)
                cb = sb(st, "cb", [128, 40], F32)
                pbank = [ps(st, f"pb{i}", [128, 512], F32) for i in range(8)]
                wt_b, stage_b, osb_b, pb_b = sc.bufs_n("wt", 2), sc.bufs_n("stage", 2), sc.bufs_n("osb", 2), sc.bufs_n("pb", 8)
                acc_b, cw_b, dtraw_b = sc.buf("acc"), sc.buf("cw"), sc.buf("dtraw")
                sc.dma("sp", lambda e: e.dma_start(out=cw[:], in_=cw_d), dst=cw_b)
                sc.dma("sp", lambda e: e.dma_start(out=cb[:], in_=cb_d), dst=cw_b)
                for i in range(2):
                    sc.op("pool", lambda e, i=i: e.memset(stage[i][:], 0.0), writes=[stage_b[i]])
                bank = 0
                w_in_v = w_in_d.rearrange("(k p) c -> p k c", p=128)
                NT = int(os.environ.get("KDBG_NT", "73"))
                for j in ([int(v) for v in os.environ["KDBG_TILES"].split(",")] if os.environ.get("KDBG_TILES") else (range(NT) if "KDBG_TILES" not in os.environ else [])):
                    a = j % 2
                    ncols = 128 if j < 72 else 48
                    sc.dma("pool", lambda e, j=j, a=a, ncols=ncols: e.dma_start(
                        out=wt[a][:, :, :ncols], in_=w_in_v[:, :, j * 128:j * 128 + ncols]), dst=wt_b[a])
                    if j == 72:
                        for i in range(32):
                            b_ = bank % 8
                            bank += 1
                            for k in range(16):
                                sc.op("pe", lambda e, a=a, k=k, i=i, b_=b_: e.matmul(
                                    pbank[b_][:, :48], lhsT=hT[:, k, i * 128:(i + 1) * 128], rhs=wt[a][:, k, :48],
                                    start=(k == 0), stop=(k == 15)), reads=[wt_b[a]], writes=[pb_b[b_]])
                            sc.op("act", lambda e, i=i, b_=b_: e.activation(out=dtraw[:, i, :], in_=pbank[b_][:, :48],
                                                                           func=AF.Copy),
                                  reads=[pb_b[b_]], writes=[dtraw_b])
                        continue
                    kind = "u" if j < 8 else "z" if j < 32 else "x" if j < 56 else "B" if j < 64 else "C"
                    if kind in ("u", "x", "B"):
                        blocks = [(b * 512, 512) for b in range(8)]
                    elif kind == "z":
                        blocks = [(b * 512, 512) for b in range(4)] + [(2048, 128)]
                    else:
                        blocks = [(b * 512, 512) for b in range(4)] + [(2048, 256)]
                    conv = kind in ("x", "B", "C")
                    sa = j % 2
                    oa = j % 2
                    for (t0, n) in blocks:
                        b_ = bank % 8
                        bank += 1
                        for k in range(16):
                            sc.op("pe", lambda e, a=a, k=k, t0=t0, n=n, b_=b_: e.matmul(
                                pbank[b_][:, :n], lhsT=wt[a][:, k, :], rhs=hT[:, k, t0:t0 + n],
                                start=(k == 0), stop=(k == 15)), reads=[wt_b[a]], writes=[pb_b[b_]])
                        if conv:
                            sc.op("act", lambda e, sa=sa, t0=t0, n=n, b_=b_: e.activation(
                                out=stage[sa][:, 2 + t0:2 + t0 + n], in_=pbank[b_][:, :n], func=AF.Copy),
                                reads=[pb_b[b_]], writes=[stage_b[sa]])
                        else:
                            fn = AF.Copy if kind == "u" else AF.Silu
                            sc.op("act", lambda e, oa=oa, t0=t0, n=n, b_=b_, fn=fn: e.activation(
                                out=osb[oa][:, t0:t0 + n], in_=pbank[b_][:, :n], func=fn),
                                reads=[pb_b[b_]], writes=[osb_b[oa]])
                    if kind == "u":
                        T, dst = S, uT_d[j * 128:(j + 1) * 128, :]
                    elif kind == "z":
                        T, dst = NLOC, zsT_d[(j - 8) * 128:(j - 7) * 128, :]
                    elif kind == "x":
                        T, dst = S, xT_d[(j - 32) * 128:(j - 31) * 128, :]
                    elif kind == "B":
                        T, dst = S, BT_d[(j - 56) * 128:(j - 55) * 128, :]
                    else:
                        T, dst = NLOC, CT_d[(j - 64) * 128:(j - 63) * 128, :]
                    if conv:
                        jj = j - 32
                        sc.op("dve", lambda e, sa=sa, jj=jj, T=T: e.tensor_scalar(
                            out=acc[:, :T], in0=stage[sa][:, 0:T], scalar1=cw[:, jj, 0:1], scalar2=cb[:, jj:jj + 1],
                            op0=ALU.mult, op1=ALU.add), reads=[stage_b[sa], cw_b], writes=[acc_b])
                        for tap in range(1, 5):
                            eng = "dve"
                            sc.op(eng, lambda e, sa=sa, jj=jj, T=T, tap=tap: e.scalar_tensor_tensor(
                                out=acc[:, :T], in0=stage[sa][:, tap:tap + T], scalar=cw[:, jj, tap:tap + 1],
                                in1=acc[:, :T], op0=ALU.mult, op1=ALU.add),
                                reads=[stage_b[sa], cw_b, acc_b], writes=[acc_b])
                        sc.op("act", lambda e, oa=oa, T=T: e.activation(out=osb[oa][:, :T], in_=acc[:, :T], func=AF.Silu),
                              reads=[acc_b], writes=[osb_b[oa]])
                    sc.dma("act", lambda e, oa=oa, T=T, dst=dst: e.dma_start(out=dst, in_=osb[oa][:, :T]), src=osb_b[oa])
                sc.emit()
        if stop_after <= 2:
            return nc

        with ExitStack() as st:
            sc = Sched(nc, pool)
            wm = sb(st, "wm", [128, 8, 128], BF16)
            ccs = sb(st, "ccs", [128, 2, 128], BF16)
            Mg = sb(st, "Mg", [128, 8, 256], BF16)
            uTg = [sb(st, f"uTg{i}", [128, S], BF16) for i in range(2)]
            Vsb = [sb(st, f"Vsb{i}", [128, 32, 256], BF16) for i in range(2)]
            pM = [ps(st, f"pM{i}", [128, 512], F32) for i in range(4)]
            wm_b, ccs_b, Mg_b = sc.buf("wm"), sc.buf("ccs"), sc.buf("Mg")
            uTg_b, Vsb_b, pM_b = sc.bufs_n("uTg", 2), sc.bufs_n("Vsb", 2), sc.bufs_n("pM", 4)
            sc.dma("pool", lambda e: e.dma_start(out=wm[:], in_=fw_d.rearrange("g c d -> c g d")), dst=wm_b)
            sc.dma("sp", lambda e: e.dma_start(out=ccs[:], in_=ccs_d.rearrange("q c d -> c q d")), dst=ccs_b)
            for g in range(8):
                b_ = g % 4
                for q in range(2):
                    sc.op("pe", lambda e, g=g, q=q, b_=b_: e.matmul(pM[b_][:, q * 128:(q + 1) * 128], lhsT=ccs[:, q, :],
                                                                  rhs=wm[:, g, :], start=True, stop=True),
                          reads=[wm_b, ccs_b], writes=[pM_b[b_]])
                sc.op("act", lambda e, g=g, b_=b_: e.activation(out=Mg[:, g, :], in_=pM[b_][:, :256], func=AF.Copy),
                      reads=[pM_b[b_]], writes=[Mg_b])
            cnt = 0
            for g in range(8):
                a = g % 2
                sc.dma("sp", lambda e, g=g, a=a: e.dma_start(out=uTg[a][:], in_=uT_d[g * 128:(g + 1) * 128, :]),
                       dst=uTg_b[a])
                for stl in range(32):
                    b_ = cnt % 4
                    cnt += 1
                    sc.op("pe", lambda e, g=g, a=a, stl=stl, b_=b_: e.matmul(
                        pM[b_][:, :256], lhsT=uTg[a][:, stl * 128:(stl + 1) * 128], rhs=Mg[:, g, :],
                        start=True, stop=True), reads=[uTg_b[a], Mg_b], writes=[pM_b[b_]])
                    eng = "act" if stl % 2 == 0 else "dve"
                    if eng == "act":
                        sc.op("act", lambda e, a=a, stl=stl, b_=b_: e.activation(out=Vsb[a][:, stl, :], in_=pM[b_][:, :256],
                                                                              func=AF.Copy),
                              reads=[pM_b[b_]], writes=[Vsb_b[a]])
                    else:
                        sc.op("dve", lambda e, a=a, stl=stl, b_=b_: e.tensor_copy(out=Vsb[a][:, stl, :], in_=pM[b_][:, :256]),
                              reads=[pM_b[b_]], writes=[Vsb_b[a]])
                sc.dma("pool", lambda e, g=g, a=a: e.dma_start(out=V_d[g], in_=Vsb[a][:]), src=Vsb_b[a])
            sc.emit()
        if stop_after <= 3:
            return nc
        with ExitStack() as st:
            sc = Sched(nc, pool)
            tabs = [sb(st, f"tab{i}", [128, 2, 32, 512], BF16) for i in range(2)]
            Vg = [sb(st, f"Vg{i}", [128, 32, 256], BF16) for i in range(2)]
            aT = [sb(st, f"aT{i}", [128, 512], BF16) for i in range(2)]
            pF = [ps(st, f"pF{i}", [128, 512], F32) for i in range(4)]
            tab_b, Vg_b, aT_b, pF_b = sc.bufs_n("tab", 2), sc.bufs_n("Vg", 2), sc.bufs_n("aT", 2), sc.bufs_n("pF", 4)
            cnt = 0
            kblocks = [(b * 512, 512) for b in range(4)] + [(2048, 128)]
            for kb, (k0, n) in enumerate(kblocks):
                ta = kb % 2
                for q in range(2):
                    sc.dma("sp", lambda e, ta=ta, q=q, k0=k0, n=n: e.dma_start(
                        out=tabs[ta][:, q, :, :n], in_=tab_d[q][:, :, k0:k0 + n]),
                        dst=tab_b[ta])
                for g in range(8):
                    a = cnt % 2
                    b_ = cnt % 4
                    cnt += 1
                    sc.dma("sp", lambda e, g=g, a=a: e.dma_start(out=Vg[a][:], in_=V_d[g]), dst=Vg_b[a])
                    for stl in range(32):
                        for q in range(2):
                            sc.op("pe", lambda e, a=a, ta=ta, stl=stl, q=q, n=n, b_=b_: e.matmul(
                                pF[b_][:, :n], lhsT=Vg[a][:, stl, q * 128:(q + 1) * 128], rhs=tabs[ta][:, q, stl, :n],
                                start=(stl == 0 and q == 0), stop=(stl == 31 and q == 1)),
                                reads=[Vg_b[a], tab_b[ta]], writes=[pF_b[b_]])
                    sc.op("act", lambda e, a=a, n=n, b_=b_: e.activation(out=aT[a][:, :n], in_=pF[b_][:, :n], func=AF.Copy),
                          reads=[pF_b[b_]], writes=[aT_b[a]])
                    sc.dma("act", lambda e, g=g, a=a, k0=k0, n=n: e.dma_start(out=mixT_d[g * 128:(g + 1) * 128, k0:k0 + n],
                                                                        in_=aT[a][:, :n]), src=aT_b[a])
            sc.emit()
        if stop_after <= 4:
            return nc

        with ExitStack() as s4:
            prm = sb(s4, "prm", [128, 5, 48], F32)
            dts = sb(s4, "dts", [128, 2, 32, 48], F32)
            adt = sb(s4, "adt", [128, 2, 32, 48], F32)
            nw = sb(s4, "nw", [128, 24], F32)
            nmk = sb(s4, "nmk", [128, 2, 128], BF16)
            stateT = sb(s4, "stateT", [128, SW], F32)
            prevb = sb(s4, "prevb", [128, SW], BF16)
            xTc = [sb(s4, f"xTc{i}", [128, 24, 128], BF16) for i in range(2)]
            BTc = [sb(s4, f"BTc{i}", [128, 8, 128], BF16) for i in range(2)]
            CTc = [sb(s4, f"CTc{i}", [128, 8, 128], BF16) for i in range(2)]
            xtok2 = [sb(s4, f"xtok{i}", [128, SW], BF16) for i in range(2)]
            Btok2 = [sb(s4, f"Btok{i}", [128, 1024], BF16) for i in range(2)]
            xd2 = [sb(s4, f"xd{i}", [128, SW], BF16) for i in range(2)]
            xdw2 = [sb(s4, f"xdw{i}", [128, SW], BF16) for i in range(2)]
            sm2 = sb(s4, "sm", [128, 2, 8, 48], F32)
            CBm8 = [sb(s4, f"CBm{i}", [128, 128], F32) for i in range(8)]
            Eb = [sb(s4, f"Eb{i}", [128, 128], F32) for i in range(4)]
            MT = [sb(s4, f"MT{i}", [128, 128], BF16) for i in range(6)]
            yoff = [sb(s4, f"yoff{i}", [128, 384], F32) for i in range(2)]
            ysb = sb(s4, "ysb", [128, SW], F32)
            ybc = sb(s4, "ybc", [128, SW], F32)
            zsc = sb(s4, "zsc", [128, 24, 128], BF16)
            yg = sb(s4, "yg", [128, 24, 128], F32)
            sq = sb(s4, "sq", [128, 24, 128], F32)
            rsg = [sb(s4, f"rsg{i}", [128, 128], F32) for i in range(2)]
            osb4 = sb(s4, "osb4", [128, 24, 128], BF16)
            pTr = [ps(s4, f"pTr{i}", [128, 1024], BF16)[:, 0:512] for i in range(2)]
            pA = [ps(s4, f"pA{i}", [128, 512], F32) for i in range(2)]
            pY = [ps(s4, f"pY{i}", [128, 512], F32) for i in range(2)]
            pO = [ps(s4, f"pO{i}", [128, 512], F32) for i in range(2)]

            LAGH = 4
            NE, NM = 4, 6
            for direction in ("b", "f"):
                sc = Sched(nc, pool)
                di = 1 if direction == "b" else 0
                sm = xd = xdw = None
                Tm = Lmat if direction == "b" else Umat
                prm_b, dts_b, adt_b, nw_b = sc.buf("prm"), sc.buf("dts"), sc.buf("adt"), sc.buf("nw")
                state_b = sc.bufs_n("state", 8)
                prev_b = sc.bufs_n("prev", 8)
                xTc_b, BTc_b, CTc_b = sc.bufs_n("xTc", 2), sc.bufs_n("BTc", 2), sc.bufs_n("CTc", 2)
                xtok_b2, Btok_b2 = sc.bufs_n("xtok", 2), sc.bufs_n("Btok", 2)
                xd_b2, xdw_b2, sm_b2 = sc.bufs_n("xd", 2), sc.bufs_n("xdw", 2), sc.bufs_n("sm", 2)
                CBm8_b, Eb_b, MT_b, yoff_b = sc.bufs_n("CBm", 8), sc.bufs_n("Eb", 4), sc.bufs_n("MT", 6), sc.bufs_n("yoff", 2)
                ysb_g = sc.bufs_n("ysb", 8)
                ybc_b, zsc_b, yg_b, sq_b, osb4_b = sc.buf("ybc"), sc.buf("zsc"), sc.buf("yg"), sc.buf("sq"), sc.buf("osb4")
                rsg_b = sc.bufs_n("rsg", 2)
                pTr_b, pY_b, pO_b = sc.bufs_n("pTr", 2), sc.bufs_n("pY", 2), sc.bufs_n("pO", 2)
                pA_b = sc.bufs_n("pA", 2)
                pSm, pSm_b = pO[1][:, 384:512], pO_b[1]
                if direction == "b":
                    sc.dma("sp", lambda e, sm=sm, xd=xd, xdw=xdw: e.dma_start(out=prm[:].rearrange("p a h -> p (a h)"),
                                                       in_=ssp_d.rearrange("a h -> (a h)").partition_broadcast(128)), dst=prm_b)
                    sc.dma("sp", lambda e, sm=sm, xd=xd, xdw=xdw: e.dma_start(out=nw[:], in_=nw_d), dst=nw_b)
                    sc.dma("sp", lambda e: e.dma_start(out=nmk[:], in_=nmk_d.rearrange("q p f -> p q f")), dst=nw_b)
                    for d2 in range(2):
                        bias = prm[:, 2 * d2, :].unsqueeze(1).to_broadcast([128, 32, 48])
                        alog = prm[:, 2 * d2 + 1, :]
                        sc.op("dve", lambda e, d2=d2, bias=bias, sm=sm, xd=xd, xdw=xdw: e.tensor_tensor(out=dts[:, d2], in0=dtraw[:], in1=bias, op=ALU.add),
                              reads=[prm_b], writes=[dts_b])
                        sc.op("act", lambda e, d2=d2, sm=sm, xd=xd, xdw=xdw: e.activation(out=dts[:, d2], in_=dts[:, d2], func=AF.Exp),
                              reads=[dts_b], writes=[dts_b])
                        sc.op("act", lambda e, d2=d2, sm=sm, xd=xd, xdw=xdw: e.activation(out=dts[:, d2], in_=dts[:, d2], func=AF.Ln, bias=1.0),
                              reads=[dts_b], writes=[dts_b])
                        sc.op("act", lambda e, alog=alog, sm=sm, xd=xd, xdw=xdw: e.activation(out=alog, in_=alog, func=AF.Exp),
                              reads=[prm_b], writes=[prm_b])
                        sc.op("dve", lambda e, alog=alog, sm=sm, xd=xd, xdw=xdw: e.tensor_scalar(out=alog, in0=alog, scalar1=-1.0, scalar2=None, op0=ALU.mult),
                              reads=[prm_b], writes=[prm_b])
                        sc.op("dve", lambda e, d2=d2, alog=alog, sm=sm, xd=xd, xdw=xdw: e.tensor_tensor(
                            out=adt[:, d2], in0=dts[:, d2], in1=alog.unsqueeze(1).to_broadcast([128, 32, 48]), op=ALU.mult),
                            reads=[prm_b, dts_b], writes=[adt_b])
                sc.op("pool", lambda e, sm=sm, xd=xd, xdw=xdw: e.memset(stateT[:], 0.0), writes=state_b)
                sc.op("pool", lambda e, sm=sm, xd=xd, xdw=xdw: e.memset(prevb[:], 0.0), writes=prev_b)
                chunks = list(range(31, -1, -1)) if direction == "b" else list(range(NCH_LOC))
                if "KDBG4_LIST" in os.environ:
                    chunks = [int(v) for v in os.environ["KDBG4_LIST"].split(",")]
                if "KDBG4_CHUNKS" in os.environ:
                    chunks = chunks[:int(os.environ["KDBG4_CHUNKS"])]
                hcnt = 0
                gcnt = 0
                def prologue(ci):
                        c = chunks[ci]
                        local = c < NCH_LOC and os.environ.get("KDBG4_LOCAL", "1") == "1"
                        a = ci % 2
                        t0 = c * 128
                        sm, sm_b = sm2[:, a], sm_b2[a]
                        xd, xd_b, xdw, xdw_b = xd2[a], xd_b2[a], xdw2[a], xdw_b2[a]
                        xtok, xtok_b, Btok, Btok_b = xtok2[a], xtok_b2[a], Btok2[a], Btok_b2[a]
                        sc.dma("sp", lambda e, a=a, t0=t0, sm=sm, xd=xd, xdw=xdw: e.dma_start(
                            out=xTc[a][:], in_=xT_d.rearrange("(j p) t -> p j t", p=128)[:, :, t0:t0 + 128]), dst=xTc_b[a])
                        sc.dma("sp", lambda e, a=a, t0=t0, sm=sm, xd=xd, xdw=xdw: e.dma_start(
                            out=BTc[a][:], in_=BT_d.rearrange("(j p) t -> p j t", p=128)[:, :, t0:t0 + 128]), dst=BTc_b[a])
                        if local:
                            sc.dma("sp", lambda e, a=a, t0=t0, sm=sm, xd=xd, xdw=xdw: e.dma_start(
                                out=CTc[a][:], in_=CT_d.rearrange("(j p) t -> p j t", p=128)[:, :, t0:t0 + 128]), dst=CTc_b[a])
                        for q in range(8):
                            pi = q % 2
                            for r in range(4):
                                j = 4 * q + r
                                src = xTc[a][:, j, :] if j < 24 else BTc[a][:, j - 24, :]
                                sc.op("pe", lambda e, src=src, pi=pi, r=r, sm=sm, xd=xd, xdw=xdw: e.transpose(out=pTr[pi][:, r * 128:(r + 1) * 128],
                                                                                      in_=src, identity=idb[:]),
                                      reads=[xTc_b[a], BTc_b[a]], writes=[pTr_b[pi]])
                            if q < 6:
                                sc.op("act", lambda e, q=q, pi=pi, sm=sm, xd=xd, xdw=xdw: e.activation(out=xtok[:, q * 512:(q + 1) * 512], in_=pTr[pi],
                                                                               func=AF.Copy), reads=[pTr_b[pi]], writes=[xtok_b])
                            else:
                                sc.op("act", lambda e, q=q, pi=pi, sm=sm, xd=xd, xdw=xdw: e.activation(out=Btok[:, (q - 6) * 512:(q - 5) * 512], in_=pTr[pi],
                                                                               func=AF.Copy), reads=[pTr_b[pi]], writes=[Btok_b])
                        sc.op("pe", lambda e, c=c, Tm=Tm, di=di, sm=sm, xd=xd, xdw=xdw: e.matmul(pSm[:, 0:48], lhsT=Tm, rhs=adt[:, di, c, :], start=True, stop=True),
                              reads=[adt_b], writes=[pSm_b])
                        sc.op("pe", lambda e, c=c, di=di, sm=sm, xd=xd, xdw=xdw: e.matmul(pSm[:, 48:96], lhsT=ones_f, rhs=adt[:, di, c, :], start=True, stop=True),
                              reads=[adt_b], writes=[pSm_b])
                        sc.op("act", lambda e, sm=sm, xd=xd, xdw=xdw: e.activation(out=sm[:, 0:2, :].rearrange("p a h -> p (a h)"), in_=pSm[:, 0:96], func=AF.Copy),
                              reads=[pSm_b], writes=[sm_b])
                        sc.op("dve", lambda e, sm=sm, xd=xd, xdw=xdw: e.tensor_tensor(out=sm[:, 2, :], in0=sm[:, 1, :], in1=sm[:, 0, :], op=ALU.subtract),
                              reads=[sm_b], writes=[sm_b])
                        sc.op("act", lambda e, sm=sm, xd=xd, xdw=xdw: e.activation(out=sm[:, 3, :], in_=sm[:, 2, :], func=AF.Exp), reads=[sm_b], writes=[sm_b])
                        sc.op("act", lambda e, sm=sm, xd=xd, xdw=xdw: e.activation(out=sm[:, 6, :], in_=sm[:, 1, :], func=AF.Exp), reads=[sm_b], writes=[sm_b])
                        if local:
                            sc.op("act", lambda e, sm=sm, xd=xd, xdw=xdw: e.activation(out=sm[:, 4, :], in_=sm[:, 0, :], func=AF.Exp), reads=[sm_b], writes=[sm_b])
                            sc.op("dve", lambda e, sm=sm, xd=xd, xdw=xdw: e.tensor_scalar(out=sm[:, 5, :], in0=sm[:, 0, :], scalar1=-1.0, scalar2=None, op0=ALU.mult),
                                  reads=[sm_b], writes=[sm_b])
                        sc.op("dve", lambda e, c=c, di=di, sm=sm, xd=xd, xdw=xdw: e.tensor_tensor(out=sm[:, 7, :], in0=dts[:, di, c, :], in1=sm[:, 3, :], op=ALU.mult),
                              reads=[sm_b, dts_b], writes=[sm_b])
                        x3 = xtok[:].rearrange("p (h q) -> p h q", q=64)
                        sc.op("pool", lambda e, x3=x3, sm=sm, xd=xd, xdw=xdw: e.tensor_tensor(
                            out=xdw[:].rearrange("p (h q) -> p h q", q=64), in0=x3,
                            in1=sm[:, 7, :].unsqueeze(2).to_broadcast([128, 48, 64]), op=ALU.mult),
                            reads=[xtok_b, sm_b], writes=[xdw_b])
                        if local:
                            sc.op("dve", lambda e, x3=x3, c=c, di=di, sm=sm, xd=xd, xdw=xdw: e.tensor_tensor(
                                out=xd[:].rearrange("p (h q) -> p h q", q=64), in0=x3,
                                in1=dts[:, di, c, :].unsqueeze(2).to_broadcast([128, 48, 64]), op=ALU.mult),
                                reads=[xtok_b, dts_b], writes=[xd_b])

                if chunks:
                    prologue(0)
                for ci, c in enumerate(chunks):
                    local = c < NCH_LOC and os.environ.get("KDBG4_LOCAL", "1") == "1"
                    a = ci % 2
                    t0 = c * 128
                    sm, sm_b = sm2[:, a], sm_b2[a]
                    xd, xd_b, xdw, xdw_b = xd2[a], xd_b2[a], xdw2[a], xdw_b2[a]
                    xtok, xtok_b, Btok, Btok_b = xtok2[a], xtok_b2[a], Btok2[a], Btok_b2[a]
                    x3 = xtok[:].rearrange("p (h q) -> p h q", q=64)
                    did_next = False
                    if local:
                        for g in range(8):
                            ga = g % 2
                            sc.op("pe", lambda e, a=a, g=g, ga=ga: e.matmul(pO[ga][:, 384:512], lhsT=BTc[a][:, g, :], rhs=CTc[a][:, g, :],
                                                                           start=True, stop=True),
                                  reads=[BTc_b[a], CTc_b[a]], writes=[pO_b[ga]])
                            sc.op("dve", lambda e, g=g, ga=ga: e.tensor_copy(out=CBm8[g][:], in_=pO[ga][:, 384:512]),
                                  reads=[pO_b[ga]], writes=[CBm8_b[g]])

                        def headA(h, sm=sm, sm_b=sm_b):
                            nonlocal hcnt
                            g = h // 6
                            ha = hcnt % NM
                            ea = hcnt % NE
                            pb_ = hcnt % 2
                            hcnt += 1
                            sc.op("pe", lambda e: e.matmul(
                                pA[pb_][:, 0:128], lhsT=sm[:, 0, h:h + 1].to_broadcast([128, 128]),
                                rhs=id_f, start=True, stop=False), reads=[sm_b], writes=[pA_b[pb_]])
                            sc.op("pe", lambda e: e.matmul(pA[pb_][:, 0:128], lhsT=idb[:], rhs=nmk[:, di, :],
                                                           start=False, stop=True), reads=[nw_b], writes=[pA_b[pb_]])
                            sc.op("act", lambda e: e.activation(
                                out=Eb[ea][:], in_=pA[pb_][:, 0:128], func=AF.Exp,
                                bias=sm[:, 5, h:h + 1], scale=1.0), reads=[pA_b[pb_], sm_b], writes=[Eb_b[ea]])
                            sc.op("dve", lambda e: e.tensor_tensor(
                                out=MT[ha][:], in0=Eb[ea][:], in1=CBm8[g][:], op=ALU.mult),
                                reads=[Eb_b[ea], CBm8_b[g]], writes=[MT_b[ha]])
                            return ha

                        def headY(h, ha, xd=xd, xd_b=xd_b):
                            g, r = h // 6, h % 6
                            ga = g % 2
                            sc.op("pe", lambda e: e.matmul(
                                pY[ga][:, r * 64:(r + 1) * 64], lhsT=MT[ha][:], rhs=xd[:, h * 64:(h + 1) * 64],
                                start=True, stop=True), reads=[MT_b[ha], xd_b], writes=[pY_b[ga]])

                        def gtail(g, a=a, sm=sm, sm_b=sm_b):
                            ga = g % 2
                            gs = slice(g * 384, (g + 1) * 384)
                            sc.op("pe", lambda e: e.matmul(pO[ga][:, 0:384], lhsT=CTc[a][:, g, :], rhs=prevb[:, gs], start=True, stop=True),
                                  reads=[CTc_b[a], prev_b[g]], writes=[pO_b[ga]])
                            for r in range(6):
                                h = 6 * g + r
                                sc.op("act", lambda e, r=r, h=h: e.activation(
                                    out=yoff[ga][:, r * 64:(r + 1) * 64], in_=pO[ga][:, r * 64:(r + 1) * 64], func=AF.Copy,
                                    scale=sm[:, 4, h:h + 1]), reads=[pO_b[ga], sm_b], writes=[yoff_b[ga]])
                            sc.op("dve", lambda e: e.tensor_tensor(out=ysb[:, gs], in0=pY[ga][:, 0:384], in1=yoff[ga][:], op=ALU.add),
                                  reads=[pY_b[ga], yoff_b[ga]], writes=[ysb_g[g]])

                        has = {}
                        for idx in range(48 + LAGH):
                            if idx < 48:
                                has[idx] = headA(idx)
                            if idx >= LAGH:
                                hh = idx - LAGH
                                headY(hh, has[hh])
                                if hh % 6 == 5:
                                    gtail(hh // 6)
                            if idx == 24 and ci + 1 < len(chunks) and not did_next:
                                prologue(ci + 1)
                                did_next = True
                    if ci + 1 < len(chunks) and not did_next:
                        prologue(ci + 1)
                        did_next = True
                    for g in range(8):
                        gs = slice(g * 384, (g + 1) * 384)
                        ga = g % 2
                        sc.op("pe", lambda e, g=g, ga=ga, gs=gs, sm=sm, xd=xd, xdw=xdw, Btok=Btok: e.matmul(pO[ga][:, 0:384],
                                                                          lhsT=Btok[:, g * 128:(g + 1) * 128], rhs=xdw[:, gs],
                                                                          start=True, stop=True),
                              reads=[Btok_b, xdw_b], writes=[pO_b[ga]])
                        st3 = stateT[:, gs].rearrange("p (h q) -> p h q", q=64)
                        sc.op("pool", lambda e, g=g, st3=st3, sm=sm, xd=xd, xdw=xdw: e.tensor_tensor(
                            out=st3, in0=st3, in1=sm[:, 6, 6 * g:6 * g + 6].unsqueeze(2).to_broadcast([128, 6, 64]), op=ALU.mult),
                            reads=[sm_b, state_b[g]], writes=[state_b[g]])
                        sc.op("dve", lambda e, ga=ga, gs=gs, sm=sm, xd=xd, xdw=xdw: e.tensor_tensor(out=stateT[:, gs], in0=pO[ga][:, 0:384], in1=stateT[:, gs], op=ALU.add),
                              reads=[pO_b[ga], state_b[g]], writes=[state_b[g]])
                        sc.op("act", lambda e, gs=gs, sm=sm, xd=xd, xdw=xdw: e.activation(out=prevb[:, gs], in_=stateT[:, gs], func=AF.Copy),
                              reads=[state_b[g]], writes=[prev_b[g]])
                    if local and direction == "b":
                        for g in range(8):
                            gs = slice(g * 384, (g + 1) * 384)
                            sc.dma("sp", lambda e, t0=t0, gs=gs, sm=sm, xd=xd, xdw=xdw: e.dma_start(out=yb_d[t0:t0 + 128, gs], in_=ysb[:, gs]), src=ysb_g[g])
                    if direction == "f":
                        sc.dma("sp", lambda e, t0=t0, sm=sm, xd=xd, xdw=xdw: e.dma_start(out=ybc[:], in_=yb_d[t0:t0 + 128, :]), dst=ybc_b)
                        sc.dma("sp", lambda e, t0=t0, sm=sm, xd=xd, xdw=xdw: e.dma_start(
                            out=zsc[:], in_=zsT_d.rearrange("(j p) t -> p j t", p=128)[:, :, t0:t0 + 128]), dst=zsc_b)
                        sc.op("pool", lambda e, sm=sm, xd=xd, xdw=xdw: e.tensor_tensor(out=ybc[:], in0=ybc[:], in1=ysb[:], op=ALU.add),
                              reads=[ybc_b] + ysb_g, writes=[ybc_b])
                        sc.op("dve", lambda e, x3=x3, sm=sm, xd=xd, xdw=xdw: e.tensor_tensor(
                            out=xd[:].rearrange("p (h q) -> p h q", q=64), in0=x3,
                            in1=prm[:, 4, :].unsqueeze(2).to_broadcast([128, 48, 64]), op=ALU.mult),
                            reads=[xtok_b, prm_b, xd_b], writes=[xd_b])
                        sc.op("dve", lambda e, sm=sm, xd=xd, xdw=xdw: e.tensor_tensor(out=ybc[:], in0=ybc[:], in1=xd[:], op=ALU.add),
                              reads=[ybc_b, xd_b], writes=[ybc_b])
                        for q in range(6):
                            pi = q % 2
                            for r in range(4):
                                j = 4 * q + r
                                sc.op("pe", lambda e, j=j, pi=pi, r=r, sm=sm, xd=xd, xdw=xdw: e.transpose(out=pA[pi][:, r * 128:(r + 1) * 128],
                                                                                  in_=ybc[:, j * 128:(j + 1) * 128], identity=id_f),
                                      reads=[ybc_b], writes=[pA_b[pi]])
                            sc.op("dve", lambda e, q=q, pi=pi, sm=sm, xd=xd, xdw=xdw: e.tensor_tensor(
                                out=yg[:, 4 * q:4 * q + 4, :], in0=pA[pi][:].rearrange("p (a b) -> p a b", a=4),
                                in1=zsc[:, 4 * q:4 * q + 4, :], op=ALU.mult), reads=[pA_b[pi], zsc_b], writes=[yg_b])
                        sc.op("act", lambda e, sm=sm, xd=xd, xdw=xdw: e.activation(out=sq[:], in_=yg[:], func=AF.Square), reads=[yg_b], writes=[sq_b])
                        for g in range(8):
                            ga = g % 2
                            for jj in range(3):
                                sc.op("pe", lambda e, g=g, jj=jj, ga=ga, sm=sm, xd=xd, xdw=xdw: e.matmul(pY[ga][:, 0:128], lhsT=ones_f, rhs=sq[:, 3 * g + jj, :],
                                                                                 start=(jj == 0), stop=(jj == 2)),
                                      reads=[sq_b], writes=[pY_b[ga]])
                            sc.op("act", lambda e, ga=ga: e.activation(out=rsg[ga][:], in_=pY[ga][:, 0:128], func=AF.Sqrt, bias=epsc[:],
                                                                      scale=1.0 / 384.0), reads=[pY_b[ga]], writes=[rsg_b[ga]])
                            sc.op("dve", lambda e, ga=ga: e.reciprocal(out=rsg[ga][:], in_=rsg[ga][:]), reads=[rsg_b[ga]], writes=[rsg_b[ga]])
                            for jj in range(3):
                                j = 3 * g + jj
                                sc.op("dve", lambda e, j=j, ga=ga, sm=sm, xd=xd, xdw=xdw: e.scalar_tensor_tensor(
                                    out=osb4[:, j, :], in0=yg[:, j, :], scalar=nw[:, j:j + 1], in1=rsg[ga][:],
                                    op0=ALU.mult, op1=ALU.mult), reads=[yg_b, rsg_b[ga], nw_b], writes=[osb4_b])
                        sc.dma("sp", lambda e, t0=t0, sm=sm, xd=xd, xdw=xdw: e.dma_start(
                            out=mixT_d[1024:4096, :].rearrange("(j p) t -> p j t", p=128)[:, :, t0:t0 + 128], in_=osb4[:]), src=osb4_b)
                sc.emit()
                if direction == "b" and stop_after <= 5:
                    return nc
        if stop_after <= 6:
            return nc

        with ExitStack() as st:
            sc = Sched(nc, pool)
            wout = sb(st, "wout", [128, 32, D], BF16)
            mt = [sb(st, f"mt{i}", [128, 32, 128], BF16) for i in range(2)]
            xt = [sb(st, f"xt5{i}", [128, D], F32) for i in range(1)]
            wst = [sb(st, f"wst5{i}", [128, D], F32) for i in range(2)]
            x1 = sb(st, "x1s", [128, D], F32)
            hb = sb(st, "hb5", [128, D], BF16)
            junk = sb(st, "junk5", [128, D], BF16)
            h2s = [sb(st, f"h2s{i}", [128, 16, 128], BF16) for i in range(1)] * 2
            gain = sb(st, "gain2", [128, D], F32)
            ss = sb(st, "ss5", [128, 64], F32)
            pb = [ps(st, f"pb5{i}", [128, 512], F32) for i in range(4)]
            pt = [ps(st, f"pt5{i}", [128, 1024], BF16)[:, 0:512] for i in range(4)]
            wout_b = sc.bufs_n("wout", 32)
            wst_b = sc.bufs_n("wst", 2)
            mt_b, xt_b, h2s_b, pb_b, pt_b = sc.bufs_n("mt", 2), sc.bufs_n("xt", 1), sc.bufs_n("h2s", 1) * 2, sc.bufs_n("pb", 4), sc.bufs_n("pt", 4)
            x1_b, hb_b, junk_b, gain_b = sc.buf("x1"), sc.buf("hb"), sc.buf("junk"), sc.buf("gain")
            sc.dma("sp", lambda e: e.dma_start(out=gain[:], in_=g2_d.partition_broadcast(128)), dst=gain_b)
            for j in range(32):
                wa = j % 2
                sc.dma("sp", lambda e, j=j, wa=wa: e.dma_start(out=wst[wa][:], in_=w_out_d[j * 128:(j + 1) * 128, :]), dst=wst_b[wa])
                if j % 2 == 0:
                    sc.op("act", lambda e, j=j, wa=wa: e.activation(out=wout[:, j, :], in_=wst[wa][:], func=AF.Copy),
                          reads=[wst_b[wa]], writes=[wout_b[j]])
                else:
                    sc.op("dve", lambda e, j=j, wa=wa: e.tensor_copy(out=wout[:, j, :], in_=wst[wa][:]),
                          reads=[wst_b[wa]], writes=[wout_b[j]])
            for i in range(NCH_LOC):
                a = i % 2
                t0 = i * 128
                ssb = sc.buf(f"ss{i}")
                sc.dma("sp", lambda e, a=a, t0=t0: e.dma_start(
                    out=mt[a][:], in_=mixT_d.rearrange("(j p) t -> p j t", p=128)[:, :, t0:t0 + 128]), dst=mt_b[a])
                sc.dma("sp", lambda e, t0=t0: e.dma_start(out=xt[0][:], in_=x_d[t0:t0 + 128, :]), dst=xt_b[0])
                for j in range(32):
                    for db in range(4):
                        sc.op("pe", lambda e, a=a, db=db, j=j: e.matmul(pb[db][:], lhsT=mt[a][:, j, :], rhs=wout[:, j, db * 512:(db + 1) * 512],
                                                                       start=(j == 0), stop=(j == 31)),
                              reads=[mt_b[a], wout_b[j]], writes=[pb_b[db]])
                for db in range(4):
                    sc.op("dve", lambda e, db=db: e.tensor_tensor(out=x1[:, db * 512:(db + 1) * 512], in0=pb[db][:],
                                                                 in1=xt[0][:, db * 512:(db + 1) * 512], op=ALU.add),
                          reads=[pb_b[db], xt_b[0]], writes=[x1_b])
                rms_rows(sc, x1[:], x1_b, ssb, ss[:, 2 * i:2 * i + 1], ss[:, 2 * i + 1:2 * i + 2], gain, gain_b, hb, hb_b, junk, junk_b)
                sc.dma("pool", lambda e, t0=t0: e.dma_start(out=x1_d[t0:t0 + 128, :], in_=x1[:]), src=x1_b)
                for q in range(4):
                    pi = q
                    for r in range(4):
                        k = 4 * q + r
                        sc.op("pe", lambda e, k=k, pi=pi, r=r: e.transpose(out=pt[pi][:, r * 128:(r + 1) * 128],
                                                                          in_=hb[:, k * 128:(k + 1) * 128], identity=idb[:]),
                              reads=[hb_b], writes=[pt_b[pi]])
                    sc.op("act", lambda e, a=a, q=q, pi=pi: e.activation(out=h2s[a][:, 4 * q:4 * q + 4, :],
                                                                        in_=pt[pi].rearrange("p (a b) -> p a b", a=4), func=AF.Copy),
                          reads=[pt_b[pi]], writes=[h2s_b[a]])
                sc.dma("act", lambda e, a=a, t0=t0: e.dma_start(
                    out=h2T_d.rearrange("(k p) t -> p k t", p=128)[:, :, t0:t0 + 128], in_=h2s[a][:]), src=h2s_b[a])
            sc.emit()
        if stop_after <= 7:
            return nc

        with ExitStack() as st:
            sc = Sched(nc, pool)
            h2r = sb(st, "h2r", [128, 16, NLOC], BF16)
            wup = [sb(st, f"wup{i}", [128, 16, 256], BF16) for i in range(2)]
            wst6 = [sb(st, f"wst6{i}", [128, 16, 128], F32) for i in range(4)]
            stg = [sb(st, f"stg{i}", [128, 2052], F32) for i in range(2)]
            accg = sb(st, "accg", [128, NOWN], F32)
            accv = sb(st, "accv", [128, NOWN], F32)
            sg = sb(st, "sg", [128, NOWN], F32)
            ao = [sb(st, f"ao{i}", [128, NOWN], BF16) for i in range(2)]
            fcw = sb(st, "fcw", [128, 88, 3], F32)
            fcb = sb(st, "fcb", [128, 88], F32)
            pb = [ps(st, f"pb6{i}", [128, 512], F32) for i in range(8)]
            h2r_b, fc_b = sc.buf("h2r"), sc.buf("fc")
            wup_b, stg_b, ao_b, pb_b = sc.bufs_n("wup", 4), sc.bufs_n("stg", 2), sc.bufs_n("ao", 2), sc.bufs_n("pb", 8)
            wst6_b = sc.bufs_n("wst6", 4)
            accg_b, accv_b, sg_b = sc.buf("accg"), sc.buf("accv"), sc.buf("sg")
            sc.dma("sp", lambda e: e.dma_start(out=h2r[:], in_=h2T_d.rearrange("(k p) t -> p k t", p=128)), dst=h2r_b)
            sc.dma("sp", lambda e: e.dma_start(out=fcw[:], in_=fcw_d), dst=fc_b)
            sc.dma("sp", lambda e: e.dma_start(out=fcb[:], in_=fcb_d), dst=fc_b)
            for i in range(2):
                sc.op("pool", lambda e, i=i: e.memset(stg[i][:], 0.0), writes=[stg_b[i]])
            wuv = w_up_d.rearrange("(k p) c -> p k c", p=128)
            bank = 0
            blocks = [(b * 512, 512) for b in range(4)] + [(2048, 1)]
            for f in range(44):
                a = f % 2
                for hv in range(2):
                    c0 = hv * FFN + f * 128
                    wa = 2 * a + hv
                    sc.dma("sp", lambda e, wa=wa, c0=c0: e.dma_start(out=wst6[wa][:], in_=wuv[:, :, c0:c0 + 128]), dst=wst6_b[wa])
                    if hv == 0:
                        sc.op("act", lambda e, a=a, hv=hv, wa=wa: e.activation(out=wup[a][:, :, hv * 128:(hv + 1) * 128], in_=wst6[wa][:],
                                                                              func=AF.Copy), reads=[wst6_b[wa]], writes=[wup_b[wa]])
                    else:
                        sc.op("act", lambda e, a=a, hv=hv, wa=wa: e.activation(out=wup[a][:, :, hv * 128:(hv + 1) * 128], in_=wst6[wa][:],
                                                                              func=AF.Copy), reads=[wst6_b[wa]], writes=[wup_b[wa]])
                for hv in range(2):
                    for (t0, n) in blocks:
                        b_ = bank % 8
                        bank += 1
                        for k in range(16):
                            sc.op("pe", lambda e, a=a, hv=hv, k=k, t0=t0, n=n, b_=b_: e.matmul(
                                pb[b_][:, :n], lhsT=wup[a][:, k, hv * 128:(hv + 1) * 128], rhs=h2r[:, k, t0:t0 + n],
                                start=(k == 0), stop=(k == 15)), reads=[wup_b[2 * a + hv], h2r_b], writes=[pb_b[b_]])
                        sc.op("act", lambda e, hv=hv, t0=t0, n=n, b_=b_: e.activation(out=stg[hv][:, 1 + t0:1 + t0 + n], in_=pb[b_][:, :n],
                                                                                     func=AF.Copy),
                              reads=[pb_b[b_]], writes=[stg_b[hv]])
                    jj = hv * 44 + f
                    eng = "dve"
                    acc_t, acc_tb = (accg, accg_b) if hv == 0 else (accv, accv_b)
                    sc.op(eng, lambda e, hv=hv, jj=jj, acc_t=acc_t: e.tensor_scalar(
                        out=acc_t[:], in0=stg[hv][:, 0:NOWN], scalar1=fcw[:, jj, 0:1], scalar2=fcb[:, jj:jj + 1],
                        op0=ALU.mult, op1=ALU.add), reads=[stg_b[hv], fc_b], writes=[acc_tb])
                    for tap in (1, 2):
                        sc.op("dve", lambda e, hv=hv, jj=jj, acc_t=acc_t, tap=tap: e.scalar_tensor_tensor(
                            out=acc_t[:], in0=stg[hv][:, tap:tap + NOWN], scalar=fcw[:, jj, tap:tap + 1], in1=acc_t[:],
                            op0=ALU.mult, op1=ALU.add), reads=[stg_b[hv], fc_b, acc_tb], writes=[acc_tb])
                sc.op("act", lambda e: e.activation(out=sg[:], in_=accg[:], func=AF.Silu), reads=[accg_b], writes=[sg_b])
                sc.op("dve", lambda e, a=a: e.tensor_tensor(out=ao[a][:], in0=sg[:], in1=accv[:], op=ALU.mult),
                      reads=[sg_b, accv_b], writes=[ao_b[a]])
                sc.dma("pool", lambda e, a=a, f=f: e.dma_start(out=actT_d[f * 128:(f + 1) * 128, :], in_=ao[a][:]), src=ao_b[a])
            sc.emit()
        if stop_after <= 8:
            return nc

        with ExitStack() as st:
            sc = Sched(nc, pool)
            wdn = [sb(st, f"wdn{i}", [128, 44, 512], BF16) for i in range(2)]
            wst7 = [sb(st, f"wst7{i}", [128, 4, 512], F32) for i in range(2)]
            at = [sb(st, f"at{i}", [128, 44, 128], BF16) for i in range(2)]
            x1t = [sb(st, f"x1t{i}", [128, 512], F32) for i in range(2)]
            x2s = [sb(st, f"x2s{i}", [128, 512], F32) for i in range(2)]
            junk = sb(st, "junk7", [128, 512], BF16)
            ssp = sb(st, "ssp", [128, 16, 4], F32)
            ss = sb(st, "ss7", [128, 32], F32)
            gain = sb(st, "gain3", [128, D], F32)
            xf = [sb(st, f"xf{i}", [128, D], F32) for i in range(2)]
            of = [sb(st, f"of{i}", [128, D], F32) for i in range(2)]
            pb = [ps(st, f"pb7{i}", [128, 512], F32) for i in range(4)]
            wdn_b, at_b, x1t_b, x2s_b, pb_b = sc.bufs_n("wdn", 22), sc.bufs_n("at", 2), sc.bufs_n("x1t", 2), sc.bufs_n("x2s", 2), sc.bufs_n("pb", 4)
            wst7_b = sc.bufs_n("wst7", 2)
            wcnt = 0
            junk_b, ssp_b, gain_b = sc.buf("junk"), sc.buf("ssp"), sc.buf("gain")
            xf_b, of_b = sc.bufs_n("xf", 2), sc.bufs_n("of", 2)
            sc.dma("sp", lambda e: e.dma_start(out=gain[:], in_=g3_d.partition_broadcast(128)), dst=gain_b)
            wdv = w_dn_d.rearrange("(j p) d -> p j d", p=128)
            cnt = 0
            for db in range(4):
                wa = db % 2
                for jg in range(11):
                    sa_ = wcnt % 2
                    wcnt += 1
                    sc.dma("sp", lambda e, sa_=sa_, jg=jg, db=db: e.dma_start(
                        out=wst7[sa_][:], in_=wdv[:, 4 * jg:4 * jg + 4, db * 512:(db + 1) * 512]), dst=wst7_b[sa_])
                    if jg % 2 == 0:
                        sc.op("act", lambda e, sa_=sa_, jg=jg, wa=wa: e.activation(out=wdn[wa][:, 4 * jg:4 * jg + 4, :], in_=wst7[sa_][:],
                                                                                 func=AF.Copy), reads=[wst7_b[sa_]], writes=[wdn_b[wa * 11 + jg]])
                    else:
                        sc.op("act", lambda e, sa_=sa_, jg=jg, wa=wa: e.activation(out=wdn[wa][:, 4 * jg:4 * jg + 4, :], in_=wst7[sa_][:],
                                                                                 func=AF.Copy), reads=[wst7_b[sa_]], writes=[wdn_b[wa * 11 + jg]])
                for i in range(16):
                    a = cnt % 2
                    b_ = cnt % 4
                    cnt += 1
                    t0 = i * 128
                    sc.dma("sp", lambda e, a=a, t0=t0: e.dma_start(
                        out=at[a][:], in_=actT_d.rearrange("(j p) t -> p j t", p=128)[:, :, t0:t0 + 128]), dst=at_b[a])
                    sc.dma("sp", lambda e, a=a, t0=t0, db=db: e.dma_start(out=x1t[a][:], in_=x1_d[t0:t0 + 128, db * 512:(db + 1) * 512]),
                           dst=x1t_b[a])
                    for j in range(44):
                        sc.op("pe", lambda e, a=a, wa=wa, j=j, b_=b_: e.matmul(pb[b_][:], lhsT=at[a][:, j, :], rhs=wdn[wa][:, j, :],
                                                                              start=(j == 0), stop=(j == 43)),
                              reads=[at_b[a], wdn_b[wa * 11 + j // 4]], writes=[pb_b[b_]])
                    sc.op("dve", lambda e, a=a, b_=b_: e.tensor_tensor(out=x2s[a][:], in0=pb[b_][:], in1=x1t[a][:], op=ALU.add),
                          reads=[pb_b[b_], x1t_b[a]], writes=[x2s_b[a]])
                    sc.op("act", lambda e, a=a, i=i, db=db: e.activation(out=junk[:], in_=x2s[a][:], func=AF.Square,
                                                                        accum_out=ssp[:, i, db:db + 1]),
                          reads=[x2s_b[a]], writes=[junk_b, ssp_b])
                    sc.dma("pool", lambda e, a=a, t0=t0, db=db: e.dma_start(out=x2_d[t0:t0 + 128, db * 512:(db + 1) * 512], in_=x2s[a][:]),
                           src=x2s_b[a])
            sc.emit()
            sc = Sched(nc, pool)
            xf_b, of_b, ssp_b, gain_b = sc.bufs_n("xf", 2), sc.bufs_n("of", 2), sc.buf("ssp"), sc.buf("gain")
            for i in range(16):
                a = i % 2
                t0 = i * 128
                sc.dma("sp", lambda e, a=a, t0=t0: e.dma_start(out=xf[a][:], in_=x2_d[t0:t0 + 128, :]), dst=xf_b[a])
                sc.op("dve", lambda e, i=i: e.tensor_reduce(out=ss[:, 2 * i:2 * i + 1], in_=ssp[:, i, :], axis=AX.X, op=ALU.add),
                      reads=[ssp_b], writes=[ssp_b])
                sc.op("act", lambda e, i=i: e.activation(out=ss[:, 2 * i + 1:2 * i + 2], in_=ss[:, 2 * i:2 * i + 1], func=AF.Sqrt,
                                                        bias=epsc[:], scale=1.0 / D), reads=[ssp_b], writes=[ssp_b])
                sc.op("dve", lambda e, i=i: e.reciprocal(out=ss[:, 2 * i + 1:2 * i + 2], in_=ss[:, 2 * i + 1:2 * i + 2]),
                      reads=[ssp_b], writes=[ssp_b])
                sc.op("dve", lambda e, a=a, i=i: e.scalar_tensor_tensor(out=of[a][:], in0=xf[a][:], scalar=ss[:, 2 * i + 1:2 * i + 2],
                                                                    in1=gain[:], op0=ALU.mult, op1=ALU.mult),
                      reads=[xf_b[a], ssp_b, gain_b], writes=[of_b[a]])
                sc.dma("pool", lambda e, a=a, t0=t0: e.dma_start(out=out_d[t0:t0 + 128, :], in_=of[a][:]), src=of_b[a])
            sc.emit()
    return nc


_CONST = {}


def _consts():
    if _CONST:
        return _CONST
    bf = ml_dtypes.bfloat16
    c = np.arange(128, dtype=np.float64)
    ang = 2.0 * np.pi * np.outer(c, c) / 128.0
    sc = 1.0 / np.sqrt(float(S) * 128.0)
    _CONST["ccs"] = np.stack([np.cos(ang) * sc, np.sin(ang) * sc]).astype(np.float32).astype(bf)
    s = np.arange(S, dtype=np.int64)[:, None]
    k = np.arange(NLOC, dtype=np.int64)[None, :]
    tabs = []
    for flip in (0, 1):
        prod = ((s + flip) * (k + flip)) % S
        a = 2.0 * np.pi * prod.astype(np.float64) / S
        t2 = np.stack([np.cos(a), -np.sin(a)]).astype(np.float32).astype(bf)
        tabs.append(np.ascontiguousarray(t2.reshape(2, 32, 128, NLOC).transpose(0, 2, 1, 3)))
    _CONST["tab"] = tabs
    i = np.arange(128)
    U = (i[:, None] <= i[None, :]).astype(np.float32)
    L = (i[:, None] >= i[None, :]).astype(np.float32)
    _CONST["cst"] = np.stack([U, L, np.ones((128, 128), np.float32), np.eye(128, dtype=np.float32)])
    _CONST["idb"] = np.eye(128, dtype=np.float32).astype(bf)
    NEG = np.float32(-1.0e5)
    mf = np.where(i[None, :] < i[:, None], NEG, np.float32(0))
    mb = np.where(i[None, :] > i[:, None], NEG, np.float32(0))
    _CONST["nmk"] = np.stack([mf, mb]).astype(np.float32).astype(bf)
    return _CONST


def _in_maps(inp):
    cs = _consts()
    f = lambda a: np.ascontiguousarray(np.asarray(a, dtype=np.float32))
    x = f(inp["x"])
    w_in, w_out, w_up, w_dn = f(inp["w_in"][0]), f(inp["w_out"][0]), f(inp["w_up"][0]), f(inp["w_down"][0])
    fw = f(inp["fourier_w"][0])
    g1, g2, g3 = f(inp["norm_mix_w"][0]), f(inp["norm_ffn_w"][0]), f(inp["norm_final_w"])
    cw, cb = f(inp["ssm_conv_w"][0]), f(inp["ssm_conv_b"][0])
    fcw, fcb = f(inp["ffn_conv_w"][0]), f(inp["ffn_conv_b"][0])
    nw = f(inp["ssm_norm_w"][0]).reshape(24, 128).T.copy()
    cb_l = cb.reshape(40, 128).T.copy()
    fcb_l = fcb.reshape(88, 128).T.copy()
    prm = [f(inp[k][0]) for k in ("dt_bias_fwd", "a_log_fwd", "dt_bias_bwd", "a_log_bwd", "ssm_d")]
    maps = []
    for c in range(8):
        b, hf = c // 2, c % 2
        if hf == 0:
            xc, cwc, fcwc = x[b], cw, fcw
            ssp = np.stack([prm[0], prm[1], prm[2], prm[3], prm[4]])
        else:
            xc, cwc, fcwc = np.ascontiguousarray(x[b, ::-1]), cw[::-1], fcw[::-1]
            ssp = np.stack([prm[2], prm[3], prm[0], prm[1], prm[4]])
        cw_l = np.ascontiguousarray(cwc.reshape(5, 40, 128).transpose(2, 1, 0))
        fcw_l = np.ascontiguousarray(fcwc.reshape(3, 88, 128).transpose(2, 1, 0))
        maps.append({"x": xc, "g1": g1, "w_in": w_in, "fw": fw, "ccs": cs["ccs"], "tab": cs["tab"][hf],
                     "cw": cw_l, "cb": cb_l, "ssp": np.ascontiguousarray(ssp), "nw": nw, "cst": cs["cst"],
                     "idb": cs["idb"], "nmk": cs["nmk"], "w_out": w_out, "g2": g2, "w_up": w_up, "fcw": fcw_l, "fcb": fcb_l,
                     "w_dn": w_dn, "g3": g3})
    return maps


_NC = {}


def kernel(**inputs):
    if "nc" not in _NC:
        _NC["nc"] = build_program()
    maps = _in_maps(inputs)
    res = run_bass_kernel_spmd(_NC["nc"], maps, core_ids=list(range(8)))
    out = np.empty((4, S, D), np.float32)
    for c in range(8):
        b, hf = c // 2, c % 2
        o = np.asarray(res.results[c]["out"], dtype=np.float32)
        if hf == 0:
            out[b, :NOWN] = o
        else:
            out[b, NOWN:] = o[::-1]
    return out
```
